# Optimizing a Trainium2 kernel written in Bass

```python
import math
import jax, jax.numpy as jnp
from jax import lax
import numpy as np

D_MODEL = 1024
BATCH = 2
SEQ = 16384
DEPTH = 2

N_MIXERS = 2
N_RWKV_LAYERS = (DEPTH + N_MIXERS - 1) // N_MIXERS
N_DIFF_LAYERS = DEPTH // N_MIXERS
NORM_EPS = 1e-6

RW_HEAD = 64
RW_HEADS = D_MODEL // RW_HEAD
RW_WIDTH = RW_HEADS * RW_HEAD
RW_N_PROJ = 4
RW_N_MIX = 6
DECAY_LORA = max(32, int(round(1.8 * D_MODEL ** 0.5 / 32)) * 32)
ICLR_LORA = max(32, int(round(1.8 * D_MODEL ** 0.5 / 32)) * 32)
GN_EPS = 64e-5

DA_HEADS = 8
DA_QK_DIM = D_MODEL // DA_HEADS // 2
DA_V_DIM = 2 * DA_QK_DIM
DA_WIDTH = DA_HEADS * DA_V_DIM
DA_SUBLN_EPS = 1e-5
ROPE_THETA = 10000.0
Q_BLOCK = 128

PLE_DIM = 256

kernel_name = 'rwkv7_diffattn_interleaved_trunk'


def rms_norm(x, g, eps=NORM_EPS):
    xf = x.astype(jnp.float32)
    y = xf * lax.rsqrt(jnp.mean(xf * xf, axis=-1, keepdims=True) + eps)
    return (y * g.astype(jnp.float32)).astype(x.dtype)


def token_shift(x):
    return jnp.pad(x, ((0, 0), (1, 0), (0, 0)))[:, :-1]


def rwkv7_mixer(h, mu, w_in, w0, w1, w2, a0, a1, a2, k_k, k_a, r_k, ln_w, ln_b, w_out):
    B, S, _ = h.shape
    H, N = RW_HEADS, RW_HEAD
    f32 = jnp.float32
    dx = token_shift(h) - h
    xm = h[:, :, None, :] + dx[:, :, None, :] * mu[:RW_N_PROJ]
    proj = jnp.einsum('bsgd,dgc->bsgc', xm, w_in.reshape(D_MODEL, RW_N_PROJ, RW_WIDTH))
    r, k, v, gate = proj[:, :, 0], proj[:, :, 1], proj[:, :, 2], proj[:, :, 3]
    xw = h + dx * mu[4]
    xa = h + dx * mu[5]
    w_log = -jax.nn.softplus(-(w0 + jnp.tanh(xw @ w1) @ w2).astype(f32)) - 0.5
    decay = jnp.exp(-jnp.exp(w_log))
    a = jax.nn.sigmoid((a0 + (xa @ a1) @ a2).astype(f32))
    kk = (k * k_k).reshape(B, S, H, N).astype(f32)
    kk = kk / jnp.maximum(jnp.sqrt(jnp.sum(kk * kk, axis=-1, keepdims=True)), 1e-12)
    k = k.astype(f32) * (1.0 + (a - 1.0) * k_a.astype(f32))

    def heads(t):
        return t.reshape(B, S, H, N).astype(f32)

    rh, kh, vh, wh, ah = heads(r), heads(k), heads(v), heads(decay), heads(a)
    bh = kk * ah

    def step(state, inp):
        r_t, w_t, k_t, v_t, kk_t, b_t = inp
        sa = jnp.einsum('bhij,bhj->bhi', state, kk_t)
        state = (state * w_t[:, :, None, :] - sa[..., None] * b_t[:, :, None, :]
                 + v_t[..., None] * k_t[:, :, None, :])
        y_t = jnp.einsum('bhij,bhj->bhi', state, r_t)
        return state, y_t

    seq_first = lambda t: jnp.swapaxes(t, 0, 1)
    s0 = jnp.zeros((B, H, N, N), f32)
    _, y = lax.scan(step, s0, (seq_first(rh), seq_first(wh), seq_first(kh),
                               seq_first(vh), seq_first(kk), seq_first(bh)))
    y = jnp.swapaxes(y, 0, 1)
    mean = jnp.mean(y, axis=-1, keepdims=True)
    var = jnp.mean(jnp.square(y - mean), axis=-1, keepdims=True)
    y = ((y - mean) * lax.rsqrt(var + GN_EPS) * ln_w.reshape(H, N).astype(f32)
         + ln_b.reshape(H, N).astype(f32))
    bonus = jnp.sum(rh * kh * r_k.astype(f32), axis=-1, keepdims=True) * vh
    y = (y + bonus).reshape(B, S, RW_WIDTH).astype(h.dtype)
    return (y * jax.nn.silu(gate)) @ w_out


def rope(x, cos, sin):
    half = x.shape[-1] // 2
    x1, x2 = x[..., :half], x[..., half:]
    rot = jnp.concatenate([-x2, x1], axis=-1)
    return (x * cos + rot * sin).astype(x.dtype)


def diff_attn_mixer(h, w_in, lq1, lk1, lq2, lk2, subln, w_out, lambda_init):
    B, S, _ = h.shape
    H, dk, dv = DA_HEADS, DA_QK_DIM, DA_V_DIM
    f32 = jnp.float32
    nb = S // Q_BLOCK
    proj = h @ w_in
    q, k, v, gate = jnp.split(proj, [2 * H * dk, 4 * H * dk, 4 * H * dk + H * dv], axis=-1)
    q = q.reshape(B, S, H, 2, dk)
    k = k.reshape(B, S, H, 2, dk)
    v = v.reshape(B, S, H, dv)
    pos = jnp.arange(S, dtype=f32)
    inv_freq = 1.0 / (ROPE_THETA ** (jnp.arange(0, dk, 2, dtype=f32) / dk))
    ang = pos[:, None] * inv_freq[None, :]
    ang = jnp.concatenate([ang, ang], axis=-1)
    cos = jnp.cos(ang)[:, None, None, :]
    sin = jnp.sin(ang)[:, None, None, :]
    q = rope(q, cos, sin)
    k = rope(k, cos, sin)
    lam = (jnp.exp(jnp.sum(lq1.astype(f32) * lk1.astype(f32)))
           - jnp.exp(jnp.sum(lq2.astype(f32) * lk2.astype(f32))) + lambda_init)
    scale = dk ** -0.5
    qb = q.reshape(B, nb, Q_BLOCK, H, 2, dk).transpose(1, 0, 3, 4, 2, 5)
    kt = k.transpose(0, 2, 3, 1, 4)
    vt = v.transpose(0, 2, 1, 3)
    key_pos = jnp.arange(S)

    def block(args):
        i, qi = args
        s = jnp.einsum('bhcqd,bhckd->bhcqk', qi, kt).astype(f32) * scale
        q_pos = i * Q_BLOCK + jnp.arange(Q_BLOCK)
        s = jnp.where(key_pos[None, :] <= q_pos[:, None], s, -jnp.inf)
        pr = jax.nn.softmax(s, axis=-1)
        att = pr[:, :, 0] - lam * pr[:, :, 1]
        return jnp.einsum('bhqk,bhkd->bhqd', att.astype(vt.dtype), vt)

    o = lax.map(block, (jnp.arange(nb), qb))
    o = o.transpose(1, 0, 3, 2, 4).reshape(B, S, H, dv)
    o = rms_norm(o, subln, DA_SUBLN_EPS) * (1.0 - lambda_init)
    o = o.reshape(B, S, DA_WIDTH)
    return (o * jax.nn.silu(gate)) @ w_out


def setup_inputs(seed: int = 0) -> dict:
    key = jax.random.key(seed)
    ks = iter(jax.random.split(key, 40))
    nrm = lambda shape, s: jax.random.normal(next(ks), shape, jnp.float32) * s
    NA, NB, D, C = N_RWKV_LAYERS, N_DIFF_LAYERS, D_MODEL, RW_WIDTH
    lin = jnp.linspace(0.0, 1.0, C, dtype=jnp.float32)
    return {
        'x': nrm((BATCH, SEQ, D), 1.0),
        'p': nrm((DEPTH, BATCH, SEQ, PLE_DIM), 1.0),
        'rw_norm': 1.0 + nrm((NA, D), 0.05),
        'rw_mu': jax.random.uniform(next(ks), (NA, RW_N_MIX, D), jnp.float32),
        'rw_w_in': nrm((NA, D, RW_N_PROJ * C), D ** -0.5),
        'rw_w0': (-7.0 + 5.0 * lin ** 0.85 + 0.5)[None, :] + nrm((NA, C), 0.1),
        'rw_w1': nrm((NA, D, DECAY_LORA), D ** -0.5),
        'rw_w2': nrm((NA, DECAY_LORA, C), 0.1 * DECAY_LORA ** -0.5),
        'rw_a0': nrm((NA, C), 0.1),
        'rw_a1': nrm((NA, D, ICLR_LORA), D ** -0.5),
        'rw_a2': nrm((NA, ICLR_LORA, C), 0.3 * ICLR_LORA ** -0.5),
        'rw_k_k': 0.85 + nrm((NA, C), 0.05),
        'rw_k_a': 1.0 + nrm((NA, C), 0.05),
        'rw_r_k': nrm((NA, RW_HEADS, RW_HEAD), 0.1),
        'rw_ln_w': 1.0 + nrm((NA, C), 0.05),
        'rw_ln_b': nrm((NA, C), 0.02),
        'rw_w_out': nrm((NA, C, D), C ** -0.5),
        'da_norm': 1.0 + nrm((NB, D), 0.05),
        'da_w_in': nrm((NB, D, 4 * DA_HEADS * DA_QK_DIM + 2 * DA_WIDTH), D ** -0.5),
        'da_lq1': nrm((NB, DA_QK_DIM), 0.1),
        'da_lk1': nrm((NB, DA_QK_DIM), 0.1),
        'da_lq2': nrm((NB, DA_QK_DIM), 0.1),
        'da_lk2': nrm((NB, DA_QK_DIM), 0.1),
        'da_subln': 1.0 + nrm((NB, DA_V_DIM), 0.05),
        'da_w_out': nrm((NB, DA_WIDTH, D), DA_WIDTH ** -0.5),
        'pe_norm': 1.0 + nrm((DEPTH, D), 0.05),
        'pe_w_gate': nrm((DEPTH, D, D), D ** -0.5),
        'pe_w_proj': nrm((DEPTH, PLE_DIM, D), PLE_DIM ** -0.5),
        'final_norm': 1.0 + nrm((D,), 0.05),
    }


def reference(x, p, rw_norm, rw_mu, rw_w_in, rw_w0, rw_w1, rw_w2, rw_a0, rw_a1, rw_a2,
              rw_k_k, rw_k_a, rw_r_k, rw_ln_w, rw_ln_b, rw_w_out,
              da_norm, da_w_in, da_lq1, da_lk1, da_lq2, da_lk2, da_subln, da_w_out,
              pe_norm, pe_w_gate, pe_w_proj, final_norm):
    h = x
    for i in range(DEPTH):
        j = i // N_MIXERS
        if i % N_MIXERS == 0:
            h = h + rwkv7_mixer(rms_norm(h, rw_norm[j]), rw_mu[j], rw_w_in[j], rw_w0[j], rw_w1[j],
                                rw_w2[j], rw_a0[j], rw_a1[j], rw_a2[j], rw_k_k[j], rw_k_a[j],
                                rw_r_k[j], rw_ln_w[j], rw_ln_b[j], rw_w_out[j])
        else:
            lambda_init = 0.8 - 0.6 * math.exp(-0.3 * i)
            h = h + diff_attn_mixer(rms_norm(h, da_norm[j]), da_w_in[j], da_lq1[j], da_lk1[j],
                                    da_lq2[j], da_lk2[j], da_subln[j], da_w_out[j], lambda_init)
        g = jax.nn.sigmoid(rms_norm(h, pe_norm[i]) @ pe_w_gate[i])
        h = h + g * (p[i] @ pe_w_proj[i])
    return rms_norm(h, final_norm)
```

```python
import math
from contextlib import ExitStack
import numpy as np
import ml_dtypes
import concourse.bass as bass
import concourse.mybir as mybir
from concourse.bass_utils import run_bass_kernel_spmd

F32 = mybir.dt.float32
BF16 = mybir.dt.bfloat16
AF = mybir.ActivationFunctionType
ALU = mybir.AluOpType
AX = mybir.AxisListType
NPBF = ml_dtypes.bfloat16

D = 1024
SEQ = 16384
TT = 512


class Tok:
    __slots__ = ("name", "w", "r", "dsem", "dcnt", "excl")

    def __init__(self, name, excl=False):
        self.name = name
        self.excl = excl
        self.w = None
        self.r = {}
        self.dsem = None
        self.dcnt = 0


class KB:
    def __init__(self, nc, es):
        self.nc = nc
        self.es = es
        self.eng = dict(pe=nc.tensor, dve=nc.vector, act=nc.scalar, pool=nc.gpsimd, sp=nc.sync)
        self.sem = {e: es.enter_context(nc.semaphore("prog_" + e)) for e in ("pe", "dve", "act", "pool")}
        self.cnt = {e: 0 for e in self.sem}
        self.seen = {e: {} for e in self.eng}
        self.ntok = 0
        self.out_dma = []
        self.coop = None

    def tok(self, name=None):
        self.ntok += 1
        return Tok(name or ("t%d" % self.ntok))

    def toks(self, n, name="t"):
        return [self.tok("%s%d_%d" % (name, self.ntok, i)) for i in range(n)]

    def sb(self, name, shape, dt=F32):
        return self.es.enter_context(self.nc.sbuf_tensor(name, list(shape), dt))

    def ps(self, name, shape, dt=F32):
        return self.es.enter_context(self.nc.psum_tensor(name, list(shape), dt))

    def _deps(self, reads, writes):
        deps = {}
        for t in reads:
            if t.w is not None:
                k = t.w[1]
                if deps.get(k, (None, None, 0))[2] < t.w[2]:
                    deps[k] = t.w
        for t in writes:
            for d in ([t.w] if t.w is not None else []) + list(t.r.values()):
                k = d[1]
                if deps.get(k, (None, None, 0))[2] < d[2]:
                    deps[k] = d
        return deps

    def _wait(self, e, deps, keep_same=False):
        for k, (sem, key, val) in deps.items():
            if key == e and e == "pe" and not keep_same:
                continue
            if self.seen[e].get(key, 0) < val:
                self.eng[e].wait_ge(sem, val)
                self.seen[e][key] = val

    def ptok(self, name=None):
        t = self.tok(name)
        t.excl = True
        return t

    def op(self, e, fn, reads=(), writes=()):
        ex = [t for t in reads if t.excl]
        if ex:
            reads = [t for t in reads if not t.excl]
            writes = list(writes) + ex
        self._wait(e, self._deps(reads, writes))
        ins = fn()
        self.cnt[e] += 1
        ins.then_inc(self.sem[e], 1)
        me = (self.sem[e], e, self.cnt[e])
        for t in reads:
            t.r[e] = me
        for t in writes:
            t.w = me
            t.r = {}
        if self.coop is not None:
            self.coop.emitted()
        return ins

    def dma(self, q, out, in_, owner, reads=(), writes=(), **kw):
        self._wait(q, self._deps(reads, writes), keep_same=True)
        if owner.dsem is None:
            owner.dsem = self.es.enter_context(self.nc.semaphore("dma_" + owner.name))
        ins = self.eng[q].dma_start(out=out, in_=in_, **kw)
        owner.dcnt += 16
        ins.then_inc(owner.dsem, 16)
        me = (owner.dsem, "dma_" + owner.name, owner.dcnt)
        for t in reads:
            t.r[me[1]] = me
        for t in writes:
            t.w = me
            t.r = {}
        return me

    def finish(self, toks):
        for t in toks:
            if t.dsem is not None:
                self.eng["sp"].wait_ge(t.dsem, t.dcnt)


import threading


class Coop:
    def __init__(self, kb):
        self.kb = kb
        self.thread = None
        self.quota = 0
        self.go = threading.Semaphore(0)
        self.back = threading.Semaphore(0)
        self.done = True
        self.err = None
        kb.coop = self

    def start(self, fn):
        self.done = False
        self.quota = 0

        def run():
            self.go.acquire()
            try:
                fn()
            except BaseException as e:
                self.err = e
            self.done = True
            self.back.release()
        self.thread = threading.Thread(target=run)
        self.thread.start()

    def emitted(self):
        if threading.current_thread() is self.thread:
            self.quota -= 1
            if self.quota <= 0:
                self.back.release()
                self.go.acquire()

    def step(self, n):
        if self.done or threading.current_thread() is self.thread:
            return
        self.quota = n
        self.go.release()
        self.back.acquire()
        if self.err is not None:
            raise self.err

    def finish(self):
        while not self.done:
            self.step(1 << 30)
        if self.thread is not None:
            self.thread.join()
        if self.err is not None:
            raise self.err


def new_nc():
    return bass.Bass("TRN2", target_bir_lowering=False)


def dram_in(nc, name, shape, dt=F32):
    return nc.dram_tensor(name, list(shape), dt, kind="ExternalInput").ap()


def dram_out(nc, name, shape, dt=F32):
    return nc.dram_tensor(name, list(shape), dt, kind="ExternalOutput").ap()


def load_weight_bf16(kb, q, w_dram, kchunks, ncols, name, stage, stage_tok, cast_eng="pool"):
    wt = kb.sb(name, [128, kchunks, ncols], BF16)
    tk = kb.tok(name)
    for kc in range(kchunks):
        s = kc % len(stage)
        kb.dma(q, stage[s][:, 0:ncols], w_dram[kc * 128:(kc + 1) * 128, :], stage_tok[s], writes=[stage_tok[s]])
        kb.op(cast_eng, lambda kc=kc, s=s: kb.eng[cast_eng].tensor_copy(out=wt[:, kc, :], in_=stage[s][:, 0:ncols]),
              reads=[stage_tok[s]], writes=[tk])
    return wt, tk


def rstd_from_sumsq(kb, ps_ap, ps_tok, out_ap, out_tok, eps):
    kb.op("act", lambda: kb.nc.scalar.activation(out=out_ap, in_=ps_ap, func=AF.Ln, bias=float(eps), scale=1.0),
          reads=[ps_tok], writes=[out_tok])
    kb.op("act", lambda: kb.nc.scalar.activation(out=out_ap, in_=out_ap, func=AF.Exp, scale=-0.5),
          reads=[out_tok], writes=[out_tok])


def build_post(NT, last):
    nc = new_nc()
    zT = dram_in(nc, "zT", [D, NT], BF16)
    hT = dram_in(nc, "hT", [D, NT], F32)
    pT = dram_in(nc, "pT", [256, NT], F32)
    w_out = dram_in(nc, "w_out", [D, D])
    w_gate = dram_in(nc, "w_gate", [D, D])
    w_proj = dram_in(nc, "w_proj", [256, D])
    vecs = dram_in(nc, "vecs", [128, 16])
    if last:
        oT = dram_out(nc, "oT", [D, NT], F32)
    else:
        h1T = dram_out(nc, "h1T", [D, NT], F32)
        hnT = dram_out(nc, "hnT", [D, NT], BF16)
    ntile = NT // TT
    with ExitStack() as es:
        kb = KB(nc, es)
        ctk = kb.tok("cst")
        ones = kb.sb("ones", [128, 128], BF16)
        kb.op("pool", lambda: nc.gpsimd.memset(ones[:], 1.0 / D), writes=[ctk])
        vec = kb.sb("vec", [128, 16])
        vtk = kb.tok("vec")
        kb.dma("sp", vec[:], vecs[:, :], vtk, writes=[vtk])
        stage = [kb.sb("stg%d" % i, [128, D]) for i in range(2)]
        stok = kb.toks(2, "stg")
        Wo, Wo_t = load_weight_bf16(kb, "sp", w_out, 8, D, "Wo", stage, stok)
        Wg, Wg_t = load_weight_bf16(kb, "sp", w_gate, 8, D, "Wg", stage, stok)
        Wp, Wp_t = load_weight_bf16(kb, "sp", w_proj, 2, D, "Wp", stage, stok)
        zin = [kb.sb("zin%d" % i, [128, 8, TT], BF16) for i in range(2)]
        hin = [kb.sb("hin%d" % i, [128, 8, TT]) for i in range(2)]
        pin = [kb.sb("pin%d" % i, [128, 2, TT]) for i in range(2)]
        pbf = [kb.sb("pbf%d" % i, [128, 2, TT], BF16) for i in range(2)]
        z_t, h_t, p_t, pb_t = kb.toks(2, "z"), kb.toks(2, "h"), kb.toks(2, "p"), kb.toks(2, "pb")
        sq = kb.sb("sq", [128, 8, TT], BF16)
        sq_t = kb.tok("sq")
        hn = kb.sb("hn", [128, 8, TT], BF16)
        hn_t = kb.tok("hn")
        rstd = kb.sb("rstd", [128, TT])
        rstd_t = kb.tok("rstd")
        gsb = [kb.sb("gsb%d" % i, [128, TT]) for i in range(2)]
        g_t = kb.toks(2, "g")
        tmp = [kb.sb("tmp%d" % i, [128, TT]) for i in range(2)]
        tmp_t = kb.toks(2, "tmp")
        obuf = [kb.sb("obuf%d" % i, [128, 8, TT], F32 if last else BF16) for i in range(2)]
        ob_t = kb.toks(2, "ob")
        psb = [kb.ps("psb%d" % i, [128, TT]) for i in range(8)]
        ps_t = [kb.ptok("ps%d" % i) for i in range(8)]
        pc = [0]

        def nextps():
            i = pc[0] % 8
            pc[0] += 1
            return psb[i], ps_t[i]

        zv = zT.rearrange("(c p) t -> p c t", p=128)
        hv = hT.rearrange("(c p) t -> p c t", p=128)
        pv = pT.rearrange("(c p) t -> p c t", p=128)

        def norm(src, src_t, gcol0, dst_fn, dst_toks):
            for dc in range(8):
                kb.op("act", lambda dc=dc: nc.scalar.activation(out=sq[:, dc, :], in_=src[:, dc, :], func=AF.Square),
                      reads=[src_t], writes=[sq_t])
            pb_, pt_ = nextps()
            for dc in range(8):
                kb.op("pe", lambda dc=dc: nc.tensor.matmul(pb_[:], lhsT=ones[:], rhs=sq[:, dc, :], start=(dc == 0), stop=(dc == 7)),
                      reads=[sq_t, ctk], writes=[pt_])
            rstd_from_sumsq(kb, pb_[:], pt_, rstd[:], rstd_t, 1e-6)
            for dc in range(8):
                kb.op("dve", lambda dc=dc: nc.vector.scalar_tensor_tensor(
                    out=dst_fn(dc), in0=src[:, dc, :], scalar=vec[:, gcol0 + dc:gcol0 + dc + 1], in1=rstd[:],
                    op0=ALU.mult, op1=ALU.mult), reads=[src_t, rstd_t, vtk], writes=dst_toks)

        def loads(j):
            s = j % 2
            cs = slice(j * TT, (j + 1) * TT)
            kb.dma("sp", zin[s][:], zv[:, :, cs], z_t[s], writes=[z_t[s]])
            kb.dma("sp", hin[s][:], hv[:, :, cs], h_t[s], writes=[h_t[s]])
            kb.dma("sp", pin[s][:], pv[:, :, cs], p_t[s], writes=[p_t[s]])

        loads(0)
        for j in range(ntile):
            s = j % 2
            cs = slice(j * TT, (j + 1) * TT)
            if j + 1 < ntile:
                loads(j + 1)
            kb.op("pool", lambda s=s: nc.gpsimd.tensor_copy(out=pbf[s][:], in_=pin[s][:]), reads=[p_t[s]], writes=[pb_t[s]])
            h2 = hin[s]
            for dc in range(8):
                pb_, pt_ = nextps()
                for kc in range(8):
                    kb.op("pe", lambda dc=dc, kc=kc, pb_=pb_: nc.tensor.matmul(
                        pb_[:], lhsT=Wo[:, kc, dc * 128:(dc + 1) * 128], rhs=zin[s][:, kc, :], start=(kc == 0), stop=(kc == 7)),
                        reads=[Wo_t, z_t[s]], writes=[pt_])
                kb.op("dve", lambda dc=dc, pb_=pb_: nc.vector.tensor_tensor(out=h2[:, dc, :], in0=h2[:, dc, :], in1=pb_[:], op=ALU.add),
                      reads=[pt_, h_t[s]], writes=[h_t[s]])
            norm(h2, h_t[s], 0, lambda dc: hn[:, dc, :], [hn_t])
            for dc in range(8):
                pg, pgt = nextps()
                for kc in range(8):
                    kb.op("pe", lambda dc=dc, kc=kc, pg=pg: nc.tensor.matmul(
                        pg[:], lhsT=Wg[:, kc, dc * 128:(dc + 1) * 128], rhs=hn[:, kc, :], start=(kc == 0), stop=(kc == 7)),
                        reads=[Wg_t, hn_t], writes=[pgt])
                pp, ppt = nextps()
                for kc in range(2):
                    kb.op("pe", lambda dc=dc, kc=kc, pp=pp: nc.tensor.matmul(
                        pp[:], lhsT=Wp[:, kc, dc * 128:(dc + 1) * 128], rhs=pbf[s][:, kc, :], start=(kc == 0), stop=(kc == 1)),
                        reads=[Wp_t, pb_t[s]], writes=[ppt])
                b = dc % 2
                kb.op("act", lambda pg=pg, b=b: nc.scalar.activation(out=gsb[b][:], in_=pg[:], func=AF.Sigmoid),
                      reads=[pgt], writes=[g_t[b]])
                kb.op("dve", lambda pp=pp, b=b: nc.vector.tensor_tensor(out=tmp[b][:], in0=gsb[b][:], in1=pp[:], op=ALU.mult),
                      reads=[g_t[b], ppt], writes=[tmp_t[b]])
                kb.op("pool", lambda dc=dc, b=b: nc.gpsimd.tensor_tensor(out=h2[:, dc, :], in0=h2[:, dc, :], in1=tmp[b][:], op=ALU.add),
                      reads=[tmp_t[b], h_t[s]], writes=[h_t[s]])
            norm(h2, h_t[s], 8, lambda dc: obuf[s][:, dc, :], [ob_t[s]])
            if last:
                kb.dma("sp", oT.rearrange("(c p) t -> p c t", p=128)[:, :, cs], obuf[s][:], ob_t[s], reads=[ob_t[s]])
            else:
                kb.dma("sp", hnT.rearrange("(c p) t -> p c t", p=128)[:, :, cs], obuf[s][:], ob_t[s], reads=[ob_t[s]])
                kb.dma("sp", h1T.rearrange("(c p) t -> p c t", p=128)[:, :, cs], h2[:], h_t[s], reads=[h_t[s]])
        kb.finish(ob_t + h_t)
    return nc


def run_post(zT_list, hT_list, pT_list, w_out, w_gate, w_proj, pe_norm, nxt_norm, last):
    n = len(zT_list)
    NT = zT_list[0].shape[1]
    nc = build_post(NT, last)
    vecs = np.concatenate([pe_norm.reshape(8, 128).T, nxt_norm.reshape(8, 128).T], axis=1).astype(np.float32)
    maps = [dict(zT=np.ascontiguousarray(zT_list[i]), hT=np.ascontiguousarray(hT_list[i]), pT=np.ascontiguousarray(pT_list[i]),
                 w_out=w_out, w_gate=w_gate, w_proj=w_proj, vecs=np.ascontiguousarray(vecs)) for i in range(n)]
    res = run_bass_kernel_spmd(nc, maps, core_ids=list(range(n))).results
    return res


TA = 256
NCH = TA // 64
GN_EPS = 64e-5
FILL = 6
RW_STOP = 0
RW_DBG = None


def build_rwkv(NT):
    nc = new_nc()
    xT = dram_in(nc, "xT", [D, NT])
    Wc = dram_in(nc, "Wc", [D, 4 * 256])
    w1 = dram_in(nc, "w1", [D, 64])
    a1 = dram_in(nc, "a1", [D, 64])
    w2c = dram_in(nc, "w2c", [64, 256])
    a2c = dram_in(nc, "a2c", [64, 256])
    vecA = dram_in(nc, "vecA", [128, 56])
    vecC = dram_in(nc, "vecC", [128, 2 * 8])
    lnwb = dram_in(nc, "lnwb", [1, 512])
    identd = dram_in(nc, "ident", [128, 128])
    zT = dram_out(nc, "zT", [256, NT], BF16)
    dbg_d = dram_out(nc, "dbg", [128, 2048]) if RW_DBG else None
    dbg_tok = []

    def dbg(name, ap, tok, j=0):
        if RW_DBG == name and j == 0 and not dbg_tok:
            t = kb.tok("dbgt")
            dbg_tok.append(t)
            p, f = ap.shape[0], int(np.prod(ap.shape[1:]))
            kb.dma("sp", dbg_d[0:p, 0:f], ap, t, reads=[tok])
    ntile = NT // TA
    with ExitStack() as es:
        kb = KB(nc, es)
        V, S, G, P = nc.vector, nc.scalar, nc.gpsimd, nc.tensor
        ctk = kb.tok("cst")
        ident = kb.sb("ident_sb", [128, 128])
        kb.dma("sp", ident[:], identd[:, :], ctk, writes=[ctk])
        onesb = kb.sb("onesb", [128, 128], BF16)
        blk = kb.sb("blk", [128, 128], BF16)
        bind = kb.sb("bind", [128, 2])
        kb.op("pool", lambda: G.memset(onesb[:], 1.0 / D), writes=[ctk])
        kb.op("pool", lambda: G.memset(blk[:], 0.0), writes=[ctk])
        kb.op("pool", lambda: G.memset(blk[0:64, 0:64], 1.0), writes=[ctk])
        kb.op("pool", lambda: G.memset(blk[64:128, 64:128], 1.0), writes=[ctk])
        kb.op("pool", lambda: G.memset(bind[:], 0.0), writes=[ctk])
        kb.op("pool", lambda: G.memset(bind[0:64, 0:1], 1.0), writes=[ctk])
        kb.op("pool", lambda: G.memset(bind[64:128, 1:2], 1.0), writes=[ctk])
        mAT = kb.sb("mAT", [64, 4, 4, 64])
        mL = kb.sb("mL", [64, 4, 64])
        kb.op("pool", lambda: G.memset(mAT[:], 1.0), writes=[ctk])
        kb.op("pool", lambda: G.memset(mL[:], 1.0), writes=[ctk])
        for q in range(4):
            kb.op("pool", lambda q=q: G.affine_select(out=mAT[:, :, q, :], in_=mAT[:, :, q, :], pattern=[[0, 4], [1, 64]],
                                                      compare_op=(ALU.is_gt if q % 2 == 0 else ALU.is_ge), fill=0.0, base=0,
                                                      channel_multiplier=-1), writes=[ctk])
        kb.op("pool", lambda: G.affine_select(out=mL[:], in_=mL[:], pattern=[[0, 4], [-1, 64]], compare_op=ALU.is_gt, fill=0.0,
                                              base=0, channel_multiplier=1), writes=[ctk])
        vA = kb.sb("vA", [128, 56])
        vC = kb.sb("vC", [128, 2, 8])
        kb.dma("sp", vA[:], vecA[:, :], ctk, writes=[ctk])
        kb.dma("sp", vC[:].rearrange("p a b -> p (a b)"), vecC[:, :], ctk, writes=[ctk])
        lnw = kb.sb("lnw", [64, 512])
        kb.dma("sp", lnw[:], lnwb.partition_broadcast(64), ctk, writes=[ctk])
        kb.op("dve", lambda: V.tensor_scalar(out=vC[:, :, 5:6], in0=vC[:, :, 0:1], scalar1=-1.0, scalar2=None, op0=ALU.mult),
              reads=[ctk], writes=[ctk])
        kb.op("dve", lambda: V.tensor_scalar(out=vC[:, :, 6:7], in0=vC[:, :, 3:4], scalar1=-1.0, scalar2=1.0, op0=ALU.mult, op1=ALU.add),
              reads=[ctk], writes=[ctk])
        W0, A0, KK_, KA, RK, NW0, OMKA = range(7)

        xin = kb.sb("xin", [128, 8, TA])
        x_t = kb.tok("xin")
        xin_flat = xin[:].rearrange("p a b -> p (a b)")
        stage = [xin_flat[:, i * 1024:(i + 1) * 1024] for i in range(2)]
        stok = [x_t, x_t]
        W, W_t = load_weight_bf16(kb, "sp", Wc, 8, 1024, "W", stage, stok, cast_eng="dve")
        W1 = kb.sb("W1", [128, 8, 64], BF16)
        A1 = kb.sb("A1", [128, 8, 64], BF16)
        W2 = kb.sb("W2", [64, 256], BF16)
        A2 = kb.sb("A2", [64, 256], BF16)
        kb.dma("sp", stage[0][:, 0:512].rearrange("p (c k) -> p c k", c=8), w1.rearrange("(c p) k -> p c k", p=128), x_t, writes=[x_t])
        kb.op("dve", lambda: V.tensor_copy(out=W1[:].rearrange("p c k -> p (c k)"), in_=stage[0][:, 0:512]), reads=[x_t], writes=[W_t])
        kb.dma("sp", stage[0][:, 0:512].rearrange("p (c k) -> p c k", c=8), a1.rearrange("(c p) k -> p c k", p=128), x_t, writes=[x_t])
        kb.op("dve", lambda: V.tensor_copy(out=A1[:].rearrange("p c k -> p (c k)"), in_=stage[0][:, 0:512]), reads=[x_t], writes=[W_t])
        kb.dma("sp", stage[0][0:64, 0:256], w2c[:, :], x_t, writes=[x_t])
        kb.op("dve", lambda: V.tensor_copy(out=W2[:], in_=stage[0][0:64, 0:256]), reads=[x_t], writes=[W_t])
        kb.dma("sp", stage[0][0:64, 0:256], a2c[:, :], x_t, writes=[x_t])
        kb.op("dve", lambda: V.tensor_copy(out=A2[:], in_=stage[0][0:64, 0:256]), reads=[x_t], writes=[W_t])

        hn = kb.sb("hn", [128, 8, TA + 1])
        hn_t = kb.tok("hn")
        kb.op("pool", lambda: G.memset(hn[:], 0.0), writes=[hn_t])
        sqx = kb.sb("sqx", [128, 8, TA], BF16)
        sqx_t = kb.tok("sqx")
        rstd = kb.sb("rstd", [128, TA])
        rstd_t = kb.tok("rstd")
        xm = [kb.sb("xm%d" % i, [128, 8, TA], BF16) for i in range(2)]
        xm_t = kb.toks(2, "xm")
        th = kb.sb("th", [64, TA], BF16)
        th_t = kb.tok("th")

        def cm(name):
            return kb.sb(name, [128, 2, TA]), kb.tok(name)
        r_, r_t = cm("r_")
        k_, k_t = cm("k_")
        v_, v_t = cm("v_")
        sg2 = [kb.sb("sg%d" % i, [128, 2, TA]) for i in range(2)]
        sg2_t = kb.toks(2, "sg")
        nlw, nlw_t = cm("nlw")
        a_, a_t = cm("a_")
        kk, kk_t = cm("kk")
        t1, t1_t = cm("t1")
        prod, prod_t = cm("prod")
        CA, CA_t = cm("CA")
        CB, CB_t = cm("CB")
        rn, rn_t = cm("rn")
        sqk = kb.sb("sqk", [128, 2, TA], BF16)
        sqk_t = kb.tok("sqk")
        AR = kb.sb("AR", [128, 2, NCH, 2, 64])
        AR_t = kb.tok("AR")
        BK = kb.sb("BK", [128, 2, NCH, 2, 64])
        BK_t = kb.tok("BK")
        ARh2 = [kb.sb("ARh%d" % i, [64, 4, NCH, 2, 64], BF16) for i in range(2)]
        ARh2_t = kb.toks(2, "ARh")
        BKh2 = [kb.sb("BKh%d" % i, [64, 4, NCH, 2, 64], BF16) for i in range(2)]
        BKh2_t = kb.toks(2, "BKh")
        WCs2 = [kb.sb("WCs%d" % i, [64, 4, NCH]) for i in range(2)]
        WCs2_t = kb.toks(2, "WCs")
        TM2 = [[kb.sb("TM%d_%d" % (pp, i), [64, NCH, 4, 64], BF16) for i in range(4)] for pp in range(2)]
        TM2_t = [kb.toks(4, "TM") for pp in range(2)]
        TA_, TB_, TK_, TV_ = range(4)
        RKs2 = [kb.sb("RKs%d" % i, [64, NCH, 4]) for i in range(2)]
        RKs2_t = kb.toks(2, "RKs")
        Ybuf2 = [kb.sb("Ybuf%d" % i, [64, NCH, 4, 64]) for i in range(2)]
        Y2_t = kb.toks(2, "Ybuf")
        ATs = kb.sb("ATs", [64, 4, 4, 64], BF16)
        ATs_t = kb.tok("ATs")
        Ls = kb.sb("Ls", [64, 4, 64], BF16)
        Ls_t = kb.tok("Ls")
        Z = [kb.sb("Z%d" % i, [64, 4, 128]) for i in range(2)]
        Z_t = kb.toks(2, "Z")
        Zb = [kb.sb("Zb%d" % i, [64, 4, 128], BF16) for i in range(2)]
        Zb_t = kb.toks(2, "Zb")
        identb = kb.sb("identb", [64, 64], BF16)
        kb.op("dve", lambda: V.tensor_copy(out=identb[:], in_=ident[0:64, 0:64]), reads=[ctk], writes=[ctk])
        LB = [kb.sb("LB%d" % i, [64, 4, 2, 64], BF16) for i in range(2)]
        LB_t = kb.toks(2, "LB")
        RHs = kb.sb("RHs", [64, 4, 64])
        RHs_t = kb.tok("RHs")
        MTs = kb.sb("MTs", [64, 4, 64])
        MTs_t = kb.tok("MTs")
        Hs = [kb.sb("Hs%d" % i, [64, 4, 64]) for i in range(2)]
        Hs_t = kb.toks(2, "Hs")
        kb.op("pool", lambda: G.memset(Hs[0][:], 0.0), writes=[Hs_t[0]])
        st1 = kb.sb("st1", [64, NCH * 4])
        st2 = kb.sb("st2", [64, NCH * 4])
        st3 = kb.sb("st3", [64, NCH * 4])
        st_t = kb.tok("st")
        zout = kb.sb("zout", [128, 2, TA], BF16)
        zo_t = kb.tok("zout")

        PR = kb.ps("PR", [128, 512]); PR_t = kb.ptok("PR")
        TIN = kb.ps("TIN", [128, 512]); TIN_t = kb.ptok("TIN")
        ATp = kb.ps("ATp", [128, 512]); ATp_t = kb.ptok("ATp")
        B4 = kb.ps("B4", [128, 512]); B4a_t = kb.ptok("B4a"); B4b_t = B4a_t
        B5 = kb.ps("B5", [128, 512]); B5a_t = kb.ptok("B5a"); B5b_t = B5a_t
        B6 = kb.ps("B6", [128, 512]); B6a_t = kb.ptok("B6a"); B6b_t = B6a_t
        APp = kb.ps("APp", [128, 512]); APp_t = kb.ptok("APp")
        SQp = kb.ps("SQp", [128, 512]); SQp_t = kb.ptok("SQp")

        xv = xT.rearrange("(c p) t -> p c t", p=128)
        hstate = [0]

        def prep(j):
            p = j % 2
            cs = slice(j * TA, (j + 1) * TA)
            ARh, ARh_t, BKh, BKh_t = ARh2[p], ARh2_t[p], BKh2[p], BKh2_t[p]
            TM, TM_t = TM2[p], TM2_t[p]
            WCs, WCs_t, RKs, RKs_t = WCs2[p], WCs2_t[p], RKs2[p], RKs2_t[p]
            sg, sg_t = sg2[p], sg2_t[p]
            Ybuf, Y_t = Ybuf2[p], Y2_t[p]
            kb.dma("sp", xin[:], xv[:, :, cs], x_t, writes=[x_t])
            if j > 0:
                kb.op("pool", lambda: G.tensor_copy(out=hn[:, :, 0:1], in_=hn[:, :, TA:TA + 1]), reads=[hn_t], writes=[hn_t])
            for dc in range(8):
                kb.op("act", lambda dc=dc: S.activation(out=sqx[:, dc, :], in_=xin[:, dc, :], func=AF.Square), reads=[x_t], writes=[sqx_t])
            for dc in range(8):
                kb.op("pe", lambda dc=dc: P.matmul(PR[:, 0:TA], lhsT=onesb[:], rhs=sqx[:, dc, :], start=(dc == 0), stop=(dc == 7)),
                      reads=[sqx_t, ctk], writes=[PR_t])
            kb.op("act", lambda: S.activation(out=rstd[:], in_=PR[:, 0:TA], func=AF.Ln, bias=1e-6, scale=1.0), reads=[PR_t], writes=[rstd_t])
            kb.op("act", lambda: S.activation(out=rstd[:], in_=rstd[:], func=AF.Exp, scale=-0.5), reads=[rstd_t], writes=[rstd_t])
            for dc in range(8):
                kb.op("dve", lambda dc=dc: V.scalar_tensor_tensor(out=hn[:, dc, 1:TA + 1], in0=xin[:, dc, :], scalar=vA[:, dc:dc + 1],
                                                                 in1=rstd[:], op0=ALU.mult, op1=ALU.mult),
                      reads=[x_t, rstd_t, ctk], writes=[hn_t])
            kb.op("pool", lambda: G.tensor_tensor(out=xin[:], in0=hn[:, :, 0:TA], in1=hn[:, :, 1:TA + 1], op=ALU.subtract),
                  reads=[hn_t], writes=[x_t])
            for i in range(6):
                xb, xb_t = xm[i % 2], xm_t[i % 2]
                for dc in range(8):
                    e = "dve"
                    kb.op(e, lambda dc=dc, e=e, xb=xb, i=i: kb.eng[e].scalar_tensor_tensor(
                        out=xb[:, dc, :], in0=xin[:, dc, :], scalar=vA[:, 8 + i * 8 + dc:9 + i * 8 + dc], in1=hn[:, dc, 1:TA + 1],
                        op0=ALU.mult, op1=ALU.add), reads=[x_t, hn_t, ctk], writes=[xb_t])
                if i < 4:
                    dst, dst_t = ((r_, r_t), (k_, k_t), (v_, v_t), (sg, sg_t))[i]
                    for hc in range(2):
                        for dc in range(8):
                            kb.op("pe", lambda dc=dc, hc=hc, xb=xb, i=i: P.matmul(
                                PR[:, 0:TA], lhsT=W[:, dc, i * 256 + hc * 128:i * 256 + hc * 128 + 128], rhs=xb[:, dc, :],
                                start=(dc == 0), stop=(dc == 7)), reads=[W_t, xb_t], writes=[PR_t])
                        if i == 3:
                            kb.op("act", lambda hc=hc, dst=dst: S.activation(out=dst[:, hc, :], in_=PR[:, 0:TA], func=AF.Silu),
                                  reads=[PR_t], writes=[dst_t])
                        else:
                            kb.op("dve", lambda hc=hc, dst=dst: V.tensor_copy(out=dst[:, hc, :], in_=PR[:, 0:TA]),
                                  reads=[PR_t], writes=[dst_t])
                else:
                    Wl, W2l = (W1, W2) if i == 4 else (A1, A2)
                    for dc in range(8):
                        kb.op("pe", lambda dc=dc, xb=xb, Wl=Wl: P.matmul(PR[0:64, 0:TA], lhsT=Wl[:, dc, :], rhs=xb[:, dc, :],
                                                                        start=(dc == 0), stop=(dc == 7)), reads=[W_t, xb_t], writes=[PR_t])
                    kb.op("act", lambda i=i: S.activation(out=th[:], in_=PR[0:64, 0:TA], func=(AF.Tanh if i == 4 else AF.Copy)),
                          reads=[PR_t], writes=[th_t])
                    for hc in range(2):
                        kb.op("pe", lambda hc=hc, W2l=W2l: P.matmul(PR[:, 0:TA], lhsT=W2l[:, hc * 128:(hc + 1) * 128], rhs=th[:],
                                                                    start=True, stop=True), reads=[W_t, th_t], writes=[PR_t])
                        if i == 4:
                            kb.op("act", lambda hc=hc: S.activation(out=nlw[:, hc, :], in_=PR[:, 0:TA], func=AF.Exp, scale=-1.0,
                                                                    bias=vC[:, hc, NW0:NW0 + 1]), reads=[PR_t, ctk], writes=[nlw_t])
                            kb.op("act", lambda hc=hc: S.activation(out=nlw[:, hc, :], in_=nlw[:, hc, :], func=AF.Ln, scale=1.0, bias=1.0),
                                  reads=[nlw_t], writes=[nlw_t])
                            kb.op("act", lambda hc=hc: S.activation(out=nlw[:, hc, :], in_=nlw[:, hc, :], func=AF.Exp, scale=-1.0, bias=-0.5),
                                  reads=[nlw_t], writes=[nlw_t])
                        else:
                            kb.op("act", lambda hc=hc: S.activation(out=a_[:, hc, :], in_=PR[:, 0:TA], func=AF.Sigmoid, scale=1.0,
                                                                    bias=vC[:, hc, A0:A0 + 1]), reads=[PR_t, ctk], writes=[a_t])
            for hc in range(2):
                kb.op("dve", lambda hc=hc: V.tensor_scalar(out=kk[:, hc, :], in0=k_[:, hc, :], scalar1=vC[:, hc, KK_:KK_ + 1], scalar2=None,
                                                           op0=ALU.mult), reads=[k_t, ctk], writes=[kk_t])
                kb.op("act", lambda hc=hc: S.activation(out=sqk[:, hc, :], in_=kk[:, hc, :], func=AF.Square), reads=[kk_t], writes=[sqk_t])
                kb.op("pe", lambda hc=hc: P.matmul(PR[:, 0:TA], lhsT=blk[:], rhs=sqk[:, hc, :], start=True, stop=True),
                      reads=[sqk_t, ctk], writes=[PR_t])
                kb.op("act", lambda hc=hc: S.activation(out=rn[:, hc, :], in_=PR[:, 0:TA], func=AF.Ln, bias=1e-24, scale=1.0),
                      reads=[PR_t], writes=[rn_t])
                kb.op("act", lambda hc=hc: S.activation(out=rn[:, hc, :], in_=rn[:, hc, :], func=AF.Exp, scale=-0.5), reads=[rn_t], writes=[rn_t])
            kb.op("dve", lambda: V.tensor_tensor(out=kk[:], in0=kk[:], in1=rn[:], op=ALU.mult), reads=[kk_t, rn_t], writes=[kk_t])
            for hc in range(2):
                kb.op("dve", lambda hc=hc: V.tensor_scalar(out=t1[:, hc, :], in0=a_[:, hc, :], scalar1=vC[:, hc, KA:KA + 1],
                                                           scalar2=vC[:, hc, OMKA:OMKA + 1], op0=ALU.mult, op1=ALU.add),
                      reads=[a_t, ctk], writes=[t1_t])
            kb.op("dve", lambda: V.tensor_tensor(out=t1[:], in0=t1[:], in1=k_[:], op=ALU.mult), reads=[t1_t, k_t], writes=[t1_t])
            kb.op("pool", lambda: G.tensor_tensor(out=a_[:], in0=a_[:], in1=kk[:], op=ALU.mult), reads=[a_t, kk_t], writes=[a_t])
            for hc in range(2):
                kb.op("dve", lambda hc=hc: V.scalar_tensor_tensor(out=prod[:, hc, :], in0=r_[:, hc, :], scalar=vC[:, hc, RK:RK + 1],
                                                                 in1=t1[:, hc, :], op0=ALU.mult, op1=ALU.mult),
                      reads=[r_t, t1_t, ctk], writes=[prod_t])
            def v3(t):
                return t[:].rearrange("p a (c t) -> p (a c) t", t=64)
            src, src_t = nlw, nlw_t
            pp = [(CA, CA_t), (CB, CB_t)]
            for li, sft in enumerate((1, 2, 4, 8, 16, 32)):
                dst, dst_t = pp[li % 2]
                kb.op("pool", lambda src=src, dst=dst, sft=sft: G.tensor_tensor(out=v3(dst)[:, :, sft:], in0=v3(src)[:, :, sft:],
                                                                               in1=v3(src)[:, :, 0:64 - sft], op=ALU.add),
                      reads=[src_t], writes=[dst_t])
                kb.op("pool", lambda src=src, dst=dst, sft=sft: G.tensor_copy(out=v3(dst)[:, :, 0:sft], in_=v3(src)[:, :, 0:sft]),
                      reads=[src_t], writes=[dst_t])
                src, src_t = dst, dst_t
            cn, cn_t = src, src_t
            assert cn is CB
            kb.op("pool", lambda: G.tensor_tensor(out=nlw[:], in0=cn[:], in1=nlw[:], op=ALU.subtract), reads=[cn_t, nlw_t], writes=[nlw_t])
            kb.op("act", lambda: S.activation(out=CA[:], in_=cn[:], func=AF.Exp, scale=-1.0), reads=[cn_t], writes=[CA_t])
            kb.op("act", lambda: S.activation(out=nlw[:], in_=nlw[:], func=AF.Exp, scale=-1.0), reads=[nlw_t], writes=[nlw_t])
            kb.op("act", lambda: S.activation(out=CB[:], in_=cn[:], func=AF.Exp, scale=1.0), reads=[cn_t], writes=[CB_t])
            eneg, eneg_t, enegx, enegx_t, epos, epos_t = CA, CA_t, nlw, nlw_t, CB, CB_t

            def c3(t, hc):
                return t[:, hc, :].rearrange("p (c t) -> p c t", t=64)
            for hc in range(2):
                kb.op("dve", lambda hc=hc: V.tensor_tensor(out=AR[:, hc, :, 0, :], in0=c3(kk, hc), in1=c3(enegx, hc), op=ALU.mult),
                      reads=[kk_t, enegx_t], writes=[AR_t])
                kb.op("pool", lambda hc=hc: G.tensor_tensor(out=AR[:, hc, :, 1, :], in0=c3(r_, hc), in1=c3(eneg, hc), op=ALU.mult),
                      reads=[r_t, eneg_t], writes=[AR_t])
                kb.op("dve", lambda hc=hc: V.scalar_tensor_tensor(out=BK[:, hc, :, 0, :], in0=c3(a_, hc), scalar=-1.0, in1=c3(epos, hc),
                                                                 op0=ALU.mult, op1=ALU.mult), reads=[a_t, epos_t], writes=[BK_t])
                kb.op("pool", lambda hc=hc: G.tensor_tensor(out=BK[:, hc, :, 1, :], in0=c3(t1, hc), in1=c3(epos, hc), op=ALU.mult),
                      reads=[t1_t, epos_t], writes=[BK_t])
                for hh in range(2):
                    h = 2 * hc + hh
                    kb.op("dve", lambda hc=hc, hh=hh, h=h: V.tensor_copy(out=WCs[:, h, :], in_=c3(eneg, hc)[hh * 64:(hh + 1) * 64, :, 63]),
                          reads=[eneg_t], writes=[WCs_t])
                    kb.op("dve", lambda hc=hc, hh=hh, h=h: V.tensor_copy(out=ARh[:, h, :, :, :].rearrange("p c a t -> p (c a t)"),
                                                                         in_=AR[hh * 64:(hh + 1) * 64, hc, :, :, :].rearrange("p c a t -> p (c a t)")),
                          reads=[AR_t], writes=[ARh_t])
                    kb.op("dve", lambda hc=hc, hh=hh, h=h: V.tensor_copy(out=BKh[:, h, :, :, :].rearrange("p c a t -> p (c a t)"),
                                                                         in_=BK[hh * 64:(hh + 1) * 64, hc, :, :, :].rearrange("p c a t -> p (c a t)")),
                          reads=[BK_t], writes=[BKh_t])
            srcs = [(lambda hc, c: AR[:, hc, c, 0, :], AR_t), (lambda hc, c: BK[:, hc, c, 0, :], BK_t),
                    (lambda hc, c: BK[:, hc, c, 1, :], BK_t), (lambda hc, c: v_[:, hc, c * 64:(c + 1) * 64], v_t)]
            ne = 0
            for q in range(4):
                fn, ft = srcs[q]
                for c0 in range(0, NCH, 2):
                    for cc in range(2):
                        for hc in range(2):
                            kb.op("pe", lambda fn=fn, c0=c0, cc=cc, hc=hc: P.transpose(
                                TIN[0:64, cc * 256 + hc * 128:cc * 256 + hc * 128 + 128], fn(hc, c0 + cc), ident[:]),
                                reads=[ft, ctk], writes=[TIN_t])
                    e = "act" if ne % 2 == 0 else "dve"
                    ne += 1
                    dstv = TM[q][:, c0:c0 + 2, :, :].rearrange("p c h n -> p (c h n)")
                    if e == "act":
                        kb.op("act", lambda dstv=dstv: S.copy(out=dstv, in_=TIN[0:64, :]), reads=[TIN_t], writes=[TM_t[q]])
                    else:
                        kb.op("dve", lambda dstv=dstv: V.tensor_copy(out=dstv, in_=TIN[0:64, :]), reads=[TIN_t], writes=[TM_t[q]])
            for c in range(NCH):
                for hc in range(2):
                    kb.op("pe", lambda c=c, hc=hc: P.matmul(TIN[0:64, c * 4 + hc * 2:c * 4 + hc * 2 + 2], lhsT=prod[:, hc, c * 64:(c + 1) * 64],
                                                            rhs=bind[:], start=True, stop=True), reads=[prod_t, ctk], writes=[TIN_t])
            kb.op("dve", lambda: V.tensor_copy(out=RKs[:].rearrange("p c h -> p (c h)"), in_=TIN[0:64, 0:NCH * 4]), reads=[TIN_t], writes=[RKs_t])

        def scan(j):
            p = j % 2
            cs = slice(j * TA, (j + 1) * TA)
            ARh, ARh_t, BKh, BKh_t = ARh2[p], ARh2_t[p], BKh2[p], BKh2_t[p]
            TM, TM_t = TM2[p], TM2_t[p]
            WCs, WCs_t, RKs, RKs_t = WCs2[p], WCs2_t[p], RKs2[p], RKs2_t[p]
            sg, sg_t = sg2[p], sg2_t[p]
            Ybuf, Y_t = Ybuf2[p], Y2_t[p]
            for c in range(NCH):
                def opnd(h):
                    return h // 2, 64 * (h % 2)
                for half in range(2):
                    for hl in range(2):
                        h = 2 * half + hl
                        hc, pb = opnd(h)
                        rhs = ARh[:, h, c, :, :].rearrange("p a t -> p (a t)")
                        kb.op("pe", lambda hl=hl, h=h, rhs=rhs: P.matmul(ATp[0:64, hl * 256:hl * 256 + 128], lhsT=BKh[:, h, c, 0, :],
                                                                                 rhs=rhs, start=True, stop=True), reads=[ARh_t, BKh_t], writes=[ATp_t])
                        kb.op("pe", lambda hl=hl, h=h, rhs=rhs: P.matmul(ATp[0:64, hl * 256 + 128:hl * 256 + 256], lhsT=BKh[:, h, c, 1, :],
                                                                                 rhs=rhs, start=True, stop=True), reads=[ARh_t, BKh_t], writes=[ATp_t])
                    kb.op("dve", lambda half=half: V.tensor_tensor(
                        out=ATs[:, 2 * half:2 * half + 2, :, :].rearrange("p h q t -> p (h q t)"), in0=ATp[0:64, :],
                        in1=mAT[:, 2 * half:2 * half + 2, :, :].rearrange("p h q t -> p (h q t)"), op=ALU.mult),
                        reads=[ATp_t, ctk], writes=[ATs_t])
                    coop.step(FILL)
                for h in range(4):
                    hc, pb = opnd(h)
                    kb.op("pe", lambda h=h, hc=hc, pb=pb: P.matmul(B4[0:64, h * 64:(h + 1) * 64], lhsT=ARh[:, h, c, 0, :],
                                                                   rhs=BKh[:, h, c, 0, :], start=True, stop=True),
                          reads=[ARh_t, BKh_t], writes=[B4a_t])
                kb.op("dve", lambda: V.tensor_tensor(out=Ls[:].rearrange("p h s -> p (h s)"), in0=B4[0:64, 0:256],
                                                     in1=mL[:].rearrange("p h s -> p (h s)"), op=ALU.mult), reads=[B4a_t, ctk], writes=[Ls_t])
                coop.step(FILL)
                for h in range(4):
                    kb.op("pe", lambda h=h: P.matmul(B4[0:64, 256 + h * 64:256 + (h + 1) * 64], lhsT=ATs[:, h, 2, :], rhs=TM[TV_][:, c, h, :],
                                                     start=True, stop=True), reads=[ATs_t, TM_t[TV_]], writes=[B4b_t])
                zc = 0
                kb.op("act", lambda: S.copy(out=Zb[0][:, :, 64:128], in_=B4[0:64, 256:512].rearrange("p (h v) -> p h v", h=4)),
                      reads=[B4b_t], writes=[Zb_t[0]])
                kb.op("act", lambda: S.copy(out=Z[0][:, :, 64:128], in_=B4[0:64, 256:512].rearrange("p (h v) -> p h v", h=4)),
                      reads=[B4b_t], writes=[Z_t[0]])
                coop.step(FILL)
                kb.op("pool", lambda: G.tensor_copy(out=Zb[0][:, :, 0:64], in_=TM[TA_][:, c, :, :]), reads=[TM_t[TA_]], writes=[Zb_t[0]])
                kb.op("pool", lambda: G.tensor_copy(out=Z[0][:, :, 0:64], in_=TM[TA_][:, c, :, :]), reads=[TM_t[TA_]], writes=[Z_t[0]])
                Bm = lambda h: ATs[:, h, 0, :]
                Lm = lambda h: Ls[:, h, :]
                Bm_t, Lm_t = ATs_t, Ls_t
                for lvl in range(6):
                    for h in range(4):
                        kb.op("pe", lambda h=h, Bm=Bm, zc=zc: P.matmul(APp[0:64, h * 128:(h + 1) * 128], lhsT=Bm(h), rhs=Zb[zc][:, h, :],
                                                                       start=True, stop=True), reads=[Bm_t, Zb_t[zc]], writes=[APp_t])
                    if lvl < 5:
                        for h in range(4):
                            kb.op("pe", lambda h=h, Bm=Bm, Lm=Lm: P.matmul(SQp[0:64, h * 128:h * 128 + 64], lhsT=Bm(h), rhs=Lm(h), start=True, stop=True),
                                  reads=[Bm_t, Lm_t], writes=[SQp_t])
                            kb.op("pe", lambda h=h, Bm=Bm, Lm=Lm: P.matmul(SQp[0:64, h * 128 + 64:h * 128 + 128], lhsT=Lm(h), rhs=Bm(h), start=True, stop=True),
                                  reads=[Bm_t, Lm_t], writes=[SQp_t])
                    kb.op("dve", lambda zc=zc: V.tensor_tensor(out=Zb[1 - zc][:].rearrange("p h x -> p (h x)"), in0=Z[zc][:].rearrange("p h x -> p (h x)"),
                                                               in1=APp[0:64, :], op=ALU.add), reads=[APp_t, Z_t[zc]], writes=[Zb_t[1 - zc]])
                    if lvl < 5:
                        kb.op("dve", lambda zc=zc: V.tensor_tensor(out=Z[1 - zc][:].rearrange("p h x -> p (h x)"), in0=Z[zc][:].rearrange("p h x -> p (h x)"),
                                                                   in1=APp[0:64, :], op=ALU.add), reads=[APp_t, Z_t[zc]], writes=[Z_t[1 - zc]])
                    coop.step(FILL)
                    zc = 1 - zc
                    if lvl < 5:
                        nb = lvl % 2
                        kb.op("act", lambda nb=nb: S.copy(out=LB[nb][:].rearrange("p h q s -> p (h q s)"), in_=SQp[0:64, :]),
                              reads=[SQp_t], writes=[LB_t[nb]])
                        coop.step(FILL)
                        Lm = lambda h, nb=nb: LB[nb][:, h, 0, :]
                        Bm = lambda h, nb=nb: LB[nb][:, h, 1, :]
                        Bm_t = Lm_t = LB_t[nb]
                Zf, Zf_t = Zb[zc], Zb_t[zc]
                for h in range(4):
                    kb.op("pe", lambda h=h: P.matmul(B5[0:64, h * 64:(h + 1) * 64], lhsT=Zf[:, h, 0:64], rhs=ATs[:, h, 1, :], start=True, stop=True),
                          reads=[Zf_t, ATs_t], writes=[B5a_t])
                kb.op("dve", lambda: V.tensor_tensor(out=RHs[:], in0=B5[0:64, 0:256].rearrange("p (h t) -> p h t", h=4), in1=ARh[:, :, c, 1, :], op=ALU.add),
                      reads=[B5a_t, ARh_t], writes=[RHs_t])
                coop.step(FILL)
                for h in range(4):
                    kb.op("pe", lambda h=h: P.matmul(B5[0:64, 256 + h * 64:256 + (h + 1) * 64], lhsT=Zf[:, h, 0:64], rhs=TM[TB_][:, c, h, :], start=True, stop=False),
                          reads=[Zf_t, TM_t[TB_]], writes=[B5b_t])
                    kb.op("pe", lambda h=h: P.matmul(B5[0:64, 256 + h * 64:256 + (h + 1) * 64], lhsT=identb[:], rhs=identb[:], start=False, stop=True),
                          reads=[ctk], writes=[B5b_t])
                kb.op("act", lambda: S.copy(out=MTs[:].rearrange("p h n -> p (h n)"), in_=B5[0:64, 256:512]), reads=[B5b_t], writes=[MTs_t])
                coop.step(FILL)
                hcur, hnew = hstate[0], 1 - hstate[0]
                for h in range(4):
                    o = B6[0:64, h * 64:(h + 1) * 64]
                    kb.op("pe", lambda h=h, o=o: P.matmul(o, lhsT=RHs[:, h, :], rhs=Hs[hcur][:, h, :], start=True, stop=False),
                          reads=[RHs_t, Hs_t[hcur]], writes=[B6a_t])
                    kb.op("pe", lambda h=h, o=o: P.matmul(o, lhsT=ATs[:, h, 1, :], rhs=Zf[:, h, 64:128], start=False, stop=False),
                          reads=[ATs_t, Zf_t], writes=[B6a_t])
                    kb.op("pe", lambda h=h, o=o: P.matmul(o, lhsT=ATs[:, h, 3, :], rhs=TM[TV_][:, c, h, :], start=False, stop=True),
                          reads=[ATs_t, TM_t[TV_]], writes=[B6a_t])
                kb.op("act", lambda: S.copy(out=Ybuf[:, c, :, :].rearrange("p h v -> p (h v)"), in_=B6[0:64, 0:256]), reads=[B6a_t], writes=[Y_t])
                coop.step(FILL)
                for h in range(4):
                    o = B6[0:64, 256 + h * 64:256 + (h + 1) * 64]
                    kb.op("pe", lambda h=h, o=o: P.matmul(o, lhsT=MTs[:, h, :], rhs=Hs[hcur][:, h, :], start=True, stop=False),
                          reads=[MTs_t, Hs_t[hcur]], writes=[B6b_t])
                    kb.op("pe", lambda h=h, o=o: P.matmul(o, lhsT=TM[TB_][:, c, h, :], rhs=Zf[:, h, 64:128], start=False, stop=False),
                          reads=[TM_t[TB_], Zf_t], writes=[B6b_t])
                    kb.op("pe", lambda h=h, o=o: P.matmul(o, lhsT=TM[TK_][:, c, h, :], rhs=TM[TV_][:, c, h, :], start=False, stop=True),
                          reads=[TM_t[TK_], TM_t[TV_]], writes=[B6b_t])
                for h in range(4):
                    kb.op("dve", lambda h=h: V.tensor_scalar(out=Hs[hnew][:, h, :], in0=B6[0:64, 256 + h * 64:256 + (h + 1) * 64],
                                                             scalar1=WCs[:, h, c:c + 1], scalar2=None, op0=ALU.mult),
                          reads=[B6b_t, WCs_t], writes=[Hs_t[hnew]])
                    coop.step(FILL)
                hstate[0] = hnew

        def post9(j):
            p = j % 2
            cs = slice(j * TA, (j + 1) * TA)
            ARh, ARh_t, BKh, BKh_t = ARh2[p], ARh2_t[p], BKh2[p], BKh2_t[p]
            TM, TM_t = TM2[p], TM2_t[p]
            WCs, WCs_t, RKs, RKs_t = WCs2[p], WCs2_t[p], RKs2[p], RKs2_t[p]
            sg, sg_t = sg2[p], sg2_t[p]
            Ybuf, Y_t = Ybuf2[p], Y2_t[p]
            Yv = Ybuf[:].rearrange("p c h v -> p (c h) v")
            W1b = AR[0:64, :, :, :, :].rearrange("p a c q t -> p (a c q t)").rearrange("p (c h v) -> p c h v", c=NCH, h=4)
            W2b = BK[0:64, :, :, :, :].rearrange("p a c q t -> p (a c q t)").rearrange("p (c h v) -> p c h v", c=NCH, h=4)
            W1_t, W2_t = AR_t, BK_t
            W1v = W1b.rearrange("p c h v -> p (c h) v")
            W2v = W2b.rearrange("p c h v -> p (c h) v")
            kb.op("dve", lambda: V.tensor_reduce(out=st1[:], in_=Yv, axis=AX.X, op=ALU.add), reads=[Y_t], writes=[st_t])
            kb.op("act", lambda: S.activation(out=W1b.rearrange("p c h v -> p (c h v)"), in_=Ybuf[:].rearrange("p c h v -> p (c h v)"), func=AF.Square),
                  reads=[Y_t], writes=[W1_t])
            kb.op("dve", lambda: V.tensor_reduce(out=st2[:], in_=W1v, axis=AX.X, op=ALU.add), reads=[W1_t], writes=[st_t])
            kb.op("dve", lambda: V.tensor_scalar(out=st1[:], in0=st1[:], scalar1=1.0 / 64, scalar2=None, op0=ALU.mult), reads=[st_t], writes=[st_t])
            kb.op("dve", lambda: V.tensor_tensor(out=st3[:], in0=st1[:], in1=st1[:], op=ALU.mult), reads=[st_t], writes=[st_t])
            kb.op("dve", lambda: V.tensor_scalar(out=st2[:], in0=st2[:], scalar1=1.0 / 64, scalar2=None, op0=ALU.mult), reads=[st_t], writes=[st_t])
            kb.op("dve", lambda: V.tensor_tensor(out=st2[:], in0=st2[:], in1=st3[:], op=ALU.subtract), reads=[st_t], writes=[st_t])
            kb.op("dve", lambda: V.tensor_scalar(out=st2[:], in0=st2[:], scalar1=0.0, scalar2=None, op0=ALU.max), reads=[st_t], writes=[st_t])
            kb.op("act", lambda: S.activation(out=st2[:], in_=st2[:], func=AF.Ln, bias=GN_EPS, scale=1.0), reads=[st_t], writes=[st_t])
            kb.op("act", lambda: S.activation(out=st2[:], in_=st2[:], func=AF.Exp, scale=-0.5), reads=[st_t], writes=[st_t])
            bc = lambda t: t[:].unsqueeze(2).broadcast_to([64, NCH * 4, 64])
            kb.op("dve", lambda: V.tensor_tensor(out=W1v, in0=Yv, in1=bc(st1), op=ALU.subtract), reads=[Y_t, st_t], writes=[W1_t])
            kb.op("dve", lambda: V.tensor_tensor(out=W1v, in0=W1v, in1=bc(st2), op=ALU.mult), reads=[st_t], writes=[W1_t])
            lw_b = lnw[:, 0:256].unsqueeze(1).broadcast_to([64, NCH, 256])
            lb_b = lnw[:, 256:512].unsqueeze(1).broadcast_to([64, NCH, 256])
            W1c = W1b.rearrange("p c h v -> p c (h v)")
            W2c = W2b.rearrange("p c h v -> p c (h v)")
            kb.op("pool", lambda: G.tensor_tensor(out=W1c, in0=W1c, in1=lw_b, op=ALU.mult), reads=[ctk], writes=[W1_t])
            kb.op("pool", lambda: G.tensor_tensor(out=W1c, in0=W1c, in1=lb_b, op=ALU.add), reads=[ctk], writes=[W1_t])
            kb.op("dve", lambda: V.tensor_tensor(out=W2v, in0=TM[TV_][:].rearrange("p c h v -> p (c h) v"),
                                                 in1=RKs[:].rearrange("p c h -> p (c h)").unsqueeze(2).broadcast_to([64, NCH * 4, 64]), op=ALU.mult),
                  reads=[TM_t[TV_], RKs_t], writes=[W2_t])
            kb.op("dve", lambda: V.tensor_tensor(out=W1v, in0=W1v, in1=W2v, op=ALU.add), reads=[W2_t], writes=[W1_t])
            for hc in range(2):
                for c in range(NCH):
                    kb.op("pe", lambda hc=hc, c=c: P.transpose(PR[:, c * 64:(c + 1) * 64], W1b[:, c, 2 * hc:2 * hc + 2, :].rearrange("p h v -> p (h v)"),
                                                               ident[0:64, 0:64]), reads=[W1_t, ctk], writes=[PR_t])
                kb.op("dve", lambda hc=hc: V.tensor_tensor(out=zout[:, hc, :], in0=PR[:, 0:TA], in1=sg[:, hc, :], op=ALU.mult),
                      reads=[PR_t, sg_t], writes=[zo_t])
            kb.dma("sp", zT.rearrange("(c p) t -> p c t", p=128)[:, :, cs], zout[:], zo_t, reads=[zo_t])

        coop = Coop(kb)
        prep(0)
        for j in range(ntile):
            def filler(j=j):
                if j > 0:
                    post9(j - 1)
                if j + 1 < ntile:
                    prep(j + 1)
            coop.start(filler)
            scan(j)
            coop.finish()
        post9(ntile - 1)
        kb.finish([zo_t] + dbg_tok)
    return nc


def rwkv_maps(x, I, NT):
    maps = []
    Wfull = I["rw_w_in"][0].reshape(D, 4, 1024)
    vecA = np.concatenate([I["rw_norm"][0].reshape(8, 128).T] + [I["rw_mu"][0][i].reshape(8, 128).T for i in range(6)], axis=1)
    ident = np.eye(128, dtype=np.float32)
    for c in range(8):
        b, g = c // 4, c % 4
        cs = slice(g * 256, (g + 1) * 256)
        vc = np.zeros((128, 2, 8), np.float32)
        for vi, nm in enumerate(["rw_w0", "rw_a0", "rw_k_k", "rw_k_a"]):
            vc[:, :, vi] = I[nm][0][cs].reshape(2, 128).T
        vc[:, :, 4] = I["rw_r_k"][0].reshape(1024)[cs].reshape(2, 128).T
        maps.append(dict(
            xT=np.ascontiguousarray(x[b, :NT].T),
            Wc=np.ascontiguousarray(Wfull[:, :, cs].reshape(D, 1024)),
            w1=I["rw_w1"][0], a1=I["rw_a1"][0],
            w2c=np.ascontiguousarray(I["rw_w2"][0][:, cs]), a2c=np.ascontiguousarray(I["rw_a2"][0][:, cs]),
            vecA=np.ascontiguousarray(vecA.astype(np.float32)), vecC=np.ascontiguousarray(vc.reshape(128, 16)),
            lnwb=np.ascontiguousarray(np.concatenate([I["rw_ln_w"][0][cs], I["rw_ln_b"][0][cs]])[None, :]),
            ident=ident))
    return maps


LAM_INIT = 0.8 - 0.6 * math.exp(-0.3 * 1)


def build_attn(NT):
    nc = new_nc()
    hnT = dram_in(nc, "hnT", [D, NT], BF16)
    Wd = dram_in(nc, "Wd", [D, 4 * 256])
    sub = dram_in(nc, "sub", [1, 128])
    lqk = dram_in(nc, "lqk", [1, 256])
    invf = dram_in(nc, "invf", [128, 1])
    pos0 = dram_in(nc, "pos0", [1, TT])
    cosd = dram_in(nc, "cosd", [128, NT])
    sind = dram_in(nc, "sind", [128, NT])
    identd = dram_in(nc, "ident", [128, 128])
    zT = dram_out(nc, "zT", [256, NT], BF16)
    ntile = NT // TT
    nblk = NT // 128
    PI = math.pi
    with ExitStack() as es:
        kb = KB(nc, es)
        V, S, G, P = nc.vector, nc.scalar, nc.gpsimd, nc.tensor
        ctk = kb.tok("cst")
        ident = kb.sb("ident_sb", [128, 128])
        kb.dma("sp", ident[:], identd[:, :], ctk, writes=[ctk])
        tri = kb.sb("tri", [128, 128], BF16)
        kb.op("pool", lambda: G.memset(tri[:], 1.0), writes=[ctk])
        kb.op("pool", lambda: G.affine_select(out=tri[:], in_=tri[:], pattern=[[1, 128]], compare_op=ALU.is_ge, fill=0.0, base=0,
                                              channel_multiplier=-1), writes=[ctk])
        subb = kb.sb("subb", [128, 128])
        kb.dma("sp", subb[:], sub.partition_broadcast(128), ctk, writes=[ctk])
        kb.op("dve", lambda: V.tensor_scalar(out=subb[:], in0=subb[:], scalar1=1.0 - LAM_INIT, scalar2=None, op0=ALU.mult), reads=[ctk], writes=[ctk])
        lq = kb.sb("lq", [128, 256])
        kb.dma("sp", lq[:], lqk.partition_broadcast(128), ctk, writes=[ctk])
        lam = kb.sb("lam", [128, 4])
        kb.op("dve", lambda: V.tensor_tensor(out=lq[:, 0:64], in0=lq[:, 0:64], in1=lq[:, 64:128], op=ALU.mult), reads=[ctk], writes=[ctk])
        kb.op("dve", lambda: V.tensor_tensor(out=lq[:, 128:192], in0=lq[:, 128:192], in1=lq[:, 192:256], op=ALU.mult), reads=[ctk], writes=[ctk])
        kb.op("dve", lambda: V.tensor_reduce(out=lam[:, 0:1], in_=lq[:, 0:64], axis=AX.X, op=ALU.add), reads=[ctk], writes=[ctk])
        kb.op("dve", lambda: V.tensor_reduce(out=lam[:, 1:2], in_=lq[:, 128:192], axis=AX.X, op=ALU.add), reads=[ctk], writes=[ctk])
        kb.op("act", lambda: S.activation(out=lam[:, 0:2], in_=lam[:, 0:2], func=AF.Exp), reads=[ctk], writes=[ctk])
        kb.op("dve", lambda: V.tensor_tensor(out=lam[:, 2:3], in0=lam[:, 1:2], in1=lam[:, 0:1], op=ALU.subtract), reads=[ctk], writes=[ctk])
        kb.op("dve", lambda: V.tensor_scalar(out=lam[:, 3:4], in0=lam[:, 2:3], scalar1=-LAM_INIT, scalar2=None, op0=ALU.add), reads=[ctk], writes=[ctk])
        ivf = kb.sb("ivf", [128, 1])
        kb.dma("sp", ivf[:], invf[:, :], ctk, writes=[ctk])
        p0 = kb.sb("p0", [128, TT])
        kb.dma("sp", p0[:], pos0.partition_broadcast(128), ctk, writes=[ctk])
        ngpi = kb.sb("ngpi", [128, 1])
        kb.op("pool", lambda: G.memset(ngpi[:], -PI), writes=[ctk])

        QT = kb.sb("QT", [128, NT], BF16); QT_t = kb.tok("QT")
        KT = kb.sb("KT", [128, NT], BF16); KT_t = kb.tok("KT")
        SG = kb.sb("SG", [128, NT], BF16); SG_t = kb.tok("SG")
        Va = kb.sb("Va", [128, nblk, 129], BF16); Va_t = kb.tok("Va")
        kb.op("pool", lambda: G.memset(Va[:], 1.0), writes=[Va_t])
        hin = [kb.sb("ahin%d" % i, [128, 8, TT], BF16) for i in range(2)]
        hin_t = kb.toks(2, "ahin")
        stg = kb.sb("astg", [128, 1024])
        stg_t = kb.tok("astg")
        Wt = {nm: kb.sb("W" + nm, [128, 8, 128], BF16) for nm in ("q", "qr", "k", "kr", "v", "g")}
        W_t = kb.tok("Wt")
        sn = kb.sb("sn", [128, TT]); cs_ = kb.sb("cs_", [128, TT]); sc_t = kb.tok("sincos")
        ta = kb.sb("ta", [128, TT]); tb = kb.sb("tb", [128, TT]); tab_t = kb.tok("tab")
        PT = [kb.sb("PT%d" % i, [128, TT], BF16) for i in range(4)]
        PT_t = kb.toks(4, "PT")
        o1 = kb.sb("o1", [128, 128]); o1_t = kb.tok("o1")
        o4 = kb.sb("o4", [128, 4, 128]); o_t = kb.tok("o4")
        on4 = kb.sb("on4", [128, 4, 128]); on_t = kb.tok("on4")
        rc8 = kb.sb("rc8", [128, 9]); ss4 = kb.sb("ss4", [128, 4]); ss_t = kb.tok("ss4")
        junk = kb.sb("junk", [128, 128])
        rc = kb.sb("rc", [128, 4]); rc_t = kb.tok("rc")
        zout = [kb.sb("azout%d" % i, [128, TT], BF16) for i in range(2)]
        zo_t = kb.toks(2, "azo")
        SB = [kb.ps("SB%d" % i, [128, 512]) for i in range(4)]
        SB_t = [kb.ptok("SB%d" % i) for i in range(4)]
        AC = [kb.ps("AC%d" % i, [128, 512]) for i in range(3)]
        AC_t = [kb.ptok("AC%d" % i) for i in range(3)]
        TP_ = kb.ps("TPp", [128, 512]); TP_t = kb.ptok("TPp")

        def acc(c, qs):
            i = c * 4 + qs
            return AC[i // 3][:, (i % 3) * 129:(i % 3) * 129 + 129], AC_t[i // 3]

        hv = hnT.rearrange("(c p) t -> p c t", p=128)
        for hh in range(2):
            for gi, nm in enumerate(("q", "k", "v", "g")):
                for dc in range(8):
                    kb.dma("sp", stg[:, 0:128], Wd[dc * 128:(dc + 1) * 128, gi * 256 + hh * 128:gi * 256 + hh * 128 + 128], stg_t, writes=[stg_t])
                    kb.op("dve", lambda nm=nm, dc=dc: V.tensor_copy(out=Wt[nm][:, dc, :], in_=stg[:, 0:128]), reads=[stg_t], writes=[W_t])
                    if nm in ("q", "k"):
                        for c in range(2):
                            kb.op("dve", lambda nm=nm, dc=dc, c=c: V.tensor_scalar(out=Wt[nm + "r"][:, dc, c * 64:c * 64 + 32], in0=stg[:, c * 64 + 32:c * 64 + 64],
                                                                                   scalar1=-1.0, scalar2=None, op0=ALU.mult), reads=[stg_t], writes=[W_t])
                            kb.op("dve", lambda nm=nm, dc=dc, c=c: V.tensor_copy(out=Wt[nm + "r"][:, dc, c * 64 + 32:c * 64 + 64], in_=stg[:, c * 64:c * 64 + 32]),
                                  reads=[stg_t], writes=[W_t])
            kb.dma("sp", hin[0][:], hv[:, :, 0:TT], hin_t[0], writes=[hin_t[0]])
            for j in range(ntile):
                s = j % 2
                cs = slice(j * TT, (j + 1) * TT)
                if j + 1 < ntile:
                    kb.dma("sp", hin[1 - s][:], hv[:, :, (j + 1) * TT:(j + 2) * TT], hin_t[1 - s], writes=[hin_t[1 - s]])
                hb, hb_t = hin[s], hin_t[s]
                kb.dma("sp", sn[:], sind[:, cs], sc_t, writes=[sc_t])
                kb.dma("sp", cs_[:], cosd[:, cs], sc_t, writes=[sc_t])
                for nm, dstT, dst_t in (("q", QT, QT_t), ("k", KT, KT_t)):
                    for dc in range(8):
                        kb.op("pe", lambda nm=nm, dc=dc: P.matmul(SB[0][:], lhsT=Wt[nm][:, dc, :], rhs=hb[:, dc, :], start=(dc == 0), stop=(dc == 7)),
                              reads=[W_t, hb_t], writes=[SB_t[0]])
                    for dc in range(8):
                        kb.op("pe", lambda nm=nm, dc=dc: P.matmul(SB[1][:], lhsT=Wt[nm + "r"][:, dc, :], rhs=hb[:, dc, :], start=(dc == 0), stop=(dc == 7)),
                              reads=[W_t, hb_t], writes=[SB_t[1]])
                    kb.op("dve", lambda: V.tensor_tensor(out=ta[:], in0=SB[0][:], in1=cs_[:], op=ALU.mult), reads=[SB_t[0], sc_t], writes=[tab_t])
                    kb.op("dve", lambda: V.tensor_tensor(out=tb[:], in0=SB[1][:], in1=sn[:], op=ALU.mult), reads=[SB_t[1], sc_t], writes=[tab_t])
                    kb.op("dve", lambda dstT=dstT, cs=cs: V.tensor_tensor(out=dstT[:, cs], in0=ta[:], in1=tb[:], op=ALU.add), reads=[tab_t], writes=[dst_t])
                for dc in range(8):
                    kb.op("pe", lambda dc=dc: P.matmul(SB[2][:], lhsT=Wt["g"][:, dc, :], rhs=hb[:, dc, :], start=(dc == 0), stop=(dc == 7)),
                          reads=[W_t, hb_t], writes=[SB_t[2]])
                kb.op("act", lambda cs=cs: S.activation(out=SG[:, cs], in_=SB[2][:], func=AF.Silu), reads=[SB_t[2]], writes=[SG_t])
                for bi in range(4):
                    for dc in range(8):
                        kb.op("pe", lambda dc=dc, bi=bi: P.matmul(SB[3][:, bi * 128:(bi + 1) * 128], lhsT=hb[:, dc, bi * 128:(bi + 1) * 128], rhs=Wt["v"][:, dc, :],
                                                                  start=(dc == 0), stop=(dc == 7)), reads=[W_t, hb_t], writes=[SB_t[3]])
                kb.op("act", lambda j=j: S.copy(out=Va[:, j * 4:j * 4 + 4, 0:128], in_=SB[3][:].rearrange("p (b v) -> p b v", b=4)), reads=[SB_t[3]], writes=[Va_t])
            gstep = [0]
            for g in range(ntile):
                for a3 in range(3):
                    kb.op("dve", lambda a3=a3: V.memset(AC[a3][:], 0.0), writes=[AC_t[a3]])
                steps = []
                for kbk in range(4 * g + 4):
                    for c in range(2):
                        steps.append((kbk, c, gstep[0] % 4))
                        gstep[0] += 1

                def qk(st, g=g):
                    kbk, c, bi = st
                    kb.op("pe", lambda: P.matmul(SB[bi][:], lhsT=KT[c * 64:(c + 1) * 64, kbk * 128:(kbk + 1) * 128],
                                                 rhs=QT[c * 64:(c + 1) * 64, g * TT:(g + 1) * TT], start=True, stop=True),
                          reads=[KT_t, QT_t], writes=[SB_t[bi]])

                def ex(st, g=g):
                    kbk, c, bi = st
                    m = kbk - 4 * g
                    kb.op("act", lambda: S.activation(out=PT[bi][:], in_=SB[bi][:], func=AF.Exp, scale=0.125),
                          reads=[SB_t[bi]], writes=[PT_t[bi]])
                    if m >= 0:
                        kb.op("pool", lambda: G.tensor_tensor(out=PT[bi][:, m * 128:(m + 1) * 128], in0=PT[bi][:, m * 128:(m + 1) * 128],
                                                              in1=tri[:], op=ALU.mult), reads=[ctk], writes=[PT_t[bi]])

                def av(st, g=g):
                    kbk, c, bi = st
                    m = kbk - 4 * g
                    for qs in range(4):
                        if m > qs:
                            continue
                        ap_, at_ = acc(c, qs)
                        kb.op("pe", lambda ap_=ap_, qs=qs: P.matmul(ap_, lhsT=PT[bi][:, qs * 128:(qs + 1) * 128], rhs=Va[:, kbk, :],
                                                                    start=False, stop=False, skip_group_check=True),
                              reads=[PT_t[bi], Va_t], writes=[at_])

                LA = 2
                for i in range(min(LA, len(steps))):
                    qk(steps[i])
                for i, st in enumerate(steps):
                    ex(st)
                    if i + LA < len(steps):
                        qk(steps[i + LA])
                    av(st)
                zb, zb_t = zout[g % 2], zo_t[g % 2]
                for a3 in range(3):
                    na = 3 if a3 < 2 else 2
                    kb.op("dve", lambda a3=a3, na=na: V.reciprocal(out=rc8[:, a3 * 3:a3 * 3 + na],
                                                                   in_=AC[a3][:, 0:na * 129].rearrange("p (a w) -> p a w", w=129)[:, :, 128]),
                          reads=[AC_t[a3]], writes=[rc_t])
                for qs in range(4):
                    a0, a0t = acc(0, qs)
                    a1, a1t = acc(1, qs)
                    kb.op("dve", lambda a1=a1, qs=qs: V.tensor_scalar(out=o1[:], in0=a1[:, 0:128], scalar1=rc8[:, 4 + qs:5 + qs], scalar2=lam[:, 3:4],
                                                                     op0=ALU.mult, op1=ALU.mult), reads=[a1t, rc_t, ctk], writes=[o1_t])
                    kb.op("dve", lambda a0=a0, qs=qs: V.scalar_tensor_tensor(out=o4[:, qs, :], in0=a0[:, 0:128], scalar=rc8[:, qs:qs + 1], in1=o1[:],
                                                                            op0=ALU.mult, op1=ALU.add), reads=[a0t, rc_t, o1_t], writes=[o_t])
                for qs in range(4):
                    kb.op("act", lambda qs=qs: S.activation(out=junk[:], in_=o4[:, qs, :], func=AF.Square, accum_out=ss4[:, qs:qs + 1]),
                          reads=[o_t], writes=[ss_t])
                kb.op("act", lambda: S.activation(out=ss4[:], in_=ss4[:], func=AF.Ln, scale=1.0 / 128, bias=1e-5), reads=[ss_t], writes=[ss_t])
                kb.op("act", lambda: S.activation(out=ss4[:], in_=ss4[:], func=AF.Exp, scale=-0.5), reads=[ss_t], writes=[ss_t])
                for qs in range(4):
                    kb.op("dve", lambda qs=qs: V.scalar_tensor_tensor(out=on4[:, qs, :], in0=o4[:, qs, :], scalar=ss4[:, qs:qs + 1], in1=subb[:],
                                                                      op0=ALU.mult, op1=ALU.mult), reads=[o_t, ss_t, ctk], writes=[on_t])
                for qs in range(4):
                    kb.op("pe", lambda qs=qs: P.transpose(TP_[:, qs * 128:(qs + 1) * 128], on4[:, qs, :], ident[:]), reads=[on_t, ctk], writes=[TP_t])
                kb.op("dve", lambda g=g, zb=zb: V.tensor_tensor(out=zb[:], in0=TP_[:], in1=SG[:, g * TT:(g + 1) * TT], op=ALU.mult),
                      reads=[TP_t, SG_t], writes=[zb_t])
                kb.dma("sp", zT[hh * 128:(hh + 1) * 128, g * TT:(g + 1) * TT], zb[:], zb_t, reads=[zb_t])
        kb.finish(zo_t)
    return nc


def attn_maps(hn_list, I, NT):
    maps = []
    Wfull = I["da_w_in"][0].reshape(D, 4, 1024)
    ident = np.eye(128, dtype=np.float32)
    inv = (1.0 / (10000.0 ** (np.arange(0, 64, 2, dtype=np.float32) / 64))).astype(np.float32)
    invf = np.tile(inv, 4).reshape(128, 1).astype(np.float32)
    pos0 = np.arange(TT, dtype=np.float32)[None, :]
    ang = np.arange(NT, dtype=np.float32)[None, :] * invf
    cosd = np.cos(ang).astype(np.float32)
    sind = np.sin(ang).astype(np.float32)
    lqk = np.concatenate([I["da_lq1"][0], I["da_lk1"][0], I["da_lq2"][0], I["da_lk2"][0]])[None, :].astype(np.float32)
    for c in range(8):
        b, hp = c // 4, c % 4
        cs = slice(hp * 256, (hp + 1) * 256)
        maps.append(dict(hnT=np.ascontiguousarray(hn_list[b]), Wd=np.ascontiguousarray(Wfull[:, :, cs].reshape(D, 1024)),
                         sub=np.ascontiguousarray(I["da_subln"][0][None, :]), lqk=np.ascontiguousarray(lqk), invf=invf, pos0=pos0, ident=ident, cosd=cosd, sind=sind))
    return maps


def kernel(**inputs):
    I = {k: np.asarray(v) for k, v in inputs.items()}
    x, p = I["x"], I["p"]
    B, S = x.shape[0], x.shape[1]
    QT_ = S // 4
    nc = build_rwkv(S)
    res = run_bass_kernel_spmd(nc, rwkv_maps(x, I, S), core_ids=list(range(8))).results
    z0T = [np.concatenate([res[b * 4 + g]["zT"] for g in range(4)], axis=0) for b in range(B)]
    rng = [(c // 4, slice((c % 4) * QT_, (c % 4 + 1) * QT_)) for c in range(8)]
    res2 = run_post([z0T[b][:, sl] for b, sl in rng], [x[b, sl].T for b, sl in rng], [p[0, b, sl].T for b, sl in rng],
                    I["rw_w_out"][0], I["pe_w_gate"][0], I["pe_w_proj"][0], I["pe_norm"][0], I["da_norm"][0], last=False)
    hn = [np.concatenate([res2[b * 4 + q]["hnT"] for q in range(4)], axis=1) for b in range(B)]
    nc3 = build_attn(S)
    res3 = run_bass_kernel_spmd(nc3, attn_maps(hn, I, S), core_ids=list(range(8))).results
    z1T = [np.concatenate([res3[b * 4 + g]["zT"] for g in range(4)], axis=0) for b in range(B)]
    res4 = run_post([z1T[b][:, sl] for b, sl in rng], [res2[c]["h1T"] for c in range(8)], [p[1, b, sl].T for b, sl in rng],
                    I["da_w_out"][0], I["pe_w_gate"][1], I["pe_w_proj"][1], I["pe_norm"][1], I["final_norm"], last=True)
    out = np.empty((B, S, D), np.float32)
    for c, (b, sl) in enumerate(rng):
        out[b, sl, :] = res4[c]["oT"].T
    return out
```

```python
import math
from contextlib import ExitStack
import numpy as np
import ml_dtypes
import concourse.bass as bass
import concourse.mybir as mybir
from concourse.bass_utils import run_bass_kernel_spmd

F32 = mybir.dt.float32
BF16 = mybir.dt.bfloat16
AF = mybir.ActivationFunctionType
ALU = mybir.AluOpType
AX = mybir.AxisListType
NPBF = ml_dtypes.bfloat16

D = 1024
SEQ = 16384
TT = 512


class Tok:
    __slots__ = ("name", "w", "r", "dsem", "dcnt", "excl")

    def __init__(self, name, excl=False):
        self.name = name
        self.excl = excl
        self.w = None
        self.r = {}
        self.dsem = None
        self.dcnt = 0


class KB:
    def __init__(self, nc, es):
        self.nc = nc
        self.es = es
        self.eng = dict(pe=nc.tensor, dve=nc.vector, act=nc.scalar, pool=nc.gpsimd, sp=nc.sync)
        self.sem = {e: es.enter_context(nc.semaphore("prog_" + e)) for e in ("pe", "dve", "act", "pool")}
        self.cnt = {e: 0 for e in self.sem}
        self.seen = {e: {} for e in self.eng}
        self.ntok = 0
        self.out_dma = []
        self.coop = None

    def tok(self, name=None):
        self.ntok += 1
        return Tok(name or ("t%d" % self.ntok))

    def toks(self, n, name="t"):
        return [self.tok("%s%d_%d" % (name, self.ntok, i)) for i in range(n)]

    def sb(self, name, shape, dt=F32):
        return self.es.enter_context(self.nc.sbuf_tensor(name, list(shape), dt))

    def ps(self, name, shape, dt=F32):
        return self.es.enter_context(self.nc.psum_tensor(name, list(shape), dt))

    def _deps(self, reads, writes):
        deps = {}
        for t in reads:
            if t.w is not None:
                k = t.w[1]
                if deps.get(k, (None, None, 0))[2] < t.w[2]:
                    deps[k] = t.w
        for t in writes:
            for d in ([t.w] if t.w is not None else []) + list(t.r.values()):
                k = d[1]
                if deps.get(k, (None, None, 0))[2] < d[2]:
                    deps[k] = d
        return deps

    def _wait(self, e, deps, keep_same=False):
        for k, (sem, key, val) in deps.items():
            if key == e and e == "pe" and not keep_same:
                continue
            if self.seen[e].get(key, 0) < val:
                self.eng[e].wait_ge(sem, val)
                self.seen[e][key] = val

    def ptok(self, name=None):
        t = self.tok(name)
        t.excl = True
        return t

    def op(self, e, fn, reads=(), writes=()):
        ex = [t for t in reads if t.excl]
        if ex:
            reads = [t for t in reads if not t.excl]
            writes = list(writes) + ex
        self._wait(e, self._deps(reads, writes))
        ins = fn()
        self.cnt[e] += 1
        ins.then_inc(self.sem[e], 1)
        me = (self.sem[e], e, self.cnt[e])
        for t in reads:
            t.r[e] = me
        for t in writes:
            t.w = me
            t.r = {}
        if self.coop is not None:
            self.coop.emitted()
        return ins

    def dma(self, q, out, in_, owner, reads=(), writes=(), **kw):
        self._wait(q, self._deps(reads, writes), keep_same=True)
        if owner.dsem is None:
            owner.dsem = self.es.enter_context(self.nc.semaphore("dma_" + owner.name))
        ins = self.eng[q].dma_start(out=out, in_=in_, **kw)
        owner.dcnt += 16
        ins.then_inc(owner.dsem, 16)
        me = (owner.dsem, "dma_" + owner.name, owner.dcnt)
        for t in reads:
            t.r[me[1]] = me
        for t in writes:
            t.w = me
            t.r = {}
        return me

    def finish(self, toks):
        for t in toks:
            if t.dsem is not None:
                self.eng["sp"].wait_ge(t.dsem, t.dcnt)


import threading


class Coop:
    def __init__(self, kb):
        self.kb = kb
        self.thread = None
        self.quota = 0
        self.go = threading.Semaphore(0)
        self.back = threading.Semaphore(0)
        self.done = True
        self.err = None
        kb.coop = self

    def start(self, fn):
        self.done = False
        self.quota = 0

        def run():
            self.go.acquire()
            try:
                fn()
            except BaseException as e:
                self.err = e
            self.done = True
            self.back.release()
        self.thread = threading.Thread(target=run)
        self.thread.start()

    def emitted(self):
        if threading.current_thread() is self.thread:
            self.quota -= 1
            if self.quota <= 0:
                self.back.release()
                self.go.acquire()

    def step(self, n):
        if self.done or threading.current_thread() is self.thread:
            return
        self.quota = n
        self.go.release()
        self.back.acquire()
        if self.err is not None:
            raise self.err

    def finish(self):
        while not self.done:
            self.step(1 << 30)
        if self.thread is not None:
            self.thread.join()
        if self.err is not None:
            raise self.err


def new_nc():
    return bass.Bass("TRN2", target_bir_lowering=False)


def dram_in(nc, name, shape, dt=F32):
    return nc.dram_tensor(name, list(shape), dt, kind="ExternalInput").ap()


def dram_out(nc, name, shape, dt=F32):
    return nc.dram_tensor(name, list(shape), dt, kind="ExternalOutput").ap()


def load_weight_bf16(kb, q, w_dram, kchunks, ncols, name, stage, stage_tok, cast_eng="pool"):
    wt = kb.sb(name, [128, kchunks, ncols], BF16)
    tk = kb.tok(name)
    for kc in range(kchunks):
        s = kc % len(stage)
        kb.dma(q, stage[s][:, 0:ncols], w_dram[kc * 128:(kc + 1) * 128, :], stage_tok[s], writes=[stage_tok[s]])
        kb.op(cast_eng, lambda kc=kc, s=s: kb.eng[cast_eng].tensor_copy(out=wt[:, kc, :], in_=stage[s][:, 0:ncols]),
              reads=[stage_tok[s]], writes=[tk])
    return wt, tk


def rstd_from_sumsq(kb, ps_ap, ps_tok, out_ap, out_tok, eps):
    kb.op("act", lambda: kb.nc.scalar.activation(out=out_ap, in_=ps_ap, func=AF.Ln, bias=float(eps), scale=1.0),
          reads=[ps_tok], writes=[out_tok])
    kb.op("act", lambda: kb.nc.scalar.activation(out=out_ap, in_=out_ap, func=AF.Exp, scale=-0.5),
          reads=[out_tok], writes=[out_tok])


def build_post(NT, last):
    nc = new_nc()
    zT = dram_in(nc, "zT", [D, NT], BF16)
    hT = dram_in(nc, "hT", [D, NT], F32)
    pT = dram_in(nc, "pT", [256, NT], F32)
    w_out = dram_in(nc, "w_out", [D, D])
    w_gate = dram_in(nc, "w_gate", [D, D])
    w_proj = dram_in(nc, "w_proj", [256, D])
    vecs = dram_in(nc, "vecs", [128, 16])
    if last:
        oT = dram_out(nc, "oT", [D, NT], F32)
    else:
        h1T = dram_out(nc, "h1T", [D, NT], F32)
        hnT = dram_out(nc, "hnT", [D, NT], BF16)
    ntile = NT // TT
    with ExitStack() as es:
        kb = KB(nc, es)
        ctk = kb.tok("cst")
        ones = kb.sb("ones", [128, 128], BF16)
        kb.op("pool", lambda: nc.gpsimd.memset(ones[:], 1.0 / D), writes=[ctk])
        vec = kb.sb("vec", [128, 16])
        vtk = kb.tok("vec")
        kb.dma("sp", vec[:], vecs[:, :], vtk, writes=[vtk])
        stage = [kb.sb("stg%d" % i, [128, D]) for i in range(2)]
        stok = kb.toks(2, "stg")
        Wo, Wo_t = load_weight_bf16(kb, "sp", w_out, 8, D, "Wo", stage, stok)
        Wg, Wg_t = load_weight_bf16(kb, "sp", w_gate, 8, D, "Wg", stage, stok)
        Wp, Wp_t = load_weight_bf16(kb, "sp", w_proj, 2, D, "Wp", stage, stok)
        zin = [kb.sb("zin%d" % i, [128, 8, TT], BF16) for i in range(2)]
        hin = [kb.sb("hin%d" % i, [128, 8, TT]) for i in range(2)]
        pin = [kb.sb("pin%d" % i, [128, 2, TT]) for i in range(2)]
        pbf = [kb.sb("pbf%d" % i, [128, 2, TT], BF16) for i in range(2)]
        z_t, h_t, p_t, pb_t = kb.toks(2, "z"), kb.toks(2, "h"), kb.toks(2, "p"), kb.toks(2, "pb")
        sq = kb.sb("sq", [128, 8, TT], BF16)
        sq_t = kb.tok("sq")
        hn = kb.sb("hn", [128, 8, TT], BF16)
        hn_t = kb.tok("hn")
        rstd = kb.sb("rstd", [128, TT])
        rstd_t = kb.tok("rstd")
        gsb = [kb.sb("gsb%d" % i, [128, TT]) for i in range(2)]
        g_t = kb.toks(2, "g")
        tmp = [kb.sb("tmp%d" % i, [128, TT]) for i in range(2)]
        tmp_t = kb.toks(2, "tmp")
        obuf = [kb.sb("obuf%d" % i, [128, 8, TT], F32 if last else BF16) for i in range(2)]
        ob_t = kb.toks(2, "ob")
        psb = [kb.ps("psb%d" % i, [128, TT]) for i in range(8)]
        ps_t = [kb.ptok("ps%d" % i) for i in range(8)]
        pc = [0]

        def nextps():
            i = pc[0] % 8
            pc[0] += 1
            return psb[i], ps_t[i]

        zv = zT.rearrange("(c p) t -> p c t", p=128)
        hv = hT.rearrange("(c p) t -> p c t", p=128)
        pv = pT.rearrange("(c p) t -> p c t", p=128)

        def norm(src, src_t, gcol0, dst_fn, dst_toks):
            for dc in range(8):
                kb.op("act", lambda dc=dc: nc.scalar.activation(out=sq[:, dc, :], in_=src[:, dc, :], func=AF.Square),
                      reads=[src_t], writes=[sq_t])
            pb_, pt_ = nextps()
            for dc in range(8):
                kb.op("pe", lambda dc=dc: nc.tensor.matmul(pb_[:], lhsT=ones[:], rhs=sq[:, dc, :], start=(dc == 0), stop=(dc == 7)),
                      reads=[sq_t, ctk], writes=[pt_])
            rstd_from_sumsq(kb, pb_[:], pt_, rstd[:], rstd_t, 1e-6)
            for dc in range(8):
                kb.op("dve", lambda dc=dc: nc.vector.scalar_tensor_tensor(
                    out=dst_fn(dc), in0=src[:, dc, :], scalar=vec[:, gcol0 + dc:gcol0 + dc + 1], in1=rstd[:],
                    op0=ALU.mult, op1=ALU.mult), reads=[src_t, rstd_t, vtk], writes=dst_toks)

        def loads(j):
            s = j % 2
            cs = slice(j * TT, (j + 1) * TT)
            kb.dma("sp", zin[s][:], zv[:, :, cs], z_t[s], writes=[z_t[s]])
            kb.dma("sp", hin[s][:], hv[:, :, cs], h_t[s], writes=[h_t[s]])
            kb.dma("sp", pin[s][:], pv[:, :, cs], p_t[s], writes=[p_t[s]])

        loads(0)
        for j in range(ntile):
            s = j % 2
            cs = slice(j * TT, (j + 1) * TT)
            if j + 1 < ntile:
                loads(j + 1)
            kb.op("pool", lambda s=s: nc.gpsimd.tensor_copy(out=pbf[s][:], in_=pin[s][:]), reads=[p_t[s]], writes=[pb_t[s]])
            h2 = hin[s]
            for dc in range(8):
                pb_, pt_ = nextps()
                for kc in range(8):
                    kb.op("pe", lambda dc=dc, kc=kc, pb_=pb_: nc.tensor.matmul(
                        pb_[:], lhsT=Wo[:, kc, dc * 128:(dc + 1) * 128], rhs=zin[s][:, kc, :], start=(kc == 0), stop=(kc == 7)),
                        reads=[Wo_t, z_t[s]], writes=[pt_])
                kb.op("dve", lambda dc=dc, pb_=pb_: nc.vector.tensor_tensor(out=h2[:, dc, :], in0=h2[:, dc, :], in1=pb_[:], op=ALU.add),
                      reads=[pt_, h_t[s]], writes=[h_t[s]])
            norm(h2, h_t[s], 0, lambda dc: hn[:, dc, :], [hn_t])
            for dc in range(8):
                pg, pgt = nextps()
                for kc in range(8):
                    kb.op("pe", lambda dc=dc, kc=kc, pg=pg: nc.tensor.matmul(
                        pg[:], lhsT=Wg[:, kc, dc * 128:(dc + 1) * 128], rhs=hn[:, kc, :], start=(kc == 0), stop=(kc == 7)),
                        reads=[Wg_t, hn_t], writes=[pgt])
                pp, ppt = nextps()
                for kc in range(2):
                    kb.op("pe", lambda dc=dc, kc=kc, pp=pp: nc.tensor.matmul(
                        pp[:], lhsT=Wp[:, kc, dc * 128:(dc + 1) * 128], rhs=pbf[s][:, kc, :], start=(kc == 0), stop=(kc == 1)),
                        reads=[Wp_t, pb_t[s]], writes=[ppt])
                b = dc % 2
                kb.op("act", lambda pg=pg, b=b: nc.scalar.activation(out=gsb[b][:], in_=pg[:], func=AF.Sigmoid),
                      reads=[pgt], writes=[g_t[b]])
                kb.op("dve", lambda pp=pp, b=b: nc.vector.tensor_tensor(out=tmp[b][:], in0=gsb[b][:], in1=pp[:], op=ALU.mult),
                      reads=[g_t[b], ppt], writes=[tmp_t[b]])
                kb.op("pool", lambda dc=dc, b=b: nc.gpsimd.tensor_tensor(out=h2[:, dc, :], in0=h2[:, dc, :], in1=tmp[b][:], op=ALU.add),
                      reads=[tmp_t[b], h_t[s]], writes=[h_t[s]])
            norm(h2, h_t[s], 8, lambda dc: obuf[s][:, dc, :], [ob_t[s]])
            if last:
                kb.dma("sp", oT.rearrange("(c p) t -> p c t", p=128)[:, :, cs], obuf[s][:], ob_t[s], reads=[ob_t[s]])
            else:
                kb.dma("sp", hnT.rearrange("(c p) t -> p c t", p=128)[:, :, cs], obuf[s][:], ob_t[s], reads=[ob_t[s]])
                kb.dma("sp", h1T.rearrange("(c p) t -> p c t", p=128)[:, :, cs], h2[:], h_t[s], reads=[h_t[s]])
        kb.finish(ob_t + h_t)
    return nc


def run_post(zT_list, hT_list, pT_list, w_out, w_gate, w_proj, pe_norm, nxt_norm, last):
    n = len(zT_list)
    NT = zT_list[0].shape[1]
    nc = build_post(NT, last)
    vecs = np.concatenate([pe_norm.reshape(8, 128).T, nxt_norm.reshape(8, 128).T], axis=1).astype(np.float32)
    maps = [dict(zT=np.ascontiguousarray(zT_list[i]), hT=np.ascontiguousarray(hT_list[i]), pT=np.ascontiguousarray(pT_list[i]),
                 w_out=w_out, w_gate=w_gate, w_proj=w_proj, vecs=np.ascontiguousarray(vecs)) for i in range(n)]
    res = run_bass_kernel_spmd(nc, maps, core_ids=list(range(n))).results
    return res


TA = 256
NCH = TA // 64
GN_EPS = 64e-5
FILL = 6
RW_STOP = 0
RW_DBG = None


def build_rwkv(NT):
    nc = new_nc()
    xT = dram_in(nc, "xT", [D, NT])
    Wc = dram_in(nc, "Wc", [D, 4 * 256])
    w1 = dram_in(nc, "w1", [D, 64])
    a1 = dram_in(nc, "a1", [D, 64])
    w2c = dram_in(nc, "w2c", [64, 256])
    a2c = dram_in(nc, "a2c", [64, 256])
    vecA = dram_in(nc, "vecA", [128, 56])
    vecC = dram_in(nc, "vecC", [128, 2 * 8])
    lnwb = dram_in(nc, "lnwb", [1, 512])
    identd = dram_in(nc, "ident", [128, 128])
    zT = dram_out(nc, "zT", [256, NT], BF16)
    dbg_d = dram_out(nc, "dbg", [128, 2048]) if RW_DBG else None
    dbg_tok = []

    def dbg(name, ap, tok, j=0):
        if RW_DBG == name and j == 0 and not dbg_tok:
            t = kb.tok("dbgt")
            dbg_tok.append(t)
            p, f = ap.shape[0], int(np.prod(ap.shape[1:]))
            kb.dma("sp", dbg_d[0:p, 0:f], ap, t, reads=[tok])
    ntile = NT // TA
    with ExitStack() as es:
        kb = KB(nc, es)
        V, S, G, P = nc.vector, nc.scalar, nc.gpsimd, nc.tensor
        ctk = kb.tok("cst")
        ident = kb.sb("ident_sb", [128, 128])
        kb.dma("sp", ident[:], identd[:, :], ctk, writes=[ctk])
        onesb = kb.sb("onesb", [128, 128], BF16)
        blk = kb.sb("blk", [128, 128], BF16)
        bind = kb.sb("bind", [128, 2])
        kb.op("pool", lambda: G.memset(onesb[:], 1.0 / D), writes=[ctk])
        kb.op("pool", lambda: G.memset(blk[:], 0.0), writes=[ctk])
        kb.op("pool", lambda: G.memset(blk[0:64, 0:64], 1.0), writes=[ctk])
        kb.op("pool", lambda: G.memset(blk[64:128, 64:128], 1.0), writes=[ctk])
        kb.op("pool", lambda: G.memset(bind[:], 0.0), writes=[ctk])
        kb.op("pool", lambda: G.memset(bind[0:64, 0:1], 1.0), writes=[ctk])
        kb.op("pool", lambda: G.memset(bind[64:128, 1:2], 1.0), writes=[ctk])
        mAT = kb.sb("mAT", [64, 4, 4, 64])
        mL = kb.sb("mL", [64, 4, 64])
        kb.op("pool", lambda: G.memset(mAT[:], 1.0), writes=[ctk])
        kb.op("pool", lambda: G.memset(mL[:], 1.0), writes=[ctk])
        for q in range(4):
            kb.op("pool", lambda q=q: G.affine_select(out=mAT[:, :, q, :], in_=mAT[:, :, q, :], pattern=[[0, 4], [1, 64]],
                                                      compare_op=(ALU.is_gt if q % 2 == 0 else ALU.is_ge), fill=0.0, base=0,
                                                      channel_multiplier=-1), writes=[ctk])
        kb.op("pool", lambda: G.affine_select(out=mL[:], in_=mL[:], pattern=[[0, 4], [-1, 64]], compare_op=ALU.is_gt, fill=0.0,
                                              base=0, channel_multiplier=1), writes=[ctk])
        vA = kb.sb("vA", [128, 56])
        vC = kb.sb("vC", [128, 2, 8])
        kb.dma("sp", vA[:], vecA[:, :], ctk, writes=[ctk])
        kb.dma("sp", vC[:].rearrange("p a b -> p (a b)"), vecC[:, :], ctk, writes=[ctk])
        lnw = kb.sb("lnw", [64, 512])
        kb.dma("sp", lnw[:], lnwb.partition_broadcast(64), ctk, writes=[ctk])
        kb.op("dve", lambda: V.tensor_scalar(out=vC[:, :, 5:6], in0=vC[:, :, 0:1], scalar1=-1.0, scalar2=None, op0=ALU.mult),
              reads=[ctk], writes=[ctk])
        kb.op("dve", lambda: V.tensor_scalar(out=vC[:, :, 6:7], in0=vC[:, :, 3:4], scalar1=-1.0, scalar2=1.0, op0=ALU.mult, op1=ALU.add),
              reads=[ctk], writes=[ctk])
        W0, A0, KK_, KA, RK, NW0, OMKA = range(7)

        xin = kb.sb("xin", [128, 8, TA])
        x_t = kb.tok("xin")
        xin_flat = xin[:].rearrange("p a b -> p (a b)")
        stage = [xin_flat[:, i * 1024:(i + 1) * 1024] for i in range(2)]
        stok = [x_t, x_t]
        omu = kb.sb("omu", [128, 48])
        kb.op("dve", lambda: V.tensor_scalar(out=omu[:], in0=vA[:, 8:56], scalar1=-1.0, scalar2=1.0, op0=ALU.mult, op1=ALU.add), reads=[ctk], writes=[ctk])
        W = kb.sb("Wm", [128, 8, 4, 2, 256], BF16)
        W_t = kb.tok("Wm")
        for dc in range(8):
            sgi = dc % 2
            kb.dma("sp", stage[sgi][:, 0:1024], Wc[dc * 128:(dc + 1) * 128, :], x_t, writes=[x_t])
            for i in range(4):
                kb.op("dve", lambda dc=dc, i=i, sgi=sgi: V.tensor_scalar(out=W[:, dc, i, 0, :], in0=stage[sgi][:, i * 256:(i + 1) * 256],
                                                                         scalar1=omu[:, i * 8 + dc:i * 8 + dc + 1], scalar2=None, op0=ALU.mult),
                      reads=[x_t, ctk], writes=[W_t])
                kb.op("dve", lambda dc=dc, i=i, sgi=sgi: V.tensor_scalar(out=W[:, dc, i, 1, :], in0=stage[sgi][:, i * 256:(i + 1) * 256],
                                                                         scalar1=vA[:, 8 + i * 8 + dc:9 + i * 8 + dc], scalar2=None, op0=ALU.mult),
                      reads=[x_t, ctk], writes=[W_t])
        W1 = kb.sb("W1", [128, 8, 2, 64], BF16)
        A1 = kb.sb("A1", [128, 8, 2, 64], BF16)
        W2 = kb.sb("W2", [64, 256], BF16)
        A2 = kb.sb("A2", [64, 256], BF16)
        for (src_d, dstw, i) in ((w1, W1, 4), (a1, A1, 5)):
            kb.dma("sp", stage[0][:, 0:512].rearrange("p (c k) -> p c k", c=8), src_d.rearrange("(c p) k -> p c k", p=128), x_t, writes=[x_t])
            for dc in range(8):
                kb.op("dve", lambda dc=dc, i=i, dstw=dstw: V.tensor_scalar(out=dstw[:, dc, 0, :], in0=stage[0][:, dc * 64:(dc + 1) * 64],
                                                                           scalar1=omu[:, i * 8 + dc:i * 8 + dc + 1], scalar2=None, op0=ALU.mult),
                      reads=[x_t, ctk], writes=[W_t])
                kb.op("dve", lambda dc=dc, i=i, dstw=dstw: V.tensor_scalar(out=dstw[:, dc, 1, :], in0=stage[0][:, dc * 64:(dc + 1) * 64],
                                                                           scalar1=vA[:, 8 + i * 8 + dc:9 + i * 8 + dc], scalar2=None, op0=ALU.mult),
                      reads=[x_t, ctk], writes=[W_t])
        kb.dma("sp", stage[0][0:64, 0:256], w2c[:, :], x_t, writes=[x_t])
        kb.op("dve", lambda: V.tensor_copy(out=W2[:], in_=stage[0][0:64, 0:256]), reads=[x_t], writes=[W_t])
        kb.dma("sp", stage[0][0:64, 0:256], a2c[:, :], x_t, writes=[x_t])
        kb.op("dve", lambda: V.tensor_copy(out=A2[:], in_=stage[0][0:64, 0:256]), reads=[x_t], writes=[W_t])

        hn = kb.sb("hn", [128, 8, TA + 1], BF16)
        hn_t = kb.tok("hn")
        kb.op("pool", lambda: G.memset(hn[:], 0.0), writes=[hn_t])
        sqx = kb.sb("sqx", [128, 8, TA], BF16)
        sqx_t = kb.tok("sqx")
        rstd = kb.sb("rstd", [128, TA])
        rstd_t = kb.tok("rstd")
        xm = [kb.sb("xm%d" % i, [128, 8, TA], BF16) for i in range(2)]
        xm_t = kb.toks(2, "xm")
        th = kb.sb("th", [64, TA], BF16)
        th_t = kb.tok("th")

        def cm(name):
            return kb.sb(name, [128, 2, TA]), kb.tok(name)
        r_, r_t = cm("r_")
        k_, k_t = cm("k_")
        v_, v_t = cm("v_")
        sg2 = [kb.sb("sg%d" % i, [128, 2, TA]) for i in range(2)]
        sg2_t = kb.toks(2, "sg")
        nlw, nlw_t = cm("nlw")
        a_, a_t = cm("a_")
        kk, kk_t = cm("kk")
        t1, t1_t = cm("t1")
        prod, prod_t = cm("prod")
        CA, CA_t = cm("CA")
        CB, CB_t = cm("CB")
        rn, rn_t = cm("rn")
        sqk = kb.sb("sqk", [128, 2, TA], BF16)
        sqk_t = kb.tok("sqk")
        AR = kb.sb("AR", [128, 2, NCH, 2, 64])
        AR_t = kb.tok("AR")
        BK = kb.sb("BK", [128, 2, NCH, 2, 64])
        BK_t = kb.tok("BK")
        ARh2 = [kb.sb("ARh%d" % i, [64, 4, NCH, 2, 64], BF16) for i in range(2)]
        ARh2_t = kb.toks(2, "ARh")
        BKh2 = [kb.sb("BKh%d" % i, [64, 4, NCH, 2, 64], BF16) for i in range(2)]
        BKh2_t = kb.toks(2, "BKh")
        WCs2 = [kb.sb("WCs%d" % i, [64, 4, NCH]) for i in range(2)]
        WCs2_t = kb.toks(2, "WCs")
        TM2 = [[kb.sb("TM%d_%d" % (pp, i), [64, NCH, 4, 64], BF16) for i in range(4)] for pp in range(2)]
        TM2_t = [kb.toks(4, "TM") for pp in range(2)]
        TA_, TB_, TK_, TV_ = range(4)
        RKs2 = [kb.sb("RKs%d" % i, [64, NCH, 4]) for i in range(2)]
        RKs2_t = kb.toks(2, "RKs")
        Ybuf2 = [kb.sb("Ybuf%d" % i, [64, NCH, 4, 64]) for i in range(2)]
        Y2_t = kb.toks(2, "Ybuf")
        ATsk = [kb.sb("ATs%d" % i, [64, 4, 4, 64], BF16) for i in range(2)]
        ATsk_t = kb.toks(2, "ATs")
        Lsk = [kb.sb("Ls%d" % i, [64, 4, 64], BF16) for i in range(2)]
        Lsk_t = kb.toks(2, "Ls")
        Z = [kb.sb("Z%d" % i, [64, 4, 128]) for i in range(2)]
        Z_t = kb.toks(2, "Z")
        Zbk = [[kb.sb("Zb%d_%d" % (kk_, i), [64, 4, 128], BF16) for i in range(2)] for kk_ in range(2)]
        Zbk_t = [kb.toks(2, "Zb") for kk_ in range(2)]
        identb = kb.sb("identb", [64, 64], BF16)
        kb.op("dve", lambda: V.tensor_copy(out=identb[:], in_=ident[0:64, 0:64]), reads=[ctk], writes=[ctk])
        LBk = [[kb.sb("LB%d_%d" % (kk_, i), [64, 4, 2, 64], BF16) for i in range(2)] for kk_ in range(2)]
        LBk_t = [kb.toks(2, "LB") for kk_ in range(2)]
        RHsk = [kb.sb("RHs%d" % i, [64, 4, 64]) for i in range(2)]
        RHsk_t = kb.toks(2, "RHs")
        MTsk = [kb.sb("MTs%d" % i, [64, 4, 64]) for i in range(2)]
        MTsk_t = kb.toks(2, "MTs")
        Hs = [kb.sb("Hs%d" % i, [64, 4, 64]) for i in range(2)]
        Hs_t = kb.toks(2, "Hs")
        kb.op("pool", lambda: G.memset(Hs[0][:], 0.0), writes=[Hs_t[0]])
        st1 = kb.sb("st1", [64, NCH * 4])
        st2 = kb.sb("st2", [64, NCH * 4])
        st3 = kb.sb("st3", [64, NCH * 4])
        st_t = kb.tok("st")
        zout = kb.sb("zout", [128, 2, TA], BF16)
        zo_t = kb.tok("zout")

        PR = kb.ps("PR", [128, 512]); PR_t = kb.ptok("PR")
        TIN = kb.ps("TIN", [128, 512]); TIN_t = kb.ptok("TIN")
        ATpk = [kb.ps("ATp%d" % i, [128, 512]) for i in range(2)]; ATpk_t = [kb.ptok("ATp%d" % i) for i in range(2)]
        APpk = [kb.ps("APp%d" % i, [128, 512]) for i in range(2)]; APpk_t = [kb.ptok("APp%d" % i) for i in range(2)]
        SQpk = [kb.ps("SQp%d" % i, [128, 512]) for i in range(2)]; SQpk_t = [kb.ptok("SQp%d" % i) for i in range(2)]

        xv = xT.rearrange("(c p) t -> p c t", p=128)
        hstate = [0]

        def prep(j):
            p = j % 2
            cs = slice(j * TA, (j + 1) * TA)
            ARh, ARh_t, BKh, BKh_t = ARh2[p], ARh2_t[p], BKh2[p], BKh2_t[p]
            TM, TM_t = TM2[p], TM2_t[p]
            WCs, WCs_t, RKs, RKs_t = WCs2[p], WCs2_t[p], RKs2[p], RKs2_t[p]
            sg, sg_t = sg2[p], sg2_t[p]
            Ybuf, Y_t = Ybuf2[p], Y2_t[p]
            kb.dma("sp", xin[:], xv[:, :, cs], x_t, writes=[x_t])
            if j > 0:
                kb.op("pool", lambda: G.tensor_copy(out=hn[:, :, 0:1], in_=hn[:, :, TA:TA + 1]), reads=[hn_t], writes=[hn_t])
            for dc in range(8):
                kb.op("act", lambda dc=dc: S.activation(out=sqx[:, dc, :], in_=xin[:, dc, :], func=AF.Square), reads=[x_t], writes=[sqx_t])
            for dc in range(8):
                kb.op("pe", lambda dc=dc: P.matmul(PR[:, 0:TA], lhsT=onesb[:], rhs=sqx[:, dc, :], start=(dc == 0), stop=(dc == 7)),
                      reads=[sqx_t, ctk], writes=[PR_t])
            kb.op("act", lambda: S.activation(out=rstd[:], in_=PR[:, 0:TA], func=AF.Ln, bias=1e-6, scale=1.0), reads=[PR_t], writes=[rstd_t])
            kb.op("act", lambda: S.activation(out=rstd[:], in_=rstd[:], func=AF.Exp, scale=-0.5), reads=[rstd_t], writes=[rstd_t])
            for dc in range(8):
                kb.op("dve", lambda dc=dc: V.scalar_tensor_tensor(out=hn[:, dc, 1:TA + 1], in0=xin[:, dc, :], scalar=vA[:, dc:dc + 1],
                                                                 in1=rstd[:], op0=ALU.mult, op1=ALU.mult),
                      reads=[x_t, rstd_t, ctk], writes=[hn_t])
            for i in range(6):
                if i < 4:
                    dst, dst_t = ((r_, r_t), (k_, k_t), (v_, v_t), (sg, sg_t))[i]
                    for hc in range(2):
                        for dc in range(8):
                            kb.op("pe", lambda dc=dc, hc=hc, i=i: P.matmul(
                                PR[:, 0:TA], lhsT=W[:, dc, i, 0, hc * 128:hc * 128 + 128], rhs=hn[:, dc, 1:TA + 1],
                                start=(dc == 0), stop=False), reads=[W_t, hn_t], writes=[PR_t])
                            kb.op("pe", lambda dc=dc, hc=hc, i=i: P.matmul(
                                PR[:, 0:TA], lhsT=W[:, dc, i, 1, hc * 128:hc * 128 + 128], rhs=hn[:, dc, 0:TA],
                                start=False, stop=(dc == 7)), reads=[W_t, hn_t], writes=[PR_t])
                        if i == 3:
                            kb.op("act", lambda hc=hc, dst=dst: S.activation(out=dst[:, hc, :], in_=PR[:, 0:TA], func=AF.Silu),
                                  reads=[PR_t], writes=[dst_t])
                        else:
                            kb.op("dve", lambda hc=hc, dst=dst: V.tensor_copy(out=dst[:, hc, :], in_=PR[:, 0:TA]),
                                  reads=[PR_t], writes=[dst_t])
                else:
                    Wl, W2l = (W1, W2) if i == 4 else (A1, A2)
                    for dc in range(8):
                        kb.op("pe", lambda dc=dc, Wl=Wl: P.matmul(PR[0:64, 0:TA], lhsT=Wl[:, dc, 0, :], rhs=hn[:, dc, 1:TA + 1],
                                                                 start=(dc == 0), stop=False), reads=[W_t, hn_t], writes=[PR_t])
                        kb.op("pe", lambda dc=dc, Wl=Wl: P.matmul(PR[0:64, 0:TA], lhsT=Wl[:, dc, 1, :], rhs=hn[:, dc, 0:TA],
                                                                 start=False, stop=(dc == 7)), reads=[W_t, hn_t], writes=[PR_t])
                    kb.op("act", lambda i=i: S.activation(out=th[:], in_=PR[0:64, 0:TA], func=(AF.Tanh if i == 4 else AF.Copy)),
                          reads=[PR_t], writes=[th_t])
                    for hc in range(2):
                        kb.op("pe", lambda hc=hc, W2l=W2l: P.matmul(PR[:, 0:TA], lhsT=W2l[:, hc * 128:(hc + 1) * 128], rhs=th[:],
                                                                    start=True, stop=True), reads=[W_t, th_t], writes=[PR_t])
                        if i == 4:
                            kb.op("act", lambda hc=hc: S.activation(out=nlw[:, hc, :], in_=PR[:, 0:TA], func=AF.Exp, scale=-1.0,
                                                                    bias=vC[:, hc, NW0:NW0 + 1]), reads=[PR_t, ctk], writes=[nlw_t])
                            kb.op("act", lambda hc=hc: S.activation(out=nlw[:, hc, :], in_=nlw[:, hc, :], func=AF.Ln, scale=1.0, bias=1.0),
                                  reads=[nlw_t], writes=[nlw_t])
                            kb.op("act", lambda hc=hc: S.activation(out=nlw[:, hc, :], in_=nlw[:, hc, :], func=AF.Exp, scale=-1.0, bias=-0.5),
                                  reads=[nlw_t], writes=[nlw_t])
                        else:
                            kb.op("act", lambda hc=hc: S.activation(out=a_[:, hc, :], in_=PR[:, 0:TA], func=AF.Sigmoid, scale=1.0,
                                                                    bias=vC[:, hc, A0:A0 + 1]), reads=[PR_t, ctk], writes=[a_t])
            for hc in range(2):
                kb.op("dve", lambda hc=hc: V.tensor_scalar(out=kk[:, hc, :], in0=k_[:, hc, :], scalar1=vC[:, hc, KK_:KK_ + 1], scalar2=None,
                                                           op0=ALU.mult), reads=[k_t, ctk], writes=[kk_t])
                kb.op("act", lambda hc=hc: S.activation(out=sqk[:, hc, :], in_=kk[:, hc, :], func=AF.Square), reads=[kk_t], writes=[sqk_t])
                kb.op("pe", lambda hc=hc: P.matmul(PR[:, 0:TA], lhsT=blk[:], rhs=sqk[:, hc, :], start=True, stop=True),
                      reads=[sqk_t, ctk], writes=[PR_t])
                kb.op("act", lambda hc=hc: S.activation(out=rn[:, hc, :], in_=PR[:, 0:TA], func=AF.Ln, bias=1e-24, scale=1.0),
                      reads=[PR_t], writes=[rn_t])
                kb.op("act", lambda hc=hc: S.activation(out=rn[:, hc, :], in_=rn[:, hc, :], func=AF.Exp, scale=-0.5), reads=[rn_t], writes=[rn_t])
            kb.op("dve", lambda: V.tensor_tensor(out=kk[:], in0=kk[:], in1=rn[:], op=ALU.mult), reads=[kk_t, rn_t], writes=[kk_t])
            for hc in range(2):
                kb.op("dve", lambda hc=hc: V.tensor_scalar(out=t1[:, hc, :], in0=a_[:, hc, :], scalar1=vC[:, hc, KA:KA + 1],
                                                           scalar2=vC[:, hc, OMKA:OMKA + 1], op0=ALU.mult, op1=ALU.add),
                      reads=[a_t, ctk], writes=[t1_t])
            kb.op("dve", lambda: V.tensor_tensor(out=t1[:], in0=t1[:], in1=k_[:], op=ALU.mult), reads=[t1_t, k_t], writes=[t1_t])
            kb.op("pool", lambda: G.tensor_tensor(out=a_[:], in0=a_[:], in1=kk[:], op=ALU.mult), reads=[a_t, kk_t], writes=[a_t])
            for hc in range(2):
                kb.op("dve", lambda hc=hc: V.scalar_tensor_tensor(out=prod[:, hc, :], in0=r_[:, hc, :], scalar=vC[:, hc, RK:RK + 1],
                                                                 in1=t1[:, hc, :], op0=ALU.mult, op1=ALU.mult),
                      reads=[r_t, t1_t, ctk], writes=[prod_t])
            def v3(t):
                return t[:].rearrange("p a (c t) -> p (a c) t", t=64)
            src, src_t = nlw, nlw_t
            pp = [(CA, CA_t), (CB, CB_t)]
            for li, sft in enumerate((1, 2, 4, 8, 16, 32)):
                dst, dst_t = pp[li % 2]
                kb.op("pool", lambda src=src, dst=dst, sft=sft: G.tensor_tensor(out=v3(dst)[:, :, sft:], in0=v3(src)[:, :, sft:],
                                                                               in1=v3(src)[:, :, 0:64 - sft], op=ALU.add),
                      reads=[src_t], writes=[dst_t])
                kb.op("pool", lambda src=src, dst=dst, sft=sft: G.tensor_copy(out=v3(dst)[:, :, 0:sft], in_=v3(src)[:, :, 0:sft]),
                      reads=[src_t], writes=[dst_t])
                src, src_t = dst, dst_t
            cn, cn_t = src, src_t
            assert cn is CB
            kb.op("pool", lambda: G.tensor_tensor(out=nlw[:], in0=cn[:], in1=nlw[:], op=ALU.subtract), reads=[cn_t, nlw_t], writes=[nlw_t])
            kb.op("act", lambda: S.activation(out=CA[:], in_=cn[:], func=AF.Exp, scale=-1.0), reads=[cn_t], writes=[CA_t])
            kb.op("act", lambda: S.activation(out=nlw[:], in_=nlw[:], func=AF.Exp, scale=-1.0), reads=[nlw_t], writes=[nlw_t])
            kb.op("act", lambda: S.activation(out=CB[:], in_=cn[:], func=AF.Exp, scale=1.0), reads=[cn_t], writes=[CB_t])
            eneg, eneg_t, enegx, enegx_t, epos, epos_t = CA, CA_t, nlw, nlw_t, CB, CB_t

            def c3(t, hc):
                return t[:, hc, :].rearrange("p (c t) -> p c t", t=64)
            for hc in range(2):
                kb.op("dve", lambda hc=hc: V.tensor_tensor(out=AR[:, hc, :, 0, :], in0=c3(kk, hc), in1=c3(enegx, hc), op=ALU.mult),
                      reads=[kk_t, enegx_t], writes=[AR_t])
                kb.op("pool", lambda hc=hc: G.tensor_tensor(out=AR[:, hc, :, 1, :], in0=c3(r_, hc), in1=c3(eneg, hc), op=ALU.mult),
                      reads=[r_t, eneg_t], writes=[AR_t])
                kb.op("dve", lambda hc=hc: V.scalar_tensor_tensor(out=BK[:, hc, :, 0, :], in0=c3(a_, hc), scalar=-1.0, in1=c3(epos, hc),
                                                                 op0=ALU.mult, op1=ALU.mult), reads=[a_t, epos_t], writes=[BK_t])
                kb.op("pool", lambda hc=hc: G.tensor_tensor(out=BK[:, hc, :, 1, :], in0=c3(t1, hc), in1=c3(epos, hc), op=ALU.mult),
                      reads=[t1_t, epos_t], writes=[BK_t])
                for hh in range(2):
                    h = 2 * hc + hh
                    kb.op("dve", lambda hc=hc, hh=hh, h=h: V.tensor_copy(out=WCs[:, h, :], in_=c3(eneg, hc)[hh * 64:(hh + 1) * 64, :, 63]),
                          reads=[eneg_t], writes=[WCs_t])
                    kb.op("dve", lambda hc=hc, hh=hh, h=h: V.tensor_copy(out=ARh[:, h, :, :, :].rearrange("p c a t -> p (c a t)"),
                                                                         in_=AR[hh * 64:(hh + 1) * 64, hc, :, :, :].rearrange("p c a t -> p (c a t)")),
                          reads=[AR_t], writes=[ARh_t])
                    kb.op("dve", lambda hc=hc, hh=hh, h=h: V.tensor_copy(out=BKh[:, h, :, :, :].rearrange("p c a t -> p (c a t)"),
                                                                         in_=BK[hh * 64:(hh + 1) * 64, hc, :, :, :].rearrange("p c a t -> p (c a t)")),
                          reads=[BK_t], writes=[BKh_t])
            srcs = [(lambda hc, c: AR[:, hc, c, 0, :], AR_t), (lambda hc, c: BK[:, hc, c, 0, :], BK_t),
                    (lambda hc, c: BK[:, hc, c, 1, :], BK_t), (lambda hc, c: v_[:, hc, c * 64:(c + 1) * 64], v_t)]
            ne = 0
            for q in range(4):
                fn, ft = srcs[q]
                for c0 in range(0, NCH, 2):
                    for cc in range(2):
                        for hc in range(2):
                            kb.op("pe", lambda fn=fn, c0=c0, cc=cc, hc=hc: P.transpose(
                                TIN[0:64, cc * 256 + hc * 128:cc * 256 + hc * 128 + 128], fn(hc, c0 + cc), ident[:]),
                                reads=[ft, ctk], writes=[TIN_t])
                    e = "act" if ne % 2 == 0 else "dve"
                    ne += 1
                    dstv = TM[q][:, c0:c0 + 2, :, :].rearrange("p c h n -> p (c h n)")
                    if e == "act":
                        kb.op("act", lambda dstv=dstv: S.copy(out=dstv, in_=TIN[0:64, :]), reads=[TIN_t], writes=[TM_t[q]])
                    else:
                        kb.op("dve", lambda dstv=dstv: V.tensor_copy(out=dstv, in_=TIN[0:64, :]), reads=[TIN_t], writes=[TM_t[q]])
            for c in range(NCH):
                for hc in range(2):
                    kb.op("pe", lambda c=c, hc=hc: P.matmul(TIN[0:64, c * 4 + hc * 2:c * 4 + hc * 2 + 2], lhsT=prod[:, hc, c * 64:(c + 1) * 64],
                                                            rhs=bind[:], start=True, stop=True), reads=[prod_t, ctk], writes=[TIN_t])
            kb.op("dve", lambda: V.tensor_copy(out=RKs[:].rearrange("p c h -> p (c h)"), in_=TIN[0:64, 0:NCH * 4]), reads=[TIN_t], writes=[RKs_t])

        hseq = [0]

        def chunk_gen(j, c, k):
            p = j % 2
            ARh, ARh_t, BKh, BKh_t = ARh2[p], ARh2_t[p], BKh2[p], BKh2_t[p]
            TM, TM_t = TM2[p], TM2_t[p]
            WCs, WCs_t = WCs2[p], WCs2_t[p]
            Ybuf, Y_t = Ybuf2[p], Y2_t[p]
            ATp, ATp_t, APp, APp_t, SQp, SQp_t = ATpk[k], ATpk_t[k], APpk[k], APpk_t[k], SQpk[k], SQpk_t[k]
            ATs, ATs_t, Ls, Ls_t = ATsk[k], ATsk_t[k], Lsk[k], Lsk_t[k]
            Zb, Zb_t, LB, LB_t = Zbk[k], Zbk_t[k], LBk[k], LBk_t[k]
            RHs, RHs_t, MTs, MTs_t = RHsk[k], RHsk_t[k], MTsk[k], MTsk_t[k]
            def opnd(h):
                return h // 2, 64 * (h % 2)
            for half in range(2):
                for hl in range(2):
                    h = 2 * half + hl
                    hc, pb = opnd(h)
                    rhs = ARh[:, h, c, :, :].rearrange("p a t -> p (a t)")
                    kb.op("pe", lambda hl=hl, h=h, rhs=rhs: P.matmul(ATp[0:64, hl * 256:hl * 256 + 128], lhsT=BKh[:, h, c, 0, :],
                                                                             rhs=rhs, start=True, stop=True), reads=[ARh_t, BKh_t], writes=[ATp_t])
                    kb.op("pe", lambda hl=hl, h=h, rhs=rhs: P.matmul(ATp[0:64, hl * 256 + 128:hl * 256 + 256], lhsT=BKh[:, h, c, 1, :],
                                                                             rhs=rhs, start=True, stop=True), reads=[ARh_t, BKh_t], writes=[ATp_t])
                kb.op("dve", lambda half=half: V.tensor_tensor(
                    out=ATs[:, 2 * half:2 * half + 2, :, :].rearrange("p h q t -> p (h q t)"), in0=ATp[0:64, :],
                    in1=mAT[:, 2 * half:2 * half + 2, :, :].rearrange("p h q t -> p (h q t)"), op=ALU.mult),
                    reads=[ATp_t, ctk], writes=[ATs_t])
                yield
            for h in range(4):
                hc, pb = opnd(h)
                kb.op("pe", lambda h=h, hc=hc, pb=pb: P.matmul(SQp[0:64, h * 64:(h + 1) * 64], lhsT=ARh[:, h, c, 0, :],
                                                               rhs=BKh[:, h, c, 0, :], start=True, stop=True),
                      reads=[ARh_t, BKh_t], writes=[SQp_t])
            kb.op("dve", lambda: V.tensor_tensor(out=Ls[:].rearrange("p h s -> p (h s)"), in0=SQp[0:64, 0:256],
                                                 in1=mL[:].rearrange("p h s -> p (h s)"), op=ALU.mult), reads=[SQp_t, ctk], writes=[Ls_t])
            yield
            for h in range(4):
                kb.op("pe", lambda h=h: P.matmul(APp[0:64, h * 128 + 64:h * 128 + 128], lhsT=ATs[:, h, 2, :], rhs=TM[TV_][:, c, h, :],
                                                 start=(h == 0), stop=False, skip_group_check=True), reads=[ATs_t, TM_t[TV_]], writes=[APp_t])
                kb.op("pe", lambda h=h: P.matmul(APp[0:64, h * 128:h * 128 + 64], lhsT=identb[:], rhs=TM[TA_][:, c, h, :],
                                                 start=False, stop=False, skip_group_check=True), reads=[ctk, TM_t[TA_]], writes=[APp_t])
            zc = 0
            kb.op("act", lambda: S.copy(out=Zb[0][:].rearrange("p h x -> p (h x)"), in_=APp[0:64, :]), reads=[APp_t], writes=[Zb_t[0]])
            yield
            Bm = lambda h: ATs[:, h, 0, :]
            Lm = lambda h: Ls[:, h, :]
            Bm_t, Lm_t = ATs_t, Ls_t
            for lvl in range(6):
                for h in range(4):
                    kb.op("pe", lambda h=h, Bm=Bm, zc=zc: P.matmul(APp[0:64, h * 128:(h + 1) * 128], lhsT=Bm(h), rhs=Zb[zc][:, h, :],
                                                                   start=False, stop=(lvl == 5), skip_group_check=True), reads=[Bm_t, Zb_t[zc]], writes=[APp_t])
                if lvl < 5:
                    for h in range(4):
                        kb.op("pe", lambda h=h, Bm=Bm, Lm=Lm: P.matmul(SQp[0:64, h * 128:h * 128 + 64], lhsT=Bm(h), rhs=Lm(h), start=True, stop=True),
                              reads=[Bm_t, Lm_t], writes=[SQp_t])
                        kb.op("pe", lambda h=h, Bm=Bm, Lm=Lm: P.matmul(SQp[0:64, h * 128 + 64:h * 128 + 128], lhsT=Lm(h), rhs=Bm(h), start=True, stop=True),
                              reads=[Bm_t, Lm_t], writes=[SQp_t])
                kb.op("dve", lambda zc=zc: V.tensor_copy(out=Zb[1 - zc][:].rearrange("p h x -> p (h x)"), in_=APp[0:64, :]),
                      reads=[APp_t], writes=[Zb_t[1 - zc]])
                yield
                zc = 1 - zc
                if lvl < 5:
                    nb = lvl % 2
                    kb.op("act", lambda nb=nb: S.copy(out=LB[nb][:].rearrange("p h q s -> p (h q s)"), in_=SQp[0:64, :]),
                          reads=[SQp_t], writes=[LB_t[nb]])
                    yield
                    Lm = lambda h, nb=nb: LB[nb][:, h, 0, :]
                    Bm = lambda h, nb=nb: LB[nb][:, h, 1, :]
                    Bm_t = Lm_t = LB_t[nb]
            Zf, Zf_t = Zb[zc], Zb_t[zc]
            for h in range(4):
                kb.op("pe", lambda h=h: P.matmul(ATp[0:64, h * 64:(h + 1) * 64], lhsT=Zf[:, h, 0:64], rhs=ATs[:, h, 1, :], start=True, stop=True),
                      reads=[Zf_t, ATs_t], writes=[ATp_t])
            kb.op("dve", lambda: V.tensor_tensor(out=RHs[:], in0=ATp[0:64, 0:256].rearrange("p (h t) -> p h t", h=4), in1=ARh[:, :, c, 1, :], op=ALU.add),
                  reads=[ATp_t, ARh_t], writes=[RHs_t])
            yield
            for h in range(4):
                kb.op("pe", lambda h=h: P.matmul(ATp[0:64, 256 + h * 64:256 + (h + 1) * 64], lhsT=Zf[:, h, 0:64], rhs=TM[TB_][:, c, h, :], start=True, stop=False),
                      reads=[Zf_t, TM_t[TB_]], writes=[ATp_t])
                kb.op("pe", lambda h=h: P.matmul(ATp[0:64, 256 + h * 64:256 + (h + 1) * 64], lhsT=identb[:], rhs=identb[:], start=False, stop=True),
                      reads=[ctk], writes=[ATp_t])
            kb.op("act", lambda: S.copy(out=MTs[:].rearrange("p h n -> p (h n)"), in_=ATp[0:64, 256:512]), reads=[ATp_t], writes=[MTs_t])
            yield
            hcur, hnew = hstate[0], 1 - hstate[0]
            while hseq[0] != j * NCH + c:
                yield
            hcur, hnew = hstate[0], 1 - hstate[0]
            for h in range(4):
                o = SQp[0:64, h * 64:(h + 1) * 64]
                kb.op("pe", lambda h=h, o=o: P.matmul(o, lhsT=RHs[:, h, :], rhs=Hs[hcur][:, h, :], start=True, stop=False),
                      reads=[RHs_t, Hs_t[hcur]], writes=[SQp_t])
                kb.op("pe", lambda h=h, o=o: P.matmul(o, lhsT=ATs[:, h, 1, :], rhs=Zf[:, h, 64:128], start=False, stop=False),
                      reads=[ATs_t, Zf_t], writes=[SQp_t])
                kb.op("pe", lambda h=h, o=o: P.matmul(o, lhsT=ATs[:, h, 3, :], rhs=TM[TV_][:, c, h, :], start=False, stop=True),
                      reads=[ATs_t, TM_t[TV_]], writes=[SQp_t])
            kb.op("act", lambda: S.copy(out=Ybuf[:, c, :, :].rearrange("p h v -> p (h v)"), in_=SQp[0:64, 0:256]), reads=[SQp_t], writes=[Y_t])
            yield
            for h in range(4):
                o = SQp[0:64, 256 + h * 64:256 + (h + 1) * 64]
                kb.op("pe", lambda h=h, o=o: P.matmul(o, lhsT=MTs[:, h, :], rhs=Hs[hcur][:, h, :], start=True, stop=False),
                      reads=[MTs_t, Hs_t[hcur]], writes=[SQp_t])
                kb.op("pe", lambda h=h, o=o: P.matmul(o, lhsT=TM[TB_][:, c, h, :], rhs=Zf[:, h, 64:128], start=False, stop=False),
                      reads=[TM_t[TB_], Zf_t], writes=[SQp_t])
                kb.op("pe", lambda h=h, o=o: P.matmul(o, lhsT=TM[TK_][:, c, h, :], rhs=TM[TV_][:, c, h, :], start=False, stop=True),
                      reads=[TM_t[TK_], TM_t[TV_]], writes=[SQp_t])
            kb.op("dve", lambda: V.tensor_tensor(out=Hs[hnew][:], in0=SQp[0:64, 256:512].rearrange("p (h v) -> p h v", h=4),
                                                 in1=WCs[:, :, c:c + 1].broadcast_to([64, 4, 64]), op=ALU.mult),
                  reads=[SQp_t, WCs_t], writes=[Hs_t[hnew]])
            yield
            hstate[0] = hnew
            hseq[0] += 1

        def post9(j):
            p = j % 2
            cs = slice(j * TA, (j + 1) * TA)
            ARh, ARh_t, BKh, BKh_t = ARh2[p], ARh2_t[p], BKh2[p], BKh2_t[p]
            TM, TM_t = TM2[p], TM2_t[p]
            WCs, WCs_t, RKs, RKs_t = WCs2[p], WCs2_t[p], RKs2[p], RKs2_t[p]
            sg, sg_t = sg2[p], sg2_t[p]
            Ybuf, Y_t = Ybuf2[p], Y2_t[p]
            Yv = Ybuf[:].rearrange("p c h v -> p (c h) v")
            W1b = AR[0:64, :, :, :, :].rearrange("p a c q t -> p (a c q t)").rearrange("p (c h v) -> p c h v", c=NCH, h=4)
            W2b = BK[0:64, :, :, :, :].rearrange("p a c q t -> p (a c q t)").rearrange("p (c h v) -> p c h v", c=NCH, h=4)
            W1_t, W2_t = AR_t, BK_t
            W1v = W1b.rearrange("p c h v -> p (c h) v")
            W2v = W2b.rearrange("p c h v -> p (c h) v")
            kb.op("dve", lambda: V.tensor_reduce(out=st1[:], in_=Yv, axis=AX.X, op=ALU.add), reads=[Y_t], writes=[st_t])
            kb.op("act", lambda: S.activation(out=W1b.rearrange("p c h v -> p (c h v)"), in_=Ybuf[:].rearrange("p c h v -> p (c h v)"), func=AF.Square),
                  reads=[Y_t], writes=[W1_t])
            kb.op("dve", lambda: V.tensor_reduce(out=st2[:], in_=W1v, axis=AX.X, op=ALU.add), reads=[W1_t], writes=[st_t])
            kb.op("dve", lambda: V.tensor_scalar(out=st1[:], in0=st1[:], scalar1=1.0 / 64, scalar2=None, op0=ALU.mult), reads=[st_t], writes=[st_t])
            kb.op("dve", lambda: V.tensor_tensor(out=st3[:], in0=st1[:], in1=st1[:], op=ALU.mult), reads=[st_t], writes=[st_t])
            kb.op("dve", lambda: V.tensor_scalar(out=st2[:], in0=st2[:], scalar1=1.0 / 64, scalar2=None, op0=ALU.mult), reads=[st_t], writes=[st_t])
            kb.op("dve", lambda: V.tensor_tensor(out=st2[:], in0=st2[:], in1=st3[:], op=ALU.subtract), reads=[st_t], writes=[st_t])
            kb.op("dve", lambda: V.tensor_scalar(out=st2[:], in0=st2[:], scalar1=0.0, scalar2=None, op0=ALU.max), reads=[st_t], writes=[st_t])
            kb.op("act", lambda: S.activation(out=st2[:], in_=st2[:], func=AF.Ln, bias=GN_EPS, scale=1.0), reads=[st_t], writes=[st_t])
            kb.op("act", lambda: S.activation(out=st2[:], in_=st2[:], func=AF.Exp, scale=-0.5), reads=[st_t], writes=[st_t])
            bc = lambda t: t[:].unsqueeze(2).broadcast_to([64, NCH * 4, 64])
            kb.op("dve", lambda: V.tensor_tensor(out=W1v, in0=Yv, in1=bc(st1), op=ALU.subtract), reads=[Y_t, st_t], writes=[W1_t])
            kb.op("dve", lambda: V.tensor_tensor(out=W1v, in0=W1v, in1=bc(st2), op=ALU.mult), reads=[st_t], writes=[W1_t])
            lw_b = lnw[:, 0:256].unsqueeze(1).broadcast_to([64, NCH, 256])
            lb_b = lnw[:, 256:512].unsqueeze(1).broadcast_to([64, NCH, 256])
            W1c = W1b.rearrange("p c h v -> p c (h v)")
            W2c = W2b.rearrange("p c h v -> p c (h v)")
            kb.op("pool", lambda: G.tensor_tensor(out=W1c, in0=W1c, in1=lw_b, op=ALU.mult), reads=[ctk], writes=[W1_t])
            kb.op("pool", lambda: G.tensor_tensor(out=W1c, in0=W1c, in1=lb_b, op=ALU.add), reads=[ctk], writes=[W1_t])
            kb.op("dve", lambda: V.tensor_tensor(out=W2v, in0=TM[TV_][:].rearrange("p c h v -> p (c h) v"),
                                                 in1=RKs[:].rearrange("p c h -> p (c h)").unsqueeze(2).broadcast_to([64, NCH * 4, 64]), op=ALU.mult),
                  reads=[TM_t[TV_], RKs_t], writes=[W2_t])
            kb.op("dve", lambda: V.tensor_tensor(out=W1v, in0=W1v, in1=W2v, op=ALU.add), reads=[W2_t], writes=[W1_t])
            for hc in range(2):
                for c in range(NCH):
                    kb.op("pe", lambda hc=hc, c=c: P.transpose(PR[:, c * 64:(c + 1) * 64], W1b[:, c, 2 * hc:2 * hc + 2, :].rearrange("p h v -> p (h v)"),
                                                               ident[0:64, 0:64]), reads=[W1_t, ctk], writes=[PR_t])
                kb.op("dve", lambda hc=hc: V.tensor_tensor(out=zout[:, hc, :], in0=PR[:, 0:TA], in1=sg[:, hc, :], op=ALU.mult),
                      reads=[PR_t, sg_t], writes=[zo_t])
            kb.dma("sp", zT.rearrange("(c p) t -> p c t", p=128)[:, :, cs], zout[:], zo_t, reads=[zo_t])

        coop = Coop(kb)
        prep(0)
        if ntile > 1:
            prep(1)
        pending = [(j, c) for j in range(ntile) for c in range(NCH)]
        slots = [None, None]
        sid = [None, None]
        idx = 0
        while idx < len(pending) or slots[0] is not None or slots[1] is not None:
            for k in range(2):
                if slots[k] is None and idx < len(pending):
                    jn, cn = pending[idx]
                    other = sid[1 - k]
                    if cn == 0 and jn > 1:
                        coop.finish()
                    slots[k] = chunk_gen(jn, cn, k)
                    sid[k] = (jn, cn)
                    idx += 1
                if slots[k] is not None:
                    try:
                        next(slots[k])
                    except StopIteration:
                        jj, cc = sid[k]
                        slots[k] = None
                        sid[k] = None
                        if cc == NCH - 1:
                            coop.finish()

                            def filler(jj=jj):
                                post9(jj)
                                if jj + 2 < ntile:
                                    prep(jj + 2)
                            coop.start(filler)
                    coop.step(FILL)
        coop.finish()
        kb.finish([zo_t] + dbg_tok)
    return nc


def rwkv_maps(x, I, NT):
    maps = []
    Wfull = I["rw_w_in"][0].reshape(D, 4, 1024)
    vecA = np.concatenate([I["rw_norm"][0].reshape(8, 128).T] + [I["rw_mu"][0][i].reshape(8, 128).T for i in range(6)], axis=1)
    ident = np.eye(128, dtype=np.float32)
    for c in range(8):
        b, g = c // 4, c % 4
        cs = slice(g * 256, (g + 1) * 256)
        vc = np.zeros((128, 2, 8), np.float32)
        for vi, nm in enumerate(["rw_w0", "rw_a0", "rw_k_k", "rw_k_a"]):
            vc[:, :, vi] = I[nm][0][cs].reshape(2, 128).T
        vc[:, :, 4] = I["rw_r_k"][0].reshape(1024)[cs].reshape(2, 128).T
        maps.append(dict(
            xT=np.ascontiguousarray(x[b, :NT].T),
            Wc=np.ascontiguousarray(Wfull[:, :, cs].reshape(D, 1024)),
            w1=I["rw_w1"][0], a1=I["rw_a1"][0],
            w2c=np.ascontiguousarray(I["rw_w2"][0][:, cs]), a2c=np.ascontiguousarray(I["rw_a2"][0][:, cs]),
            vecA=np.ascontiguousarray(vecA.astype(np.float32)), vecC=np.ascontiguousarray(vc.reshape(128, 16)),
            lnwb=np.ascontiguousarray(np.concatenate([I["rw_ln_w"][0][cs], I["rw_ln_b"][0][cs]])[None, :]),
            ident=ident))
    return maps


LAM_INIT = 0.8 - 0.6 * math.exp(-0.3 * 1)


def build_attn(NT):
    nc = new_nc()
    hnT = dram_in(nc, "hnT", [D, NT], BF16)
    Wd = dram_in(nc, "Wd", [D, 4 * 256])
    sub = dram_in(nc, "sub", [1, 128])
    lqk = dram_in(nc, "lqk", [1, 256])
    invf = dram_in(nc, "invf", [128, 1])
    pos0 = dram_in(nc, "pos0", [1, TT])
    cosd = dram_in(nc, "cosd", [128, NT])
    sind = dram_in(nc, "sind", [128, NT])
    identd = dram_in(nc, "ident", [128, 128])
    zT = dram_out(nc, "zT", [256, NT], BF16)
    ntile = NT // TT
    nblk = NT // 128
    PI = math.pi
    with ExitStack() as es:
        kb = KB(nc, es)
        V, S, G, P = nc.vector, nc.scalar, nc.gpsimd, nc.tensor
        ctk = kb.tok("cst")
        ident = kb.sb("ident_sb", [128, 128])
        kb.dma("sp", ident[:], identd[:, :], ctk, writes=[ctk])
        tri = kb.sb("tri", [128, 128], BF16)
        kb.op("pool", lambda: G.memset(tri[:], 1.0), writes=[ctk])
        kb.op("pool", lambda: G.affine_select(out=tri[:], in_=tri[:], pattern=[[1, 128]], compare_op=ALU.is_ge, fill=0.0, base=0,
                                              channel_multiplier=-1), writes=[ctk])
        subb = kb.sb("subb", [128, 128])
        kb.dma("sp", subb[:], sub.partition_broadcast(128), ctk, writes=[ctk])
        kb.op("dve", lambda: V.tensor_scalar(out=subb[:], in0=subb[:], scalar1=1.0 - LAM_INIT, scalar2=None, op0=ALU.mult), reads=[ctk], writes=[ctk])
        lq = kb.sb("lq", [128, 256])
        kb.dma("sp", lq[:], lqk.partition_broadcast(128), ctk, writes=[ctk])
        lam = kb.sb("lam", [128, 4])
        kb.op("dve", lambda: V.tensor_tensor(out=lq[:, 0:64], in0=lq[:, 0:64], in1=lq[:, 64:128], op=ALU.mult), reads=[ctk], writes=[ctk])
        kb.op("dve", lambda: V.tensor_tensor(out=lq[:, 128:192], in0=lq[:, 128:192], in1=lq[:, 192:256], op=ALU.mult), reads=[ctk], writes=[ctk])
        kb.op("dve", lambda: V.tensor_reduce(out=lam[:, 0:1], in_=lq[:, 0:64], axis=AX.X, op=ALU.add), reads=[ctk], writes=[ctk])
        kb.op("dve", lambda: V.tensor_reduce(out=lam[:, 1:2], in_=lq[:, 128:192], axis=AX.X, op=ALU.add), reads=[ctk], writes=[ctk])
        kb.op("act", lambda: S.activation(out=lam[:, 0:2], in_=lam[:, 0:2], func=AF.Exp), reads=[ctk], writes=[ctk])
        kb.op("dve", lambda: V.tensor_tensor(out=lam[:, 2:3], in0=lam[:, 1:2], in1=lam[:, 0:1], op=ALU.subtract), reads=[ctk], writes=[ctk])
        kb.op("dve", lambda: V.tensor_scalar(out=lam[:, 3:4], in0=lam[:, 2:3], scalar1=-LAM_INIT, scalar2=None, op0=ALU.add), reads=[ctk], writes=[ctk])
        ivf = kb.sb("ivf", [128, 1])
        kb.dma("sp", ivf[:], invf[:, :], ctk, writes=[ctk])
        p0 = kb.sb("p0", [128, TT])
        kb.dma("sp", p0[:], pos0.partition_broadcast(128), ctk, writes=[ctk])
        ngpi = kb.sb("ngpi", [128, 1])
        kb.op("pool", lambda: G.memset(ngpi[:], -PI), writes=[ctk])

        QT = kb.sb("QT", [128, NT], BF16); QT_t = kb.tok("QT")
        KT = kb.sb("KT", [128, NT], BF16); KT_t = kb.tok("KT")
        SG = kb.sb("SG", [128, NT], BF16); SG_t = kb.tok("SG")
        Va = kb.sb("Va", [128, nblk, 129], BF16); Va_t = kb.tok("Va")
        kb.op("pool", lambda: G.memset(Va[:], 1.0), writes=[Va_t])
        hin = [kb.sb("ahin%d" % i, [128, 8, TT], BF16) for i in range(2)]
        hin_t = kb.toks(2, "ahin")
        stg = kb.sb("astg", [128, 1024])
        stg_t = kb.tok("astg")
        Wt = {nm: kb.sb("W" + nm, [128, 8, 128], BF16) for nm in ("q", "qr", "k", "kr", "v", "g")}
        W_t = kb.tok("Wt")
        sn = kb.sb("sn", [128, TT]); cs_ = kb.sb("cs_", [128, TT]); sc_t = kb.tok("sincos")
        ta = kb.sb("ta", [128, TT]); tb = kb.sb("tb", [128, TT]); tab_t = kb.tok("tab")
        PT = [kb.sb("PT%d" % i, [128, TT], BF16) for i in range(4)]
        PT_t = kb.toks(4, "PT")
        o1 = kb.sb("o1", [128, 128]); o1_t = kb.tok("o1")
        o4 = kb.sb("o4", [128, 4, 128]); o_t = kb.tok("o4")
        on4 = kb.sb("on4", [128, 4, 128]); on_t = kb.tok("on4")
        rc8 = kb.sb("rc8", [128, 9]); ss4 = kb.sb("ss4", [128, 4]); ss_t = kb.tok("ss4")
        junk = kb.sb("junk", [128, 128])
        rc = kb.sb("rc", [128, 4]); rc_t = kb.tok("rc")
        zout = [kb.sb("azout%d" % i, [128, TT], BF16) for i in range(2)]
        zo_t = kb.toks(2, "azo")
        SB = [kb.ps("SB%d" % i, [128, 512]) for i in range(4)]
        SB_t = [kb.ptok("SB%d" % i) for i in range(4)]
        AC = [kb.ps("AC%d" % i, [128, 512]) for i in range(3)]
        AC_t = [kb.ptok("AC%d" % i) for i in range(3)]
        TP_ = kb.ps("TPp", [128, 512]); TP_t = kb.ptok("TPp")

        def acc(c, qs):
            i = c * 4 + qs
            return AC[i // 3][:, (i % 3) * 129:(i % 3) * 129 + 129], AC_t[i // 3]

        hv = hnT.rearrange("(c p) t -> p c t", p=128)
        for hh in range(2):
            for gi, nm in enumerate(("q", "k", "v", "g")):
                for dc in range(8):
                    kb.dma("sp", stg[:, 0:128], Wd[dc * 128:(dc + 1) * 128, gi * 256 + hh * 128:gi * 256 + hh * 128 + 128], stg_t, writes=[stg_t])
                    kb.op("dve", lambda nm=nm, dc=dc: V.tensor_copy(out=Wt[nm][:, dc, :], in_=stg[:, 0:128]), reads=[stg_t], writes=[W_t])
                    if nm in ("q", "k"):
                        for c in range(2):
                            kb.op("dve", lambda nm=nm, dc=dc, c=c: V.tensor_scalar(out=Wt[nm + "r"][:, dc, c * 64:c * 64 + 32], in0=stg[:, c * 64 + 32:c * 64 + 64],
                                                                                   scalar1=-1.0, scalar2=None, op0=ALU.mult), reads=[stg_t], writes=[W_t])
                            kb.op("dve", lambda nm=nm, dc=dc, c=c: V.tensor_copy(out=Wt[nm + "r"][:, dc, c * 64 + 32:c * 64 + 64], in_=stg[:, c * 64:c * 64 + 32]),
                                  reads=[stg_t], writes=[W_t])
            kb.dma("sp", hin[0][:], hv[:, :, 0:TT], hin_t[0], writes=[hin_t[0]])
            for j in range(ntile):
                s = j % 2
                cs = slice(j * TT, (j + 1) * TT)
                if j + 1 < ntile:
                    kb.dma("sp", hin[1 - s][:], hv[:, :, (j + 1) * TT:(j + 2) * TT], hin_t[1 - s], writes=[hin_t[1 - s]])
                hb, hb_t = hin[s], hin_t[s]
                kb.dma("sp", sn[:], sind[:, cs], sc_t, writes=[sc_t])
                kb.dma("sp", cs_[:], cosd[:, cs], sc_t, writes=[sc_t])
                for nm, dstT, dst_t in (("q", QT, QT_t), ("k", KT, KT_t)):
                    for dc in range(8):
                        kb.op("pe", lambda nm=nm, dc=dc: P.matmul(SB[0][:], lhsT=Wt[nm][:, dc, :], rhs=hb[:, dc, :], start=(dc == 0), stop=(dc == 7)),
                              reads=[W_t, hb_t], writes=[SB_t[0]])
                    for dc in range(8):
                        kb.op("pe", lambda nm=nm, dc=dc: P.matmul(SB[1][:], lhsT=Wt[nm + "r"][:, dc, :], rhs=hb[:, dc, :], start=(dc == 0), stop=(dc == 7)),
                              reads=[W_t, hb_t], writes=[SB_t[1]])
                    kb.op("dve", lambda: V.tensor_tensor(out=ta[:], in0=SB[0][:], in1=cs_[:], op=ALU.mult), reads=[SB_t[0], sc_t], writes=[tab_t])
                    kb.op("dve", lambda: V.tensor_tensor(out=tb[:], in0=SB[1][:], in1=sn[:], op=ALU.mult), reads=[SB_t[1], sc_t], writes=[tab_t])
                    kb.op("dve", lambda dstT=dstT, cs=cs: V.tensor_tensor(out=dstT[:, cs], in0=ta[:], in1=tb[:], op=ALU.add), reads=[tab_t], writes=[dst_t])
                for dc in range(8):
                    kb.op("pe", lambda dc=dc: P.matmul(SB[2][:], lhsT=Wt["g"][:, dc, :], rhs=hb[:, dc, :], start=(dc == 0), stop=(dc == 7)),
                          reads=[W_t, hb_t], writes=[SB_t[2]])
                kb.op("act", lambda cs=cs: S.activation(out=SG[:, cs], in_=SB[2][:], func=AF.Silu), reads=[SB_t[2]], writes=[SG_t])
                for bi in range(4):
                    for dc in range(8):
                        kb.op("pe", lambda dc=dc, bi=bi: P.matmul(SB[3][:, bi * 128:(bi + 1) * 128], lhsT=hb[:, dc, bi * 128:(bi + 1) * 128], rhs=Wt["v"][:, dc, :],
                                                                  start=(dc == 0), stop=(dc == 7)), reads=[W_t, hb_t], writes=[SB_t[3]])
                kb.op("act", lambda j=j: S.copy(out=Va[:, j * 4:j * 4 + 4, 0:128], in_=SB[3][:].rearrange("p (b v) -> p b v", b=4)), reads=[SB_t[3]], writes=[Va_t])
            gstep = [0]
            for g in range(ntile):
                for a3 in range(3):
                    kb.op("dve", lambda a3=a3: V.memset(AC[a3][:], 0.0), writes=[AC_t[a3]])
                steps = []
                for kbk in range(4 * g + 4):
                    for c in range(2):
                        steps.append((kbk, c, gstep[0] % 4))
                        gstep[0] += 1

                def qk(st, g=g):
                    kbk, c, bi = st
                    kb.op("pe", lambda: P.matmul(SB[bi][:], lhsT=KT[c * 64:(c + 1) * 64, kbk * 128:(kbk + 1) * 128],
                                                 rhs=QT[c * 64:(c + 1) * 64, g * TT:(g + 1) * TT], start=True, stop=True),
                          reads=[KT_t, QT_t], writes=[SB_t[bi]])

                def ex(st, g=g):
                    kbk, c, bi = st
                    m = kbk - 4 * g
                    kb.op("act", lambda: S.activation(out=PT[bi][:], in_=SB[bi][:], func=AF.Exp, scale=0.125),
                          reads=[SB_t[bi]], writes=[PT_t[bi]])
                    if m >= 0:
                        kb.op("pool", lambda: G.tensor_tensor(out=PT[bi][:, m * 128:(m + 1) * 128], in0=PT[bi][:, m * 128:(m + 1) * 128],
                                                              in1=tri[:], op=ALU.mult), reads=[ctk], writes=[PT_t[bi]])

                def av(st, g=g):
                    kbk, c, bi = st
                    m = kbk - 4 * g
                    for qs in range(4):
                        if m > qs:
                            continue
                        ap_, at_ = acc(c, qs)
                        kb.op("pe", lambda ap_=ap_, qs=qs: P.matmul(ap_, lhsT=PT[bi][:, qs * 128:(qs + 1) * 128], rhs=Va[:, kbk, :],
                                                                    start=False, stop=False, skip_group_check=True),
                              reads=[PT_t[bi], Va_t], writes=[at_])

                LA = 2
                for i in range(min(LA, len(steps))):
                    qk(steps[i])
                for i, st in enumerate(steps):
                    ex(st)
                    if i + LA < len(steps):
                        qk(steps[i + LA])
                    av(st)
                zb, zb_t = zout[g % 2], zo_t[g % 2]
                for a3 in range(3):
                    na = 3 if a3 < 2 else 2
                    kb.op("dve", lambda a3=a3, na=na: V.reciprocal(out=rc8[:, a3 * 3:a3 * 3 + na],
                                                                   in_=AC[a3][:, 0:na * 129].rearrange("p (a w) -> p a w", w=129)[:, :, 128]),
                          reads=[AC_t[a3]], writes=[rc_t])
                for qs in range(4):
                    a0, a0t = acc(0, qs)
                    a1, a1t = acc(1, qs)
                    kb.op("dve", lambda a1=a1, qs=qs: V.tensor_scalar(out=o1[:], in0=a1[:, 0:128], scalar1=rc8[:, 4 + qs:5 + qs], scalar2=lam[:, 3:4],
                                                                     op0=ALU.mult, op1=ALU.mult), reads=[a1t, rc_t, ctk], writes=[o1_t])
                    kb.op("dve", lambda a0=a0, qs=qs: V.scalar_tensor_tensor(out=o4[:, qs, :], in0=a0[:, 0:128], scalar=rc8[:, qs:qs + 1], in1=o1[:],
                                                                            op0=ALU.mult, op1=ALU.add), reads=[a0t, rc_t, o1_t], writes=[o_t])
                for qs in range(4):
                    kb.op("act", lambda qs=qs: S.activation(out=junk[:], in_=o4[:, qs, :], func=AF.Square, accum_out=ss4[:, qs:qs + 1]),
                          reads=[o_t], writes=[ss_t])
                kb.op("act", lambda: S.activation(out=ss4[:], in_=ss4[:], func=AF.Ln, scale=1.0 / 128, bias=1e-5), reads=[ss_t], writes=[ss_t])
                kb.op("act", lambda: S.activation(out=ss4[:], in_=ss4[:], func=AF.Exp, scale=-0.5), reads=[ss_t], writes=[ss_t])
                for qs in range(4):
                    kb.op("dve", lambda qs=qs: V.scalar_tensor_tensor(out=on4[:, qs, :], in0=o4[:, qs, :], scalar=ss4[:, qs:qs + 1], in1=subb[:],
                                                                      op0=ALU.mult, op1=ALU.mult), reads=[o_t, ss_t, ctk], writes=[on_t])
                for qs in range(4):
                    kb.op("pe", lambda qs=qs: P.transpose(TP_[:, qs * 128:(qs + 1) * 128], on4[:, qs, :], ident[:]), reads=[on_t, ctk], writes=[TP_t])
                kb.op("dve", lambda g=g, zb=zb: V.tensor_tensor(out=zb[:], in0=TP_[:], in1=SG[:, g * TT:(g + 1) * TT], op=ALU.mult),
                      reads=[TP_t, SG_t], writes=[zb_t])
                kb.dma("sp", zT[hh * 128:(hh + 1) * 128, g * TT:(g + 1) * TT], zb[:], zb_t, reads=[zb_t])
        kb.finish(zo_t)
    return nc


def attn_maps(hn_list, I, NT):
    maps = []
    Wfull = I["da_w_in"][0].reshape(D, 4, 1024)
    ident = np.eye(128, dtype=np.float32)
    inv = (1.0 / (10000.0 ** (np.arange(0, 64, 2, dtype=np.float32) / 64))).astype(np.float32)
    invf = np.tile(inv, 4).reshape(128, 1).astype(np.float32)
    pos0 = np.arange(TT, dtype=np.float32)[None, :]
    ang = np.arange(NT, dtype=np.float32)[None, :] * invf
    cosd = np.cos(ang).astype(np.float32)
    sind = np.sin(ang).astype(np.float32)
    lqk = np.concatenate([I["da_lq1"][0], I["da_lk1"][0], I["da_lq2"][0], I["da_lk2"][0]])[None, :].astype(np.float32)
    for c in range(8):
        b, hp = c // 4, c % 4
        cs = slice(hp * 256, (hp + 1) * 256)
        maps.append(dict(hnT=np.ascontiguousarray(hn_list[b]), Wd=np.ascontiguousarray(Wfull[:, :, cs].reshape(D, 1024)),
                         sub=np.ascontiguousarray(I["da_subln"][0][None, :]), lqk=np.ascontiguousarray(lqk), invf=invf, pos0=pos0, ident=ident, cosd=cosd, sind=sind))
    return maps


def kernel(**inputs):
    I = {k: np.asarray(v) for k, v in inputs.items()}
    x, p = I["x"], I["p"]
    B, S = x.shape[0], x.shape[1]
    QT_ = S // 4
    nc = build_rwkv(S)
    res = run_bass_kernel_spmd(nc, rwkv_maps(x, I, S), core_ids=list(range(8))).results
    z0T = [np.concatenate([res[b * 4 + g]["zT"] for g in range(4)], axis=0) for b in range(B)]
    rng = [(c // 4, slice((c % 4) * QT_, (c % 4 + 1) * QT_)) for c in range(8)]
    res2 = run_post([z0T[b][:, sl] for b, sl in rng], [x[b, sl].T for b, sl in rng], [p[0, b, sl].T for b, sl in rng],
                    I["rw_w_out"][0], I["pe_w_gate"][0], I["pe_w_proj"][0], I["pe_norm"][0], I["da_norm"][0], last=False)
    hn = [np.concatenate([res2[b * 4 + q]["hnT"] for q in range(4)], axis=1) for b in range(B)]
    nc3 = build_attn(S)
    res3 = run_bass_kernel_spmd(nc3, attn_maps(hn, I, S), core_ids=list(range(8))).results
    z1T = [np.concatenate([res3[b * 4 + g]["zT"] for g in range(4)], axis=0) for b in range(B)]
    res4 = run_post([z1T[b][:, sl] for b, sl in rng], [res2[c]["h1T"] for c in range(8)], [p[1, b, sl].T for b, sl in rng],
                    I["da_w_out"][0], I["pe_w_gate"][1], I["pe_w_proj"][1], I["pe_norm"][1], I["final_norm"], last=True)
    out = np.empty((B, S, D), np.float32)
    for c, (b, sl) in enumerate(rng):
        out[b, sl, :] = res4[c]["oT"].T
    return out
```

```python
import math
from contextlib import ExitStack
import numpy as np
import ml_dtypes
import concourse.bass as bass
import concourse.mybir as mybir
from concourse.bass_utils import run_bass_kernel_spmd

F32 = mybir.dt.float32
BF16 = mybir.dt.bfloat16
AF = mybir.ActivationFunctionType
ALU = mybir.AluOpType
AX = mybir.AxisListType
NPBF = ml_dtypes.bfloat16

D = 1024
SEQ = 16384
TT = 512


class Tok:
    __slots__ = ("name", "w", "r", "dsem", "dcnt", "excl")

    def __init__(self, name, excl=False):
        self.name = name
        self.excl = excl
        self.w = None
        self.r = {}
        self.dsem = None
        self.dcnt = 0


class KB:
    def __init__(self, nc, es):
        self.nc = nc
        self.es = es
        self.eng = dict(pe=nc.tensor, dve=nc.vector, act=nc.scalar, pool=nc.gpsimd, sp=nc.sync)
        self.sem = {e: es.enter_context(nc.semaphore("prog_" + e)) for e in ("pe", "dve", "act", "pool")}
        self.cnt = {e: 0 for e in self.sem}
        self.seen = {e: {} for e in self.eng}
        self.ntok = 0
        self.out_dma = []
        self.coop = None

    def tok(self, name=None):
        self.ntok += 1
        return Tok(name or ("t%d" % self.ntok))

    def toks(self, n, name="t"):
        return [self.tok("%s%d_%d" % (name, self.ntok, i)) for i in range(n)]

    def sb(self, name, shape, dt=F32):
        return self.es.enter_context(self.nc.sbuf_tensor(name, list(shape), dt))

    def ps(self, name, shape, dt=F32):
        return self.es.enter_context(self.nc.psum_tensor(name, list(shape), dt))

    def _deps(self, reads, writes):
        deps = {}
        for t in reads:
            if t.w is not None:
                k = t.w[1]
                if deps.get(k, (None, None, 0))[2] < t.w[2]:
                    deps[k] = t.w
        for t in writes:
            for d in ([t.w] if t.w is not None else []) + list(t.r.values()):
                k = d[1]
                if deps.get(k, (None, None, 0))[2] < d[2]:
                    deps[k] = d
        return deps

    def _wait(self, e, deps, keep_same=False):
        for k, (sem, key, val) in deps.items():
            if key == e and e == "pe" and not keep_same:
                continue
            if self.seen[e].get(key, 0) < val:
                self.eng[e].wait_ge(sem, val)
                self.seen[e][key] = val

    def ptok(self, name=None):
        t = self.tok(name)
        t.excl = True
        return t

    def op(self, e, fn, reads=(), writes=()):
        ex = [t for t in reads if t.excl]
        if ex:
            reads = [t for t in reads if not t.excl]
            writes = list(writes) + ex
        self._wait(e, self._deps(reads, writes))
        ins = fn()
        self.cnt[e] += 1
        ins.then_inc(self.sem[e], 1)
        me = (self.sem[e], e, self.cnt[e])
        for t in reads:
            t.r[e] = me
        for t in writes:
            t.w = me
            t.r = {}
        if self.coop is not None:
            self.coop.emitted()
        return ins

    def dma(self, q, out, in_, owner, reads=(), writes=(), **kw):
        self._wait(q, self._deps(reads, writes), keep_same=True)
        if owner.dsem is None:
            owner.dsem = self.es.enter_context(self.nc.semaphore("dma_" + owner.name))
        ins = self.eng[q].dma_start(out=out, in_=in_, **kw)
        owner.dcnt += 16
        ins.then_inc(owner.dsem, 16)
        me = (owner.dsem, "dma_" + owner.name, owner.dcnt)
        for t in reads:
            t.r[me[1]] = me
        for t in writes:
            t.w = me
            t.r = {}
        return me

    def finish(self, toks):
        for t in toks:
            if t.dsem is not None:
                self.eng["sp"].wait_ge(t.dsem, t.dcnt)


import threading


class Coop:
    def __init__(self, kb):
        self.kb = kb
        self.thread = None
        self.quota = 0
        self.go = threading.Semaphore(0)
        self.back = threading.Semaphore(0)
        self.done = True
        self.err = None
        kb.coop = self

    def start(self, fn):
        self.done = False
        self.quota = 0

        def run():
            self.go.acquire()
            try:
                fn()
            except BaseException as e:
                self.err = e
            self.done = True
            self.back.release()
        self.thread = threading.Thread(target=run)
        self.thread.start()

    def emitted(self):
        if threading.current_thread() is self.thread:
            self.quota -= 1
            if self.quota <= 0:
                self.back.release()
                self.go.acquire()

    def step(self, n):
        if self.done or threading.current_thread() is self.thread:
            return
        self.quota = n
        self.go.release()
        self.back.acquire()
        if self.err is not None:
            raise self.err

    def finish(self):
        while not self.done:
            self.step(1 << 30)
        if self.thread is not None:
            self.thread.join()
        if self.err is not None:
            raise self.err


def new_nc():
    return bass.Bass("TRN2", target_bir_lowering=False)


def dram_in(nc, name, shape, dt=F32):
    return nc.dram_tensor(name, list(shape), dt, kind="ExternalInput").ap()


def dram_out(nc, name, shape, dt=F32):
    return nc.dram_tensor(name, list(shape), dt, kind="ExternalOutput").ap()


def load_weight_bf16(kb, q, w_dram, kchunks, ncols, name, stage, stage_tok, cast_eng="pool"):
    wt = kb.sb(name, [128, kchunks, ncols], BF16)
    tk = kb.tok(name)
    for kc in range(kchunks):
        s = kc % len(stage)
        kb.dma(q, stage[s][:, 0:ncols], w_dram[kc * 128:(kc + 1) * 128, :], stage_tok[s], writes=[stage_tok[s]])
        kb.op(cast_eng, lambda kc=kc, s=s: kb.eng[cast_eng].tensor_copy(out=wt[:, kc, :], in_=stage[s][:, 0:ncols]),
              reads=[stage_tok[s]], writes=[tk])
    return wt, tk


def rstd_from_sumsq(kb, ps_ap, ps_tok, out_ap, out_tok, eps):
    kb.op("act", lambda: kb.nc.scalar.activation(out=out_ap, in_=ps_ap, func=AF.Ln, bias=float(eps), scale=1.0),
          reads=[ps_tok], writes=[out_tok])
    kb.op("act", lambda: kb.nc.scalar.activation(out=out_ap, in_=out_ap, func=AF.Exp, scale=-0.5),
          reads=[out_tok], writes=[out_tok])


def build_post(NT, last):
    nc = new_nc()
    zT = dram_in(nc, "zT", [D, NT], BF16)
    hT = dram_in(nc, "hT", [D, NT], F32)
    pT = dram_in(nc, "pT", [256, NT], F32)
    w_out = dram_in(nc, "w_out", [D, D])
    w_gate = dram_in(nc, "w_gate", [D, D])
    w_proj = dram_in(nc, "w_proj", [256, D])
    vecs = dram_in(nc, "vecs", [128, 16])
    if last:
        oT = dram_out(nc, "oT", [D, NT], F32)
    else:
        h1T = dram_out(nc, "h1T", [D, NT], F32)
        hnT = dram_out(nc, "hnT", [D, NT], BF16)
    ntile = NT // TT
    with ExitStack() as es:
        kb = KB(nc, es)
        ctk = kb.tok("cst")
        ones = kb.sb("ones", [128, 128], BF16)
        kb.op("pool", lambda: nc.gpsimd.memset(ones[:], 1.0 / D), writes=[ctk])
        vec = kb.sb("vec", [128, 16])
        vtk = kb.tok("vec")
        kb.dma("sp", vec[:], vecs[:, :], vtk, writes=[vtk])
        stage = [kb.sb("stg%d" % i, [128, D]) for i in range(2)]
        stok = kb.toks(2, "stg")
        Wo, Wo_t = load_weight_bf16(kb, "sp", w_out, 8, D, "Wo", stage, stok)
        Wg, Wg_t = load_weight_bf16(kb, "sp", w_gate, 8, D, "Wg", stage, stok)
        Wp, Wp_t = load_weight_bf16(kb, "sp", w_proj, 2, D, "Wp", stage, stok)
        zin = [kb.sb("zin%d" % i, [128, 8, TT], BF16) for i in range(2)]
        hin = [kb.sb("hin%d" % i, [128, 8, TT]) for i in range(2)]
        pin = [kb.sb("pin%d" % i, [128, 2, TT]) for i in range(2)]
        pbf = [kb.sb("pbf%d" % i, [128, 2, TT], BF16) for i in range(2)]
        z_t, h_t, p_t, pb_t = kb.toks(2, "z"), kb.toks(2, "h"), kb.toks(2, "p"), kb.toks(2, "pb")
        sq = kb.sb("sq", [128, 8, TT], BF16)
        sq_t = kb.tok("sq")
        hn = kb.sb("hn", [128, 8, TT], BF16)
        hn_t = kb.tok("hn")
        rstd = kb.sb("rstd", [128, TT])
        rstd_t = kb.tok("rstd")
        gsb = [kb.sb("gsb%d" % i, [128, TT]) for i in range(2)]
        g_t = kb.toks(2, "g")
        tmp = [kb.sb("tmp%d" % i, [128, TT]) for i in range(2)]
        tmp_t = kb.toks(2, "tmp")
        obuf = [kb.sb("obuf%d" % i, [128, 8, TT], F32 if last else BF16) for i in range(2)]
        ob_t = kb.toks(2, "ob")
        psb = [kb.ps("psb%d" % i, [128, TT]) for i in range(8)]
        ps_t = [kb.ptok("ps%d" % i) for i in range(8)]
        pc = [0]

        def nextps():
            i = pc[0] % 8
            pc[0] += 1
            return psb[i], ps_t[i]

        zv = zT.rearrange("(c p) t -> p c t", p=128)
        hv = hT.rearrange("(c p) t -> p c t", p=128)
        pv = pT.rearrange("(c p) t -> p c t", p=128)

        def norm(src, src_t, gcol0, dst_fn, dst_toks):
            for dc in range(8):
                kb.op("act", lambda dc=dc: nc.scalar.activation(out=sq[:, dc, :], in_=src[:, dc, :], func=AF.Square),
                      reads=[src_t], writes=[sq_t])
            pb_, pt_ = nextps()
            for dc in range(8):
                kb.op("pe", lambda dc=dc: nc.tensor.matmul(pb_[:], lhsT=ones[:], rhs=sq[:, dc, :], start=(dc == 0), stop=(dc == 7)),
                      reads=[sq_t, ctk], writes=[pt_])
            rstd_from_sumsq(kb, pb_[:], pt_, rstd[:], rstd_t, 1e-6)
            for dc in range(8):
                kb.op("dve", lambda dc=dc: nc.vector.scalar_tensor_tensor(
                    out=dst_fn(dc), in0=src[:, dc, :], scalar=vec[:, gcol0 + dc:gcol0 + dc + 1], in1=rstd[:],
                    op0=ALU.mult, op1=ALU.mult), reads=[src_t, rstd_t, vtk], writes=dst_toks)

        def loads(j):
            s = j % 2
            cs = slice(j * TT, (j + 1) * TT)
            kb.dma("sp", zin[s][:], zv[:, :, cs], z_t[s], writes=[z_t[s]])
            kb.dma("sp", hin[s][:], hv[:, :, cs], h_t[s], writes=[h_t[s]])
            kb.dma("sp", pin[s][:], pv[:, :, cs], p_t[s], writes=[p_t[s]])

        loads(0)
        for j in range(ntile):
            s = j % 2
            cs = slice(j * TT, (j + 1) * TT)
            if j + 1 < ntile:
                loads(j + 1)
            kb.op("pool", lambda s=s: nc.gpsimd.tensor_copy(out=pbf[s][:], in_=pin[s][:]), reads=[p_t[s]], writes=[pb_t[s]])
            h2 = hin[s]
            for dc in range(8):
                pb_, pt_ = nextps()
                for kc in range(8):
                    kb.op("pe", lambda dc=dc, kc=kc, pb_=pb_: nc.tensor.matmul(
                        pb_[:], lhsT=Wo[:, kc, dc * 128:(dc + 1) * 128], rhs=zin[s][:, kc, :], start=(kc == 0), stop=(kc == 7)),
                        reads=[Wo_t, z_t[s]], writes=[pt_])
                kb.op("dve", lambda dc=dc, pb_=pb_: nc.vector.tensor_tensor(out=h2[:, dc, :], in0=h2[:, dc, :], in1=pb_[:], op=ALU.add),
                      reads=[pt_, h_t[s]], writes=[h_t[s]])
            norm(h2, h_t[s], 0, lambda dc: hn[:, dc, :], [hn_t])
            for dc in range(8):
                pg, pgt = nextps()
                for kc in range(8):
                    kb.op("pe", lambda dc=dc, kc=kc, pg=pg: nc.tensor.matmul(
                        pg[:], lhsT=Wg[:, kc, dc * 128:(dc + 1) * 128], rhs=hn[:, kc, :], start=(kc == 0), stop=(kc == 7)),
                        reads=[Wg_t, hn_t], writes=[pgt])
                pp, ppt = nextps()
                for kc in range(2):
                    kb.op("pe", lambda dc=dc, kc=kc, pp=pp: nc.tensor.matmul(
                        pp[:], lhsT=Wp[:, kc, dc * 128:(dc + 1) * 128], rhs=pbf[s][:, kc, :], start=(kc == 0), stop=(kc == 1)),
                        reads=[Wp_t, pb_t[s]], writes=[ppt])
                b = dc % 2
                kb.op("act", lambda pg=pg, b=b: nc.scalar.activation(out=gsb[b][:], in_=pg[:], func=AF.Sigmoid),
                      reads=[pgt], writes=[g_t[b]])
                kb.op("dve", lambda pp=pp, b=b: nc.vector.tensor_tensor(out=tmp[b][:], in0=gsb[b][:], in1=pp[:], op=ALU.mult),
                      reads=[g_t[b], ppt], writes=[tmp_t[b]])
                kb.op("pool", lambda dc=dc, b=b: nc.gpsimd.tensor_tensor(out=h2[:, dc, :], in0=h2[:, dc, :], in1=tmp[b][:], op=ALU.add),
                      reads=[tmp_t[b], h_t[s]], writes=[h_t[s]])
            norm(h2, h_t[s], 8, lambda dc: obuf[s][:, dc, :], [ob_t[s]])
            if last:
                kb.dma("sp", oT.rearrange("(c p) t -> p c t", p=128)[:, :, cs], obuf[s][:], ob_t[s], reads=[ob_t[s]])
            else:
                kb.dma("sp", hnT.rearrange("(c p) t -> p c t", p=128)[:, :, cs], obuf[s][:], ob_t[s], reads=[ob_t[s]])
                kb.dma("sp", h1T.rearrange("(c p) t -> p c t", p=128)[:, :, cs], h2[:], h_t[s], reads=[h_t[s]])
        kb.finish(ob_t + h_t)
    return nc


def run_post(zT_list, hT_list, pT_list, w_out, w_gate, w_proj, pe_norm, nxt_norm, last):
    n = len(zT_list)
    NT = zT_list[0].shape[1]
    nc = build_post(NT, last)
    vecs = np.concatenate([pe_norm.reshape(8, 128).T, nxt_norm.reshape(8, 128).T], axis=1).astype(np.float32)
    maps = [dict(zT=np.ascontiguousarray(zT_list[i]), hT=np.ascontiguousarray(hT_list[i]), pT=np.ascontiguousarray(pT_list[i]),
                 w_out=w_out, w_gate=w_gate, w_proj=w_proj, vecs=np.ascontiguousarray(vecs)) for i in range(n)]
    res = run_bass_kernel_spmd(nc, maps, core_ids=list(range(n))).results
    return res


TA = 256
NCH = TA // 64
GN_EPS = 64e-5
FILL = 6
RW_STOP = 0
RW_DBG = None


def build_rwkv(NT):
    nc = new_nc()
    xT = dram_in(nc, "xT", [D, NT])
    Wc = dram_in(nc, "Wc", [D, 4 * 256])
    w1 = dram_in(nc, "w1", [D, 64])
    a1 = dram_in(nc, "a1", [D, 64])
    w2c = dram_in(nc, "w2c", [64, 256])
    a2c = dram_in(nc, "a2c", [64, 256])
    vecA = dram_in(nc, "vecA", [128, 56])
    vecC = dram_in(nc, "vecC", [128, 2 * 8])
    lnwb = dram_in(nc, "lnwb", [1, 512])
    identd = dram_in(nc, "ident", [128, 128])
    zT = dram_out(nc, "zT", [256, NT], BF16)
    dbg_d = dram_out(nc, "dbg", [128, 2048]) if RW_DBG else None
    dbg_tok = []

    def dbg(name, ap, tok, j=0):
        if RW_DBG == name and j == 0 and not dbg_tok:
            t = kb.tok("dbgt")
            dbg_tok.append(t)
            p, f = ap.shape[0], int(np.prod(ap.shape[1:]))
            kb.dma("sp", dbg_d[0:p, 0:f], ap, t, reads=[tok])
    ntile = NT // TA
    with ExitStack() as es:
        kb = KB(nc, es)
        V, S, G, P = nc.vector, nc.scalar, nc.gpsimd, nc.tensor
        ctk = kb.tok("cst")
        ident = kb.sb("ident_sb", [128, 128])
        kb.dma("sp", ident[:], identd[:, :], ctk, writes=[ctk])
        onesb = kb.sb("onesb", [128, 128], BF16)
        blk = kb.sb("blk", [128, 128], BF16)
        bind = kb.sb("bind", [128, 2])
        kb.op("pool", lambda: G.memset(onesb[:], 1.0 / D), writes=[ctk])
        kb.op("pool", lambda: G.memset(blk[:], 0.0), writes=[ctk])
        kb.op("pool", lambda: G.memset(blk[0:64, 0:64], 1.0), writes=[ctk])
        kb.op("pool", lambda: G.memset(blk[64:128, 64:128], 1.0), writes=[ctk])
        kb.op("pool", lambda: G.memset(bind[:], 0.0), writes=[ctk])
        kb.op("pool", lambda: G.memset(bind[0:64, 0:1], 1.0), writes=[ctk])
        kb.op("pool", lambda: G.memset(bind[64:128, 1:2], 1.0), writes=[ctk])
        mAT = kb.sb("mAT", [64, 4, 4, 64])
        mL = kb.sb("mL", [64, 4, 64])
        kb.op("pool", lambda: G.memset(mAT[:], 1.0), writes=[ctk])
        kb.op("pool", lambda: G.memset(mL[:], 1.0), writes=[ctk])
        for q in range(4):
            kb.op("pool", lambda q=q: G.affine_select(out=mAT[:, :, q, :], in_=mAT[:, :, q, :], pattern=[[0, 4], [1, 64]],
                                                      compare_op=(ALU.is_gt if q % 2 == 0 else ALU.is_ge), fill=0.0, base=0,
                                                      channel_multiplier=-1), writes=[ctk])
        kb.op("pool", lambda: G.affine_select(out=mL[:], in_=mL[:], pattern=[[0, 4], [-1, 64]], compare_op=ALU.is_gt, fill=0.0,
                                              base=0, channel_multiplier=1), writes=[ctk])
        vA = kb.sb("vA", [128, 56])
        vC = kb.sb("vC", [128, 2, 8])
        kb.dma("sp", vA[:], vecA[:, :], ctk, writes=[ctk])
        kb.dma("sp", vC[:].rearrange("p a b -> p (a b)"), vecC[:, :], ctk, writes=[ctk])
        lnw = kb.sb("lnw", [64, 512])
        kb.dma("sp", lnw[:], lnwb.partition_broadcast(64), ctk, writes=[ctk])
        kb.op("dve", lambda: V.tensor_scalar(out=vC[:, :, 5:6], in0=vC[:, :, 0:1], scalar1=-1.0, scalar2=None, op0=ALU.mult),
              reads=[ctk], writes=[ctk])
        kb.op("dve", lambda: V.tensor_scalar(out=vC[:, :, 6:7], in0=vC[:, :, 3:4], scalar1=-1.0, scalar2=1.0, op0=ALU.mult, op1=ALU.add),
              reads=[ctk], writes=[ctk])
        W0, A0, KK_, KA, RK, NW0, OMKA = range(7)

        xin = kb.sb("xin", [128, 8, TA])
        x_t = kb.tok("xin")
        xin_flat = xin[:].rearrange("p a b -> p (a b)")
        stage = [xin_flat[:, i * 1024:(i + 1) * 1024] for i in range(2)]
        stok = [x_t, x_t]
        omu = kb.sb("omu", [128, 48])
        kb.op("dve", lambda: V.tensor_scalar(out=omu[:], in0=vA[:, 8:56], scalar1=-1.0, scalar2=1.0, op0=ALU.mult, op1=ALU.add), reads=[ctk], writes=[ctk])
        W = kb.sb("Wm", [128, 8, 4, 2, 256], BF16)
        W_t = kb.tok("Wm")
        for dc in range(8):
            sgi = dc % 2
            kb.dma("sp", stage[sgi][:, 0:1024], Wc[dc * 128:(dc + 1) * 128, :], x_t, writes=[x_t])
            for i in range(4):
                kb.op("dve", lambda dc=dc, i=i, sgi=sgi: V.tensor_scalar(out=W[:, dc, i, 0, :], in0=stage[sgi][:, i * 256:(i + 1) * 256],
                                                                         scalar1=omu[:, i * 8 + dc:i * 8 + dc + 1], scalar2=None, op0=ALU.mult),
                      reads=[x_t, ctk], writes=[W_t])
                kb.op("dve", lambda dc=dc, i=i, sgi=sgi: V.tensor_scalar(out=W[:, dc, i, 1, :], in0=stage[sgi][:, i * 256:(i + 1) * 256],
                                                                         scalar1=vA[:, 8 + i * 8 + dc:9 + i * 8 + dc], scalar2=None, op0=ALU.mult),
                      reads=[x_t, ctk], writes=[W_t])
        W1 = kb.sb("W1", [128, 8, 2, 64], BF16)
        A1 = kb.sb("A1", [128, 8, 2, 64], BF16)
        W2 = kb.sb("W2", [64, 256], BF16)
        A2 = kb.sb("A2", [64, 256], BF16)
        for (src_d, dstw, i) in ((w1, W1, 4), (a1, A1, 5)):
            kb.dma("sp", stage[0][:, 0:512].rearrange("p (c k) -> p c k", c=8), src_d.rearrange("(c p) k -> p c k", p=128), x_t, writes=[x_t])
            for dc in range(8):
                kb.op("dve", lambda dc=dc, i=i, dstw=dstw: V.tensor_scalar(out=dstw[:, dc, 0, :], in0=stage[0][:, dc * 64:(dc + 1) * 64],
                                                                           scalar1=omu[:, i * 8 + dc:i * 8 + dc + 1], scalar2=None, op0=ALU.mult),
                      reads=[x_t, ctk], writes=[W_t])
                kb.op("dve", lambda dc=dc, i=i, dstw=dstw: V.tensor_scalar(out=dstw[:, dc, 1, :], in0=stage[0][:, dc * 64:(dc + 1) * 64],
                                                                           scalar1=vA[:, 8 + i * 8 + dc:9 + i * 8 + dc], scalar2=None, op0=ALU.mult),
                      reads=[x_t, ctk], writes=[W_t])
        kb.dma("sp", stage[0][0:64, 0:256], w2c[:, :], x_t, writes=[x_t])
        kb.op("dve", lambda: V.tensor_copy(out=W2[:], in_=stage[0][0:64, 0:256]), reads=[x_t], writes=[W_t])
        kb.dma("sp", stage[0][0:64, 0:256], a2c[:, :], x_t, writes=[x_t])
        kb.op("dve", lambda: V.tensor_copy(out=A2[:], in_=stage[0][0:64, 0:256]), reads=[x_t], writes=[W_t])

        hn = kb.sb("hn", [128, 8, TA + 1], BF16)
        hn_t = kb.tok("hn")
        kb.op("pool", lambda: G.memset(hn[:], 0.0), writes=[hn_t])
        sqx = kb.sb("sqx", [128, 8, TA], BF16)
        sqx_t = kb.tok("sqx")
        rstd = kb.sb("rstd", [128, TA])
        rstd_t = kb.tok("rstd")
        xm = [kb.sb("xm%d" % i, [128, 8, TA], BF16) for i in range(2)]
        xm_t = kb.toks(2, "xm")
        th = kb.sb("th", [64, TA], BF16)
        th_t = kb.tok("th")

        def cm(name):
            return kb.sb(name, [128, 2, TA]), kb.tok(name)
        r_, r_t = cm("r_")
        k_, k_t = cm("k_")
        v_, v_t = cm("v_")
        sg2 = [kb.sb("sg%d" % i, [128, 2, TA]) for i in range(2)]
        sg2_t = kb.toks(2, "sg")
        nlw, nlw_t = cm("nlw")
        a_, a_t = cm("a_")
        kk, kk_t = cm("kk")
        t1, t1_t = cm("t1")
        prod, prod_t = cm("prod")
        CA, CA_t = cm("CA")
        CB, CB_t = cm("CB")
        rn, rn_t = cm("rn")
        sqk = kb.sb("sqk", [128, 2, TA], BF16)
        sqk_t = kb.tok("sqk")
        AR = kb.sb("AR", [128, 2, NCH, 2, 64])
        AR_t = kb.tok("AR")
        BK = kb.sb("BK", [128, 2, NCH, 2, 64])
        BK_t = kb.tok("BK")
        ARh2 = [kb.sb("ARh%d" % i, [64, 4, NCH, 2, 64], BF16) for i in range(2)]
        ARh2_t = kb.toks(2, "ARh")
        BKh2 = [kb.sb("BKh%d" % i, [64, 4, NCH, 2, 64], BF16) for i in range(2)]
        BKh2_t = kb.toks(2, "BKh")
        WCs2 = [kb.sb("WCs%d" % i, [64, 4, NCH]) for i in range(2)]
        WCs2_t = kb.toks(2, "WCs")
        TM2 = [[kb.sb("TM%d_%d" % (pp, i), [64, NCH, 4, 64], BF16) for i in range(4)] for pp in range(2)]
        TM2_t = [kb.toks(4, "TM") for pp in range(2)]
        TA_, TB_, TK_, TV_ = range(4)
        RKs2 = [kb.sb("RKs%d" % i, [64, NCH, 4]) for i in range(2)]
        RKs2_t = kb.toks(2, "RKs")
        Ybuf2 = [kb.sb("Ybuf%d" % i, [64, NCH, 4, 64]) for i in range(2)]
        Y2_t = kb.toks(2, "Ybuf")
        ATsk = [kb.sb("ATs%d" % i, [64, 4, 4, 64], BF16) for i in range(2)]
        ATsk_t = kb.toks(2, "ATs")
        Lsk = [kb.sb("Ls%d" % i, [64, 4, 64], BF16) for i in range(2)]
        Lsk_t = kb.toks(2, "Ls")
        Z = [kb.sb("Z%d" % i, [64, 4, 128]) for i in range(2)]
        Z_t = kb.toks(2, "Z")
        Zbk = [[kb.sb("Zb%d_%d" % (kk_, i), [64, 4, 128], BF16) for i in range(2)] for kk_ in range(2)]
        Zbk_t = [kb.toks(2, "Zb") for kk_ in range(2)]
        identb = kb.sb("identb", [64, 64], BF16)
        kb.op("dve", lambda: V.tensor_copy(out=identb[:], in_=ident[0:64, 0:64]), reads=[ctk], writes=[ctk])
        LBk = [[kb.sb("LB%d_%d" % (kk_, i), [64, 4, 2, 64], BF16) for i in range(2)] for kk_ in range(2)]
        LBk_t = [kb.toks(2, "LB") for kk_ in range(2)]
        RHsk = [kb.sb("RHs%d" % i, [64, 4, 64]) for i in range(2)]
        RHsk_t = kb.toks(2, "RHs")
        MTsk = [kb.sb("MTs%d" % i, [64, 4, 64]) for i in range(2)]
        MTsk_t = kb.toks(2, "MTs")
        Hs = [kb.sb("Hs%d" % i, [64, 4, 64]) for i in range(2)]
        Hs_t = kb.toks(2, "Hs")
        kb.op("pool", lambda: G.memset(Hs[0][:], 0.0), writes=[Hs_t[0]])
        st1 = kb.sb("st1", [64, NCH * 4])
        st2 = kb.sb("st2", [64, NCH * 4])
        st3 = kb.sb("st3", [64, NCH * 4])
        st_t = kb.tok("st")
        zout = kb.sb("zout", [128, 2, TA], BF16)
        zo_t = kb.tok("zout")

        PR = kb.ps("PR", [128, 512]); PR_t = kb.ptok("PR")
        TIN = kb.ps("TIN", [128, 512]); TIN_t = kb.ptok("TIN")
        ATpk = [kb.ps("ATp%d" % i, [128, 512]) for i in range(2)]; ATpk_t = [kb.ptok("ATp%d" % i) for i in range(2)]
        APpk = [kb.ps("APp%d" % i, [128, 512]) for i in range(2)]; APpk_t = [kb.ptok("APp%d" % i) for i in range(2)]
        SQpk = [kb.ps("SQp%d" % i, [128, 512]) for i in range(2)]; SQpk_t = [kb.ptok("SQp%d" % i) for i in range(2)]

        xv = xT.rearrange("(c p) t -> p c t", p=128)
        hstate = [0]

        def prep(j):
            p = j % 2
            cs = slice(j * TA, (j + 1) * TA)
            ARh, ARh_t, BKh, BKh_t = ARh2[p], ARh2_t[p], BKh2[p], BKh2_t[p]
            TM, TM_t = TM2[p], TM2_t[p]
            WCs, WCs_t, RKs, RKs_t = WCs2[p], WCs2_t[p], RKs2[p], RKs2_t[p]
            sg, sg_t = sg2[p], sg2_t[p]
            Ybuf, Y_t = Ybuf2[p], Y2_t[p]
            kb.dma("sp", xin[:], xv[:, :, cs], x_t, writes=[x_t])
            if j > 0:
                kb.op("pool", lambda: G.tensor_copy(out=hn[:, :, 0:1], in_=hn[:, :, TA:TA + 1]), reads=[hn_t], writes=[hn_t])
            for dc in range(8):
                kb.op("act", lambda dc=dc: S.activation(out=sqx[:, dc, :], in_=xin[:, dc, :], func=AF.Square), reads=[x_t], writes=[sqx_t])
            for dc in range(8):
                kb.op("pe", lambda dc=dc: P.matmul(PR[:, 0:TA], lhsT=onesb[:], rhs=sqx[:, dc, :], start=(dc == 0), stop=(dc == 7)),
                      reads=[sqx_t, ctk], writes=[PR_t])
            kb.op("act", lambda: S.activation(out=rstd[:], in_=PR[:, 0:TA], func=AF.Ln, bias=1e-6, scale=1.0), reads=[PR_t], writes=[rstd_t])
            kb.op("act", lambda: S.activation(out=rstd[:], in_=rstd[:], func=AF.Exp, scale=-0.5), reads=[rstd_t], writes=[rstd_t])
            for dc in range(8):
                kb.op("dve", lambda dc=dc: V.scalar_tensor_tensor(out=hn[:, dc, 1:TA + 1], in0=xin[:, dc, :], scalar=vA[:, dc:dc + 1],
                                                                 in1=rstd[:], op0=ALU.mult, op1=ALU.mult),
                      reads=[x_t, rstd_t, ctk], writes=[hn_t])
            for i in range(6):
                if i < 4:
                    dst, dst_t = ((r_, r_t), (k_, k_t), (v_, v_t), (sg, sg_t))[i]
                    for hc in range(2):
                        for dc in range(8):
                            kb.op("pe", lambda dc=dc, hc=hc, i=i: P.matmul(
                                PR[:, 0:TA], lhsT=W[:, dc, i, 0, hc * 128:hc * 128 + 128], rhs=hn[:, dc, 1:TA + 1],
                                start=(dc == 0), stop=False), reads=[W_t, hn_t], writes=[PR_t])
                            kb.op("pe", lambda dc=dc, hc=hc, i=i: P.matmul(
                                PR[:, 0:TA], lhsT=W[:, dc, i, 1, hc * 128:hc * 128 + 128], rhs=hn[:, dc, 0:TA],
                                start=False, stop=(dc == 7)), reads=[W_t, hn_t], writes=[PR_t])
                        if i == 3:
                            kb.op("act", lambda hc=hc, dst=dst: S.activation(out=dst[:, hc, :], in_=PR[:, 0:TA], func=AF.Silu),
                                  reads=[PR_t], writes=[dst_t])
                        else:
                            kb.op("dve", lambda hc=hc, dst=dst: V.tensor_copy(out=dst[:, hc, :], in_=PR[:, 0:TA]),
                                  reads=[PR_t], writes=[dst_t])
                else:
                    Wl, W2l = (W1, W2) if i == 4 else (A1, A2)
                    for dc in range(8):
                        kb.op("pe", lambda dc=dc, Wl=Wl: P.matmul(PR[0:64, 0:TA], lhsT=Wl[:, dc, 0, :], rhs=hn[:, dc, 1:TA + 1],
                                                                 start=(dc == 0), stop=False), reads=[W_t, hn_t], writes=[PR_t])
                        kb.op("pe", lambda dc=dc, Wl=Wl: P.matmul(PR[0:64, 0:TA], lhsT=Wl[:, dc, 1, :], rhs=hn[:, dc, 0:TA],
                                                                 start=False, stop=(dc == 7)), reads=[W_t, hn_t], writes=[PR_t])
                    kb.op("act", lambda i=i: S.activation(out=th[:], in_=PR[0:64, 0:TA], func=(AF.Tanh if i == 4 else AF.Copy)),
                          reads=[PR_t], writes=[th_t])
                    for hc in range(2):
                        kb.op("pe", lambda hc=hc, W2l=W2l: P.matmul(PR[:, 0:TA], lhsT=W2l[:, hc * 128:(hc + 1) * 128], rhs=th[:],
                                                                    start=True, stop=True), reads=[W_t, th_t], writes=[PR_t])
                        if i == 4:
                            kb.op("act", lambda hc=hc: S.activation(out=nlw[:, hc, :], in_=PR[:, 0:TA], func=AF.Exp, scale=-1.0,
                                                                    bias=vC[:, hc, NW0:NW0 + 1]), reads=[PR_t, ctk], writes=[nlw_t])
                            kb.op("act", lambda hc=hc: S.activation(out=nlw[:, hc, :], in_=nlw[:, hc, :], func=AF.Ln, scale=1.0, bias=1.0),
                                  reads=[nlw_t], writes=[nlw_t])
                            kb.op("act", lambda hc=hc: S.activation(out=nlw[:, hc, :], in_=nlw[:, hc, :], func=AF.Exp, scale=-1.0, bias=-0.5),
                                  reads=[nlw_t], writes=[nlw_t])
                        else:
                            kb.op("act", lambda hc=hc: S.activation(out=a_[:, hc, :], in_=PR[:, 0:TA], func=AF.Sigmoid, scale=1.0,
                                                                    bias=vC[:, hc, A0:A0 + 1]), reads=[PR_t, ctk], writes=[a_t])
            for hc in range(2):
                kb.op("dve", lambda hc=hc: V.tensor_scalar(out=kk[:, hc, :], in0=k_[:, hc, :], scalar1=vC[:, hc, KK_:KK_ + 1], scalar2=None,
                                                           op0=ALU.mult), reads=[k_t, ctk], writes=[kk_t])
                kb.op("act", lambda hc=hc: S.activation(out=sqk[:, hc, :], in_=kk[:, hc, :], func=AF.Square), reads=[kk_t], writes=[sqk_t])
                kb.op("pe", lambda hc=hc: P.matmul(PR[:, 0:TA], lhsT=blk[:], rhs=sqk[:, hc, :], start=True, stop=True),
                      reads=[sqk_t, ctk], writes=[PR_t])
                kb.op("act", lambda hc=hc: S.activation(out=rn[:, hc, :], in_=PR[:, 0:TA], func=AF.Ln, bias=1e-24, scale=1.0),
                      reads=[PR_t], writes=[rn_t])
                kb.op("act", lambda hc=hc: S.activation(out=rn[:, hc, :], in_=rn[:, hc, :], func=AF.Exp, scale=-0.5), reads=[rn_t], writes=[rn_t])
            kb.op("dve", lambda: V.tensor_tensor(out=kk[:], in0=kk[:], in1=rn[:], op=ALU.mult), reads=[kk_t, rn_t], writes=[kk_t])
            for hc in range(2):
                kb.op("dve", lambda hc=hc: V.tensor_scalar(out=t1[:, hc, :], in0=a_[:, hc, :], scalar1=vC[:, hc, KA:KA + 1],
                                                           scalar2=vC[:, hc, OMKA:OMKA + 1], op0=ALU.mult, op1=ALU.add),
                      reads=[a_t, ctk], writes=[t1_t])
            kb.op("dve", lambda: V.tensor_tensor(out=t1[:], in0=t1[:], in1=k_[:], op=ALU.mult), reads=[t1_t, k_t], writes=[t1_t])
            kb.op("pool", lambda: G.tensor_tensor(out=a_[:], in0=a_[:], in1=kk[:], op=ALU.mult), reads=[a_t, kk_t], writes=[a_t])
            for hc in range(2):
                kb.op("dve", lambda hc=hc: V.scalar_tensor_tensor(out=prod[:, hc, :], in0=r_[:, hc, :], scalar=vC[:, hc, RK:RK + 1],
                                                                 in1=t1[:, hc, :], op0=ALU.mult, op1=ALU.mult),
                      reads=[r_t, t1_t, ctk], writes=[prod_t])
            def v3(t):
                return t[:].rearrange("p a (c t) -> p (a c) t", t=64)
            src, src_t = nlw, nlw_t
            pp = [(CA, CA_t), (CB, CB_t)]
            for li, sft in enumerate((1, 2, 4, 8, 16, 32)):
                dst, dst_t = pp[li % 2]
                kb.op("pool", lambda src=src, dst=dst, sft=sft: G.tensor_tensor(out=v3(dst)[:, :, sft:], in0=v3(src)[:, :, sft:],
                                                                               in1=v3(src)[:, :, 0:64 - sft], op=ALU.add),
                      reads=[src_t], writes=[dst_t])
                kb.op("pool", lambda src=src, dst=dst, sft=sft: G.tensor_copy(out=v3(dst)[:, :, 0:sft], in_=v3(src)[:, :, 0:sft]),
                      reads=[src_t], writes=[dst_t])
                src, src_t = dst, dst_t
            cn, cn_t = src, src_t
            assert cn is CB
            kb.op("pool", lambda: G.tensor_tensor(out=nlw[:], in0=cn[:], in1=nlw[:], op=ALU.subtract), reads=[cn_t, nlw_t], writes=[nlw_t])
            kb.op("act", lambda: S.activation(out=CA[:], in_=cn[:], func=AF.Exp, scale=-1.0), reads=[cn_t], writes=[CA_t])
            kb.op("act", lambda: S.activation(out=nlw[:], in_=nlw[:], func=AF.Exp, scale=-1.0), reads=[nlw_t], writes=[nlw_t])
            kb.op("act", lambda: S.activation(out=CB[:], in_=cn[:], func=AF.Exp, scale=1.0), reads=[cn_t], writes=[CB_t])
            eneg, eneg_t, enegx, enegx_t, epos, epos_t = CA, CA_t, nlw, nlw_t, CB, CB_t

            def c3(t, hc):
                return t[:, hc, :].rearrange("p (c t) -> p c t", t=64)
            for hc in range(2):
                kb.op("dve", lambda hc=hc: V.tensor_tensor(out=AR[:, hc, :, 0, :], in0=c3(kk, hc), in1=c3(enegx, hc), op=ALU.mult),
                      reads=[kk_t, enegx_t], writes=[AR_t])
                kb.op("pool", lambda hc=hc: G.tensor_tensor(out=AR[:, hc, :, 1, :], in0=c3(r_, hc), in1=c3(eneg, hc), op=ALU.mult),
                      reads=[r_t, eneg_t], writes=[AR_t])
                kb.op("dve", lambda hc=hc: V.scalar_tensor_tensor(out=BK[:, hc, :, 0, :], in0=c3(a_, hc), scalar=-1.0, in1=c3(epos, hc),
                                                                 op0=ALU.mult, op1=ALU.mult), reads=[a_t, epos_t], writes=[BK_t])
                kb.op("pool", lambda hc=hc: G.tensor_tensor(out=BK[:, hc, :, 1, :], in0=c3(t1, hc), in1=c3(epos, hc), op=ALU.mult),
                      reads=[t1_t, epos_t], writes=[BK_t])
                for hh in range(2):
                    h = 2 * hc + hh
                    kb.op("dve", lambda hc=hc, hh=hh, h=h: V.tensor_copy(out=WCs[:, h, :], in_=c3(eneg, hc)[hh * 64:(hh + 1) * 64, :, 63]),
                          reads=[eneg_t], writes=[WCs_t])
                    kb.op("dve", lambda hc=hc, hh=hh, h=h: V.tensor_copy(out=ARh[:, h, :, :, :].rearrange("p c a t -> p (c a t)"),
                                                                         in_=AR[hh * 64:(hh + 1) * 64, hc, :, :, :].rearrange("p c a t -> p (c a t)")),
                          reads=[AR_t], writes=[ARh_t])
                    kb.op("dve", lambda hc=hc, hh=hh, h=h: V.tensor_copy(out=BKh[:, h, :, :, :].rearrange("p c a t -> p (c a t)"),
                                                                         in_=BK[hh * 64:(hh + 1) * 64, hc, :, :, :].rearrange("p c a t -> p (c a t)")),
                          reads=[BK_t], writes=[BKh_t])
            srcs = [(lambda hc, c: AR[:, hc, c, 0, :], AR_t), (lambda hc, c: BK[:, hc, c, 0, :], BK_t),
                    (lambda hc, c: BK[:, hc, c, 1, :], BK_t), (lambda hc, c: v_[:, hc, c * 64:(c + 1) * 64], v_t)]
            ne = 0
            for q in range(4):
                fn, ft = srcs[q]
                for c0 in range(0, NCH, 2):
                    for cc in range(2):
                        for hc in range(2):
                            kb.op("pe", lambda fn=fn, c0=c0, cc=cc, hc=hc: P.transpose(
                                TIN[0:64, cc * 256 + hc * 128:cc * 256 + hc * 128 + 128], fn(hc, c0 + cc), ident[:]),
                                reads=[ft, ctk], writes=[TIN_t])
                    e = "act" if ne % 2 == 0 else "dve"
                    ne += 1
                    dstv = TM[q][:, c0:c0 + 2, :, :].rearrange("p c h n -> p (c h n)")
                    if e == "act":
                        kb.op("act", lambda dstv=dstv: S.copy(out=dstv, in_=TIN[0:64, :]), reads=[TIN_t], writes=[TM_t[q]])
                    else:
                        kb.op("dve", lambda dstv=dstv: V.tensor_copy(out=dstv, in_=TIN[0:64, :]), reads=[TIN_t], writes=[TM_t[q]])
            for c in range(NCH):
                for hc in range(2):
                    kb.op("pe", lambda c=c, hc=hc: P.matmul(TIN[0:64, c * 4 + hc * 2:c * 4 + hc * 2 + 2], lhsT=prod[:, hc, c * 64:(c + 1) * 64],
                                                            rhs=bind[:], start=True, stop=True), reads=[prod_t, ctk], writes=[TIN_t])
            kb.op("dve", lambda: V.tensor_copy(out=RKs[:].rearrange("p c h -> p (c h)"), in_=TIN[0:64, 0:NCH * 4]), reads=[TIN_t], writes=[RKs_t])

        hseq = [0]

        def chunk_gen(j, c, k):
            p = j % 2
            ARh, ARh_t, BKh, BKh_t = ARh2[p], ARh2_t[p], BKh2[p], BKh2_t[p]
            TM, TM_t = TM2[p], TM2_t[p]
            WCs, WCs_t = WCs2[p], WCs2_t[p]
            Ybuf, Y_t = Ybuf2[p], Y2_t[p]
            ATp, ATp_t, APp, APp_t, SQp, SQp_t = ATpk[k], ATpk_t[k], APpk[k], APpk_t[k], SQpk[k], SQpk_t[k]
            ATs, ATs_t, Ls, Ls_t = ATsk[k], ATsk_t[k], Lsk[k], Lsk_t[k]
            Zb, Zb_t, LB, LB_t = Zbk[k], Zbk_t[k], LBk[k], LBk_t[k]
            RHs, RHs_t, MTs, MTs_t = RHsk[k], RHsk_t[k], MTsk[k], MTsk_t[k]
            def opnd(h):
                return h // 2, 64 * (h % 2)
            for half in range(2):
                for hl in range(2):
                    h = 2 * half + hl
                    hc, pb = opnd(h)
                    rhs = ARh[:, h, c, :, :].rearrange("p a t -> p (a t)")
                    kb.op("pe", lambda hl=hl, h=h, rhs=rhs: P.matmul(ATp[0:64, hl * 256:hl * 256 + 128], lhsT=BKh[:, h, c, 0, :],
                                                                             rhs=rhs, start=True, stop=True), reads=[ARh_t, BKh_t], writes=[ATp_t])
                    kb.op("pe", lambda hl=hl, h=h, rhs=rhs: P.matmul(ATp[0:64, hl * 256 + 128:hl * 256 + 256], lhsT=BKh[:, h, c, 1, :],
                                                                             rhs=rhs, start=True, stop=True), reads=[ARh_t, BKh_t], writes=[ATp_t])
                kb.op("dve", lambda half=half: V.tensor_tensor(
                    out=ATs[:, 2 * half:2 * half + 2, :, :].rearrange("p h q t -> p (h q t)"), in0=ATp[0:64, :],
                    in1=mAT[:, 2 * half:2 * half + 2, :, :].rearrange("p h q t -> p (h q t)"), op=ALU.mult),
                    reads=[ATp_t, ctk], writes=[ATs_t])
                yield
            for h in range(4):
                hc, pb = opnd(h)
                kb.op("pe", lambda h=h, hc=hc, pb=pb: P.matmul(SQp[0:64, h * 64:(h + 1) * 64], lhsT=ARh[:, h, c, 0, :],
                                                               rhs=BKh[:, h, c, 0, :], start=True, stop=True),
                      reads=[ARh_t, BKh_t], writes=[SQp_t])
            kb.op("dve", lambda: V.tensor_tensor(out=Ls[:].rearrange("p h s -> p (h s)"), in0=SQp[0:64, 0:256],
                                                 in1=mL[:].rearrange("p h s -> p (h s)"), op=ALU.mult), reads=[SQp_t, ctk], writes=[Ls_t])
            yield
            for h in range(4):
                kb.op("pe", lambda h=h: P.matmul(APp[0:64, h * 128 + 64:h * 128 + 128], lhsT=ATs[:, h, 2, :], rhs=TM[TV_][:, c, h, :],
                                                 start=(h == 0), stop=False, skip_group_check=True), reads=[ATs_t, TM_t[TV_]], writes=[APp_t])
                kb.op("pe", lambda h=h: P.matmul(APp[0:64, h * 128:h * 128 + 64], lhsT=identb[:], rhs=TM[TA_][:, c, h, :],
                                                 start=False, stop=False, skip_group_check=True), reads=[ctk, TM_t[TA_]], writes=[APp_t])
            zc = 0
            kb.op("act", lambda: S.copy(out=Zb[0][:].rearrange("p h x -> p (h x)"), in_=APp[0:64, :]), reads=[APp_t], writes=[Zb_t[0]])
            yield
            Bm = lambda h: ATs[:, h, 0, :]
            Lm = lambda h: Ls[:, h, :]
            Bm_t, Lm_t = ATs_t, Ls_t
            for lvl in range(6):
                for h in range(4):
                    kb.op("pe", lambda h=h, Bm=Bm, zc=zc: P.matmul(APp[0:64, h * 128:(h + 1) * 128], lhsT=Bm(h), rhs=Zb[zc][:, h, :],
                                                                   start=False, stop=(lvl == 5), skip_group_check=True), reads=[Bm_t, Zb_t[zc]], writes=[APp_t])
                if lvl < 5:
                    for h in range(4):
                        kb.op("pe", lambda h=h, Bm=Bm, Lm=Lm: P.matmul(SQp[0:64, h * 128:h * 128 + 64], lhsT=Bm(h), rhs=Lm(h), start=True, stop=True),
                              reads=[Bm_t, Lm_t], writes=[SQp_t])
                        kb.op("pe", lambda h=h, Bm=Bm, Lm=Lm: P.matmul(SQp[0:64, h * 128 + 64:h * 128 + 128], lhsT=Lm(h), rhs=Bm(h), start=True, stop=True),
                              reads=[Bm_t, Lm_t], writes=[SQp_t])
                kb.op("dve", lambda zc=zc: V.tensor_copy(out=Zb[1 - zc][:].rearrange("p h x -> p (h x)"), in_=APp[0:64, :]),
                      reads=[APp_t], writes=[Zb_t[1 - zc]])
                yield
                zc = 1 - zc
                if lvl < 5:
                    nb = lvl % 2
                    kb.op("act", lambda nb=nb: S.copy(out=LB[nb][:].rearrange("p h q s -> p (h q s)"), in_=SQp[0:64, :]),
                          reads=[SQp_t], writes=[LB_t[nb]])
                    yield
                    Lm = lambda h, nb=nb: LB[nb][:, h, 0, :]
                    Bm = lambda h, nb=nb: LB[nb][:, h, 1, :]
                    Bm_t = Lm_t = LB_t[nb]
            Zf, Zf_t = Zb[zc], Zb_t[zc]
            for h in range(4):
                kb.op("pe", lambda h=h: P.matmul(ATp[0:64, h * 64:(h + 1) * 64], lhsT=Zf[:, h, 0:64], rhs=ATs[:, h, 1, :], start=True, stop=True),
                      reads=[Zf_t, ATs_t], writes=[ATp_t])
            kb.op("dve", lambda: V.tensor_tensor(out=RHs[:], in0=ATp[0:64, 0:256].rearrange("p (h t) -> p h t", h=4), in1=ARh[:, :, c, 1, :], op=ALU.add),
                  reads=[ATp_t, ARh_t], writes=[RHs_t])
            yield
            for h in range(4):
                kb.op("pe", lambda h=h: P.matmul(ATp[0:64, 256 + h * 64:256 + (h + 1) * 64], lhsT=Zf[:, h, 0:64], rhs=TM[TB_][:, c, h, :], start=True, stop=False),
                      reads=[Zf_t, TM_t[TB_]], writes=[ATp_t])
                kb.op("pe", lambda h=h: P.matmul(ATp[0:64, 256 + h * 64:256 + (h + 1) * 64], lhsT=identb[:], rhs=identb[:], start=False, stop=True),
                      reads=[ctk], writes=[ATp_t])
            kb.op("act", lambda: S.copy(out=MTs[:].rearrange("p h n -> p (h n)"), in_=ATp[0:64, 256:512]), reads=[ATp_t], writes=[MTs_t])
            yield
            hcur, hnew = hstate[0], 1 - hstate[0]
            while hseq[0] != j * NCH + c:
                yield
            hcur, hnew = hstate[0], 1 - hstate[0]
            for h in range(4):
                o = SQp[0:64, h * 64:(h + 1) * 64]
                kb.op("pe", lambda h=h, o=o: P.matmul(o, lhsT=RHs[:, h, :], rhs=Hs[hcur][:, h, :], start=True, stop=False),
                      reads=[RHs_t, Hs_t[hcur]], writes=[SQp_t])
                kb.op("pe", lambda h=h, o=o: P.matmul(o, lhsT=ATs[:, h, 1, :], rhs=Zf[:, h, 64:128], start=False, stop=False),
                      reads=[ATs_t, Zf_t], writes=[SQp_t])
                kb.op("pe", lambda h=h, o=o: P.matmul(o, lhsT=ATs[:, h, 3, :], rhs=TM[TV_][:, c, h, :], start=False, stop=True),
                      reads=[ATs_t, TM_t[TV_]], writes=[SQp_t])
            kb.op("act", lambda: S.copy(out=Ybuf[:, c, :, :].rearrange("p h v -> p (h v)"), in_=SQp[0:64, 0:256]), reads=[SQp_t], writes=[Y_t])
            yield
            for h in range(4):
                o = SQp[0:64, 256 + h * 64:256 + (h + 1) * 64]
                kb.op("pe", lambda h=h, o=o: P.matmul(o, lhsT=MTs[:, h, :], rhs=Hs[hcur][:, h, :], start=True, stop=False),
                      reads=[MTs_t, Hs_t[hcur]], writes=[SQp_t])
                kb.op("pe", lambda h=h, o=o: P.matmul(o, lhsT=TM[TB_][:, c, h, :], rhs=Zf[:, h, 64:128], start=False, stop=False),
                      reads=[TM_t[TB_], Zf_t], writes=[SQp_t])
                kb.op("pe", lambda h=h, o=o: P.matmul(o, lhsT=TM[TK_][:, c, h, :], rhs=TM[TV_][:, c, h, :], start=False, stop=True),
                      reads=[TM_t[TK_], TM_t[TV_]], writes=[SQp_t])
            kb.op("dve", lambda: V.tensor_tensor(out=Hs[hnew][:], in0=SQp[0:64, 256:512].rearrange("p (h v) -> p h v", h=4),
                                                 in1=WCs[:, :, c:c + 1].broadcast_to([64, 4, 64]), op=ALU.mult),
                  reads=[SQp_t, WCs_t], writes=[Hs_t[hnew]])
            yield
            hstate[0] = hnew
            hseq[0] += 1

        def post9(j):
            p = j % 2
            cs = slice(j * TA, (j + 1) * TA)
            ARh, ARh_t, BKh, BKh_t = ARh2[p], ARh2_t[p], BKh2[p], BKh2_t[p]
            TM, TM_t = TM2[p], TM2_t[p]
            WCs, WCs_t, RKs, RKs_t = WCs2[p], WCs2_t[p], RKs2[p], RKs2_t[p]
            sg, sg_t = sg2[p], sg2_t[p]
            Ybuf, Y_t = Ybuf2[p], Y2_t[p]
            Yv = Ybuf[:].rearrange("p c h v -> p (c h) v")
            W1b = AR[0:64, :, :, :, :].rearrange("p a c q t -> p (a c q t)").rearrange("p (c h v) -> p c h v", c=NCH, h=4)
            W2b = BK[0:64, :, :, :, :].rearrange("p a c q t -> p (a c q t)").rearrange("p (c h v) -> p c h v", c=NCH, h=4)
            W1_t, W2_t = AR_t, BK_t
            W1v = W1b.rearrange("p c h v -> p (c h) v")
            W2v = W2b.rearrange("p c h v -> p (c h) v")
            kb.op("dve", lambda: V.tensor_reduce(out=st1[:], in_=Yv, axis=AX.X, op=ALU.add), reads=[Y_t], writes=[st_t])
            kb.op("act", lambda: S.activation(out=W1b.rearrange("p c h v -> p (c h v)"), in_=Ybuf[:].rearrange("p c h v -> p (c h v)"), func=AF.Square),
                  reads=[Y_t], writes=[W1_t])
            kb.op("dve", lambda: V.tensor_reduce(out=st2[:], in_=W1v, axis=AX.X, op=ALU.add), reads=[W1_t], writes=[st_t])
            kb.op("dve", lambda: V.tensor_scalar(out=st1[:], in0=st1[:], scalar1=1.0 / 64, scalar2=None, op0=ALU.mult), reads=[st_t], writes=[st_t])
            kb.op("dve", lambda: V.tensor_tensor(out=st3[:], in0=st1[:], in1=st1[:], op=ALU.mult), reads=[st_t], writes=[st_t])
            kb.op("dve", lambda: V.tensor_scalar(out=st2[:], in0=st2[:], scalar1=1.0 / 64, scalar2=None, op0=ALU.mult), reads=[st_t], writes=[st_t])
            kb.op("dve", lambda: V.tensor_tensor(out=st2[:], in0=st2[:], in1=st3[:], op=ALU.subtract), reads=[st_t], writes=[st_t])
            kb.op("dve", lambda: V.tensor_scalar(out=st2[:], in0=st2[:], scalar1=0.0, scalar2=None, op0=ALU.max), reads=[st_t], writes=[st_t])
            kb.op("act", lambda: S.activation(out=st2[:], in_=st2[:], func=AF.Ln, bias=GN_EPS, scale=1.0), reads=[st_t], writes=[st_t])
            kb.op("act", lambda: S.activation(out=st2[:], in_=st2[:], func=AF.Exp, scale=-0.5), reads=[st_t], writes=[st_t])
            bc = lambda t: t[:].unsqueeze(2).broadcast_to([64, NCH * 4, 64])
            kb.op("dve", lambda: V.tensor_tensor(out=W1v, in0=Yv, in1=bc(st1), op=ALU.subtract), reads=[Y_t, st_t], writes=[W1_t])
            kb.op("dve", lambda: V.tensor_tensor(out=W1v, in0=W1v, in1=bc(st2), op=ALU.mult), reads=[st_t], writes=[W1_t])
            lw_b = lnw[:, 0:256].unsqueeze(1).broadcast_to([64, NCH, 256])
            lb_b = lnw[:, 256:512].unsqueeze(1).broadcast_to([64, NCH, 256])
            W1c = W1b.rearrange("p c h v -> p c (h v)")
            W2c = W2b.rearrange("p c h v -> p c (h v)")
            kb.op("pool", lambda: G.tensor_tensor(out=W1c, in0=W1c, in1=lw_b, op=ALU.mult), reads=[ctk], writes=[W1_t])
            kb.op("pool", lambda: G.tensor_tensor(out=W1c, in0=W1c, in1=lb_b, op=ALU.add), reads=[ctk], writes=[W1_t])
            kb.op("dve", lambda: V.tensor_tensor(out=W2v, in0=TM[TV_][:].rearrange("p c h v -> p (c h) v"),
                                                 in1=RKs[:].rearrange("p c h -> p (c h)").unsqueeze(2).broadcast_to([64, NCH * 4, 64]), op=ALU.mult),
                  reads=[TM_t[TV_], RKs_t], writes=[W2_t])
            kb.op("dve", lambda: V.tensor_tensor(out=W1v, in0=W1v, in1=W2v, op=ALU.add), reads=[W2_t], writes=[W1_t])
            for hc in range(2):
                for c in range(NCH):
                    kb.op("pe", lambda hc=hc, c=c: P.transpose(PR[:, c * 64:(c + 1) * 64], W1b[:, c, 2 * hc:2 * hc + 2, :].rearrange("p h v -> p (h v)"),
                                                               ident[0:64, 0:64]), reads=[W1_t, ctk], writes=[PR_t])
                kb.op("dve", lambda hc=hc: V.tensor_tensor(out=zout[:, hc, :], in0=PR[:, 0:TA], in1=sg[:, hc, :], op=ALU.mult),
                      reads=[PR_t, sg_t], writes=[zo_t])
            kb.dma("sp", zT.rearrange("(c p) t -> p c t", p=128)[:, :, cs], zout[:], zo_t, reads=[zo_t])

        coop = Coop(kb)
        prep(0)
        if ntile > 1:
            prep(1)
        pending = [(j, c) for j in range(ntile) for c in range(NCH)]
        slots = [None, None]
        sid = [None, None]
        idx = 0
        while idx < len(pending) or slots[0] is not None or slots[1] is not None:
            for k in range(2):
                if slots[k] is None and idx < len(pending):
                    jn, cn = pending[idx]
                    other = sid[1 - k]
                    if cn == 0 and jn > 1:
                        coop.finish()
                    slots[k] = chunk_gen(jn, cn, k)
                    sid[k] = (jn, cn)
                    idx += 1
                if slots[k] is not None:
                    try:
                        next(slots[k])
                    except StopIteration:
                        jj, cc = sid[k]
                        slots[k] = None
                        sid[k] = None
                        if cc == NCH - 1:
                            coop.finish()

                            def filler(jj=jj):
                                post9(jj)
                                if jj + 2 < ntile:
                                    prep(jj + 2)
                            coop.start(filler)
                    coop.step(FILL)
        coop.finish()
        kb.finish([zo_t] + dbg_tok)
    return nc


def rwkv_maps(x, I, NT):
    maps = []
    Wfull = I["rw_w_in"][0].reshape(D, 4, 1024)
    vecA = np.concatenate([I["rw_norm"][0].reshape(8, 128).T] + [I["rw_mu"][0][i].reshape(8, 128).T for i in range(6)], axis=1)
    ident = np.eye(128, dtype=np.float32)
    for c in range(8):
        b, g = c // 4, c % 4
        cs = slice(g * 256, (g + 1) * 256)
        vc = np.zeros((128, 2, 8), np.float32)
        for vi, nm in enumerate(["rw_w0", "rw_a0", "rw_k_k", "rw_k_a"]):
            vc[:, :, vi] = I[nm][0][cs].reshape(2, 128).T
        vc[:, :, 4] = I["rw_r_k"][0].reshape(1024)[cs].reshape(2, 128).T
        maps.append(dict(
            xT=np.ascontiguousarray(x[b, :NT].T),
            Wc=np.ascontiguousarray(Wfull[:, :, cs].reshape(D, 1024)),
            w1=I["rw_w1"][0], a1=I["rw_a1"][0],
            w2c=np.ascontiguousarray(I["rw_w2"][0][:, cs]), a2c=np.ascontiguousarray(I["rw_a2"][0][:, cs]),
            vecA=np.ascontiguousarray(vecA.astype(np.float32)), vecC=np.ascontiguousarray(vc.reshape(128, 16)),
            lnwb=np.ascontiguousarray(np.concatenate([I["rw_ln_w"][0][cs], I["rw_ln_b"][0][cs]])[None, :]),
            ident=ident))
    return maps


LAM_INIT = 0.8 - 0.6 * math.exp(-0.3 * 1)


def build_attn(NT):
    nc = new_nc()
    hnT = dram_in(nc, "hnT", [D, NT], BF16)
    Wd = dram_in(nc, "Wd", [D, 4 * 256])
    sub = dram_in(nc, "sub", [1, 128])
    lqk = dram_in(nc, "lqk", [1, 256])
    invf = dram_in(nc, "invf", [128, 1])
    pos0 = dram_in(nc, "pos0", [1, TT])
    cosd = dram_in(nc, "cosd", [128, NT])
    sind = dram_in(nc, "sind", [128, NT])
    identd = dram_in(nc, "ident", [128, 128])
    zT = dram_out(nc, "zT", [256, NT], BF16)
    ntile = NT // TT
    nblk = NT // 128
    PI = math.pi
    with ExitStack() as es:
        kb = KB(nc, es)
        V, S, G, P = nc.vector, nc.scalar, nc.gpsimd, nc.tensor
        ctk = kb.tok("cst")
        ident = kb.sb("ident_sb", [128, 128])
        kb.dma("sp", ident[:], identd[:, :], ctk, writes=[ctk])
        tri = kb.sb("tri", [128, 128], BF16)
        kb.op("pool", lambda: G.memset(tri[:], 1.0), writes=[ctk])
        kb.op("pool", lambda: G.affine_select(out=tri[:], in_=tri[:], pattern=[[1, 128]], compare_op=ALU.is_ge, fill=0.0, base=0,
                                              channel_multiplier=-1), writes=[ctk])
        subb = kb.sb("subb", [128, 128])
        kb.dma("sp", subb[:], sub.partition_broadcast(128), ctk, writes=[ctk])
        kb.op("dve", lambda: V.tensor_scalar(out=subb[:], in0=subb[:], scalar1=1.0 - LAM_INIT, scalar2=None, op0=ALU.mult), reads=[ctk], writes=[ctk])
        lq = kb.sb("lq", [128, 256])
        kb.dma("sp", lq[:], lqk.partition_broadcast(128), ctk, writes=[ctk])
        lam = kb.sb("lam", [128, 4])
        kb.op("dve", lambda: V.tensor_tensor(out=lq[:, 0:64], in0=lq[:, 0:64], in1=lq[:, 64:128], op=ALU.mult), reads=[ctk], writes=[ctk])
        kb.op("dve", lambda: V.tensor_tensor(out=lq[:, 128:192], in0=lq[:, 128:192], in1=lq[:, 192:256], op=ALU.mult), reads=[ctk], writes=[ctk])
        kb.op("dve", lambda: V.tensor_reduce(out=lam[:, 0:1], in_=lq[:, 0:64], axis=AX.X, op=ALU.add), reads=[ctk], writes=[ctk])
        kb.op("dve", lambda: V.tensor_reduce(out=lam[:, 1:2], in_=lq[:, 128:192], axis=AX.X, op=ALU.add), reads=[ctk], writes=[ctk])
        kb.op("act", lambda: S.activation(out=lam[:, 0:2], in_=lam[:, 0:2], func=AF.Exp), reads=[ctk], writes=[ctk])
        kb.op("dve", lambda: V.tensor_tensor(out=lam[:, 2:3], in0=lam[:, 1:2], in1=lam[:, 0:1], op=ALU.subtract), reads=[ctk], writes=[ctk])
        kb.op("dve", lambda: V.tensor_scalar(out=lam[:, 3:4], in0=lam[:, 2:3], scalar1=-LAM_INIT, scalar2=None, op0=ALU.add), reads=[ctk], writes=[ctk])
        ivf = kb.sb("ivf", [128, 1])
        kb.dma("sp", ivf[:], invf[:, :], ctk, writes=[ctk])
        ngpi = kb.sb("ngpi", [128, 1])
        kb.op("pool", lambda: G.memset(ngpi[:], -PI), writes=[ctk])

        QT2 = [kb.sb("QT%d" % i, [128, NT], BF16) for i in range(2)]
        QT_t = kb.tok("QT")
        for i in range(2):
            kb.op("pool", lambda i=i: G.memset(QT2[i][:], 0.0), writes=[QT_t])
        KT = kb.sb("KT", [128, NT], BF16); KT_t = kb.tok("KT")
        sgd = nc.dram_tensor("sgd", [2, 128, NT], BF16).ap()
        sgd_t = kb.tok("sgd")
        SGt = [kb.sb("SGt%d" % i, [128, TT], BF16) for i in range(2)]
        SGt_t = kb.toks(2, "SGt")
        sgl = [kb.sb("sgl%d" % i, [128, TT], BF16) for i in range(2)]
        sgl_t = kb.toks(2, "sgl")
        Va = kb.sb("Va", [128, nblk, 129], BF16); Va_t = kb.tok("Va")
        kb.op("pool", lambda: G.memset(Va[:], 1.0), writes=[Va_t])
        hin = [kb.sb("ahin%d" % i, [128, 8, TT], BF16) for i in range(2)]
        hin_t = kb.toks(2, "ahin")
        stg = kb.sb("astg", [128, 1024])
        stg_t = kb.tok("astg")
        Wt = {nm: kb.sb("W" + nm, [128, 8, 128], BF16) for nm in ("q", "qr", "k", "kr", "v", "g")}
        W_t = kb.tok("Wt")
        sn = kb.sb("sn", [128, TT]); cs_ = kb.sb("cs_", [128, TT]); sc_t = kb.tok("sincos")
        ta = kb.sb("ta", [128, TT]); tb = kb.sb("tb", [128, TT]); tab_t = kb.tok("tab")
        PT = [kb.sb("PT%d" % i, [128, TT], BF16) for i in range(4)]
        PT_t = kb.toks(4, "PT")
        o1 = kb.sb("o1", [128, 128]); o1_t = kb.tok("o1")
        o4 = kb.sb("o4", [128, 4, 128]); o_t = kb.tok("o4")
        on4 = kb.sb("on4", [128, 4, 128]); on_t = kb.tok("on4")
        rc8 = kb.sb("rc8", [128, 9]); ss4 = kb.sb("ss4", [128, 4]); ss_t = kb.tok("ss4")
        junk = kb.sb("junk", [128, 128])
        rc = kb.sb("rc", [128, 4]); rc_t = kb.tok("rc")
        zout = [kb.sb("azout%d" % i, [128, TT], BF16) for i in range(2)]
        zo_t = kb.toks(2, "azo")
        SB = [kb.ps("SB%d" % i, [128, 512]) for i in range(4)]
        SB_t = [kb.ptok("SB%d" % i) for i in range(4)]
        AC = [kb.ps("AC%d" % i, [128, 512]) for i in range(3)]
        AC_t = [kb.ptok("AC%d" % i) for i in range(3)]
        TP_ = kb.ps("TPp", [128, 512]); TP_t = kb.ptok("TPp")

        def acc(c, qs):
            i = c * 4 + qs
            return AC[i // 3][:, (i % 3) * 129:(i % 3) * 129 + 129], AC_t[i // 3]

        hv = hnT.rearrange("(c p) t -> p c t", p=128)
        for hh in range(2):
            for gi, nm in enumerate(("q", "k", "v", "g")):
                for dc in range(8):
                    kb.dma("sp", stg[:, 0:128], Wd[dc * 128:(dc + 1) * 128, gi * 256 + hh * 128:gi * 256 + hh * 128 + 128], stg_t, writes=[stg_t])
                    kb.op("dve", lambda nm=nm, dc=dc: V.tensor_copy(out=Wt[nm][:, dc, :], in_=stg[:, 0:128]), reads=[stg_t], writes=[W_t])
                    if nm in ("q", "k"):
                        for c in range(2):
                            kb.op("dve", lambda nm=nm, dc=dc, c=c: V.tensor_scalar(out=Wt[nm + "r"][:, dc, c * 64:c * 64 + 32], in0=stg[:, c * 64 + 32:c * 64 + 64],
                                                                                   scalar1=-1.0, scalar2=None, op0=ALU.mult), reads=[stg_t], writes=[W_t])
                            kb.op("dve", lambda nm=nm, dc=dc, c=c: V.tensor_copy(out=Wt[nm + "r"][:, dc, c * 64 + 32:c * 64 + 64], in_=stg[:, c * 64:c * 64 + 32]),
                                  reads=[stg_t], writes=[W_t])
            kb.dma("sp", hin[0][:], hv[:, :, 0:TT], hin_t[0], writes=[hin_t[0]])
            for j in range(ntile):
                s = j % 2
                cs = slice(j * TT, (j + 1) * TT)
                if j + 1 < ntile:
                    kb.dma("sp", hin[1 - s][:], hv[:, :, (j + 1) * TT:(j + 2) * TT], hin_t[1 - s], writes=[hin_t[1 - s]])
                hb, hb_t = hin[s], hin_t[s]
                kb.dma("sp", sn[:], sind[:, cs], sc_t, writes=[sc_t])
                kb.dma("sp", cs_[:], cosd[:, cs], sc_t, writes=[sc_t])
                for nm, dstT, dst_t in (("q", None, QT_t), ("k", KT, KT_t)):
                    for dc in range(8):
                        kb.op("pe", lambda nm=nm, dc=dc: P.matmul(SB[0][:], lhsT=Wt[nm][:, dc, :], rhs=hb[:, dc, :], start=(dc == 0), stop=(dc == 7)),
                              reads=[W_t, hb_t], writes=[SB_t[0]])
                    for dc in range(8):
                        kb.op("pe", lambda nm=nm, dc=dc: P.matmul(SB[1][:], lhsT=Wt[nm + "r"][:, dc, :], rhs=hb[:, dc, :], start=(dc == 0), stop=(dc == 7)),
                              reads=[W_t, hb_t], writes=[SB_t[1]])
                    kb.op("dve", lambda: V.tensor_tensor(out=ta[:], in0=SB[0][:], in1=cs_[:], op=ALU.mult), reads=[SB_t[0], sc_t], writes=[tab_t])
                    kb.op("dve", lambda: V.tensor_tensor(out=tb[:], in0=SB[1][:], in1=sn[:], op=ALU.mult), reads=[SB_t[1], sc_t], writes=[tab_t])
                    if nm == "k":
                        kb.op("dve", lambda dstT=dstT, cs=cs: V.tensor_tensor(out=dstT[:, cs], in0=ta[:], in1=tb[:], op=ALU.add), reads=[tab_t], writes=[dst_t])
                    else:
                        kb.op("dve", lambda cs=cs: V.tensor_tensor(out=QT2[0][0:64, cs], in0=ta[0:64, :], in1=tb[0:64, :], op=ALU.add), reads=[tab_t], writes=[dst_t])
                        kb.op("dve", lambda cs=cs: V.tensor_tensor(out=QT2[1][64:128, cs], in0=ta[64:128, :], in1=tb[64:128, :], op=ALU.add), reads=[tab_t], writes=[dst_t])
                for dc in range(8):
                    kb.op("pe", lambda dc=dc: P.matmul(SB[2][:], lhsT=Wt["g"][:, dc, :], rhs=hb[:, dc, :], start=(dc == 0), stop=(dc == 7)),
                          reads=[W_t, hb_t], writes=[SB_t[2]])
                kb.op("act", lambda s=s: S.activation(out=SGt[s][:], in_=SB[2][:], func=AF.Silu), reads=[SB_t[2]], writes=[SGt_t[s]])
                kb.dma("sp", sgd[hh][:, cs], SGt[s][:], SGt_t[s], reads=[SGt_t[s]], writes=[sgd_t])
                for bi in range(4):
                    for dc in range(8):
                        kb.op("pe", lambda dc=dc, bi=bi: P.matmul(SB[3][:, bi * 128:(bi + 1) * 128], lhsT=hb[:, dc, bi * 128:(bi + 1) * 128], rhs=Wt["v"][:, dc, :],
                                                                  start=(dc == 0), stop=(dc == 7)), reads=[W_t, hb_t], writes=[SB_t[3]])
                kb.op("act", lambda j=j: S.copy(out=Va[:, j * 4:j * 4 + 4, 0:128], in_=SB[3][:].rearrange("p (b v) -> p b v", b=4)), reads=[SB_t[3]], writes=[Va_t])
            gstep = [0]
            for g in range(ntile):
                for a3 in range(3):
                    kb.op("dve", lambda a3=a3: V.memset(AC[a3][:], 0.0), writes=[AC_t[a3]])
                kb.dma("sp", sgl[g % 2][:], sgd[hh][:, g * TT:(g + 1) * TT], sgl_t[g % 2], reads=[sgd_t], writes=[sgl_t[g % 2]])
                steps = []
                for kbk in range(4 * g + 4):
                    for c in range(2):
                        steps.append((kbk, c, gstep[0] % 4))
                        gstep[0] += 1

                def qk(st, g=g):
                    kbk, c, bi = st
                    kb.op("pe", lambda: P.matmul(SB[bi][:], lhsT=KT[:, kbk * 128:(kbk + 1) * 128],
                                                 rhs=QT2[c][:, g * TT:(g + 1) * TT], start=True, stop=True),
                          reads=[KT_t, QT_t], writes=[SB_t[bi]])

                def ex(st, g=g):
                    kbk, c, bi = st
                    m = kbk - 4 * g
                    kb.op("act", lambda: S.activation(out=PT[bi][:], in_=SB[bi][:], func=AF.Exp, scale=0.125),
                          reads=[SB_t[bi]], writes=[PT_t[bi]])
                    if m >= 0:
                        kb.op("pool", lambda: G.tensor_tensor(out=PT[bi][:, m * 128:(m + 1) * 128], in0=PT[bi][:, m * 128:(m + 1) * 128],
                                                              in1=tri[:], op=ALU.mult), reads=[ctk], writes=[PT_t[bi]])

                def av(st, g=g):
                    kbk, c, bi = st
                    m = kbk - 4 * g
                    for qs in range(4):
                        if m > qs:
                            continue
                        ap_, at_ = acc(c, qs)
                        kb.op("pe", lambda ap_=ap_, qs=qs: P.matmul(ap_, lhsT=PT[bi][:, qs * 128:(qs + 1) * 128], rhs=Va[:, kbk, :],
                                                                    start=False, stop=False, skip_group_check=True),
                              reads=[PT_t[bi], Va_t], writes=[at_])

                LA = 2
                for i in range(min(LA, len(steps))):
                    qk(steps[i])
                for i, st in enumerate(steps):
                    ex(st)
                    if i + LA < len(steps):
                        qk(steps[i + LA])
                    av(st)
                zb, zb_t = zout[g % 2], zo_t[g % 2]
                for a3 in range(3):
                    na = 3 if a3 < 2 else 2
                    kb.op("dve", lambda a3=a3, na=na: V.reciprocal(out=rc8[:, a3 * 3:a3 * 3 + na],
                                                                   in_=AC[a3][:, 0:na * 129].rearrange("p (a w) -> p a w", w=129)[:, :, 128]),
                          reads=[AC_t[a3]], writes=[rc_t])
                for qs in range(4):
                    a0, a0t = acc(0, qs)
                    a1, a1t = acc(1, qs)
                    kb.op("dve", lambda a1=a1, qs=qs: V.tensor_scalar(out=o1[:], in0=a1[:, 0:128], scalar1=rc8[:, 4 + qs:5 + qs], scalar2=lam[:, 3:4],
                                                                     op0=ALU.mult, op1=ALU.mult), reads=[a1t, rc_t, ctk], writes=[o1_t])
                    kb.op("dve", lambda a0=a0, qs=qs: V.scalar_tensor_tensor(out=o4[:, qs, :], in0=a0[:, 0:128], scalar=rc8[:, qs:qs + 1], in1=o1[:],
                                                                            op0=ALU.mult, op1=ALU.add), reads=[a0t, rc_t, o1_t], writes=[o_t])
                for qs in range(4):
                    kb.op("act", lambda qs=qs: S.activation(out=junk[:], in_=o4[:, qs, :], func=AF.Square, accum_out=ss4[:, qs:qs + 1]),
                          reads=[o_t], writes=[ss_t])
                kb.op("act", lambda: S.activation(out=ss4[:], in_=ss4[:], func=AF.Ln, scale=1.0 / 128, bias=1e-5), reads=[ss_t], writes=[ss_t])
                kb.op("act", lambda: S.activation(out=ss4[:], in_=ss4[:], func=AF.Exp, scale=-0.5), reads=[ss_t], writes=[ss_t])
                for qs in range(4):
                    kb.op("dve", lambda qs=qs: V.scalar_tensor_tensor(out=on4[:, qs, :], in0=o4[:, qs, :], scalar=ss4[:, qs:qs + 1], in1=subb[:],
                                                                      op0=ALU.mult, op1=ALU.mult), reads=[o_t, ss_t, ctk], writes=[on_t])
                for qs in range(4):
                    kb.op("pe", lambda qs=qs: P.transpose(TP_[:, qs * 128:(qs + 1) * 128], on4[:, qs, :], ident[:]), reads=[on_t, ctk], writes=[TP_t])
                kb.op("dve", lambda g=g, zb=zb: V.tensor_tensor(out=zb[:], in0=TP_[:], in1=sgl[g % 2][:], op=ALU.mult),
                      reads=[TP_t, sgl_t[g % 2]], writes=[zb_t])
                kb.dma("sp", zT[hh * 128:(hh + 1) * 128, g * TT:(g + 1) * TT], zb[:], zb_t, reads=[zb_t])
        kb.finish(zo_t)
    return nc


def attn_maps(hn_list, I, NT):
    maps = []
    Wfull = I["da_w_in"][0].reshape(D, 4, 1024)
    ident = np.eye(128, dtype=np.float32)
    inv = (1.0 / (10000.0 ** (np.arange(0, 64, 2, dtype=np.float32) / 64))).astype(np.float32)
    invf = np.tile(inv, 4).reshape(128, 1).astype(np.float32)
    pos0 = np.arange(TT, dtype=np.float32)[None, :]
    ang = np.arange(NT, dtype=np.float32)[None, :] * invf
    cosd = np.cos(ang).astype(np.float32)
    sind = np.sin(ang).astype(np.float32)
    lqk = np.concatenate([I["da_lq1"][0], I["da_lk1"][0], I["da_lq2"][0], I["da_lk2"][0]])[None, :].astype(np.float32)
    for c in range(8):
        b, hp = c // 4, c % 4
        cs = slice(hp * 256, (hp + 1) * 256)
        maps.append(dict(hnT=np.ascontiguousarray(hn_list[b]), Wd=np.ascontiguousarray(Wfull[:, :, cs].reshape(D, 1024)),
                         sub=np.ascontiguousarray(I["da_subln"][0][None, :]), lqk=np.ascontiguousarray(lqk), invf=invf, pos0=pos0, ident=ident, cosd=cosd, sind=sind))
    return maps


def kernel(**inputs):
    I = {k: np.asarray(v) for k, v in inputs.items()}
    x, p = I["x"], I["p"]
    B, S = x.shape[0], x.shape[1]
    QT_ = S // 4
    nc = build_rwkv(S)
    res = run_bass_kernel_spmd(nc, rwkv_maps(x, I, S), core_ids=list(range(8))).results
    z0T = [np.concatenate([res[b * 4 + g]["zT"] for g in range(4)], axis=0) for b in range(B)]
    rng = [(c // 4, slice((c % 4) * QT_, (c % 4 + 1) * QT_)) for c in range(8)]
    res2 = run_post([z0T[b][:, sl] for b, sl in rng], [x[b, sl].T for b, sl in rng], [p[0, b, sl].T for b, sl in rng],
                    I["rw_w_out"][0], I["pe_w_gate"][0], I["pe_w_proj"][0], I["pe_norm"][0], I["da_norm"][0], last=False)
    hn = [np.concatenate([res2[b * 4 + q]["hnT"] for q in range(4)], axis=1) for b in range(B)]
    nc3 = build_attn(S)
    res3 = run_bass_kernel_spmd(nc3, attn_maps(hn, I, S), core_ids=list(range(8))).results
    z1T = [np.concatenate([res3[b * 4 + g]["zT"] for g in range(4)], axis=0) for b in range(B)]
    res4 = run_post([z1T[b][:, sl] for b, sl in rng], [res2[c]["h1T"] for c in range(8)], [p[1, b, sl].T for b, sl in rng],
                    I["da_w_out"][0], I["pe_w_gate"][1], I["pe_w_proj"][1], I["pe_norm"][1], I["final_norm"], last=True)
    out = np.empty((B, S, D), np.float32)
    for c, (b, sl) in enumerate(rng):
        out[b, sl, :] = res4[c]["oT"].T
    return out
```

```python
import math
from contextlib import ExitStack
import numpy as np
import ml_dtypes
import concourse.bass as bass
import concourse.mybir as mybir
from concourse.bass_utils import run_bass_kernel_spmd

F32 = mybir.dt.float32
BF16 = mybir.dt.bfloat16
AF = mybir.ActivationFunctionType
ALU = mybir.AluOpType
AX = mybir.AxisListType
NPBF = ml_dtypes.bfloat16

D = 1024
SEQ = 16384
TT = 512


class Tok:
    __slots__ = ("name", "w", "r", "dsem", "dcnt", "excl")

    def __init__(self, name, excl=False):
        self.name = name
        self.excl = excl
        self.w = None
        self.r = {}
        self.dsem = None
        self.dcnt = 0


class KB:
    def __init__(self, nc, es):
        self.nc = nc
        self.es = es
        self.eng = dict(pe=nc.tensor, dve=nc.vector, act=nc.scalar, pool=nc.gpsimd, sp=nc.sync)
        self.sem = {e: es.enter_context(nc.semaphore("prog_" + e)) for e in ("pe", "dve", "act", "pool")}
        self.cnt = {e: 0 for e in self.sem}
        self.seen = {e: {} for e in self.eng}
        self.ntok = 0
        self.out_dma = []
        self.coop = None

    def tok(self, name=None):
        self.ntok += 1
        return Tok(name or ("t%d" % self.ntok))

    def toks(self, n, name="t"):
        return [self.tok("%s%d_%d" % (name, self.ntok, i)) for i in range(n)]

    def sb(self, name, shape, dt=F32):
        return self.es.enter_context(self.nc.sbuf_tensor(name, list(shape), dt))

    def ps(self, name, shape, dt=F32):
        return self.es.enter_context(self.nc.psum_tensor(name, list(shape), dt))

    def _deps(self, reads, writes):
        deps = {}
        for t in reads:
            if t.w is not None:
                k = t.w[1]
                if deps.get(k, (None, None, 0))[2] < t.w[2]:
                    deps[k] = t.w
        for t in writes:
            for d in ([t.w] if t.w is not None else []) + list(t.r.values()):
                k = d[1]
                if deps.get(k, (None, None, 0))[2] < d[2]:
                    deps[k] = d
        return deps

    def _wait(self, e, deps, keep_same=False):
        for k, (sem, key, val) in deps.items():
            if key == e and e == "pe" and not keep_same:
                continue
            if self.seen[e].get(key, 0) < val:
                self.eng[e].wait_ge(sem, val)
                self.seen[e][key] = val

    def ptok(self, name=None):
        t = self.tok(name)
        t.excl = True
        return t

    def op(self, e, fn, reads=(), writes=()):
        ex = [t for t in reads if t.excl]
        if ex:
            reads = [t for t in reads if not t.excl]
            writes = list(writes) + ex
        self._wait(e, self._deps(reads, writes))
        ins = fn()
        self.cnt[e] += 1
        ins.then_inc(self.sem[e], 1)
        me = (self.sem[e], e, self.cnt[e])
        for t in reads:
            t.r[e] = me
        for t in writes:
            t.w = me
            t.r = {}
        if self.coop is not None:
            self.coop.emitted()
        return ins

    def dma(self, q, out, in_, owner, reads=(), writes=(), **kw):
        self._wait(q, self._deps(reads, writes), keep_same=True)
        if owner.dsem is None:
            owner.dsem = self.es.enter_context(self.nc.semaphore("dma_" + owner.name))
        ins = self.eng[q].dma_start(out=out, in_=in_, **kw)
        owner.dcnt += 16
        ins.then_inc(owner.dsem, 16)
        me = (owner.dsem, "dma_" + owner.name, owner.dcnt)
        for t in reads:
            t.r[me[1]] = me
        for t in writes:
            t.w = me
            t.r = {}
        return me

    def finish(self, toks):
        for t in toks:
            if t.dsem is not None:
                self.eng["sp"].wait_ge(t.dsem, t.dcnt)


import threading


class Coop:
    def __init__(self, kb):
        self.kb = kb
        self.thread = None
        self.quota = 0
        self.go = threading.Semaphore(0)
        self.back = threading.Semaphore(0)
        self.done = True
        self.err = None
        kb.coop = self

    def start(self, fn):
        self.done = False
        self.quota = 0

        def run():
            self.go.acquire()
            try:
                fn()
            except BaseException as e:
                self.err = e
            self.done = True
            self.back.release()
        self.thread = threading.Thread(target=run)
        self.thread.start()

    def emitted(self):
        if threading.current_thread() is self.thread:
            self.quota -= 1
            if self.quota <= 0:
                self.back.release()
                self.go.acquire()

    def step(self, n):
        if self.done or threading.current_thread() is self.thread:
            return
        self.quota = n
        self.go.release()
        self.back.acquire()
        if self.err is not None:
            raise self.err

    def finish(self):
        while not self.done:
            self.step(1 << 30)
        if self.thread is not None:
            self.thread.join()
        if self.err is not None:
            raise self.err


def new_nc():
    return bass.Bass("TRN2", target_bir_lowering=False)


def dram_in(nc, name, shape, dt=F32):
    return nc.dram_tensor(name, list(shape), dt, kind="ExternalInput").ap()


def dram_out(nc, name, shape, dt=F32):
    return nc.dram_tensor(name, list(shape), dt, kind="ExternalOutput").ap()


def load_weight_bf16(kb, q, w_dram, kchunks, ncols, name, stage, stage_tok, cast_eng="pool"):
    wt = kb.sb(name, [128, kchunks, ncols], BF16)
    tk = kb.tok(name)
    for kc in range(kchunks):
        s = kc % len(stage)
        kb.dma(q, stage[s][:, 0:ncols], w_dram[kc * 128:(kc + 1) * 128, :], stage_tok[s], writes=[stage_tok[s]])
        kb.op(cast_eng, lambda kc=kc, s=s: kb.eng[cast_eng].tensor_copy(out=wt[:, kc, :], in_=stage[s][:, 0:ncols]),
              reads=[stage_tok[s]], writes=[tk])
    return wt, tk


def rstd_from_sumsq(kb, ps_ap, ps_tok, out_ap, out_tok, eps):
    kb.op("act", lambda: kb.nc.scalar.activation(out=out_ap, in_=ps_ap, func=AF.Ln, bias=float(eps), scale=1.0),
          reads=[ps_tok], writes=[out_tok])
    kb.op("act", lambda: kb.nc.scalar.activation(out=out_ap, in_=out_ap, func=AF.Exp, scale=-0.5),
          reads=[out_tok], writes=[out_tok])


def build_post(NT, last):
    nc = new_nc()
    zT = dram_in(nc, "zT", [D, NT], BF16)
    hT = dram_in(nc, "hT", [D, NT], F32)
    pT = dram_in(nc, "pT", [256, NT], F32)
    w_out = dram_in(nc, "w_out", [D, D])
    w_gate = dram_in(nc, "w_gate", [D, D])
    w_proj = dram_in(nc, "w_proj", [256, D])
    vecs = dram_in(nc, "vecs", [128, 16])
    if last:
        oT = dram_out(nc, "oT", [D, NT], F32)
    else:
        h1T = dram_out(nc, "h1T", [D, NT], F32)
        hnT = dram_out(nc, "hnT", [D, NT], BF16)
    ntile = NT // TT
    with ExitStack() as es:
        kb = KB(nc, es)
        ctk = kb.tok("cst")
        ones = kb.sb("ones", [128, 128], BF16)
        kb.op("pool", lambda: nc.gpsimd.memset(ones[:], 1.0 / D), writes=[ctk])
        vec = kb.sb("vec", [128, 16])
        vtk = kb.tok("vec")
        kb.dma("sp", vec[:], vecs[:, :], vtk, writes=[vtk])
        stage = [kb.sb("stg%d" % i, [128, D]) for i in range(2)]
        stok = kb.toks(2, "stg")
        Wo, Wo_t = load_weight_bf16(kb, "sp", w_out, 8, D, "Wo", stage, stok)
        Wg, Wg_t = load_weight_bf16(kb, "sp", w_gate, 8, D, "Wg", stage, stok)
        Wp, Wp_t = load_weight_bf16(kb, "sp", w_proj, 2, D, "Wp", stage, stok)
        zin = [kb.sb("zin%d" % i, [128, 8, TT], BF16) for i in range(2)]
        hin = [kb.sb("hin%d" % i, [128, 8, TT]) for i in range(2)]
        pin = [kb.sb("pin%d" % i, [128, 2, TT]) for i in range(2)]
        pbf = [kb.sb("pbf%d" % i, [128, 2, TT], BF16) for i in range(2)]
        z_t, h_t, p_t, pb_t = kb.toks(2, "z"), kb.toks(2, "h"), kb.toks(2, "p"), kb.toks(2, "pb")
        sq = kb.sb("sq", [128, 8, TT], BF16)
        sq_t = kb.tok("sq")
        hn = kb.sb("hn", [128, 8, TT], BF16)
        hn_t = kb.tok("hn")
        rstd = kb.sb("rstd", [128, TT])
        rstd_t = kb.tok("rstd")
        gsb = [kb.sb("gsb%d" % i, [128, TT]) for i in range(2)]
        g_t = kb.toks(2, "g")
        tmp = [kb.sb("tmp%d" % i, [128, TT]) for i in range(2)]
        tmp_t = kb.toks(2, "tmp")
        obuf = [kb.sb("obuf%d" % i, [128, 8, TT], F32 if last else BF16) for i in range(2)]
        ob_t = kb.toks(2, "ob")
        psb = [kb.ps("psb%d" % i, [128, TT]) for i in range(8)]
        ps_t = [kb.ptok("ps%d" % i) for i in range(8)]
        pc = [0]

        def nextps():
            i = pc[0] % 8
            pc[0] += 1
            return psb[i], ps_t[i]

        zv = zT.rearrange("(c p) t -> p c t", p=128)
        hv = hT.rearrange("(c p) t -> p c t", p=128)
        pv = pT.rearrange("(c p) t -> p c t", p=128)

        def norm(src, src_t, gcol0, dst_fn, dst_toks):
            for dc in range(8):
                kb.op("act", lambda dc=dc: nc.scalar.activation(out=sq[:, dc, :], in_=src[:, dc, :], func=AF.Square),
                      reads=[src_t], writes=[sq_t])
            pb_, pt_ = nextps()
            for dc in range(8):
                kb.op("pe", lambda dc=dc: nc.tensor.matmul(pb_[:], lhsT=ones[:], rhs=sq[:, dc, :], start=(dc == 0), stop=(dc == 7)),
                      reads=[sq_t, ctk], writes=[pt_])
            rstd_from_sumsq(kb, pb_[:], pt_, rstd[:], rstd_t, 1e-6)
            for dc in range(8):
                kb.op("dve", lambda dc=dc: nc.vector.scalar_tensor_tensor(
                    out=dst_fn(dc), in0=src[:, dc, :], scalar=vec[:, gcol0 + dc:gcol0 + dc + 1], in1=rstd[:],
                    op0=ALU.mult, op1=ALU.mult), reads=[src_t, rstd_t, vtk], writes=dst_toks)

        def loads(j):
            s = j % 2
            cs = slice(j * TT, (j + 1) * TT)
            kb.dma("sp", zin[s][:], zv[:, :, cs], z_t[s], writes=[z_t[s]])
            kb.dma("sp", hin[s][:], hv[:, :, cs], h_t[s], writes=[h_t[s]])
            kb.dma("sp", pin[s][:], pv[:, :, cs], p_t[s], writes=[p_t[s]])

        loads(0)
        for j in range(ntile):
            s = j % 2
            cs = slice(j * TT, (j + 1) * TT)
            if j + 1 < ntile:
                loads(j + 1)
            kb.op("pool", lambda s=s: nc.gpsimd.tensor_copy(out=pbf[s][:], in_=pin[s][:]), reads=[p_t[s]], writes=[pb_t[s]])
            h2 = hin[s]
            for dc in range(8):
                pb_, pt_ = nextps()
                for kc in range(8):
                    kb.op("pe", lambda dc=dc, kc=kc, pb_=pb_: nc.tensor.matmul(
                        pb_[:], lhsT=Wo[:, kc, dc * 128:(dc + 1) * 128], rhs=zin[s][:, kc, :], start=(kc == 0), stop=(kc == 7)),
                        reads=[Wo_t, z_t[s]], writes=[pt_])
                kb.op("dve", lambda dc=dc, pb_=pb_: nc.vector.tensor_tensor(out=h2[:, dc, :], in0=h2[:, dc, :], in1=pb_[:], op=ALU.add),
                      reads=[pt_, h_t[s]], writes=[h_t[s]])
            norm(h2, h_t[s], 0, lambda dc: hn[:, dc, :], [hn_t])
            for dc in range(8):
                pg, pgt = nextps()
                for kc in range(8):
                    kb.op("pe", lambda dc=dc, kc=kc, pg=pg: nc.tensor.matmul(
                        pg[:], lhsT=Wg[:, kc, dc * 128:(dc + 1) * 128], rhs=hn[:, kc, :], start=(kc == 0), stop=(kc == 7)),
                        reads=[Wg_t, hn_t], writes=[pgt])
                pp, ppt = nextps()
                for kc in range(2):
                    kb.op("pe", lambda dc=dc, kc=kc, pp=pp: nc.tensor.matmul(
                        pp[:], lhsT=Wp[:, kc, dc * 128:(dc + 1) * 128], rhs=pbf[s][:, kc, :], start=(kc == 0), stop=(kc == 1)),
                        reads=[Wp_t, pb_t[s]], writes=[ppt])
                b = dc % 2
                kb.op("act", lambda pg=pg, b=b: nc.scalar.activation(out=gsb[b][:], in_=pg[:], func=AF.Sigmoid),
                      reads=[pgt], writes=[g_t[b]])
                kb.op("dve", lambda pp=pp, b=b: nc.vector.tensor_tensor(out=tmp[b][:], in0=gsb[b][:], in1=pp[:], op=ALU.mult),
                      reads=[g_t[b], ppt], writes=[tmp_t[b]])
                kb.op("pool", lambda dc=dc, b=b: nc.gpsimd.tensor_tensor(out=h2[:, dc, :], in0=h2[:, dc, :], in1=tmp[b][:], op=ALU.add),
                      reads=[tmp_t[b], h_t[s]], writes=[h_t[s]])
            norm(h2, h_t[s], 8, lambda dc: obuf[s][:, dc, :], [ob_t[s]])
            if last:
                kb.dma("sp", oT.rearrange("(c p) t -> p c t", p=128)[:, :, cs], obuf[s][:], ob_t[s], reads=[ob_t[s]])
            else:
                kb.dma("sp", hnT.rearrange("(c p) t -> p c t", p=128)[:, :, cs], obuf[s][:], ob_t[s], reads=[ob_t[s]])
                kb.dma("sp", h1T.rearrange("(c p) t -> p c t", p=128)[:, :, cs], h2[:], h_t[s], reads=[h_t[s]])
        kb.finish(ob_t + h_t)
    return nc


def run_post(zT_list, hT_list, pT_list, w_out, w_gate, w_proj, pe_norm, nxt_norm, last):
    n = len(zT_list)
    NT = zT_list[0].shape[1]
    nc = build_post(NT, last)
    vecs = np.concatenate([pe_norm.reshape(8, 128).T, nxt_norm.reshape(8, 128).T], axis=1).astype(np.float32)
    maps = [dict(zT=np.ascontiguousarray(zT_list[i]), hT=np.ascontiguousarray(hT_list[i]), pT=np.ascontiguousarray(pT_list[i]),
                 w_out=w_out, w_gate=w_gate, w_proj=w_proj, vecs=np.ascontiguousarray(vecs)) for i in range(n)]
    res = run_bass_kernel_spmd(nc, maps, core_ids=list(range(n))).results
    return res


TA = 256
NCH = TA // 64
GN_EPS = 64e-5
FILL = 6
RW_STOP = 0
RW_DBG = None


def build_rwkv(NT):
    nc = new_nc()
    xT = dram_in(nc, "xT", [D, NT])
    Wc = dram_in(nc, "Wc", [D, 4 * 256])
    w1 = dram_in(nc, "w1", [D, 64])
    a1 = dram_in(nc, "a1", [D, 64])
    w2c = dram_in(nc, "w2c", [64, 256])
    a2c = dram_in(nc, "a2c", [64, 256])
    vecA = dram_in(nc, "vecA", [128, 56])
    vecC = dram_in(nc, "vecC", [128, 2 * 8])
    lnwb = dram_in(nc, "lnwb", [1, 512])
    identd = dram_in(nc, "ident", [128, 128])
    zT = dram_out(nc, "zT", [256, NT], BF16)
    dbg_d = dram_out(nc, "dbg", [128, 2048]) if RW_DBG else None
    dbg_tok = []

    def dbg(name, ap, tok, j=0):
        if RW_DBG == name and j == 0 and not dbg_tok:
            t = kb.tok("dbgt")
            dbg_tok.append(t)
            p, f = ap.shape[0], int(np.prod(ap.shape[1:]))
            kb.dma("sp", dbg_d[0:p, 0:f], ap, t, reads=[tok])
    ntile = NT // TA
    with ExitStack() as es:
        kb = KB(nc, es)
        V, S, G, P = nc.vector, nc.scalar, nc.gpsimd, nc.tensor
        ctk = kb.tok("cst")
        ident = kb.sb("ident_sb", [128, 128])
        kb.dma("sp", ident[:], identd[:, :], ctk, writes=[ctk])
        onesb = kb.sb("onesb", [128, 128], BF16)
        blk = kb.sb("blk", [128, 128], BF16)
        bind = kb.sb("bind", [128, 2])
        kb.op("pool", lambda: G.memset(onesb[:], 1.0 / D), writes=[ctk])
        kb.op("pool", lambda: G.memset(blk[:], 0.0), writes=[ctk])
        kb.op("pool", lambda: G.memset(blk[0:64, 0:64], 1.0), writes=[ctk])
        kb.op("pool", lambda: G.memset(blk[64:128, 64:128], 1.0), writes=[ctk])
        kb.op("pool", lambda: G.memset(bind[:], 0.0), writes=[ctk])
        kb.op("pool", lambda: G.memset(bind[0:64, 0:1], 1.0), writes=[ctk])
        kb.op("pool", lambda: G.memset(bind[64:128, 1:2], 1.0), writes=[ctk])
        mAT = kb.sb("mAT", [64, 4, 4, 64])
        mL = kb.sb("mL", [64, 4, 64])
        kb.op("pool", lambda: G.memset(mAT[:], 1.0), writes=[ctk])
        kb.op("pool", lambda: G.memset(mL[:], 1.0), writes=[ctk])
        for q in range(4):
            kb.op("pool", lambda q=q: G.affine_select(out=mAT[:, :, q, :], in_=mAT[:, :, q, :], pattern=[[0, 4], [1, 64]],
                                                      compare_op=(ALU.is_gt if q % 2 == 0 else ALU.is_ge), fill=0.0, base=0,
                                                      channel_multiplier=-1), writes=[ctk])
        kb.op("pool", lambda: G.affine_select(out=mL[:], in_=mL[:], pattern=[[0, 4], [-1, 64]], compare_op=ALU.is_gt, fill=0.0,
                                              base=0, channel_multiplier=1), writes=[ctk])
        vA = kb.sb("vA", [128, 56])
        vC = kb.sb("vC", [128, 2, 8])
        kb.dma("sp", vA[:], vecA[:, :], ctk, writes=[ctk])
        kb.dma("sp", vC[:].rearrange("p a b -> p (a b)"), vecC[:, :], ctk, writes=[ctk])
        lnw = kb.sb("lnw", [64, 512])
        kb.dma("sp", lnw[:], lnwb.partition_broadcast(64), ctk, writes=[ctk])
        kb.op("dve", lambda: V.tensor_scalar(out=vC[:, :, 5:6], in0=vC[:, :, 0:1], scalar1=-1.0, scalar2=None, op0=ALU.mult),
              reads=[ctk], writes=[ctk])
        kb.op("dve", lambda: V.tensor_scalar(out=vC[:, :, 6:7], in0=vC[:, :, 3:4], scalar1=-1.0, scalar2=1.0, op0=ALU.mult, op1=ALU.add),
              reads=[ctk], writes=[ctk])
        W0, A0, KK_, KA, RK, NW0, OMKA = range(7)

        xin = kb.sb("xin", [128, 8, TA])
        x_t = kb.tok("xin")
        xin_flat = xin[:].rearrange("p a b -> p (a b)")
        stage = [xin_flat[:, i * 1024:(i + 1) * 1024] for i in range(2)]
        stok = [x_t, x_t]
        omu = kb.sb("omu", [128, 48])
        kb.op("dve", lambda: V.tensor_scalar(out=omu[:], in0=vA[:, 8:56], scalar1=-1.0, scalar2=1.0, op0=ALU.mult, op1=ALU.add), reads=[ctk], writes=[ctk])
        W = kb.sb("Wm", [128, 8, 4, 2, 256], BF16)
        W_t = kb.tok("Wm")
        for dc in range(8):
            sgi = dc % 2
            kb.dma("sp", stage[sgi][:, 0:1024], Wc[dc * 128:(dc + 1) * 128, :], x_t, writes=[x_t])
            for i in range(4):
                kb.op("dve", lambda dc=dc, i=i, sgi=sgi: V.tensor_scalar(out=W[:, dc, i, 0, :], in0=stage[sgi][:, i * 256:(i + 1) * 256],
                                                                         scalar1=omu[:, i * 8 + dc:i * 8 + dc + 1], scalar2=None, op0=ALU.mult),
                      reads=[x_t, ctk], writes=[W_t])
                kb.op("dve", lambda dc=dc, i=i, sgi=sgi: V.tensor_scalar(out=W[:, dc, i, 1, :], in0=stage[sgi][:, i * 256:(i + 1) * 256],
                                                                         scalar1=vA[:, 8 + i * 8 + dc:9 + i * 8 + dc], scalar2=None, op0=ALU.mult),
                      reads=[x_t, ctk], writes=[W_t])
        W1 = kb.sb("W1", [128, 8, 2, 64], BF16)
        A1 = kb.sb("A1", [128, 8, 2, 64], BF16)
        W2 = kb.sb("W2", [64, 256], BF16)
        A2 = kb.sb("A2", [64, 256], BF16)
        for (src_d, dstw, i) in ((w1, W1, 4), (a1, A1, 5)):
            kb.dma("sp", stage[0][:, 0:512].rearrange("p (c k) -> p c k", c=8), src_d.rearrange("(c p) k -> p c k", p=128), x_t, writes=[x_t])
            for dc in range(8):
                kb.op("dve", lambda dc=dc, i=i, dstw=dstw: V.tensor_scalar(out=dstw[:, dc, 0, :], in0=stage[0][:, dc * 64:(dc + 1) * 64],
                                                                           scalar1=omu[:, i * 8 + dc:i * 8 + dc + 1], scalar2=None, op0=ALU.mult),
                      reads=[x_t, ctk], writes=[W_t])
                kb.op("dve", lambda dc=dc, i=i, dstw=dstw: V.tensor_scalar(out=dstw[:, dc, 1, :], in0=stage[0][:, dc * 64:(dc + 1) * 64],
                                                                           scalar1=vA[:, 8 + i * 8 + dc:9 + i * 8 + dc], scalar2=None, op0=ALU.mult),
                      reads=[x_t, ctk], writes=[W_t])
        kb.dma("sp", stage[0][0:64, 0:256], w2c[:, :], x_t, writes=[x_t])
        kb.op("dve", lambda: V.tensor_copy(out=W2[:], in_=stage[0][0:64, 0:256]), reads=[x_t], writes=[W_t])
        kb.dma("sp", stage[0][0:64, 0:256], a2c[:, :], x_t, writes=[x_t])
        kb.op("dve", lambda: V.tensor_copy(out=A2[:], in_=stage[0][0:64, 0:256]), reads=[x_t], writes=[W_t])

        hn = kb.sb("hn", [128, 8, TA + 1], BF16)
        hn_t = kb.tok("hn")
        kb.op("pool", lambda: G.memset(hn[:], 0.0), writes=[hn_t])
        sqx = kb.sb("sqx", [128, 8, TA], BF16)
        sqx_t = kb.tok("sqx")
        rstd = kb.sb("rstd", [128, TA])
        rstd_t = kb.tok("rstd")
        xm = [kb.sb("xm%d" % i, [128, 8, TA], BF16) for i in range(2)]
        xm_t = kb.toks(2, "xm")
        th = kb.sb("th", [64, TA], BF16)
        th_t = kb.tok("th")

        def cm(name):
            return kb.sb(name, [128, 2, TA]), kb.tok(name)
        r_, r_t = cm("r_")
        k_, k_t = cm("k_")
        v_, v_t = cm("v_")
        sg2 = [kb.sb("sg%d" % i, [128, 2, TA]) for i in range(2)]
        sg2_t = kb.toks(2, "sg")
        nlw, nlw_t = cm("nlw")
        a_, a_t = cm("a_")
        kk, kk_t = cm("kk")
        t1, t1_t = cm("t1")
        prod, prod_t = cm("prod")
        CA, CA_t = cm("CA")
        CB, CB_t = cm("CB")
        rn, rn_t = cm("rn")
        sqk = kb.sb("sqk", [128, 2, TA], BF16)
        sqk_t = kb.tok("sqk")
        AR = kb.sb("AR", [128, 2, NCH, 2, 64])
        AR_t = kb.tok("AR")
        BK = kb.sb("BK", [128, 2, NCH, 2, 64])
        BK_t = kb.tok("BK")
        def sb64(name, rest, dt, tok):
            full = kb.sb(name, [128] + list(rest), dt)
            kb.op("pool", lambda: G.memset(full[64:128], 0.0), writes=[tok])
            return full, full[0:64]
        ARh2_t = kb.toks(2, "ARh")
        BKh2_t = kb.toks(2, "BKh")
        ARh2F, ARh2 = zip(*[sb64("ARh%d" % i, [4, NCH, 2, 64], BF16, ARh2_t[i]) for i in range(2)])
        BKh2F, BKh2 = zip(*[sb64("BKh%d" % i, [4, NCH, 2, 64], BF16, BKh2_t[i]) for i in range(2)])
        WCs2 = [kb.sb("WCs%d" % i, [64, 4, NCH]) for i in range(2)]
        WCs2_t = kb.toks(2, "WCs")
        TM2_t = [kb.toks(4, "TM") for pp in range(2)]
        _tm = [[sb64("TM%d_%d" % (pp, i), [NCH, 4, 64], BF16, TM2_t[pp][i]) for i in range(4)] for pp in range(2)]
        TM2F = [[x[0] for x in row] for row in _tm]
        TM2 = [[x[1] for x in row] for row in _tm]
        TA_, TB_, TK_, TV_ = range(4)
        RKs2 = [kb.sb("RKs%d" % i, [64, NCH, 4]) for i in range(2)]
        RKs2_t = kb.toks(2, "RKs")
        Ybuf2 = [kb.sb("Ybuf%d" % i, [64, NCH, 4, 64]) for i in range(2)]
        Y2_t = kb.toks(2, "Ybuf")
        ATsk_t = kb.toks(2, "ATs")
        Lsk_t = kb.toks(2, "Ls")
        ATskF, ATsk = zip(*[sb64("ATs%d" % i, [4, 4, 64], BF16, ATsk_t[i]) for i in range(2)])
        LskF, Lsk = zip(*[sb64("Ls%d" % i, [4, 64], BF16, Lsk_t[i]) for i in range(2)])
        Z = [kb.sb("Z%d" % i, [64, 4, 128]) for i in range(2)]
        Z_t = kb.toks(2, "Z")
        Zbk_t = [kb.toks(2, "Zb") for kk_ in range(2)]
        _zb = [[sb64("Zb%d_%d" % (kk_, i), [4, 128], BF16, Zbk_t[kk_][i]) for i in range(2)] for kk_ in range(2)]
        ZbkF = [[x[0] for x in row] for row in _zb]
        Zbk = [[x[1] for x in row] for row in _zb]
        identbF = kb.sb("identb", [128, 64], BF16)
        kb.op("dve", lambda: V.tensor_copy(out=identbF[:], in_=ident[:, 0:64]), reads=[ctk], writes=[ctk])
        LBk_t = [kb.toks(2, "LB") for kk_ in range(2)]
        _lb = [[sb64("LB%d_%d" % (kk_, i), [4, 2, 64], BF16, LBk_t[kk_][i]) for i in range(2)] for kk_ in range(2)]
        LBkF = [[x[0] for x in row] for row in _lb]
        LBk = [[x[1] for x in row] for row in _lb]
        RHsk_t = kb.toks(2, "RHs")
        MTsk_t = kb.toks(2, "MTs")
        RHskF, RHsk = zip(*[sb64("RHs%d" % i, [4, 64], F32, RHsk_t[i]) for i in range(2)])
        MTskF, MTsk = zip(*[sb64("MTs%d" % i, [4, 64], F32, MTsk_t[i]) for i in range(2)])
        Hs_t = kb.toks(2, "Hs")
        HsF, Hs = zip(*[sb64("Hs%d" % i, [4, 64], F32, Hs_t[i]) for i in range(2)])
        kb.op("pool", lambda: G.memset(Hs[0][:], 0.0), writes=[Hs_t[0]])
        st1 = kb.sb("st1", [64, NCH * 4])
        st2 = kb.sb("st2", [64, NCH * 4])
        st3 = kb.sb("st3", [64, NCH * 4])
        st_t = kb.tok("st")
        zout = kb.sb("zout", [128, 2, TA], BF16)
        zo_t = kb.tok("zout")

        PR = kb.ps("PR", [128, 512]); PR_t = kb.ptok("PR")
        TIN = kb.ps("TIN", [128, 512]); TIN_t = kb.ptok("TIN")
        ATpk = [kb.ps("ATp%d" % i, [128, 512]) for i in range(2)]; ATpk_t = [kb.ptok("ATp%d" % i) for i in range(2)]
        APpk = [kb.ps("APp%d" % i, [128, 512]) for i in range(2)]; APpk_t = [kb.ptok("APp%d" % i) for i in range(2)]
        SQpk = [kb.ps("SQp%d" % i, [128, 512]) for i in range(2)]; SQpk_t = [kb.ptok("SQp%d" % i) for i in range(2)]

        xv = xT.rearrange("(c p) t -> p c t", p=128)
        hstate = [0]

        def prep(j):
            p = j % 2
            cs = slice(j * TA, (j + 1) * TA)
            ARh, ARh_t, BKh, BKh_t = ARh2[p], ARh2_t[p], BKh2[p], BKh2_t[p]
            TM, TM_t = TM2[p], TM2_t[p]
            WCs, WCs_t, RKs, RKs_t = WCs2[p], WCs2_t[p], RKs2[p], RKs2_t[p]
            sg, sg_t = sg2[p], sg2_t[p]
            Ybuf, Y_t = Ybuf2[p], Y2_t[p]
            kb.dma("sp", xin[:], xv[:, :, cs], x_t, writes=[x_t])
            if j > 0:
                kb.op("pool", lambda: G.tensor_copy(out=hn[:, :, 0:1], in_=hn[:, :, TA:TA + 1]), reads=[hn_t], writes=[hn_t])
            for dc in range(8):
                kb.op("act", lambda dc=dc: S.activation(out=sqx[:, dc, :], in_=xin[:, dc, :], func=AF.Square), reads=[x_t], writes=[sqx_t])
            for dc in range(8):
                kb.op("pe", lambda dc=dc: P.matmul(PR[:, 0:TA], lhsT=onesb[:], rhs=sqx[:, dc, :], start=(dc == 0), stop=(dc == 7)),
                      reads=[sqx_t, ctk], writes=[PR_t])
            kb.op("act", lambda: S.activation(out=rstd[:], in_=PR[:, 0:TA], func=AF.Ln, bias=1e-6, scale=1.0), reads=[PR_t], writes=[rstd_t])
            kb.op("act", lambda: S.activation(out=rstd[:], in_=rstd[:], func=AF.Exp, scale=-0.5), reads=[rstd_t], writes=[rstd_t])
            for dc in range(8):
                kb.op("dve", lambda dc=dc: V.scalar_tensor_tensor(out=hn[:, dc, 1:TA + 1], in0=xin[:, dc, :], scalar=vA[:, dc:dc + 1],
                                                                 in1=rstd[:], op0=ALU.mult, op1=ALU.mult),
                      reads=[x_t, rstd_t, ctk], writes=[hn_t])
            for i in range(6):
                if i < 4:
                    dst, dst_t = ((r_, r_t), (k_, k_t), (v_, v_t), (sg, sg_t))[i]
                    for hc in range(2):
                        for dc in range(8):
                            kb.op("pe", lambda dc=dc, hc=hc, i=i: P.matmul(
                                PR[:, 0:TA], lhsT=W[:, dc, i, 0, hc * 128:hc * 128 + 128], rhs=hn[:, dc, 1:TA + 1],
                                start=(dc == 0), stop=False), reads=[W_t, hn_t], writes=[PR_t])
                            kb.op("pe", lambda dc=dc, hc=hc, i=i: P.matmul(
                                PR[:, 0:TA], lhsT=W[:, dc, i, 1, hc * 128:hc * 128 + 128], rhs=hn[:, dc, 0:TA],
                                start=False, stop=(dc == 7)), reads=[W_t, hn_t], writes=[PR_t])
                        if i == 3:
                            kb.op("act", lambda hc=hc, dst=dst: S.activation(out=dst[:, hc, :], in_=PR[:, 0:TA], func=AF.Silu),
                                  reads=[PR_t], writes=[dst_t])
                        else:
                            kb.op("dve", lambda hc=hc, dst=dst: V.tensor_copy(out=dst[:, hc, :], in_=PR[:, 0:TA]),
                                  reads=[PR_t], writes=[dst_t])
                else:
                    Wl, W2l = (W1, W2) if i == 4 else (A1, A2)
                    for dc in range(8):
                        kb.op("pe", lambda dc=dc, Wl=Wl: P.matmul(PR[0:64, 0:TA], lhsT=Wl[:, dc, 0, :], rhs=hn[:, dc, 1:TA + 1],
                                                                 start=(dc == 0), stop=False), reads=[W_t, hn_t], writes=[PR_t])
                        kb.op("pe", lambda dc=dc, Wl=Wl: P.matmul(PR[0:64, 0:TA], lhsT=Wl[:, dc, 1, :], rhs=hn[:, dc, 0:TA],
                                                                 start=False, stop=(dc == 7)), reads=[W_t, hn_t], writes=[PR_t])
                    kb.op("act", lambda i=i: S.activation(out=th[:], in_=PR[0:64, 0:TA], func=(AF.Tanh if i == 4 else AF.Copy)),
                          reads=[PR_t], writes=[th_t])
                    for hc in range(2):
                        kb.op("pe", lambda hc=hc, W2l=W2l: P.matmul(PR[:, 0:TA], lhsT=W2l[:, hc * 128:(hc + 1) * 128], rhs=th[:],
                                                                    start=True, stop=True), reads=[W_t, th_t], writes=[PR_t])
                        if i == 4:
                            kb.op("act", lambda hc=hc: S.activation(out=nlw[:, hc, :], in_=PR[:, 0:TA], func=AF.Exp, scale=-1.0,
                                                                    bias=vC[:, hc, NW0:NW0 + 1]), reads=[PR_t, ctk], writes=[nlw_t])
                            kb.op("act", lambda hc=hc: S.activation(out=nlw[:, hc, :], in_=nlw[:, hc, :], func=AF.Ln, scale=1.0, bias=1.0),
                                  reads=[nlw_t], writes=[nlw_t])
                            kb.op("act", lambda hc=hc: S.activation(out=nlw[:, hc, :], in_=nlw[:, hc, :], func=AF.Exp, scale=-1.0, bias=-0.5),
                                  reads=[nlw_t], writes=[nlw_t])
                        else:
                            kb.op("act", lambda hc=hc: S.activation(out=a_[:, hc, :], in_=PR[:, 0:TA], func=AF.Sigmoid, scale=1.0,
                                                                    bias=vC[:, hc, A0:A0 + 1]), reads=[PR_t, ctk], writes=[a_t])
            for hc in range(2):
                kb.op("dve", lambda hc=hc: V.tensor_scalar(out=kk[:, hc, :], in0=k_[:, hc, :], scalar1=vC[:, hc, KK_:KK_ + 1], scalar2=None,
                                                           op0=ALU.mult), reads=[k_t, ctk], writes=[kk_t])
                kb.op("act", lambda hc=hc: S.activation(out=sqk[:, hc, :], in_=kk[:, hc, :], func=AF.Square), reads=[kk_t], writes=[sqk_t])
                kb.op("pe", lambda hc=hc: P.matmul(PR[:, 0:TA], lhsT=blk[:], rhs=sqk[:, hc, :], start=True, stop=True),
                      reads=[sqk_t, ctk], writes=[PR_t])
                kb.op("act", lambda hc=hc: S.activation(out=rn[:, hc, :], in_=PR[:, 0:TA], func=AF.Ln, bias=1e-24, scale=1.0),
                      reads=[PR_t], writes=[rn_t])
                kb.op("act", lambda hc=hc: S.activation(out=rn[:, hc, :], in_=rn[:, hc, :], func=AF.Exp, scale=-0.5), reads=[rn_t], writes=[rn_t])
            kb.op("dve", lambda: V.tensor_tensor(out=kk[:], in0=kk[:], in1=rn[:], op=ALU.mult), reads=[kk_t, rn_t], writes=[kk_t])
            for hc in range(2):
                kb.op("dve", lambda hc=hc: V.tensor_scalar(out=t1[:, hc, :], in0=a_[:, hc, :], scalar1=vC[:, hc, KA:KA + 1],
                                                           scalar2=vC[:, hc, OMKA:OMKA + 1], op0=ALU.mult, op1=ALU.add),
                      reads=[a_t, ctk], writes=[t1_t])
            kb.op("dve", lambda: V.tensor_tensor(out=t1[:], in0=t1[:], in1=k_[:], op=ALU.mult), reads=[t1_t, k_t], writes=[t1_t])
            kb.op("pool", lambda: G.tensor_tensor(out=a_[:], in0=a_[:], in1=kk[:], op=ALU.mult), reads=[a_t, kk_t], writes=[a_t])
            for hc in range(2):
                kb.op("dve", lambda hc=hc: V.scalar_tensor_tensor(out=prod[:, hc, :], in0=r_[:, hc, :], scalar=vC[:, hc, RK:RK + 1],
                                                                 in1=t1[:, hc, :], op0=ALU.mult, op1=ALU.mult),
                      reads=[r_t, t1_t, ctk], writes=[prod_t])
            def v3(t):
                return t[:].rearrange("p a (c t) -> p (a c) t", t=64)
            src, src_t = nlw, nlw_t
            pp = [(CA, CA_t), (CB, CB_t)]
            for li, sft in enumerate((1, 2, 4, 8, 16, 32)):
                dst, dst_t = pp[li % 2]
                kb.op("pool", lambda src=src, dst=dst, sft=sft: G.tensor_tensor(out=v3(dst)[:, :, sft:], in0=v3(src)[:, :, sft:],
                                                                               in1=v3(src)[:, :, 0:64 - sft], op=ALU.add),
                      reads=[src_t], writes=[dst_t])
                kb.op("pool", lambda src=src, dst=dst, sft=sft: G.tensor_copy(out=v3(dst)[:, :, 0:sft], in_=v3(src)[:, :, 0:sft]),
                      reads=[src_t], writes=[dst_t])
                src, src_t = dst, dst_t
            cn, cn_t = src, src_t
            assert cn is CB
            kb.op("pool", lambda: G.tensor_tensor(out=nlw[:], in0=cn[:], in1=nlw[:], op=ALU.subtract), reads=[cn_t, nlw_t], writes=[nlw_t])
            kb.op("act", lambda: S.activation(out=CA[:], in_=cn[:], func=AF.Exp, scale=-1.0), reads=[cn_t], writes=[CA_t])
            kb.op("act", lambda: S.activation(out=nlw[:], in_=nlw[:], func=AF.Exp, scale=-1.0), reads=[nlw_t], writes=[nlw_t])
            kb.op("act", lambda: S.activation(out=CB[:], in_=cn[:], func=AF.Exp, scale=1.0), reads=[cn_t], writes=[CB_t])
            eneg, eneg_t, enegx, enegx_t, epos, epos_t = CA, CA_t, nlw, nlw_t, CB, CB_t

            def c3(t, hc):
                return t[:, hc, :].rearrange("p (c t) -> p c t", t=64)
            for hc in range(2):
                kb.op("dve", lambda hc=hc: V.tensor_tensor(out=AR[:, hc, :, 0, :], in0=c3(kk, hc), in1=c3(enegx, hc), op=ALU.mult),
                      reads=[kk_t, enegx_t], writes=[AR_t])
                kb.op("pool", lambda hc=hc: G.tensor_tensor(out=AR[:, hc, :, 1, :], in0=c3(r_, hc), in1=c3(eneg, hc), op=ALU.mult),
                      reads=[r_t, eneg_t], writes=[AR_t])
                kb.op("dve", lambda hc=hc: V.scalar_tensor_tensor(out=BK[:, hc, :, 0, :], in0=c3(a_, hc), scalar=-1.0, in1=c3(epos, hc),
                                                                 op0=ALU.mult, op1=ALU.mult), reads=[a_t, epos_t], writes=[BK_t])
                kb.op("pool", lambda hc=hc: G.tensor_tensor(out=BK[:, hc, :, 1, :], in0=c3(t1, hc), in1=c3(epos, hc), op=ALU.mult),
                      reads=[t1_t, epos_t], writes=[BK_t])
                for hh in range(2):
                    h = 2 * hc + hh
                    kb.op("dve", lambda hc=hc, hh=hh, h=h: V.tensor_copy(out=WCs[:, h, :], in_=c3(eneg, hc)[hh * 64:(hh + 1) * 64, :, 63]),
                          reads=[eneg_t], writes=[WCs_t])
                    kb.op("dve", lambda hc=hc, hh=hh, h=h: V.tensor_copy(out=ARh[:, h, :, :, :].rearrange("p c a t -> p (c a t)"),
                                                                         in_=AR[hh * 64:(hh + 1) * 64, hc, :, :, :].rearrange("p c a t -> p (c a t)")),
                          reads=[AR_t], writes=[ARh_t])
                    kb.op("dve", lambda hc=hc, hh=hh, h=h: V.tensor_copy(out=BKh[:, h, :, :, :].rearrange("p c a t -> p (c a t)"),
                                                                         in_=BK[hh * 64:(hh + 1) * 64, hc, :, :, :].rearrange("p c a t -> p (c a t)")),
                          reads=[BK_t], writes=[BKh_t])
            srcs = [(lambda hc, c: AR[:, hc, c, 0, :], AR_t), (lambda hc, c: BK[:, hc, c, 0, :], BK_t),
                    (lambda hc, c: BK[:, hc, c, 1, :], BK_t), (lambda hc, c: v_[:, hc, c * 64:(c + 1) * 64], v_t)]
            ne = 0
            for q in range(4):
                fn, ft = srcs[q]
                for c0 in range(0, NCH, 2):
                    for cc in range(2):
                        for hc in range(2):
                            kb.op("pe", lambda fn=fn, c0=c0, cc=cc, hc=hc: P.transpose(
                                TIN[0:64, cc * 256 + hc * 128:cc * 256 + hc * 128 + 128], fn(hc, c0 + cc), ident[:]),
                                reads=[ft, ctk], writes=[TIN_t])
                    e = "act" if ne % 2 == 0 else "dve"
                    ne += 1
                    dstv = TM[q][:, c0:c0 + 2, :, :].rearrange("p c h n -> p (c h n)")
                    if e == "act":
                        kb.op("act", lambda dstv=dstv: S.copy(out=dstv, in_=TIN[0:64, :]), reads=[TIN_t], writes=[TM_t[q]])
                    else:
                        kb.op("dve", lambda dstv=dstv: V.tensor_copy(out=dstv, in_=TIN[0:64, :]), reads=[TIN_t], writes=[TM_t[q]])
            for c in range(NCH):
                for hc in range(2):
                    kb.op("pe", lambda c=c, hc=hc: P.matmul(TIN[0:64, c * 4 + hc * 2:c * 4 + hc * 2 + 2], lhsT=prod[:, hc, c * 64:(c + 1) * 64],
                                                            rhs=bind[:], start=True, stop=True), reads=[prod_t, ctk], writes=[TIN_t])
            kb.op("dve", lambda: V.tensor_copy(out=RKs[:].rearrange("p c h -> p (c h)"), in_=TIN[0:64, 0:NCH * 4]), reads=[TIN_t], writes=[RKs_t])

        hseq = [0]

        def chunk_gen(j, c, k):
            p = j % 2
            ARh, ARh_t, BKh, BKh_t = ARh2[p], ARh2_t[p], BKh2[p], BKh2_t[p]
            TM, TM_t = TM2[p], TM2_t[p]
            WCs, WCs_t = WCs2[p], WCs2_t[p]
            Ybuf, Y_t = Ybuf2[p], Y2_t[p]
            ATp, ATp_t, APp, APp_t, SQp, SQp_t = ATpk[k], ATpk_t[k], APpk[k], APpk_t[k], SQpk[k], SQpk_t[k]
            ATs, ATs_t, Ls, Ls_t = ATsk[k], ATsk_t[k], Lsk[k], Lsk_t[k]
            Zb, Zb_t, LB, LB_t = Zbk[k], Zbk_t[k], LBk[k], LBk_t[k]
            RHs, RHs_t, MTs, MTs_t = RHsk[k], RHsk_t[k], MTsk[k], MTsk_t[k]
            ARhF, BKhF, TMF, ATsF, LsF, ZbF, LBF, RHsF, MTsF = ARh2F[p], BKh2F[p], TM2F[p], ATskF[k], LskF[k], ZbkF[k], LBkF[k], RHskF[k], MTskF[k]
            def opnd(h):
                return h // 2, 64 * (h % 2)
            for half in range(2):
                for hl in range(2):
                    h = 2 * half + hl
                    hc, pb = opnd(h)
                    rhs = ARhF[:, h, c, :, :].rearrange("p a t -> p (a t)")
                    kb.op("pe", lambda hl=hl, h=h, rhs=rhs: P.matmul(ATp[0:64, hl * 256:hl * 256 + 128], lhsT=BKhF[:, h, c, 0, :],
                                                                             rhs=rhs, start=True, stop=True), reads=[ARh_t, BKh_t], writes=[ATp_t])
                    kb.op("pe", lambda hl=hl, h=h, rhs=rhs: P.matmul(ATp[0:64, hl * 256 + 128:hl * 256 + 256], lhsT=BKhF[:, h, c, 1, :],
                                                                             rhs=rhs, start=True, stop=True), reads=[ARh_t, BKh_t], writes=[ATp_t])
                kb.op("dve", lambda half=half: V.tensor_tensor(
                    out=ATs[:, 2 * half:2 * half + 2, :, :].rearrange("p h q t -> p (h q t)"), in0=ATp[0:64, :],
                    in1=mAT[:, 2 * half:2 * half + 2, :, :].rearrange("p h q t -> p (h q t)"), op=ALU.mult),
                    reads=[ATp_t, ctk], writes=[ATs_t])
                yield
            for h in range(4):
                hc, pb = opnd(h)
                kb.op("pe", lambda h=h, hc=hc, pb=pb: P.matmul(SQp[0:64, h * 64:(h + 1) * 64], lhsT=ARhF[:, h, c, 0, :],
                                                               rhs=BKhF[:, h, c, 0, :], start=True, stop=True),
                      reads=[ARh_t, BKh_t], writes=[SQp_t])
            kb.op("dve", lambda: V.tensor_tensor(out=Ls[:].rearrange("p h s -> p (h s)"), in0=SQp[0:64, 0:256],
                                                 in1=mL[:].rearrange("p h s -> p (h s)"), op=ALU.mult), reads=[SQp_t, ctk], writes=[Ls_t])
            yield
            for h in range(4):
                kb.op("pe", lambda h=h: P.matmul(APp[0:64, h * 128 + 64:h * 128 + 128], lhsT=ATsF[:, h, 2, :], rhs=TMF[TV_][:, c, h, :],
                                                 start=(h == 0), stop=False, skip_group_check=True), reads=[ATs_t, TM_t[TV_]], writes=[APp_t])
                kb.op("pe", lambda h=h: P.matmul(APp[0:64, h * 128:h * 128 + 64], lhsT=identbF[:], rhs=TMF[TA_][:, c, h, :],
                                                 start=False, stop=False, skip_group_check=True), reads=[ctk, TM_t[TA_]], writes=[APp_t])
            zc = 0
            kb.op("act", lambda: S.copy(out=Zb[0][:].rearrange("p h x -> p (h x)"), in_=APp[0:64, :]), reads=[APp_t], writes=[Zb_t[0]])
            yield
            Bm = lambda h: ATsF[:, h, 0, :]
            Lm = lambda h: LsF[:, h, :]
            Bm_t, Lm_t = ATs_t, Ls_t
            for lvl in range(6):
                for h in range(4):
                    kb.op("pe", lambda h=h, Bm=Bm, zc=zc: P.matmul(APp[0:64, h * 128:(h + 1) * 128], lhsT=Bm(h), rhs=ZbF[zc][:, h, :],
                                                                   start=False, stop=(lvl == 5), skip_group_check=True), reads=[Bm_t, Zb_t[zc]], writes=[APp_t])
                if lvl < 5:
                    for h in range(4):
                        kb.op("pe", lambda h=h, Bm=Bm, Lm=Lm: P.matmul(SQp[0:64, h * 128:h * 128 + 64], lhsT=Bm(h), rhs=Lm(h), start=True, stop=True),
                              reads=[Bm_t, Lm_t], writes=[SQp_t])
                        kb.op("pe", lambda h=h, Bm=Bm, Lm=Lm: P.matmul(SQp[0:64, h * 128 + 64:h * 128 + 128], lhsT=Lm(h), rhs=Bm(h), start=True, stop=True),
                              reads=[Bm_t, Lm_t], writes=[SQp_t])
                kb.op("dve", lambda zc=zc: V.tensor_copy(out=Zb[1 - zc][:].rearrange("p h x -> p (h x)"), in_=APp[0:64, :]),
                      reads=[APp_t], writes=[Zb_t[1 - zc]])
                yield
                zc = 1 - zc
                if lvl < 5:
                    nb = lvl % 2
                    kb.op("act", lambda nb=nb: S.copy(out=LB[nb][:].rearrange("p h q s -> p (h q s)"), in_=SQp[0:64, :]),
                          reads=[SQp_t], writes=[LB_t[nb]])
                    yield
                    Lm = lambda h, nb=nb: LBF[nb][:, h, 0, :]
                    Bm = lambda h, nb=nb: LBF[nb][:, h, 1, :]
                    Bm_t = Lm_t = LB_t[nb]
            Zf, Zf_t = Zb[zc], Zb_t[zc]
            ZfF = ZbF[zc]
            for h in range(4):
                kb.op("pe", lambda h=h: P.matmul(ATp[0:64, h * 64:(h + 1) * 64], lhsT=ZfF[:, h, 0:64], rhs=ATsF[:, h, 1, :], start=True, stop=True),
                      reads=[Zf_t, ATs_t], writes=[ATp_t])
            kb.op("dve", lambda: V.tensor_tensor(out=RHs[:], in0=ATp[0:64, 0:256].rearrange("p (h t) -> p h t", h=4), in1=ARh[:, :, c, 1, :], op=ALU.add),
                  reads=[ATp_t, ARh_t], writes=[RHs_t])
            yield
            for h in range(4):
                kb.op("pe", lambda h=h: P.matmul(ATp[0:64, 256 + h * 64:256 + (h + 1) * 64], lhsT=ZfF[:, h, 0:64], rhs=TMF[TB_][:, c, h, :], start=True, stop=False),
                      reads=[Zf_t, TM_t[TB_]], writes=[ATp_t])
                kb.op("pe", lambda h=h: P.matmul(ATp[0:64, 256 + h * 64:256 + (h + 1) * 64], lhsT=identbF[:], rhs=identbF[:], start=False, stop=True),
                      reads=[ctk], writes=[ATp_t])
            kb.op("act", lambda: S.copy(out=MTs[:].rearrange("p h n -> p (h n)"), in_=ATp[0:64, 256:512]), reads=[ATp_t], writes=[MTs_t])
            yield
            hcur, hnew = hstate[0], 1 - hstate[0]
            while hseq[0] != j * NCH + c:
                yield
            hcur, hnew = hstate[0], 1 - hstate[0]
            for h in range(4):
                o = SQp[0:64, h * 64:(h + 1) * 64]
                kb.op("pe", lambda h=h, o=o: P.matmul(o, lhsT=RHsF[:, h, :], rhs=HsF[hcur][:, h, :], start=True, stop=False),
                      reads=[RHs_t, Hs_t[hcur]], writes=[SQp_t])
                kb.op("pe", lambda h=h, o=o: P.matmul(o, lhsT=ATsF[:, h, 1, :], rhs=ZfF[:, h, 64:128], start=False, stop=False),
                      reads=[ATs_t, Zf_t], writes=[SQp_t])
                kb.op("pe", lambda h=h, o=o: P.matmul(o, lhsT=ATsF[:, h, 3, :], rhs=TMF[TV_][:, c, h, :], start=False, stop=True),
                      reads=[ATs_t, TM_t[TV_]], writes=[SQp_t])
            kb.op("act", lambda: S.copy(out=Ybuf[:, c, :, :].rearrange("p h v -> p (h v)"), in_=SQp[0:64, 0:256]), reads=[SQp_t], writes=[Y_t])
            yield
            for h in range(4):
                o = SQp[0:64, 256 + h * 64:256 + (h + 1) * 64]
                kb.op("pe", lambda h=h, o=o: P.matmul(o, lhsT=MTsF[:, h, :], rhs=HsF[hcur][:, h, :], start=True, stop=False),
                      reads=[MTs_t, Hs_t[hcur]], writes=[SQp_t])
                kb.op("pe", lambda h=h, o=o: P.matmul(o, lhsT=TMF[TB_][:, c, h, :], rhs=ZfF[:, h, 64:128], start=False, stop=False),
                      reads=[TM_t[TB_], Zf_t], writes=[SQp_t])
                kb.op("pe", lambda h=h, o=o: P.matmul(o, lhsT=TMF[TK_][:, c, h, :], rhs=TMF[TV_][:, c, h, :], start=False, stop=True),
                      reads=[TM_t[TK_], TM_t[TV_]], writes=[SQp_t])
            kb.op("dve", lambda: V.tensor_tensor(out=Hs[hnew][:], in0=SQp[0:64, 256:512].rearrange("p (h v) -> p h v", h=4),
                                                 in1=WCs[:, :, c:c + 1].broadcast_to([64, 4, 64]), op=ALU.mult),
                  reads=[SQp_t, WCs_t], writes=[Hs_t[hnew]])
            yield
            hstate[0] = hnew
            hseq[0] += 1

        def post9(j):
            p = j % 2
            cs = slice(j * TA, (j + 1) * TA)
            ARh, ARh_t, BKh, BKh_t = ARh2[p], ARh2_t[p], BKh2[p], BKh2_t[p]
            TM, TM_t = TM2[p], TM2_t[p]
            WCs, WCs_t, RKs, RKs_t = WCs2[p], WCs2_t[p], RKs2[p], RKs2_t[p]
            sg, sg_t = sg2[p], sg2_t[p]
            Ybuf, Y_t = Ybuf2[p], Y2_t[p]
            Yv = Ybuf[:].rearrange("p c h v -> p (c h) v")
            W1b = AR[0:64, :, :, :, :].rearrange("p a c q t -> p (a c q t)").rearrange("p (c h v) -> p c h v", c=NCH, h=4)
            W2b = BK[0:64, :, :, :, :].rearrange("p a c q t -> p (a c q t)").rearrange("p (c h v) -> p c h v", c=NCH, h=4)
            W1_t, W2_t = AR_t, BK_t
            W1v = W1b.rearrange("p c h v -> p (c h) v")
            W2v = W2b.rearrange("p c h v -> p (c h) v")
            kb.op("dve", lambda: V.tensor_reduce(out=st1[:], in_=Yv, axis=AX.X, op=ALU.add), reads=[Y_t], writes=[st_t])
            kb.op("act", lambda: S.activation(out=W1b.rearrange("p c h v -> p (c h v)"), in_=Ybuf[:].rearrange("p c h v -> p (c h v)"), func=AF.Square),
                  reads=[Y_t], writes=[W1_t])
            kb.op("dve", lambda: V.tensor_reduce(out=st2[:], in_=W1v, axis=AX.X, op=ALU.add), reads=[W1_t], writes=[st_t])
            kb.op("dve", lambda: V.tensor_scalar(out=st1[:], in0=st1[:], scalar1=1.0 / 64, scalar2=None, op0=ALU.mult), reads=[st_t], writes=[st_t])
            kb.op("dve", lambda: V.tensor_tensor(out=st3[:], in0=st1[:], in1=st1[:], op=ALU.mult), reads=[st_t], writes=[st_t])
            kb.op("dve", lambda: V.tensor_scalar(out=st2[:], in0=st2[:], scalar1=1.0 / 64, scalar2=None, op0=ALU.mult), reads=[st_t], writes=[st_t])
            kb.op("dve", lambda: V.tensor_tensor(out=st2[:], in0=st2[:], in1=st3[:], op=ALU.subtract), reads=[st_t], writes=[st_t])
            kb.op("dve", lambda: V.tensor_scalar(out=st2[:], in0=st2[:], scalar1=0.0, scalar2=None, op0=ALU.max), reads=[st_t], writes=[st_t])
            kb.op("act", lambda: S.activation(out=st2[:], in_=st2[:], func=AF.Ln, bias=GN_EPS, scale=1.0), reads=[st_t], writes=[st_t])
            kb.op("act", lambda: S.activation(out=st2[:], in_=st2[:], func=AF.Exp, scale=-0.5), reads=[st_t], writes=[st_t])
            bc = lambda t: t[:].unsqueeze(2).broadcast_to([64, NCH * 4, 64])
            kb.op("dve", lambda: V.tensor_tensor(out=W1v, in0=Yv, in1=bc(st1), op=ALU.subtract), reads=[Y_t, st_t], writes=[W1_t])
            kb.op("dve", lambda: V.tensor_tensor(out=W1v, in0=W1v, in1=bc(st2), op=ALU.mult), reads=[st_t], writes=[W1_t])
            lw_b = lnw[:, 0:256].unsqueeze(1).broadcast_to([64, NCH, 256])
            lb_b = lnw[:, 256:512].unsqueeze(1).broadcast_to([64, NCH, 256])
            W1c = W1b.rearrange("p c h v -> p c (h v)")
            W2c = W2b.rearrange("p c h v -> p c (h v)")
            kb.op("pool", lambda: G.tensor_tensor(out=W1c, in0=W1c, in1=lw_b, op=ALU.mult), reads=[ctk], writes=[W1_t])
            kb.op("pool", lambda: G.tensor_tensor(out=W1c, in0=W1c, in1=lb_b, op=ALU.add), reads=[ctk], writes=[W1_t])
            kb.op("dve", lambda: V.tensor_tensor(out=W2v, in0=TM[TV_][:].rearrange("p c h v -> p (c h) v"),
                                                 in1=RKs[:].rearrange("p c h -> p (c h)").unsqueeze(2).broadcast_to([64, NCH * 4, 64]), op=ALU.mult),
                  reads=[TM_t[TV_], RKs_t], writes=[W2_t])
            kb.op("dve", lambda: V.tensor_tensor(out=W1v, in0=W1v, in1=W2v, op=ALU.add), reads=[W2_t], writes=[W1_t])
            for hc in range(2):
                for c in range(NCH):
                    kb.op("pe", lambda hc=hc, c=c: P.transpose(PR[:, c * 64:(c + 1) * 64], W1b[:, c, 2 * hc:2 * hc + 2, :].rearrange("p h v -> p (h v)"),
                                                               ident[0:64, 0:64]), reads=[W1_t, ctk], writes=[PR_t])
                kb.op("dve", lambda hc=hc: V.tensor_tensor(out=zout[:, hc, :], in0=PR[:, 0:TA], in1=sg[:, hc, :], op=ALU.mult),
                      reads=[PR_t, sg_t], writes=[zo_t])
            kb.dma("sp", zT.rearrange("(c p) t -> p c t", p=128)[:, :, cs], zout[:], zo_t, reads=[zo_t])

        coop = Coop(kb)
        prep(0)
        if ntile > 1:
            prep(1)
        pending = [(j, c) for j in range(ntile) for c in range(NCH)]
        slots = [None, None]
        sid = [None, None]
        idx = 0
        while idx < len(pending) or slots[0] is not None or slots[1] is not None:
            for k in range(2):
                if slots[k] is None and idx < len(pending):
                    jn, cn = pending[idx]
                    other = sid[1 - k]
                    if cn == 0 and jn > 1:
                        coop.finish()
                    slots[k] = chunk_gen(jn, cn, k)
                    sid[k] = (jn, cn)
                    idx += 1
                if slots[k] is not None:
                    try:
                        next(slots[k])
                    except StopIteration:
                        jj, cc = sid[k]
                        slots[k] = None
                        sid[k] = None
                        if cc == NCH - 1:
                            coop.finish()

                            def filler(jj=jj):
                                post9(jj)
                                if jj + 2 < ntile:
                                    prep(jj + 2)
                            coop.start(filler)
                    coop.step(FILL)
        coop.finish()
        kb.finish([zo_t] + dbg_tok)
    return nc


def rwkv_maps(x, I, NT):
    maps = []
    Wfull = I["rw_w_in"][0].reshape(D, 4, 1024)
    vecA = np.concatenate([I["rw_norm"][0].reshape(8, 128).T] + [I["rw_mu"][0][i].reshape(8, 128).T for i in range(6)], axis=1)
    ident = np.eye(128, dtype=np.float32)
    for c in range(8):
        b, g = c // 4, c % 4
        cs = slice(g * 256, (g + 1) * 256)
        vc = np.zeros((128, 2, 8), np.float32)
        for vi, nm in enumerate(["rw_w0", "rw_a0", "rw_k_k", "rw_k_a"]):
            vc[:, :, vi] = I[nm][0][cs].reshape(2, 128).T
        vc[:, :, 4] = I["rw_r_k"][0].reshape(1024)[cs].reshape(2, 128).T
        maps.append(dict(
            xT=np.ascontiguousarray(x[b, :NT].T),
            Wc=np.ascontiguousarray(Wfull[:, :, cs].reshape(D, 1024)),
            w1=I["rw_w1"][0], a1=I["rw_a1"][0],
            w2c=np.ascontiguousarray(I["rw_w2"][0][:, cs]), a2c=np.ascontiguousarray(I["rw_a2"][0][:, cs]),
            vecA=np.ascontiguousarray(vecA.astype(np.float32)), vecC=np.ascontiguousarray(vc.reshape(128, 16)),
            lnwb=np.ascontiguousarray(np.concatenate([I["rw_ln_w"][0][cs], I["rw_ln_b"][0][cs]])[None, :]),
            ident=ident))
    return maps


LAM_INIT = 0.8 - 0.6 * math.exp(-0.3 * 1)


def build_attn(NT):
    nc = new_nc()
    hnT = dram_in(nc, "hnT", [D, NT], BF16)
    Wd = dram_in(nc, "Wd", [D, 4 * 256])
    sub = dram_in(nc, "sub", [1, 128])
    lqk = dram_in(nc, "lqk", [1, 256])
    invf = dram_in(nc, "invf", [128, 1])
    pos0 = dram_in(nc, "pos0", [1, TT])
    cosd = dram_in(nc, "cosd", [128, NT])
    sind = dram_in(nc, "sind", [128, NT])
    identd = dram_in(nc, "ident", [128, 128])
    zT = dram_out(nc, "zT", [256, NT], BF16)
    ntile = NT // TT
    nblk = NT // 128
    PI = math.pi
    with ExitStack() as es:
        kb = KB(nc, es)
        V, S, G, P = nc.vector, nc.scalar, nc.gpsimd, nc.tensor
        ctk = kb.tok("cst")
        ident = kb.sb("ident_sb", [128, 128])
        kb.dma("sp", ident[:], identd[:, :], ctk, writes=[ctk])
        tri = kb.sb("tri", [128, 128], BF16)
        kb.op("pool", lambda: G.memset(tri[:], 1.0), writes=[ctk])
        kb.op("pool", lambda: G.affine_select(out=tri[:], in_=tri[:], pattern=[[1, 128]], compare_op=ALU.is_ge, fill=0.0, base=0,
                                              channel_multiplier=-1), writes=[ctk])
        subb = kb.sb("subb", [128, 128])
        kb.dma("sp", subb[:], sub.partition_broadcast(128), ctk, writes=[ctk])
        kb.op("dve", lambda: V.tensor_scalar(out=subb[:], in0=subb[:], scalar1=1.0 - LAM_INIT, scalar2=None, op0=ALU.mult), reads=[ctk], writes=[ctk])
        lq = kb.sb("lq", [128, 256])
        kb.dma("sp", lq[:], lqk.partition_broadcast(128), ctk, writes=[ctk])
        lam = kb.sb("lam", [128, 4])
        kb.op("dve", lambda: V.tensor_tensor(out=lq[:, 0:64], in0=lq[:, 0:64], in1=lq[:, 64:128], op=ALU.mult), reads=[ctk], writes=[ctk])
        kb.op("dve", lambda: V.tensor_tensor(out=lq[:, 128:192], in0=lq[:, 128:192], in1=lq[:, 192:256], op=ALU.mult), reads=[ctk], writes=[ctk])
        kb.op("dve", lambda: V.tensor_reduce(out=lam[:, 0:1], in_=lq[:, 0:64], axis=AX.X, op=ALU.add), reads=[ctk], writes=[ctk])
        kb.op("dve", lambda: V.tensor_reduce(out=lam[:, 1:2], in_=lq[:, 128:192], axis=AX.X, op=ALU.add), reads=[ctk], writes=[ctk])
        kb.op("act", lambda: S.activation(out=lam[:, 0:2], in_=lam[:, 0:2], func=AF.Exp), reads=[ctk], writes=[ctk])
        kb.op("dve", lambda: V.tensor_tensor(out=lam[:, 2:3], in0=lam[:, 1:2], in1=lam[:, 0:1], op=ALU.subtract), reads=[ctk], writes=[ctk])
        kb.op("dve", lambda: V.tensor_scalar(out=lam[:, 3:4], in0=lam[:, 2:3], scalar1=-LAM_INIT, scalar2=None, op0=ALU.add), reads=[ctk], writes=[ctk])
        ivf = kb.sb("ivf", [128, 1])
        kb.dma("sp", ivf[:], invf[:, :], ctk, writes=[ctk])
        ngpi = kb.sb("ngpi", [128, 1])
        kb.op("pool", lambda: G.memset(ngpi[:], -PI), writes=[ctk])

        QT2 = [kb.sb("QT%d" % i, [128, NT], BF16) for i in range(2)]
        QT_t = kb.tok("QT")
        for i in range(2):
            kb.op("pool", lambda i=i: G.memset(QT2[i][:], 0.0), writes=[QT_t])
        KT = kb.sb("KT", [128, NT], BF16); KT_t = kb.tok("KT")
        sgd = nc.dram_tensor("sgd", [2, 128, NT], BF16).ap()
        sgd_t = kb.tok("sgd")
        SGt = [kb.sb("SGt%d" % i, [128, TT], BF16) for i in range(2)]
        SGt_t = kb.toks(2, "SGt")
        sgl = [kb.sb("sgl%d" % i, [128, TT], BF16) for i in range(2)]
        sgl_t = kb.toks(2, "sgl")
        Va = kb.sb("Va", [128, nblk, 129], BF16); Va_t = kb.tok("Va")
        kb.op("pool", lambda: G.memset(Va[:], 1.0), writes=[Va_t])
        hin = [kb.sb("ahin%d" % i, [128, 8, TT], BF16) for i in range(2)]
        hin_t = kb.toks(2, "ahin")
        stg = kb.sb("astg", [128, 1024])
        stg_t = kb.tok("astg")
        Wt = {nm: kb.sb("W" + nm, [128, 8, 128], BF16) for nm in ("q", "qr", "k", "kr", "v", "g")}
        W_t = kb.tok("Wt")
        sn = kb.sb("sn", [128, TT]); cs_ = kb.sb("cs_", [128, TT]); sc_t = kb.tok("sincos")
        ta = kb.sb("ta", [128, TT]); tb = kb.sb("tb", [128, TT]); tab_t = kb.tok("tab")
        PT = [kb.sb("PT%d" % i, [128, TT], BF16) for i in range(4)]
        PT_t = kb.toks(4, "PT")
        o1 = kb.sb("o1", [128, 128]); o1_t = kb.tok("o1")
        o4 = kb.sb("o4", [128, 4, 128]); o_t = kb.tok("o4")
        on4 = kb.sb("on4", [128, 4, 128]); on_t = kb.tok("on4")
        rc8 = kb.sb("rc8", [128, 9]); ss4 = kb.sb("ss4", [128, 4]); ss_t = kb.tok("ss4")
        junk = kb.sb("junk", [128, 128])
        rc = kb.sb("rc", [128, 4]); rc_t = kb.tok("rc")
        zout = [kb.sb("azout%d" % i, [128, TT], BF16) for i in range(2)]
        zo_t = kb.toks(2, "azo")
        SB = [kb.ps("SB%d" % i, [128, 512]) for i in range(4)]
        SB_t = [kb.ptok("SB%d" % i) for i in range(4)]
        AC = [kb.ps("AC%d" % i, [128, 512]) for i in range(3)]
        AC_t = [kb.ptok("AC%d" % i) for i in range(3)]
        TP_ = kb.ps("TPp", [128, 512]); TP_t = kb.ptok("TPp")

        def acc(c, qs):
            i = c * 4 + qs
            return AC[i // 3][:, (i % 3) * 129:(i % 3) * 129 + 129], AC_t[i // 3]

        hv = hnT.rearrange("(c p) t -> p c t", p=128)
        for hh in range(2):
            for gi, nm in enumerate(("q", "k", "v", "g")):
                for dc in range(8):
                    kb.dma("sp", stg[:, 0:128], Wd[dc * 128:(dc + 1) * 128, gi * 256 + hh * 128:gi * 256 + hh * 128 + 128], stg_t, writes=[stg_t])
                    kb.op("dve", lambda nm=nm, dc=dc: V.tensor_copy(out=Wt[nm][:, dc, :], in_=stg[:, 0:128]), reads=[stg_t], writes=[W_t])
                    if nm in ("q", "k"):
                        for c in range(2):
                            kb.op("dve", lambda nm=nm, dc=dc, c=c: V.tensor_scalar(out=Wt[nm + "r"][:, dc, c * 64:c * 64 + 32], in0=stg[:, c * 64 + 32:c * 64 + 64],
                                                                                   scalar1=-1.0, scalar2=None, op0=ALU.mult), reads=[stg_t], writes=[W_t])
                            kb.op("dve", lambda nm=nm, dc=dc, c=c: V.tensor_copy(out=Wt[nm + "r"][:, dc, c * 64 + 32:c * 64 + 64], in_=stg[:, c * 64:c * 64 + 32]),
                                  reads=[stg_t], writes=[W_t])
            kb.dma("sp", hin[0][:], hv[:, :, 0:TT], hin_t[0], writes=[hin_t[0]])
            for j in range(ntile):
                s = j % 2
                cs = slice(j * TT, (j + 1) * TT)
                if j + 1 < ntile:
                    kb.dma("sp", hin[1 - s][:], hv[:, :, (j + 1) * TT:(j + 2) * TT], hin_t[1 - s], writes=[hin_t[1 - s]])
                hb, hb_t = hin[s], hin_t[s]
                kb.dma("sp", sn[:], sind[:, cs], sc_t, writes=[sc_t])
                kb.dma("sp", cs_[:], cosd[:, cs], sc_t, writes=[sc_t])
                for nm, dstT, dst_t in (("q", None, QT_t), ("k", KT, KT_t)):
                    for dc in range(8):
                        kb.op("pe", lambda nm=nm, dc=dc: P.matmul(SB[0][:], lhsT=Wt[nm][:, dc, :], rhs=hb[:, dc, :], start=(dc == 0), stop=(dc == 7)),
                              reads=[W_t, hb_t], writes=[SB_t[0]])
                    for dc in range(8):
                        kb.op("pe", lambda nm=nm, dc=dc: P.matmul(SB[1][:], lhsT=Wt[nm + "r"][:, dc, :], rhs=hb[:, dc, :], start=(dc == 0), stop=(dc == 7)),
                              reads=[W_t, hb_t], writes=[SB_t[1]])
                    kb.op("dve", lambda: V.tensor_tensor(out=ta[:], in0=SB[0][:], in1=cs_[:], op=ALU.mult), reads=[SB_t[0], sc_t], writes=[tab_t])
                    kb.op("dve", lambda: V.tensor_tensor(out=tb[:], in0=SB[1][:], in1=sn[:], op=ALU.mult), reads=[SB_t[1], sc_t], writes=[tab_t])
                    if nm == "k":
                        kb.op("dve", lambda dstT=dstT, cs=cs: V.tensor_tensor(out=dstT[:, cs], in0=ta[:], in1=tb[:], op=ALU.add), reads=[tab_t], writes=[dst_t])
                    else:
                        kb.op("dve", lambda cs=cs: V.tensor_tensor(out=QT2[0][0:64, cs], in0=ta[0:64, :], in1=tb[0:64, :], op=ALU.add), reads=[tab_t], writes=[dst_t])
                        kb.op("dve", lambda cs=cs: V.tensor_tensor(out=QT2[1][64:128, cs], in0=ta[64:128, :], in1=tb[64:128, :], op=ALU.add), reads=[tab_t], writes=[dst_t])
                for dc in range(8):
                    kb.op("pe", lambda dc=dc: P.matmul(SB[2][:], lhsT=Wt["g"][:, dc, :], rhs=hb[:, dc, :], start=(dc == 0), stop=(dc == 7)),
                          reads=[W_t, hb_t], writes=[SB_t[2]])
                kb.op("act", lambda s=s: S.activation(out=SGt[s][:], in_=SB[2][:], func=AF.Silu), reads=[SB_t[2]], writes=[SGt_t[s]])
                kb.dma("sp", sgd[hh][:, cs], SGt[s][:], SGt_t[s], reads=[SGt_t[s]], writes=[sgd_t])
                for bi in range(4):
                    for dc in range(8):
                        kb.op("pe", lambda dc=dc, bi=bi: P.matmul(SB[3][:, bi * 128:(bi + 1) * 128], lhsT=hb[:, dc, bi * 128:(bi + 1) * 128], rhs=Wt["v"][:, dc, :],
                                                                  start=(dc == 0), stop=(dc == 7)), reads=[W_t, hb_t], writes=[SB_t[3]])
                kb.op("act", lambda j=j: S.copy(out=Va[:, j * 4:j * 4 + 4, 0:128], in_=SB[3][:].rearrange("p (b v) -> p b v", b=4)), reads=[SB_t[3]], writes=[Va_t])
            gstep = [0]
            for g in range(ntile):
                for a3 in range(3):
                    kb.op("dve", lambda a3=a3: V.memset(AC[a3][:], 0.0), writes=[AC_t[a3]])
                kb.dma("sp", sgl[g % 2][:], sgd[hh][:, g * TT:(g + 1) * TT], sgl_t[g % 2], reads=[sgd_t], writes=[sgl_t[g % 2]])
                steps = []
                for kbk in range(4 * g + 4):
                    for c in range(2):
                        steps.append((kbk, c, gstep[0] % 4))
                        gstep[0] += 1

                def qk(st, g=g):
                    kbk, c, bi = st
                    kb.op("pe", lambda: P.matmul(SB[bi][:], lhsT=KT[:, kbk * 128:(kbk + 1) * 128],
                                                 rhs=QT2[c][:, g * TT:(g + 1) * TT], start=True, stop=True),
                          reads=[KT_t, QT_t], writes=[SB_t[bi]])

                def ex(st, g=g):
                    kbk, c, bi = st
                    m = kbk - 4 * g
                    kb.op("act", lambda: S.activation(out=PT[bi][:], in_=SB[bi][:], func=AF.Exp, scale=0.125),
                          reads=[SB_t[bi]], writes=[PT_t[bi]])
                    if m >= 0:
                        kb.op("pool", lambda: G.tensor_tensor(out=PT[bi][:, m * 128:(m + 1) * 128], in0=PT[bi][:, m * 128:(m + 1) * 128],
                                                              in1=tri[:], op=ALU.mult), reads=[ctk], writes=[PT_t[bi]])

                def av(st, g=g):
                    kbk, c, bi = st
                    m = kbk - 4 * g
                    for qs in range(4):
                        if m > qs:
                            continue
                        ap_, at_ = acc(c, qs)
                        kb.op("pe", lambda ap_=ap_, qs=qs: P.matmul(ap_, lhsT=PT[bi][:, qs * 128:(qs + 1) * 128], rhs=Va[:, kbk, :],
                                                                    start=False, stop=False, skip_group_check=True),
                              reads=[PT_t[bi], Va_t], writes=[at_])

                LA = 2
                for i in range(min(LA, len(steps))):
                    qk(steps[i])
                for i, st in enumerate(steps):
                    ex(st)
                    if i + LA < len(steps):
                        qk(steps[i + LA])
                    av(st)
                zb, zb_t = zout[g % 2], zo_t[g % 2]
                for a3 in range(3):
                    na = 3 if a3 < 2 else 2
                    kb.op("dve", lambda a3=a3, na=na: V.reciprocal(out=rc8[:, a3 * 3:a3 * 3 + na],
                                                                   in_=AC[a3][:, 0:na * 129].rearrange("p (a w) -> p a w", w=129)[:, :, 128]),
                          reads=[AC_t[a3]], writes=[rc_t])
                for qs in range(4):
                    a0, a0t = acc(0, qs)
                    a1, a1t = acc(1, qs)
                    kb.op("dve", lambda a1=a1, qs=qs: V.tensor_scalar(out=o1[:], in0=a1[:, 0:128], scalar1=rc8[:, 4 + qs:5 + qs], scalar2=lam[:, 3:4],
                                                                     op0=ALU.mult, op1=ALU.mult), reads=[a1t, rc_t, ctk], writes=[o1_t])
                    kb.op("dve", lambda a0=a0, qs=qs: V.scalar_tensor_tensor(out=o4[:, qs, :], in0=a0[:, 0:128], scalar=rc8[:, qs:qs + 1], in1=o1[:],
                                                                            op0=ALU.mult, op1=ALU.add), reads=[a0t, rc_t, o1_t], writes=[o_t])
                for qs in range(4):
                    kb.op("act", lambda qs=qs: S.activation(out=junk[:], in_=o4[:, qs, :], func=AF.Square, accum_out=ss4[:, qs:qs + 1]),
                          reads=[o_t], writes=[ss_t])
                kb.op("act", lambda: S.activation(out=ss4[:], in_=ss4[:], func=AF.Ln, scale=1.0 / 128, bias=1e-5), reads=[ss_t], writes=[ss_t])
                kb.op("act", lambda: S.activation(out=ss4[:], in_=ss4[:], func=AF.Exp, scale=-0.5), reads=[ss_t], writes=[ss_t])
                for qs in range(4):
                    kb.op("dve", lambda qs=qs: V.scalar_tensor_tensor(out=on4[:, qs, :], in0=o4[:, qs, :], scalar=ss4[:, qs:qs + 1], in1=subb[:],
                                                                      op0=ALU.mult, op1=ALU.mult), reads=[o_t, ss_t, ctk], writes=[on_t])
                for qs in range(4):
                    kb.op("pe", lambda qs=qs: P.transpose(TP_[:, qs * 128:(qs + 1) * 128], on4[:, qs, :], ident[:]), reads=[on_t, ctk], writes=[TP_t])
                kb.op("dve", lambda g=g, zb=zb: V.tensor_tensor(out=zb[:], in0=TP_[:], in1=sgl[g % 2][:], op=ALU.mult),
                      reads=[TP_t, sgl_t[g % 2]], writes=[zb_t])
                kb.dma("sp", zT[hh * 128:(hh + 1) * 128, g * TT:(g + 1) * TT], zb[:], zb_t, reads=[zb_t])
        kb.finish(zo_t)
    return nc


def attn_maps(hn_list, I, NT):
    maps = []
    Wfull = I["da_w_in"][0].reshape(D, 4, 1024)
    ident = np.eye(128, dtype=np.float32)
    inv = (1.0 / (10000.0 ** (np.arange(0, 64, 2, dtype=np.float32) / 64))).astype(np.float32)
    invf = np.tile(inv, 4).reshape(128, 1).astype(np.float32)
    pos0 = np.arange(TT, dtype=np.float32)[None, :]
    ang = np.arange(NT, dtype=np.float32)[None, :] * invf
    cosd = np.cos(ang).astype(np.float32)
    sind = np.sin(ang).astype(np.float32)
    lqk = np.concatenate([I["da_lq1"][0], I["da_lk1"][0], I["da_lq2"][0], I["da_lk2"][0]])[None, :].astype(np.float32)
    for c in range(8):
        b, hp = c // 4, c % 4
        cs = slice(hp * 256, (hp + 1) * 256)
        maps.append(dict(hnT=np.ascontiguousarray(hn_list[b]), Wd=np.ascontiguousarray(Wfull[:, :, cs].reshape(D, 1024)),
                         sub=np.ascontiguousarray(I["da_subln"][0][None, :]), lqk=np.ascontiguousarray(lqk), invf=invf, pos0=pos0, ident=ident, cosd=cosd, sind=sind))
    return maps


def kernel(**inputs):
    I = {k: np.asarray(v) for k, v in inputs.items()}
    x, p = I["x"], I["p"]
    B, S = x.shape[0], x.shape[1]
    QT_ = S // 4
    nc = build_rwkv(S)
    res = run_bass_kernel_spmd(nc, rwkv_maps(x, I, S), core_ids=list(range(8))).results
    z0T = [np.concatenate([res[b * 4 + g]["zT"] for g in range(4)], axis=0) for b in range(B)]
    rng = [(c // 4, slice((c % 4) * QT_, (c % 4 + 1) * QT_)) for c in range(8)]
    res2 = run_post([z0T[b][:, sl] for b, sl in rng], [x[b, sl].T for b, sl in rng], [p[0, b, sl].T for b, sl in rng],
                    I["rw_w_out"][0], I["pe_w_gate"][0], I["pe_w_proj"][0], I["pe_norm"][0], I["da_norm"][0], last=False)
    hn = [np.concatenate([res2[b * 4 + q]["hnT"] for q in range(4)], axis=1) for b in range(B)]
    nc3 = build_attn(S)
    res3 = run_bass_kernel_spmd(nc3, attn_maps(hn, I, S), core_ids=list(range(8))).results
    z1T = [np.concatenate([res3[b * 4 + g]["zT"] for g in range(4)], axis=0) for b in range(B)]
    res4 = run_post([z1T[b][:, sl] for b, sl in rng], [res2[c]["h1T"] for c in range(8)], [p[1, b, sl].T for b, sl in rng],
                    I["da_w_out"][0], I["pe_w_gate"][1], I["pe_w_proj"][1], I["pe_norm"][1], I["final_norm"], last=True)
    out = np.empty((B, S, D), np.float32)
    for c, (b, sl) in enumerate(rng):
        out[b, sl, :] = res4[c]["oT"].T
    return out
```

```python
import math
from contextlib import ExitStack
import numpy as np
import ml_dtypes
import concourse.bass as bass
import concourse.mybir as mybir
from concourse.bass_utils import run_bass_kernel_spmd

F32 = mybir.dt.float32
BF16 = mybir.dt.bfloat16
AF = mybir.ActivationFunctionType
ALU = mybir.AluOpType
AX = mybir.AxisListType
NPBF = ml_dtypes.bfloat16

D = 1024
SEQ = 16384
TT = 512


class Tok:
    __slots__ = ("name", "w", "r", "dsem", "dcnt", "excl")

    def __init__(self, name, excl=False):
        self.name = name
        self.excl = excl
        self.w = None
        self.r = {}
        self.dsem = None
        self.dcnt = 0


class KB:
    def __init__(self, nc, es):
        self.nc = nc
        self.es = es
        self.eng = dict(pe=nc.tensor, dve=nc.vector, act=nc.scalar, pool=nc.gpsimd, sp=nc.sync)
        self.sem = {e: es.enter_context(nc.semaphore("prog_" + e)) for e in ("pe", "dve", "act", "pool")}
        self.cnt = {e: 0 for e in self.sem}
        self.seen = {e: {} for e in self.eng}
        self.ntok = 0
        self.out_dma = []
        self.coop = None

    def tok(self, name=None):
        self.ntok += 1
        return Tok(name or ("t%d" % self.ntok))

    def toks(self, n, name="t"):
        return [self.tok("%s%d_%d" % (name, self.ntok, i)) for i in range(n)]

    def sb(self, name, shape, dt=F32):
        return self.es.enter_context(self.nc.sbuf_tensor(name, list(shape), dt))

    def ps(self, name, shape, dt=F32):
        return self.es.enter_context(self.nc.psum_tensor(name, list(shape), dt))

    def _deps(self, reads, writes):
        deps = {}
        for t in reads:
            if t.w is not None:
                k = t.w[1]
                if deps.get(k, (None, None, 0))[2] < t.w[2]:
                    deps[k] = t.w
        for t in writes:
            for d in ([t.w] if t.w is not None else []) + list(t.r.values()):
                k = d[1]
                if deps.get(k, (None, None, 0))[2] < d[2]:
                    deps[k] = d
        return deps

    def _wait(self, e, deps, keep_same=False):
        for k, (sem, key, val) in deps.items():
            if key == e and e == "pe" and not keep_same:
                continue
            if self.seen[e].get(key, 0) < val:
                self.eng[e].wait_ge(sem, val)
                self.seen[e][key] = val

    def ptok(self, name=None):
        t = self.tok(name)
        t.excl = True
        return t

    def op(self, e, fn, reads=(), writes=()):
        ex = [t for t in reads if t.excl]
        if ex:
            reads = [t for t in reads if not t.excl]
            writes = list(writes) + ex
        self._wait(e, self._deps(reads, writes))
        ins = fn()
        self.cnt[e] += 1
        ins.then_inc(self.sem[e], 1)
        me = (self.sem[e], e, self.cnt[e])
        for t in reads:
            t.r[e] = me
        for t in writes:
            t.w = me
            t.r = {}
        if self.coop is not None:
            self.coop.emitted()
        return ins

    def dma(self, q, out, in_, owner, reads=(), writes=(), **kw):
        self._wait(q, self._deps(reads, writes), keep_same=True)
        if owner.dsem is None:
            owner.dsem = self.es.enter_context(self.nc.semaphore("dma_" + owner.name))
        ins = self.eng[q].dma_start(out=out, in_=in_, **kw)
        owner.dcnt += 16
        ins.then_inc(owner.dsem, 16)
        me = (owner.dsem, "dma_" + owner.name, owner.dcnt)
        for t in reads:
            t.r[me[1]] = me
        for t in writes:
            t.w = me
            t.r = {}
        return me

    def finish(self, toks):
        for t in toks:
            if t.dsem is not None:
                self.eng["sp"].wait_ge(t.dsem, t.dcnt)


import threading


class Coop:
    def __init__(self, kb):
        self.kb = kb
        self.thread = None
        self.quota = 0
        self.go = threading.Semaphore(0)
        self.back = threading.Semaphore(0)
        self.done = True
        self.err = None
        kb.coop = self

    def start(self, fn):
        self.done = False
        self.quota = 0

        def run():
            self.go.acquire()
            try:
                fn()
            except BaseException as e:
                self.err = e
            self.done = True
            self.back.release()
        self.thread = threading.Thread(target=run)
        self.thread.start()

    def emitted(self):
        if threading.current_thread() is self.thread:
            self.quota -= 1
            if self.quota <= 0:
                self.back.release()
                self.go.acquire()

    def step(self, n):
        if self.done or threading.current_thread() is self.thread:
            return
        self.quota = n
        self.go.release()
        self.back.acquire()
        if self.err is not None:
            raise self.err

    def finish(self):
        while not self.done:
            self.step(1 << 30)
        if self.thread is not None:
            self.thread.join()
        if self.err is not None:
            raise self.err


def new_nc():
    return bass.Bass("TRN2", target_bir_lowering=False)


def dram_in(nc, name, shape, dt=F32):
    return nc.dram_tensor(name, list(shape), dt, kind="ExternalInput").ap()


def dram_out(nc, name, shape, dt=F32):
    return nc.dram_tensor(name, list(shape), dt, kind="ExternalOutput").ap()


def load_weight_bf16(kb, q, w_dram, kchunks, ncols, name, stage, stage_tok, cast_eng="pool"):
    wt = kb.sb(name, [128, kchunks, ncols], BF16)
    tk = kb.tok(name)
    for kc in range(kchunks):
        s = kc % len(stage)
        kb.dma(q, stage[s][:, 0:ncols], w_dram[kc * 128:(kc + 1) * 128, :], stage_tok[s], writes=[stage_tok[s]])
        kb.op(cast_eng, lambda kc=kc, s=s: kb.eng[cast_eng].tensor_copy(out=wt[:, kc, :], in_=stage[s][:, 0:ncols]),
              reads=[stage_tok[s]], writes=[tk])
    return wt, tk


def rstd_from_sumsq(kb, ps_ap, ps_tok, out_ap, out_tok, eps):
    kb.op("act", lambda: kb.nc.scalar.activation(out=out_ap, in_=ps_ap, func=AF.Ln, bias=float(eps), scale=1.0),
          reads=[ps_tok], writes=[out_tok])
    kb.op("act", lambda: kb.nc.scalar.activation(out=out_ap, in_=out_ap, func=AF.Exp, scale=-0.5),
          reads=[out_tok], writes=[out_tok])


def build_post(NT, last):
    nc = new_nc()
    zT = dram_in(nc, "zT", [D, NT], BF16)
    hT = dram_in(nc, "hT", [D, NT], F32)
    pT = dram_in(nc, "pT", [256, NT], F32)
    w_out = dram_in(nc, "w_out", [D, D])
    w_gate = dram_in(nc, "w_gate", [D, D])
    w_proj = dram_in(nc, "w_proj", [256, D])
    vecs = dram_in(nc, "vecs", [128, 16])
    if last:
        oT = dram_out(nc, "oT", [D, NT], F32)
    else:
        h1T = dram_out(nc, "h1T", [D, NT], F32)
        hnT = dram_out(nc, "hnT", [D, NT], BF16)
    ntile = NT // TT
    with ExitStack() as es:
        kb = KB(nc, es)
        ctk = kb.tok("cst")
        ones = kb.sb("ones", [128, 128], BF16)
        kb.op("pool", lambda: nc.gpsimd.memset(ones[:], 1.0 / D), writes=[ctk])
        vec = kb.sb("vec", [128, 16])
        vtk = kb.tok("vec")
        kb.dma("sp", vec[:], vecs[:, :], vtk, writes=[vtk])
        stage = [kb.sb("stg%d" % i, [128, D]) for i in range(2)]
        stok = kb.toks(2, "stg")
        Wo, Wo_t = load_weight_bf16(kb, "sp", w_out, 8, D, "Wo", stage, stok)
        Wg, Wg_t = load_weight_bf16(kb, "sp", w_gate, 8, D, "Wg", stage, stok)
        Wp, Wp_t = load_weight_bf16(kb, "sp", w_proj, 2, D, "Wp", stage, stok)
        zin = [kb.sb("zin%d" % i, [128, 8, TT], BF16) for i in range(2)]
        hin = [kb.sb("hin%d" % i, [128, 8, TT]) for i in range(2)]
        pin = [kb.sb("pin%d" % i, [128, 2, TT]) for i in range(2)]
        pbf = [kb.sb("pbf%d" % i, [128, 2, TT], BF16) for i in range(2)]
        z_t, h_t, p_t, pb_t = kb.toks(2, "z"), kb.toks(2, "h"), kb.toks(2, "p"), kb.toks(2, "pb")
        sq = kb.sb("sq", [128, 8, TT], BF16)
        sq_t = kb.tok("sq")
        hn = kb.sb("hn", [128, 8, TT], BF16)
        hn_t = kb.tok("hn")
        rstd = kb.sb("rstd", [128, TT])
        rstd_t = kb.tok("rstd")
        gsb = [kb.sb("gsb%d" % i, [128, TT]) for i in range(2)]
        g_t = kb.toks(2, "g")
        tmp = [kb.sb("tmp%d" % i, [128, TT]) for i in range(2)]
        tmp_t = kb.toks(2, "tmp")
        obuf = [kb.sb("obuf%d" % i, [128, 8, TT], F32 if last else BF16) for i in range(2)]
        ob_t = kb.toks(2, "ob")
        psb = [kb.ps("psb%d" % i, [128, TT]) for i in range(8)]
        ps_t = [kb.ptok("ps%d" % i) for i in range(8)]
        pc = [0]

        def nextps():
            i = pc[0] % 8
            pc[0] += 1
            return psb[i], ps_t[i]

        zv = zT.rearrange("(c p) t -> p c t", p=128)
        hv = hT.rearrange("(c p) t -> p c t", p=128)
        pv = pT.rearrange("(c p) t -> p c t", p=128)

        def norm(src, src_t, gcol0, dst_fn, dst_toks):
            for dc in range(8):
                kb.op("act", lambda dc=dc: nc.scalar.activation(out=sq[:, dc, :], in_=src[:, dc, :], func=AF.Square),
                      reads=[src_t], writes=[sq_t])
            pb_, pt_ = nextps()
            for dc in range(8):
                kb.op("pe", lambda dc=dc: nc.tensor.matmul(pb_[:], lhsT=ones[:], rhs=sq[:, dc, :], start=(dc == 0), stop=(dc == 7)),
                      reads=[sq_t, ctk], writes=[pt_])
            rstd_from_sumsq(kb, pb_[:], pt_, rstd[:], rstd_t, 1e-6)
            for dc in range(8):
                kb.op("dve", lambda dc=dc: nc.vector.scalar_tensor_tensor(
                    out=dst_fn(dc), in0=src[:, dc, :], scalar=vec[:, gcol0 + dc:gcol0 + dc + 1], in1=rstd[:],
                    op0=ALU.mult, op1=ALU.mult), reads=[src_t, rstd_t, vtk], writes=dst_toks)

        def loads(j):
            s = j % 2
            cs = slice(j * TT, (j + 1) * TT)
            kb.dma("sp", zin[s][:], zv[:, :, cs], z_t[s], writes=[z_t[s]])
            kb.dma("sp", hin[s][:], hv[:, :, cs], h_t[s], writes=[h_t[s]])
            kb.dma("sp", pin[s][:], pv[:, :, cs], p_t[s], writes=[p_t[s]])

        loads(0)
        for j in range(ntile):
            s = j % 2
            cs = slice(j * TT, (j + 1) * TT)
            if j + 1 < ntile:
                loads(j + 1)
            kb.op("pool", lambda s=s: nc.gpsimd.tensor_copy(out=pbf[s][:], in_=pin[s][:]), reads=[p_t[s]], writes=[pb_t[s]])
            h2 = hin[s]
            for dc in range(8):
                pb_, pt_ = nextps()
                for kc in range(8):
                    kb.op("pe", lambda dc=dc, kc=kc, pb_=pb_: nc.tensor.matmul(
                        pb_[:], lhsT=Wo[:, kc, dc * 128:(dc + 1) * 128], rhs=zin[s][:, kc, :], start=(kc == 0), stop=(kc == 7)),
                        reads=[Wo_t, z_t[s]], writes=[pt_])
                kb.op("dve", lambda dc=dc, pb_=pb_: nc.vector.tensor_tensor(out=h2[:, dc, :], in0=h2[:, dc, :], in1=pb_[:], op=ALU.add),
                      reads=[pt_, h_t[s]], writes=[h_t[s]])
            norm(h2, h_t[s], 0, lambda dc: hn[:, dc, :], [hn_t])
            for dc in range(8):
                pg, pgt = nextps()
                for kc in range(8):
                    kb.op("pe", lambda dc=dc, kc=kc, pg=pg: nc.tensor.matmul(
                        pg[:], lhsT=Wg[:, kc, dc * 128:(dc + 1) * 128], rhs=hn[:, kc, :], start=(kc == 0), stop=(kc == 7)),
                        reads=[Wg_t, hn_t], writes=[pgt])
                pp, ppt = nextps()
                for kc in range(2):
                    kb.op("pe", lambda dc=dc, kc=kc, pp=pp: nc.tensor.matmul(
                        pp[:], lhsT=Wp[:, kc, dc * 128:(dc + 1) * 128], rhs=pbf[s][:, kc, :], start=(kc == 0), stop=(kc == 1)),
                        reads=[Wp_t, pb_t[s]], writes=[ppt])
                b = dc % 2
                kb.op("act", lambda pg=pg, b=b: nc.scalar.activation(out=gsb[b][:], in_=pg[:], func=AF.Sigmoid),
                      reads=[pgt], writes=[g_t[b]])
                kb.op("dve", lambda pp=pp, b=b: nc.vector.tensor_tensor(out=tmp[b][:], in0=gsb[b][:], in1=pp[:], op=ALU.mult),
                      reads=[g_t[b], ppt], writes=[tmp_t[b]])
                kb.op("pool", lambda dc=dc, b=b: nc.gpsimd.tensor_tensor(out=h2[:, dc, :], in0=h2[:, dc, :], in1=tmp[b][:], op=ALU.add),
                      reads=[tmp_t[b], h_t[s]], writes=[h_t[s]])
            norm(h2, h_t[s], 8, lambda dc: obuf[s][:, dc, :], [ob_t[s]])
            if last:
                kb.dma("sp", oT.rearrange("(c p) t -> p c t", p=128)[:, :, cs], obuf[s][:], ob_t[s], reads=[ob_t[s]])
            else:
                kb.dma("sp", hnT.rearrange("(c p) t -> p c t", p=128)[:, :, cs], obuf[s][:], ob_t[s], reads=[ob_t[s]])
                kb.dma("sp", h1T.rearrange("(c p) t -> p c t", p=128)[:, :, cs], h2[:], h_t[s], reads=[h_t[s]])
        kb.finish(ob_t + h_t)
    return nc


def run_post(zT_list, hT_list, pT_list, w_out, w_gate, w_proj, pe_norm, nxt_norm, last):
    n = len(zT_list)
    NT = zT_list[0].shape[1]
    nc = build_post(NT, last)
    vecs = np.concatenate([pe_norm.reshape(8, 128).T, nxt_norm.reshape(8, 128).T], axis=1).astype(np.float32)
    maps = [dict(zT=np.ascontiguousarray(zT_list[i]), hT=np.ascontiguousarray(hT_list[i]), pT=np.ascontiguousarray(pT_list[i]),
                 w_out=w_out, w_gate=w_gate, w_proj=w_proj, vecs=np.ascontiguousarray(vecs)) for i in range(n)]
    res = run_bass_kernel_spmd(nc, maps, core_ids=list(range(n))).results
    return res


TA = 256
NCH = TA // 64
GN_EPS = 64e-5
FILL = 3
RW_STOP = 0
RW_DBG = None


def build_rwkv(NT):
    nc = new_nc()
    xT = dram_in(nc, "xT", [D, NT])
    Wc = dram_in(nc, "Wc", [D, 4 * 256])
    w1 = dram_in(nc, "w1", [D, 64])
    a1 = dram_in(nc, "a1", [D, 64])
    w2c = dram_in(nc, "w2c", [64, 256])
    a2c = dram_in(nc, "a2c", [64, 256])
    vecA = dram_in(nc, "vecA", [128, 56])
    vecC = dram_in(nc, "vecC", [128, 2 * 8])
    lnwb = dram_in(nc, "lnwb", [1, 512])
    identd = dram_in(nc, "ident", [128, 128])
    zT = dram_out(nc, "zT", [256, NT], BF16)
    dbg_d = dram_out(nc, "dbg", [128, 2048]) if RW_DBG else None
    dbg_tok = []

    def dbg(name, ap, tok, j=0):
        if RW_DBG == name and j == 0 and not dbg_tok:
            t = kb.tok("dbgt")
            dbg_tok.append(t)
            p, f = ap.shape[0], int(np.prod(ap.shape[1:]))
            kb.dma("sp", dbg_d[0:p, 0:f], ap, t, reads=[tok])
    ntile = NT // TA
    with ExitStack() as es:
        kb = KB(nc, es)
        V, S, G, P = nc.vector, nc.scalar, nc.gpsimd, nc.tensor
        ctk = kb.tok("cst")
        ident = kb.sb("ident_sb", [128, 128])
        kb.dma("sp", ident[:], identd[:, :], ctk, writes=[ctk])
        onesb = kb.sb("onesb", [128, 128], BF16)
        blk = kb.sb("blk", [128, 128], BF16)
        bind = kb.sb("bind", [128, 2])
        kb.op("pool", lambda: G.memset(onesb[:], 1.0 / D), writes=[ctk])
        kb.op("pool", lambda: G.memset(blk[:], 0.0), writes=[ctk])
        kb.op("pool", lambda: G.memset(blk[0:64, 0:64], 1.0), writes=[ctk])
        kb.op("pool", lambda: G.memset(blk[64:128, 64:128], 1.0), writes=[ctk])
        kb.op("pool", lambda: G.memset(bind[:], 0.0), writes=[ctk])
        kb.op("pool", lambda: G.memset(bind[0:64, 0:1], 1.0), writes=[ctk])
        kb.op("pool", lambda: G.memset(bind[64:128, 1:2], 1.0), writes=[ctk])
        mAT = kb.sb("mAT", [64, 4, 4, 64])
        mL = kb.sb("mL", [64, 4, 64])
        kb.op("pool", lambda: G.memset(mAT[:], 1.0), writes=[ctk])
        kb.op("pool", lambda: G.memset(mL[:], 1.0), writes=[ctk])
        for q in range(4):
            kb.op("pool", lambda q=q: G.affine_select(out=mAT[:, :, q, :], in_=mAT[:, :, q, :], pattern=[[0, 4], [1, 64]],
                                                      compare_op=(ALU.is_gt if q % 2 == 0 else ALU.is_ge), fill=0.0, base=0,
                                                      channel_multiplier=-1), writes=[ctk])
        kb.op("pool", lambda: G.affine_select(out=mL[:], in_=mL[:], pattern=[[0, 4], [-1, 64]], compare_op=ALU.is_gt, fill=0.0,
                                              base=0, channel_multiplier=1), writes=[ctk])
        vA = kb.sb("vA", [128, 56])
        vC = kb.sb("vC", [128, 2, 8])
        kb.dma("sp", vA[:], vecA[:, :], ctk, writes=[ctk])
        kb.dma("sp", vC[:].rearrange("p a b -> p (a b)"), vecC[:, :], ctk, writes=[ctk])
        lnw = kb.sb("lnw", [64, 512])
        kb.dma("sp", lnw[:], lnwb.partition_broadcast(64), ctk, writes=[ctk])
        kb.op("dve", lambda: V.tensor_scalar(out=vC[:, :, 5:6], in0=vC[:, :, 0:1], scalar1=-1.0, scalar2=None, op0=ALU.mult),
              reads=[ctk], writes=[ctk])
        kb.op("dve", lambda: V.tensor_scalar(out=vC[:, :, 6:7], in0=vC[:, :, 3:4], scalar1=-1.0, scalar2=1.0, op0=ALU.mult, op1=ALU.add),
              reads=[ctk], writes=[ctk])
        W0, A0, KK_, KA, RK, NW0, OMKA = range(7)

        xin = kb.sb("xin", [128, 8, TA])
        x_t = kb.tok("xin")
        xin_flat = xin[:].rearrange("p a b -> p (a b)")
        stage = [xin_flat[:, i * 1024:(i + 1) * 1024] for i in range(2)]
        stok = [x_t, x_t]
        omu = kb.sb("omu", [128, 48])
        kb.op("dve", lambda: V.tensor_scalar(out=omu[:], in0=vA[:, 8:56], scalar1=-1.0, scalar2=1.0, op0=ALU.mult, op1=ALU.add), reads=[ctk], writes=[ctk])
        W = kb.sb("Wm", [128, 8, 4, 2, 256], BF16)
        W_t = kb.tok("Wm")
        for dc in range(8):
            sgi = dc % 2
            kb.dma("sp", stage[sgi][:, 0:1024], Wc[dc * 128:(dc + 1) * 128, :], x_t, writes=[x_t])
            for i in range(4):
                kb.op("dve", lambda dc=dc, i=i, sgi=sgi: V.tensor_scalar(out=W[:, dc, i, 0, :], in0=stage[sgi][:, i * 256:(i + 1) * 256],
                                                                         scalar1=omu[:, i * 8 + dc:i * 8 + dc + 1], scalar2=None, op0=ALU.mult),
                      reads=[x_t, ctk], writes=[W_t])
                kb.op("dve", lambda dc=dc, i=i, sgi=sgi: V.tensor_scalar(out=W[:, dc, i, 1, :], in0=stage[sgi][:, i * 256:(i + 1) * 256],
                                                                         scalar1=vA[:, 8 + i * 8 + dc:9 + i * 8 + dc], scalar2=None, op0=ALU.mult),
                      reads=[x_t, ctk], writes=[W_t])
        W1 = kb.sb("W1", [128, 8, 2, 64], BF16)
        A1 = kb.sb("A1", [128, 8, 2, 64], BF16)
        W2 = kb.sb("W2", [64, 256], BF16)
        A2 = kb.sb("A2", [64, 256], BF16)
        for (src_d, dstw, i) in ((w1, W1, 4), (a1, A1, 5)):
            kb.dma("sp", stage[0][:, 0:512].rearrange("p (c k) -> p c k", c=8), src_d.rearrange("(c p) k -> p c k", p=128), x_t, writes=[x_t])
            for dc in range(8):
                kb.op("dve", lambda dc=dc, i=i, dstw=dstw: V.tensor_scalar(out=dstw[:, dc, 0, :], in0=stage[0][:, dc * 64:(dc + 1) * 64],
                                                                           scalar1=omu[:, i * 8 + dc:i * 8 + dc + 1], scalar2=None, op0=ALU.mult),
                      reads=[x_t, ctk], writes=[W_t])
                kb.op("dve", lambda dc=dc, i=i, dstw=dstw: V.tensor_scalar(out=dstw[:, dc, 1, :], in0=stage[0][:, dc * 64:(dc + 1) * 64],
                                                                           scalar1=vA[:, 8 + i * 8 + dc:9 + i * 8 + dc], scalar2=None, op0=ALU.mult),
                      reads=[x_t, ctk], writes=[W_t])
        kb.dma("sp", stage[0][0:64, 0:256], w2c[:, :], x_t, writes=[x_t])
        kb.op("dve", lambda: V.tensor_copy(out=W2[:], in_=stage[0][0:64, 0:256]), reads=[x_t], writes=[W_t])
        kb.dma("sp", stage[0][0:64, 0:256], a2c[:, :], x_t, writes=[x_t])
        kb.op("dve", lambda: V.tensor_copy(out=A2[:], in_=stage[0][0:64, 0:256]), reads=[x_t], writes=[W_t])

        hn = kb.sb("hn", [128, 8, TA + 1], BF16)
        hn_t = kb.tok("hn")
        kb.op("pool", lambda: G.memset(hn[:], 0.0), writes=[hn_t])
        sqx = kb.sb("sqx", [128, 8, TA], BF16)
        sqx_t = kb.tok("sqx")
        rstd = kb.sb("rstd", [128, TA])
        rstd_t = kb.tok("rstd")
        xm = [kb.sb("xm%d" % i, [128, 8, TA], BF16) for i in range(2)]
        xm_t = kb.toks(2, "xm")
        th = kb.sb("th", [64, TA], BF16)
        th_t = kb.tok("th")

        def cm(name):
            return kb.sb(name, [128, 2, TA]), kb.tok(name)
        r_, r_t = cm("r_")
        k_, k_t = cm("k_")
        v_, v_t = cm("v_")
        sg2 = [kb.sb("sg%d" % i, [128, 2, TA]) for i in range(2)]
        sg2_t = kb.toks(2, "sg")
        nlw, nlw_t = cm("nlw")
        a_, a_t = cm("a_")
        kk, kk_t = cm("kk")
        t1, t1_t = cm("t1")
        prod, prod_t = cm("prod")
        CA, CA_t = cm("CA")
        CB, CB_t = cm("CB")
        rn, rn_t = cm("rn")
        sqk = kb.sb("sqk", [128, 2, TA], BF16)
        sqk_t = kb.tok("sqk")
        AR = kb.sb("AR", [128, 2, NCH, 2, 64])
        AR_t = kb.tok("AR")
        BK = kb.sb("BK", [128, 2, NCH, 2, 64])
        BK_t = kb.tok("BK")
        def sb64(name, rest, dt, tok):
            full = kb.sb(name, [128] + list(rest), dt)
            kb.op("pool", lambda: G.memset(full[64:128], 0.0), writes=[tok])
            return full, full[0:64]
        ARh2_t = kb.toks(2, "ARh")
        BKh2_t = kb.toks(2, "BKh")
        ARh2F, ARh2 = zip(*[sb64("ARh%d" % i, [4, NCH, 2, 64], BF16, ARh2_t[i]) for i in range(2)])
        BKh2F, BKh2 = zip(*[sb64("BKh%d" % i, [4, NCH, 2, 64], BF16, BKh2_t[i]) for i in range(2)])
        WCs2 = [kb.sb("WCs%d" % i, [64, 4, NCH]) for i in range(2)]
        WCs2_t = kb.toks(2, "WCs")
        TM2_t = [kb.toks(4, "TM") for pp in range(2)]
        _tm = [[sb64("TM%d_%d" % (pp, i), [NCH, 4, 64], BF16, TM2_t[pp][i]) for i in range(4)] for pp in range(2)]
        TM2F = [[x[0] for x in row] for row in _tm]
        TM2 = [[x[1] for x in row] for row in _tm]
        TA_, TB_, TK_, TV_ = range(4)
        RKs2 = [kb.sb("RKs%d" % i, [64, NCH, 4]) for i in range(2)]
        RKs2_t = kb.toks(2, "RKs")
        Ybuf2 = [kb.sb("Ybuf%d" % i, [64, NCH, 4, 64]) for i in range(2)]
        Y2_t = kb.toks(2, "Ybuf")
        ATsk_t = kb.toks(2, "ATs")
        Lsk_t = kb.toks(2, "Ls")
        ATskF, ATsk = zip(*[sb64("ATs%d" % i, [4, 4, 64], BF16, ATsk_t[i]) for i in range(2)])
        LskF, Lsk = zip(*[sb64("Ls%d" % i, [4, 64], BF16, Lsk_t[i]) for i in range(2)])
        Z = [kb.sb("Z%d" % i, [64, 4, 128]) for i in range(2)]
        Z_t = kb.toks(2, "Z")
        Zbk_t = [kb.toks(2, "Zb") for kk_ in range(2)]
        _zb = [[sb64("Zb%d_%d" % (kk_, i), [4, 128], BF16, Zbk_t[kk_][i]) for i in range(2)] for kk_ in range(2)]
        ZbkF = [[x[0] for x in row] for row in _zb]
        Zbk = [[x[1] for x in row] for row in _zb]
        identbF = kb.sb("identb", [128, 64], BF16)
        kb.op("dve", lambda: V.tensor_copy(out=identbF[:], in_=ident[:, 0:64]), reads=[ctk], writes=[ctk])
        LBk_t = [kb.toks(2, "LB") for kk_ in range(2)]
        _lb = [[sb64("LB%d_%d" % (kk_, i), [4, 2, 64], BF16, LBk_t[kk_][i]) for i in range(2)] for kk_ in range(2)]
        LBkF = [[x[0] for x in row] for row in _lb]
        LBk = [[x[1] for x in row] for row in _lb]
        RHsk_t = kb.toks(2, "RHs")
        MTsk_t = kb.toks(2, "MTs")
        RHskF, RHsk = zip(*[sb64("RHs%d" % i, [4, 64], F32, RHsk_t[i]) for i in range(2)])
        MTskF, MTsk = zip(*[sb64("MTs%d" % i, [4, 64], F32, MTsk_t[i]) for i in range(2)])
        Hs_t = kb.toks(2, "Hs")
        HsF, Hs = zip(*[sb64("Hs%d" % i, [4, 64], F32, Hs_t[i]) for i in range(2)])
        kb.op("pool", lambda: G.memset(Hs[0][:], 0.0), writes=[Hs_t[0]])
        st1 = kb.sb("st1", [64, NCH * 4])
        st2 = kb.sb("st2", [64, NCH * 4])
        st3 = kb.sb("st3", [64, NCH * 4])
        st_t = kb.tok("st")
        zout = kb.sb("zout", [128, 2, TA], BF16)
        zo_t = kb.tok("zout")

        PR = kb.ps("PR", [128, 512]); PR_t = kb.ptok("PR")
        TIN = kb.ps("TIN", [128, 512]); TIN_t = kb.ptok("TIN")
        ATpk = [kb.ps("ATp%d" % i, [128, 512]) for i in range(2)]; ATpk_t = [kb.ptok("ATp%d" % i) for i in range(2)]
        APpk = [kb.ps("APp%d" % i, [128, 512]) for i in range(2)]; APpk_t = [kb.ptok("APp%d" % i) for i in range(2)]
        SQpk = [kb.ps("SQp%d" % i, [128, 512]) for i in range(2)]; SQpk_t = [kb.ptok("SQp%d" % i) for i in range(2)]

        xv = xT.rearrange("(c p) t -> p c t", p=128)
        hstate = [0]

        def prep(j):
            p = j % 2
            cs = slice(j * TA, (j + 1) * TA)
            ARh, ARh_t, BKh, BKh_t = ARh2[p], ARh2_t[p], BKh2[p], BKh2_t[p]
            TM, TM_t = TM2[p], TM2_t[p]
            WCs, WCs_t, RKs, RKs_t = WCs2[p], WCs2_t[p], RKs2[p], RKs2_t[p]
            sg, sg_t = sg2[p], sg2_t[p]
            Ybuf, Y_t = Ybuf2[p], Y2_t[p]
            kb.dma("sp", xin[:], xv[:, :, cs], x_t, writes=[x_t])
            if j > 0:
                kb.op("pool", lambda: G.tensor_copy(out=hn[:, :, 0:1], in_=hn[:, :, TA:TA + 1]), reads=[hn_t], writes=[hn_t])
            for dc in range(8):
                kb.op("act", lambda dc=dc: S.activation(out=sqx[:, dc, :], in_=xin[:, dc, :], func=AF.Square), reads=[x_t], writes=[sqx_t])
            for dc in range(8):
                kb.op("pe", lambda dc=dc: P.matmul(PR[:, 0:TA], lhsT=onesb[:], rhs=sqx[:, dc, :], start=(dc == 0), stop=(dc == 7)),
                      reads=[sqx_t, ctk], writes=[PR_t])
            kb.op("act", lambda: S.activation(out=rstd[:], in_=PR[:, 0:TA], func=AF.Ln, bias=1e-6, scale=1.0), reads=[PR_t], writes=[rstd_t])
            kb.op("act", lambda: S.activation(out=rstd[:], in_=rstd[:], func=AF.Exp, scale=-0.5), reads=[rstd_t], writes=[rstd_t])
            for dc in range(8):
                kb.op("dve", lambda dc=dc: V.scalar_tensor_tensor(out=hn[:, dc, 1:TA + 1], in0=xin[:, dc, :], scalar=vA[:, dc:dc + 1],
                                                                 in1=rstd[:], op0=ALU.mult, op1=ALU.mult),
                      reads=[x_t, rstd_t, ctk], writes=[hn_t])
            for i in range(6):
                if i < 4:
                    dst, dst_t = ((r_, r_t), (k_, k_t), (v_, v_t), (sg, sg_t))[i]
                    for hc in range(2):
                        for dc in range(8):
                            kb.op("pe", lambda dc=dc, hc=hc, i=i: P.matmul(
                                PR[:, 0:TA], lhsT=W[:, dc, i, 0, hc * 128:hc * 128 + 128], rhs=hn[:, dc, 1:TA + 1],
                                start=(dc == 0), stop=False), reads=[W_t, hn_t], writes=[PR_t])
                            kb.op("pe", lambda dc=dc, hc=hc, i=i: P.matmul(
                                PR[:, 0:TA], lhsT=W[:, dc, i, 1, hc * 128:hc * 128 + 128], rhs=hn[:, dc, 0:TA],
                                start=False, stop=(dc == 7)), reads=[W_t, hn_t], writes=[PR_t])
                        if i == 3:
                            kb.op("act", lambda hc=hc, dst=dst: S.activation(out=dst[:, hc, :], in_=PR[:, 0:TA], func=AF.Silu),
                                  reads=[PR_t], writes=[dst_t])
                        else:
                            kb.op("dve", lambda hc=hc, dst=dst: V.tensor_copy(out=dst[:, hc, :], in_=PR[:, 0:TA]),
                                  reads=[PR_t], writes=[dst_t])
                else:
                    Wl, W2l = (W1, W2) if i == 4 else (A1, A2)
                    for dc in range(8):
                        kb.op("pe", lambda dc=dc, Wl=Wl: P.matmul(PR[0:64, 0:TA], lhsT=Wl[:, dc, 0, :], rhs=hn[:, dc, 1:TA + 1],
                                                                 start=(dc == 0), stop=False), reads=[W_t, hn_t], writes=[PR_t])
                        kb.op("pe", lambda dc=dc, Wl=Wl: P.matmul(PR[0:64, 0:TA], lhsT=Wl[:, dc, 1, :], rhs=hn[:, dc, 0:TA],
                                                                 start=False, stop=(dc == 7)), reads=[W_t, hn_t], writes=[PR_t])
                    kb.op("act", lambda i=i: S.activation(out=th[:], in_=PR[0:64, 0:TA], func=(AF.Tanh if i == 4 else AF.Copy)),
                          reads=[PR_t], writes=[th_t])
                    for hc in range(2):
                        kb.op("pe", lambda hc=hc, W2l=W2l: P.matmul(PR[:, 0:TA], lhsT=W2l[:, hc * 128:(hc + 1) * 128], rhs=th[:],
                                                                    start=True, stop=True), reads=[W_t, th_t], writes=[PR_t])
                        if i == 4:
                            kb.op("act", lambda hc=hc: S.activation(out=nlw[:, hc, :], in_=PR[:, 0:TA], func=AF.Exp, scale=-1.0,
                                                                    bias=vC[:, hc, NW0:NW0 + 1]), reads=[PR_t, ctk], writes=[nlw_t])
                            kb.op("act", lambda hc=hc: S.activation(out=nlw[:, hc, :], in_=nlw[:, hc, :], func=AF.Ln, scale=1.0, bias=1.0),
                                  reads=[nlw_t], writes=[nlw_t])
                            kb.op("act", lambda hc=hc: S.activation(out=nlw[:, hc, :], in_=nlw[:, hc, :], func=AF.Exp, scale=-1.0, bias=-0.5),
                                  reads=[nlw_t], writes=[nlw_t])
                        else:
                            kb.op("act", lambda hc=hc: S.activation(out=a_[:, hc, :], in_=PR[:, 0:TA], func=AF.Sigmoid, scale=1.0,
                                                                    bias=vC[:, hc, A0:A0 + 1]), reads=[PR_t, ctk], writes=[a_t])
            for hc in range(2):
                kb.op("dve", lambda hc=hc: V.tensor_scalar(out=kk[:, hc, :], in0=k_[:, hc, :], scalar1=vC[:, hc, KK_:KK_ + 1], scalar2=None,
                                                           op0=ALU.mult), reads=[k_t, ctk], writes=[kk_t])
                kb.op("act", lambda hc=hc: S.activation(out=sqk[:, hc, :], in_=kk[:, hc, :], func=AF.Square), reads=[kk_t], writes=[sqk_t])
                kb.op("pe", lambda hc=hc: P.matmul(PR[:, 0:TA], lhsT=blk[:], rhs=sqk[:, hc, :], start=True, stop=True),
                      reads=[sqk_t, ctk], writes=[PR_t])
                kb.op("act", lambda hc=hc: S.activation(out=rn[:, hc, :], in_=PR[:, 0:TA], func=AF.Ln, bias=1e-24, scale=1.0),
                      reads=[PR_t], writes=[rn_t])
                kb.op("act", lambda hc=hc: S.activation(out=rn[:, hc, :], in_=rn[:, hc, :], func=AF.Exp, scale=-0.5), reads=[rn_t], writes=[rn_t])
            kb.op("dve", lambda: V.tensor_tensor(out=kk[:], in0=kk[:], in1=rn[:], op=ALU.mult), reads=[kk_t, rn_t], writes=[kk_t])
            for hc in range(2):
                kb.op("dve", lambda hc=hc: V.tensor_scalar(out=t1[:, hc, :], in0=a_[:, hc, :], scalar1=vC[:, hc, KA:KA + 1],
                                                           scalar2=vC[:, hc, OMKA:OMKA + 1], op0=ALU.mult, op1=ALU.add),
                      reads=[a_t, ctk], writes=[t1_t])
            kb.op("dve", lambda: V.tensor_tensor(out=t1[:], in0=t1[:], in1=k_[:], op=ALU.mult), reads=[t1_t, k_t], writes=[t1_t])
            kb.op("pool", lambda: G.tensor_tensor(out=a_[:], in0=a_[:], in1=kk[:], op=ALU.mult), reads=[a_t, kk_t], writes=[a_t])
            for hc in range(2):
                kb.op("dve", lambda hc=hc: V.scalar_tensor_tensor(out=prod[:, hc, :], in0=r_[:, hc, :], scalar=vC[:, hc, RK:RK + 1],
                                                                 in1=t1[:, hc, :], op0=ALU.mult, op1=ALU.mult),
                      reads=[r_t, t1_t, ctk], writes=[prod_t])
            def v3(t):
                return t[:].rearrange("p a (c t) -> p (a c) t", t=64)
            src, src_t = nlw, nlw_t
            pp = [(CA, CA_t), (CB, CB_t)]
            for li, sft in enumerate((1, 2, 4, 8, 16, 32)):
                dst, dst_t = pp[li % 2]
                kb.op("pool", lambda src=src, dst=dst, sft=sft: G.tensor_tensor(out=v3(dst)[:, :, sft:], in0=v3(src)[:, :, sft:],
                                                                               in1=v3(src)[:, :, 0:64 - sft], op=ALU.add),
                      reads=[src_t], writes=[dst_t])
                kb.op("pool", lambda src=src, dst=dst, sft=sft: G.tensor_copy(out=v3(dst)[:, :, 0:sft], in_=v3(src)[:, :, 0:sft]),
                      reads=[src_t], writes=[dst_t])
                src, src_t = dst, dst_t
            cn, cn_t = src, src_t
            assert cn is CB
            kb.op("pool", lambda: G.tensor_tensor(out=nlw[:], in0=cn[:], in1=nlw[:], op=ALU.subtract), reads=[cn_t, nlw_t], writes=[nlw_t])
            kb.op("act", lambda: S.activation(out=CA[:], in_=cn[:], func=AF.Exp, scale=-1.0), reads=[cn_t], writes=[CA_t])
            kb.op("act", lambda: S.activation(out=nlw[:], in_=nlw[:], func=AF.Exp, scale=-1.0), reads=[nlw_t], writes=[nlw_t])
            kb.op("act", lambda: S.activation(out=CB[:], in_=cn[:], func=AF.Exp, scale=1.0), reads=[cn_t], writes=[CB_t])
            eneg, eneg_t, enegx, enegx_t, epos, epos_t = CA, CA_t, nlw, nlw_t, CB, CB_t

            def c3(t, hc):
                return t[:, hc, :].rearrange("p (c t) -> p c t", t=64)
            for hc in range(2):
                kb.op("dve", lambda hc=hc: V.tensor_tensor(out=AR[:, hc, :, 0, :], in0=c3(kk, hc), in1=c3(enegx, hc), op=ALU.mult),
                      reads=[kk_t, enegx_t], writes=[AR_t])
                kb.op("pool", lambda hc=hc: G.tensor_tensor(out=AR[:, hc, :, 1, :], in0=c3(r_, hc), in1=c3(eneg, hc), op=ALU.mult),
                      reads=[r_t, eneg_t], writes=[AR_t])
                kb.op("dve", lambda hc=hc: V.scalar_tensor_tensor(out=BK[:, hc, :, 0, :], in0=c3(a_, hc), scalar=-1.0, in1=c3(epos, hc),
                                                                 op0=ALU.mult, op1=ALU.mult), reads=[a_t, epos_t], writes=[BK_t])
                kb.op("pool", lambda hc=hc: G.tensor_tensor(out=BK[:, hc, :, 1, :], in0=c3(t1, hc), in1=c3(epos, hc), op=ALU.mult),
                      reads=[t1_t, epos_t], writes=[BK_t])
                for hh in range(2):
                    h = 2 * hc + hh
                    kb.op("dve", lambda hc=hc, hh=hh, h=h: V.tensor_copy(out=WCs[:, h, :], in_=c3(eneg, hc)[hh * 64:(hh + 1) * 64, :, 63]),
                          reads=[eneg_t], writes=[WCs_t])
                    kb.op("dve", lambda hc=hc, hh=hh, h=h: V.tensor_copy(out=ARh[:, h, :, :, :].rearrange("p c a t -> p (c a t)"),
                                                                         in_=AR[hh * 64:(hh + 1) * 64, hc, :, :, :].rearrange("p c a t -> p (c a t)")),
                          reads=[AR_t], writes=[ARh_t])
                    kb.op("dve", lambda hc=hc, hh=hh, h=h: V.tensor_copy(out=BKh[:, h, :, :, :].rearrange("p c a t -> p (c a t)"),
                                                                         in_=BK[hh * 64:(hh + 1) * 64, hc, :, :, :].rearrange("p c a t -> p (c a t)")),
                          reads=[BK_t], writes=[BKh_t])
            srcs = [(lambda hc, c: AR[:, hc, c, 0, :], AR_t), (lambda hc, c: BK[:, hc, c, 0, :], BK_t),
                    (lambda hc, c: BK[:, hc, c, 1, :], BK_t), (lambda hc, c: v_[:, hc, c * 64:(c + 1) * 64], v_t)]
            ne = 0
            for q in range(4):
                fn, ft = srcs[q]
                for c0 in range(0, NCH, 2):
                    for cc in range(2):
                        for hc in range(2):
                            kb.op("pe", lambda fn=fn, c0=c0, cc=cc, hc=hc: P.transpose(
                                TIN[0:64, cc * 256 + hc * 128:cc * 256 + hc * 128 + 128], fn(hc, c0 + cc), ident[:]),
                                reads=[ft, ctk], writes=[TIN_t])
                    e = "act" if ne % 2 == 0 else "dve"
                    ne += 1
                    dstv = TM[q][:, c0:c0 + 2, :, :].rearrange("p c h n -> p (c h n)")
                    if e == "act":
                        kb.op("act", lambda dstv=dstv: S.copy(out=dstv, in_=TIN[0:64, :]), reads=[TIN_t], writes=[TM_t[q]])
                    else:
                        kb.op("dve", lambda dstv=dstv: V.tensor_copy(out=dstv, in_=TIN[0:64, :]), reads=[TIN_t], writes=[TM_t[q]])
            for c in range(NCH):
                for hc in range(2):
                    kb.op("pe", lambda c=c, hc=hc: P.matmul(TIN[0:64, c * 4 + hc * 2:c * 4 + hc * 2 + 2], lhsT=prod[:, hc, c * 64:(c + 1) * 64],
                                                            rhs=bind[:], start=True, stop=True), reads=[prod_t, ctk], writes=[TIN_t])
            kb.op("dve", lambda: V.tensor_copy(out=RKs[:].rearrange("p c h -> p (c h)"), in_=TIN[0:64, 0:NCH * 4]), reads=[TIN_t], writes=[RKs_t])

        hseq = [0]

        def chunk_gen(j, c, k):
            p = j % 2
            ARh, ARh_t, BKh, BKh_t = ARh2[p], ARh2_t[p], BKh2[p], BKh2_t[p]
            TM, TM_t = TM2[p], TM2_t[p]
            WCs, WCs_t = WCs2[p], WCs2_t[p]
            Ybuf, Y_t = Ybuf2[p], Y2_t[p]
            ATp, ATp_t, APp, APp_t, SQp, SQp_t = ATpk[k], ATpk_t[k], APpk[k], APpk_t[k], SQpk[k], SQpk_t[k]
            ATs, ATs_t, Ls, Ls_t = ATsk[k], ATsk_t[k], Lsk[k], Lsk_t[k]
            Zb, Zb_t, LB, LB_t = Zbk[k], Zbk_t[k], LBk[k], LBk_t[k]
            RHs, RHs_t, MTs, MTs_t = RHsk[k], RHsk_t[k], MTsk[k], MTsk_t[k]
            ARhF, BKhF, TMF, ATsF, LsF, ZbF, LBF, RHsF, MTsF = ARh2F[p], BKh2F[p], TM2F[p], ATskF[k], LskF[k], ZbkF[k], LBkF[k], RHskF[k], MTskF[k]
            def opnd(h):
                return h // 2, 64 * (h % 2)
            for half in range(2):
                for hl in range(2):
                    h = 2 * half + hl
                    hc, pb = opnd(h)
                    rhs = ARhF[:, h, c, :, :].rearrange("p a t -> p (a t)")
                    kb.op("pe", lambda hl=hl, h=h, rhs=rhs: P.matmul(ATp[0:64, hl * 256:hl * 256 + 128], lhsT=BKhF[:, h, c, 0, :],
                                                                             rhs=rhs, start=True, stop=True), reads=[ARh_t, BKh_t], writes=[ATp_t])
                    kb.op("pe", lambda hl=hl, h=h, rhs=rhs: P.matmul(ATp[0:64, hl * 256 + 128:hl * 256 + 256], lhsT=BKhF[:, h, c, 1, :],
                                                                             rhs=rhs, start=True, stop=True), reads=[ARh_t, BKh_t], writes=[ATp_t])
                kb.op("dve", lambda half=half: V.tensor_tensor(
                    out=ATs[:, 2 * half:2 * half + 2, :, :].rearrange("p h q t -> p (h q t)"), in0=ATp[0:64, :],
                    in1=mAT[:, 2 * half:2 * half + 2, :, :].rearrange("p h q t -> p (h q t)"), op=ALU.mult),
                    reads=[ATp_t, ctk], writes=[ATs_t])
                yield
            for h in range(4):
                hc, pb = opnd(h)
                kb.op("pe", lambda h=h, hc=hc, pb=pb: P.matmul(SQp[0:64, h * 64:(h + 1) * 64], lhsT=ARhF[:, h, c, 0, :],
                                                               rhs=BKhF[:, h, c, 0, :], start=True, stop=True),
                      reads=[ARh_t, BKh_t], writes=[SQp_t])
            kb.op("dve", lambda: V.tensor_tensor(out=Ls[:].rearrange("p h s -> p (h s)"), in0=SQp[0:64, 0:256],
                                                 in1=mL[:].rearrange("p h s -> p (h s)"), op=ALU.mult), reads=[SQp_t, ctk], writes=[Ls_t])
            yield
            for h in range(4):
                kb.op("pe", lambda h=h: P.matmul(APp[0:64, h * 128 + 64:h * 128 + 128], lhsT=ATsF[:, h, 2, :], rhs=TMF[TV_][:, c, h, :],
                                                 start=(h == 0), stop=False, skip_group_check=True), reads=[ATs_t, TM_t[TV_]], writes=[APp_t])
                kb.op("pe", lambda h=h: P.matmul(APp[0:64, h * 128:h * 128 + 64], lhsT=identbF[:], rhs=TMF[TA_][:, c, h, :],
                                                 start=False, stop=False, skip_group_check=True), reads=[ctk, TM_t[TA_]], writes=[APp_t])
            zc = 0
            kb.op("act", lambda: S.copy(out=Zb[0][:].rearrange("p h x -> p (h x)"), in_=APp[0:64, :]), reads=[APp_t], writes=[Zb_t[0]])
            yield
            Bm = lambda h: ATsF[:, h, 0, :]
            Lm = lambda h: LsF[:, h, :]
            Bm_t, Lm_t = ATs_t, Ls_t
            for lvl in range(6):
                for h in range(4):
                    kb.op("pe", lambda h=h, Bm=Bm, zc=zc: P.matmul(APp[0:64, h * 128:(h + 1) * 128], lhsT=Bm(h), rhs=ZbF[zc][:, h, :],
                                                                   start=False, stop=(lvl == 5), skip_group_check=True), reads=[Bm_t, Zb_t[zc]], writes=[APp_t])
                if lvl < 5:
                    for h in range(4):
                        kb.op("pe", lambda h=h, Bm=Bm, Lm=Lm: P.matmul(SQp[0:64, h * 128:h * 128 + 64], lhsT=Bm(h), rhs=Lm(h), start=True, stop=True),
                              reads=[Bm_t, Lm_t], writes=[SQp_t])
                        kb.op("pe", lambda h=h, Bm=Bm, Lm=Lm: P.matmul(SQp[0:64, h * 128 + 64:h * 128 + 128], lhsT=Lm(h), rhs=Bm(h), start=True, stop=True),
                              reads=[Bm_t, Lm_t], writes=[SQp_t])
                kb.op("dve", lambda zc=zc: V.tensor_copy(out=Zb[1 - zc][:].rearrange("p h x -> p (h x)"), in_=APp[0:64, :]),
                      reads=[APp_t], writes=[Zb_t[1 - zc]])
                yield
                zc = 1 - zc
                if lvl < 5:
                    nb = lvl % 2
                    kb.op("act", lambda nb=nb: S.copy(out=LB[nb][:].rearrange("p h q s -> p (h q s)"), in_=SQp[0:64, :]),
                          reads=[SQp_t], writes=[LB_t[nb]])
                    yield
                    Lm = lambda h, nb=nb: LBF[nb][:, h, 0, :]
                    Bm = lambda h, nb=nb: LBF[nb][:, h, 1, :]
                    Bm_t = Lm_t = LB_t[nb]
            Zf, Zf_t = Zb[zc], Zb_t[zc]
            ZfF = ZbF[zc]
            for h in range(4):
                kb.op("pe", lambda h=h: P.matmul(ATp[0:64, h * 64:(h + 1) * 64], lhsT=ZfF[:, h, 0:64], rhs=ATsF[:, h, 1, :], start=True, stop=True),
                      reads=[Zf_t, ATs_t], writes=[ATp_t])
            kb.op("dve", lambda: V.tensor_tensor(out=RHs[:], in0=ATp[0:64, 0:256].rearrange("p (h t) -> p h t", h=4), in1=ARh[:, :, c, 1, :], op=ALU.add),
                  reads=[ATp_t, ARh_t], writes=[RHs_t])
            yield
            for h in range(4):
                kb.op("pe", lambda h=h: P.matmul(ATp[0:64, 256 + h * 64:256 + (h + 1) * 64], lhsT=ZfF[:, h, 0:64], rhs=TMF[TB_][:, c, h, :], start=True, stop=False),
                      reads=[Zf_t, TM_t[TB_]], writes=[ATp_t])
                kb.op("pe", lambda h=h: P.matmul(ATp[0:64, 256 + h * 64:256 + (h + 1) * 64], lhsT=identbF[:], rhs=identbF[:], start=False, stop=True),
                      reads=[ctk], writes=[ATp_t])
            kb.op("act", lambda: S.copy(out=MTs[:].rearrange("p h n -> p (h n)"), in_=ATp[0:64, 256:512]), reads=[ATp_t], writes=[MTs_t])
            yield
            hcur, hnew = hstate[0], 1 - hstate[0]
            while hseq[0] != j * NCH + c:
                yield
            hcur, hnew = hstate[0], 1 - hstate[0]
            for h in range(4):
                o = SQp[0:64, h * 64:(h + 1) * 64]
                kb.op("pe", lambda h=h, o=o: P.matmul(o, lhsT=RHsF[:, h, :], rhs=HsF[hcur][:, h, :], start=True, stop=False),
                      reads=[RHs_t, Hs_t[hcur]], writes=[SQp_t])
                kb.op("pe", lambda h=h, o=o: P.matmul(o, lhsT=ATsF[:, h, 1, :], rhs=ZfF[:, h, 64:128], start=False, stop=False),
                      reads=[ATs_t, Zf_t], writes=[SQp_t])
                kb.op("pe", lambda h=h, o=o: P.matmul(o, lhsT=ATsF[:, h, 3, :], rhs=TMF[TV_][:, c, h, :], start=False, stop=True),
                      reads=[ATs_t, TM_t[TV_]], writes=[SQp_t])
            kb.op("act", lambda: S.copy(out=Ybuf[:, c, :, :].rearrange("p h v -> p (h v)"), in_=SQp[0:64, 0:256]), reads=[SQp_t], writes=[Y_t])
            yield
            for h in range(4):
                o = SQp[0:64, 256 + h * 64:256 + (h + 1) * 64]
                kb.op("pe", lambda h=h, o=o: P.matmul(o, lhsT=MTsF[:, h, :], rhs=HsF[hcur][:, h, :], start=True, stop=False),
                      reads=[MTs_t, Hs_t[hcur]], writes=[SQp_t])
                kb.op("pe", lambda h=h, o=o: P.matmul(o, lhsT=TMF[TB_][:, c, h, :], rhs=ZfF[:, h, 64:128], start=False, stop=False),
                      reads=[TM_t[TB_], Zf_t], writes=[SQp_t])
                kb.op("pe", lambda h=h, o=o: P.matmul(o, lhsT=TMF[TK_][:, c, h, :], rhs=TMF[TV_][:, c, h, :], start=False, stop=True),
                      reads=[TM_t[TK_], TM_t[TV_]], writes=[SQp_t])
            kb.op("dve", lambda: V.tensor_tensor(out=Hs[hnew][:], in0=SQp[0:64, 256:512].rearrange("p (h v) -> p h v", h=4),
                                                 in1=WCs[:, :, c:c + 1].broadcast_to([64, 4, 64]), op=ALU.mult),
                  reads=[SQp_t, WCs_t], writes=[Hs_t[hnew]])
            yield
            hstate[0] = hnew
            hseq[0] += 1

        def post9(j):
            p = j % 2
            cs = slice(j * TA, (j + 1) * TA)
            ARh, ARh_t, BKh, BKh_t = ARh2[p], ARh2_t[p], BKh2[p], BKh2_t[p]
            TM, TM_t = TM2[p], TM2_t[p]
            WCs, WCs_t, RKs, RKs_t = WCs2[p], WCs2_t[p], RKs2[p], RKs2_t[p]
            sg, sg_t = sg2[p], sg2_t[p]
            Ybuf, Y_t = Ybuf2[p], Y2_t[p]
            Yv = Ybuf[:].rearrange("p c h v -> p (c h) v")
            W1b = AR[0:64, :, :, :, :].rearrange("p a c q t -> p (a c q t)").rearrange("p (c h v) -> p c h v", c=NCH, h=4)
            W2b = BK[0:64, :, :, :, :].rearrange("p a c q t -> p (a c q t)").rearrange("p (c h v) -> p c h v", c=NCH, h=4)
            W1_t, W2_t = AR_t, BK_t
            W1v = W1b.rearrange("p c h v -> p (c h) v")
            W2v = W2b.rearrange("p c h v -> p (c h) v")
            kb.op("dve", lambda: V.tensor_reduce(out=st1[:], in_=Yv, axis=AX.X, op=ALU.add), reads=[Y_t], writes=[st_t])
            kb.op("act", lambda: S.activation(out=W1b.rearrange("p c h v -> p (c h v)"), in_=Ybuf[:].rearrange("p c h v -> p (c h v)"), func=AF.Square),
                  reads=[Y_t], writes=[W1_t])
            kb.op("dve", lambda: V.tensor_reduce(out=st2[:], in_=W1v, axis=AX.X, op=ALU.add), reads=[W1_t], writes=[st_t])
            kb.op("dve", lambda: V.tensor_scalar(out=st1[:], in0=st1[:], scalar1=1.0 / 64, scalar2=None, op0=ALU.mult), reads=[st_t], writes=[st_t])
            kb.op("dve", lambda: V.tensor_tensor(out=st3[:], in0=st1[:], in1=st1[:], op=ALU.mult), reads=[st_t], writes=[st_t])
            kb.op("dve", lambda: V.tensor_scalar(out=st2[:], in0=st2[:], scalar1=1.0 / 64, scalar2=None, op0=ALU.mult), reads=[st_t], writes=[st_t])
            kb.op("dve", lambda: V.tensor_tensor(out=st2[:], in0=st2[:], in1=st3[:], op=ALU.subtract), reads=[st_t], writes=[st_t])
            kb.op("dve", lambda: V.tensor_scalar(out=st2[:], in0=st2[:], scalar1=0.0, scalar2=None, op0=ALU.max), reads=[st_t], writes=[st_t])
            kb.op("act", lambda: S.activation(out=st2[:], in_=st2[:], func=AF.Ln, bias=GN_EPS, scale=1.0), reads=[st_t], writes=[st_t])
            kb.op("act", lambda: S.activation(out=st2[:], in_=st2[:], func=AF.Exp, scale=-0.5), reads=[st_t], writes=[st_t])
            bc = lambda t: t[:].unsqueeze(2).broadcast_to([64, NCH * 4, 64])
            kb.op("dve", lambda: V.tensor_tensor(out=W1v, in0=Yv, in1=bc(st1), op=ALU.subtract), reads=[Y_t, st_t], writes=[W1_t])
            kb.op("dve", lambda: V.tensor_tensor(out=W1v, in0=W1v, in1=bc(st2), op=ALU.mult), reads=[st_t], writes=[W1_t])
            lw_b = lnw[:, 0:256].unsqueeze(1).broadcast_to([64, NCH, 256])
            lb_b = lnw[:, 256:512].unsqueeze(1).broadcast_to([64, NCH, 256])
            W1c = W1b.rearrange("p c h v -> p c (h v)")
            W2c = W2b.rearrange("p c h v -> p c (h v)")
            kb.op("pool", lambda: G.tensor_tensor(out=W1c, in0=W1c, in1=lw_b, op=ALU.mult), reads=[ctk], writes=[W1_t])
            kb.op("pool", lambda: G.tensor_tensor(out=W1c, in0=W1c, in1=lb_b, op=ALU.add), reads=[ctk], writes=[W1_t])
            kb.op("dve", lambda: V.tensor_tensor(out=W2v, in0=TM[TV_][:].rearrange("p c h v -> p (c h) v"),
                                                 in1=RKs[:].rearrange("p c h -> p (c h)").unsqueeze(2).broadcast_to([64, NCH * 4, 64]), op=ALU.mult),
                  reads=[TM_t[TV_], RKs_t], writes=[W2_t])
            kb.op("dve", lambda: V.tensor_tensor(out=W1v, in0=W1v, in1=W2v, op=ALU.add), reads=[W2_t], writes=[W1_t])
            for hc in range(2):
                for c in range(NCH):
                    kb.op("pe", lambda hc=hc, c=c: P.transpose(PR[:, c * 64:(c + 1) * 64], W1b[:, c, 2 * hc:2 * hc + 2, :].rearrange("p h v -> p (h v)"),
                                                               ident[0:64, 0:64]), reads=[W1_t, ctk], writes=[PR_t])
                kb.op("dve", lambda hc=hc: V.tensor_tensor(out=zout[:, hc, :], in0=PR[:, 0:TA], in1=sg[:, hc, :], op=ALU.mult),
                      reads=[PR_t, sg_t], writes=[zo_t])
            kb.dma("sp", zT.rearrange("(c p) t -> p c t", p=128)[:, :, cs], zout[:], zo_t, reads=[zo_t])

        coop = Coop(kb)
        prep(0)
        if ntile > 1:
            prep(1)
        pending = [(j, c) for j in range(ntile) for c in range(NCH)]
        slots = [None, None]
        sid = [None, None]
        idx = 0
        while idx < len(pending) or slots[0] is not None or slots[1] is not None:
            for k in range(2):
                if slots[k] is None and idx < len(pending):
                    jn, cn = pending[idx]
                    other = sid[1 - k]
                    if cn == 0 and jn > 1:
                        coop.finish()
                    slots[k] = chunk_gen(jn, cn, k)
                    sid[k] = (jn, cn)
                    idx += 1
                if slots[k] is not None:
                    try:
                        next(slots[k])
                    except StopIteration:
                        jj, cc = sid[k]
                        slots[k] = None
                        sid[k] = None
                        if cc == NCH - 1:
                            coop.finish()

                            def filler(jj=jj):
                                post9(jj)
                                if jj + 2 < ntile:
                                    prep(jj + 2)
                            coop.start(filler)
                    coop.step(FILL)
        coop.finish()
        kb.finish([zo_t] + dbg_tok)
    return nc


def rwkv_maps(x, I, NT):
    maps = []
    Wfull = I["rw_w_in"][0].reshape(D, 4, 1024)
    vecA = np.concatenate([I["rw_norm"][0].reshape(8, 128).T] + [I["rw_mu"][0][i].reshape(8, 128).T for i in range(6)], axis=1)
    ident = np.eye(128, dtype=np.float32)
    for c in range(8):
        b, g = c // 4, c % 4
        cs = slice(g * 256, (g + 1) * 256)
        vc = np.zeros((128, 2, 8), np.float32)
        for vi, nm in enumerate(["rw_w0", "rw_a0", "rw_k_k", "rw_k_a"]):
            vc[:, :, vi] = I[nm][0][cs].reshape(2, 128).T
        vc[:, :, 4] = I["rw_r_k"][0].reshape(1024)[cs].reshape(2, 128).T
        maps.append(dict(
            xT=np.ascontiguousarray(x[b, :NT].T),
            Wc=np.ascontiguousarray(Wfull[:, :, cs].reshape(D, 1024)),
            w1=I["rw_w1"][0], a1=I["rw_a1"][0],
            w2c=np.ascontiguousarray(I["rw_w2"][0][:, cs]), a2c=np.ascontiguousarray(I["rw_a2"][0][:, cs]),
            vecA=np.ascontiguousarray(vecA.astype(np.float32)), vecC=np.ascontiguousarray(vc.reshape(128, 16)),
            lnwb=np.ascontiguousarray(np.concatenate([I["rw_ln_w"][0][cs], I["rw_ln_b"][0][cs]])[None, :]),
            ident=ident))
    return maps


LAM_INIT = 0.8 - 0.6 * math.exp(-0.3 * 1)


def build_attn(NT):
    nc = new_nc()
    hnT = dram_in(nc, "hnT", [D, NT], BF16)
    Wd = dram_in(nc, "Wd", [D, 4 * 256])
    sub = dram_in(nc, "sub", [1, 128])
    lqk = dram_in(nc, "lqk", [1, 256])
    invf = dram_in(nc, "invf", [128, 1])
    pos0 = dram_in(nc, "pos0", [1, TT])
    cosd = dram_in(nc, "cosd", [128, NT])
    sind = dram_in(nc, "sind", [128, NT])
    identd = dram_in(nc, "ident", [128, 128])
    zT = dram_out(nc, "zT", [256, NT], BF16)
    ntile = NT // TT
    nblk = NT // 128
    PI = math.pi
    with ExitStack() as es:
        kb = KB(nc, es)
        V, S, G, P = nc.vector, nc.scalar, nc.gpsimd, nc.tensor
        ctk = kb.tok("cst")
        ident = kb.sb("ident_sb", [128, 128])
        kb.dma("sp", ident[:], identd[:, :], ctk, writes=[ctk])
        tri = kb.sb("tri", [128, 128], BF16)
        kb.op("pool", lambda: G.memset(tri[:], 1.0), writes=[ctk])
        kb.op("pool", lambda: G.affine_select(out=tri[:], in_=tri[:], pattern=[[1, 128]], compare_op=ALU.is_ge, fill=0.0, base=0,
                                              channel_multiplier=-1), writes=[ctk])
        subb = kb.sb("subb", [128, 128])
        kb.dma("sp", subb[:], sub.partition_broadcast(128), ctk, writes=[ctk])
        kb.op("dve", lambda: V.tensor_scalar(out=subb[:], in0=subb[:], scalar1=1.0 - LAM_INIT, scalar2=None, op0=ALU.mult), reads=[ctk], writes=[ctk])
        lq = kb.sb("lq", [128, 256])
        kb.dma("sp", lq[:], lqk.partition_broadcast(128), ctk, writes=[ctk])
        lam = kb.sb("lam", [128, 4])
        kb.op("dve", lambda: V.tensor_tensor(out=lq[:, 0:64], in0=lq[:, 0:64], in1=lq[:, 64:128], op=ALU.mult), reads=[ctk], writes=[ctk])
        kb.op("dve", lambda: V.tensor_tensor(out=lq[:, 128:192], in0=lq[:, 128:192], in1=lq[:, 192:256], op=ALU.mult), reads=[ctk], writes=[ctk])
        kb.op("dve", lambda: V.tensor_reduce(out=lam[:, 0:1], in_=lq[:, 0:64], axis=AX.X, op=ALU.add), reads=[ctk], writes=[ctk])
        kb.op("dve", lambda: V.tensor_reduce(out=lam[:, 1:2], in_=lq[:, 128:192], axis=AX.X, op=ALU.add), reads=[ctk], writes=[ctk])
        kb.op("act", lambda: S.activation(out=lam[:, 0:2], in_=lam[:, 0:2], func=AF.Exp), reads=[ctk], writes=[ctk])
        kb.op("dve", lambda: V.tensor_tensor(out=lam[:, 2:3], in0=lam[:, 1:2], in1=lam[:, 0:1], op=ALU.subtract), reads=[ctk], writes=[ctk])
        kb.op("dve", lambda: V.tensor_scalar(out=lam[:, 3:4], in0=lam[:, 2:3], scalar1=-LAM_INIT, scalar2=None, op0=ALU.add), reads=[ctk], writes=[ctk])
        ivf = kb.sb("ivf", [128, 1])
        kb.dma("sp", ivf[:], invf[:, :], ctk, writes=[ctk])
        ngpi = kb.sb("ngpi", [128, 1])
        kb.op("pool", lambda: G.memset(ngpi[:], -PI), writes=[ctk])

        QT2 = [kb.sb("QT%d" % i, [128, NT], BF16) for i in range(2)]
        QT_t = kb.tok("QT")
        for i in range(2):
            kb.op("pool", lambda i=i: G.memset(QT2[i][:], 0.0), writes=[QT_t])
        KT = kb.sb("KT", [128, NT], BF16); KT_t = kb.tok("KT")
        sgd = nc.dram_tensor("sgd", [2, 128, NT], BF16).ap()
        sgd_t = kb.tok("sgd")
        SGt = [kb.sb("SGt%d" % i, [128, TT], BF16) for i in range(2)]
        SGt_t = kb.toks(2, "SGt")
        sgl = [kb.sb("sgl%d" % i, [128, TT], BF16) for i in range(2)]
        sgl_t = kb.toks(2, "sgl")
        Va = kb.sb("Va", [128, nblk, 129], BF16); Va_t = kb.tok("Va")
        kb.op("pool", lambda: G.memset(Va[:], 1.0), writes=[Va_t])
        hin = [kb.sb("ahin%d" % i, [128, 8, TT], BF16) for i in range(2)]
        hin_t = kb.toks(2, "ahin")
        stg = kb.sb("astg", [128, 1024])
        stg_t = kb.tok("astg")
        Wt = {nm: kb.sb("W" + nm, [128, 8, 128], BF16) for nm in ("q", "qr", "k", "kr", "v", "g")}
        W_t = kb.tok("Wt")
        sn = kb.sb("sn", [128, TT]); cs_ = kb.sb("cs_", [128, TT]); sc_t = kb.tok("sincos")
        ta = kb.sb("ta", [128, TT]); tb = kb.sb("tb", [128, TT]); tab_t = kb.tok("tab")
        PT = [kb.sb("PT%d" % i, [128, TT], BF16) for i in range(4)]
        PT_t = kb.toks(4, "PT")
        o1 = kb.sb("o1", [128, 128]); o1_t = kb.tok("o1")
        o4 = kb.sb("o4", [128, 4, 128]); o_t = kb.tok("o4")
        on4 = kb.sb("on4", [128, 4, 128]); on_t = kb.tok("on4")
        rc8 = kb.sb("rc8", [128, 9]); ss4 = kb.sb("ss4", [128, 4]); ss_t = kb.tok("ss4")
        junk = kb.sb("junk", [128, 128])
        rc = kb.sb("rc", [128, 4]); rc_t = kb.tok("rc")
        zout = [kb.sb("azout%d" % i, [128, TT], BF16) for i in range(2)]
        zo_t = kb.toks(2, "azo")
        SB = [kb.ps("SB%d" % i, [128, 512]) for i in range(4)]
        SB_t = [kb.ptok("SB%d" % i) for i in range(4)]
        AC = [kb.ps("AC%d" % i, [128, 512]) for i in range(3)]
        AC_t = [kb.ptok("AC%d" % i) for i in range(3)]
        TP_ = kb.ps("TPp", [128, 512]); TP_t = kb.ptok("TPp")

        def acc(c, qs):
            i = c * 4 + qs
            return AC[i // 3][:, (i % 3) * 129:(i % 3) * 129 + 129], AC_t[i // 3]

        hv = hnT.rearrange("(c p) t -> p c t", p=128)
        for hh in range(2):
            for gi, nm in enumerate(("q", "k", "v", "g")):
                for dc in range(8):
                    kb.dma("sp", stg[:, 0:128], Wd[dc * 128:(dc + 1) * 128, gi * 256 + hh * 128:gi * 256 + hh * 128 + 128], stg_t, writes=[stg_t])
                    kb.op("dve", lambda nm=nm, dc=dc: V.tensor_copy(out=Wt[nm][:, dc, :], in_=stg[:, 0:128]), reads=[stg_t], writes=[W_t])
                    if nm in ("q", "k"):
                        for c in range(2):
                            kb.op("dve", lambda nm=nm, dc=dc, c=c: V.tensor_scalar(out=Wt[nm + "r"][:, dc, c * 64:c * 64 + 32], in0=stg[:, c * 64 + 32:c * 64 + 64],
                                                                                   scalar1=-1.0, scalar2=None, op0=ALU.mult), reads=[stg_t], writes=[W_t])
                            kb.op("dve", lambda nm=nm, dc=dc, c=c: V.tensor_copy(out=Wt[nm + "r"][:, dc, c * 64 + 32:c * 64 + 64], in_=stg[:, c * 64:c * 64 + 32]),
                                  reads=[stg_t], writes=[W_t])
            kb.dma("sp", hin[0][:], hv[:, :, 0:TT], hin_t[0], writes=[hin_t[0]])
            for j in range(ntile):
                s = j % 2
                cs = slice(j * TT, (j + 1) * TT)
                if j + 1 < ntile:
                    kb.dma("sp", hin[1 - s][:], hv[:, :, (j + 1) * TT:(j + 2) * TT], hin_t[1 - s], writes=[hin_t[1 - s]])
                hb, hb_t = hin[s], hin_t[s]
                kb.dma("sp", sn[:], sind[:, cs], sc_t, writes=[sc_t])
                kb.dma("sp", cs_[:], cosd[:, cs], sc_t, writes=[sc_t])
                for nm, dstT, dst_t in (("q", None, QT_t), ("k", KT, KT_t)):
                    for dc in range(8):
                        kb.op("pe", lambda nm=nm, dc=dc: P.matmul(SB[0][:], lhsT=Wt[nm][:, dc, :], rhs=hb[:, dc, :], start=(dc == 0), stop=(dc == 7)),
                              reads=[W_t, hb_t], writes=[SB_t[0]])
                    for dc in range(8):
                        kb.op("pe", lambda nm=nm, dc=dc: P.matmul(SB[1][:], lhsT=Wt[nm + "r"][:, dc, :], rhs=hb[:, dc, :], start=(dc == 0), stop=(dc == 7)),
                              reads=[W_t, hb_t], writes=[SB_t[1]])
                    kb.op("dve", lambda: V.tensor_tensor(out=ta[:], in0=SB[0][:], in1=cs_[:], op=ALU.mult), reads=[SB_t[0], sc_t], writes=[tab_t])
                    kb.op("dve", lambda: V.tensor_tensor(out=tb[:], in0=SB[1][:], in1=sn[:], op=ALU.mult), reads=[SB_t[1], sc_t], writes=[tab_t])
                    if nm == "k":
                        kb.op("dve", lambda dstT=dstT, cs=cs: V.tensor_tensor(out=dstT[:, cs], in0=ta[:], in1=tb[:], op=ALU.add), reads=[tab_t], writes=[dst_t])
                    else:
                        kb.op("dve", lambda cs=cs: V.tensor_tensor(out=QT2[0][0:64, cs], in0=ta[0:64, :], in1=tb[0:64, :], op=ALU.add), reads=[tab_t], writes=[dst_t])
                        kb.op("dve", lambda cs=cs: V.tensor_tensor(out=QT2[1][64:128, cs], in0=ta[64:128, :], in1=tb[64:128, :], op=ALU.add), reads=[tab_t], writes=[dst_t])
                for dc in range(8):
                    kb.op("pe", lambda dc=dc: P.matmul(SB[2][:], lhsT=Wt["g"][:, dc, :], rhs=hb[:, dc, :], start=(dc == 0), stop=(dc == 7)),
                          reads=[W_t, hb_t], writes=[SB_t[2]])
                kb.op("act", lambda s=s: S.activation(out=SGt[s][:], in_=SB[2][:], func=AF.Silu), reads=[SB_t[2]], writes=[SGt_t[s]])
                kb.dma("sp", sgd[hh][:, cs], SGt[s][:], SGt_t[s], reads=[SGt_t[s]], writes=[sgd_t])
                for bi in range(4):
                    for dc in range(8):
                        kb.op("pe", lambda dc=dc, bi=bi: P.matmul(SB[3][:, bi * 128:(bi + 1) * 128], lhsT=hb[:, dc, bi * 128:(bi + 1) * 128], rhs=Wt["v"][:, dc, :],
                                                                  start=(dc == 0), stop=(dc == 7)), reads=[W_t, hb_t], writes=[SB_t[3]])
                kb.op("act", lambda j=j: S.copy(out=Va[:, j * 4:j * 4 + 4, 0:128], in_=SB[3][:].rearrange("p (b v) -> p b v", b=4)), reads=[SB_t[3]], writes=[Va_t])
            gstep = [0]
            pend_tail = [None]
            for g in range(ntile):
                for a3 in range(3):
                    kb.op("dve", lambda a3=a3: V.memset(AC[a3][:], 0.0), writes=[AC_t[a3]])
                kb.dma("sp", sgl[g % 2][:], sgd[hh][:, g * TT:(g + 1) * TT], sgl_t[g % 2], reads=[sgd_t], writes=[sgl_t[g % 2]])
                steps = []
                for kbk in range(4 * g + 4):
                    for c in range(2):
                        steps.append((kbk, c, gstep[0] % 4))
                        gstep[0] += 1

                def qk(st, g=g):
                    kbk, c, bi = st
                    kb.op("pe", lambda: P.matmul(SB[bi][:], lhsT=KT[:, kbk * 128:(kbk + 1) * 128],
                                                 rhs=QT2[c][:, g * TT:(g + 1) * TT], start=True, stop=True),
                          reads=[KT_t, QT_t], writes=[SB_t[bi]])

                def ex(st, g=g):
                    kbk, c, bi = st
                    m = kbk - 4 * g
                    kb.op("act", lambda: S.activation(out=PT[bi][:], in_=SB[bi][:], func=AF.Exp, scale=0.125),
                          reads=[SB_t[bi]], writes=[PT_t[bi]])
                    if m >= 0:
                        kb.op("pool", lambda: G.tensor_tensor(out=PT[bi][:, m * 128:(m + 1) * 128], in0=PT[bi][:, m * 128:(m + 1) * 128],
                                                              in1=tri[:], op=ALU.mult), reads=[ctk], writes=[PT_t[bi]])

                def av(st, g=g):
                    kbk, c, bi = st
                    m = kbk - 4 * g
                    for qs in range(4):
                        if m > qs:
                            continue
                        ap_, at_ = acc(c, qs)
                        kb.op("pe", lambda ap_=ap_, qs=qs: P.matmul(ap_, lhsT=PT[bi][:, qs * 128:(qs + 1) * 128], rhs=Va[:, kbk, :],
                                                                    start=False, stop=False, skip_group_check=True),
                              reads=[PT_t[bi], Va_t], writes=[at_])

                LA = 3
                for i in range(min(LA, len(steps))):
                    qk(steps[i])
                for i, st in enumerate(steps):
                    ex(st)
                    if i + LA < len(steps):
                        qk(steps[i + LA])
                    av(st)
                    if i == 5 and pend_tail[0] is not None:
                        pend_tail[0]()
                        pend_tail[0] = None
                if pend_tail[0] is not None:
                    pend_tail[0]()
                    pend_tail[0] = None
                zb, zb_t = zout[g % 2], zo_t[g % 2]
                for a3 in range(3):
                    na = 3 if a3 < 2 else 2
                    kb.op("dve", lambda a3=a3, na=na: V.reciprocal(out=rc8[:, a3 * 3:a3 * 3 + na],
                                                                   in_=AC[a3][:, 0:na * 129].rearrange("p (a w) -> p a w", w=129)[:, :, 128]),
                          reads=[AC_t[a3]], writes=[rc_t])
                for qs in range(4):
                    a0, a0t = acc(0, qs)
                    a1, a1t = acc(1, qs)
                    kb.op("dve", lambda a1=a1, qs=qs: V.tensor_scalar(out=o1[:], in0=a1[:, 0:128], scalar1=rc8[:, 4 + qs:5 + qs], scalar2=lam[:, 3:4],
                                                                     op0=ALU.mult, op1=ALU.mult), reads=[a1t, rc_t, ctk], writes=[o1_t])
                    kb.op("dve", lambda a0=a0, qs=qs: V.scalar_tensor_tensor(out=o4[:, qs, :], in0=a0[:, 0:128], scalar=rc8[:, qs:qs + 1], in1=o1[:],
                                                                            op0=ALU.mult, op1=ALU.add), reads=[a0t, rc_t, o1_t], writes=[o_t])
                for qs in range(4):
                    kb.op("act", lambda qs=qs: S.activation(out=junk[:], in_=o4[:, qs, :], func=AF.Square, accum_out=ss4[:, qs:qs + 1]),
                          reads=[o_t], writes=[ss_t])
                kb.op("act", lambda: S.activation(out=ss4[:], in_=ss4[:], func=AF.Ln, scale=1.0 / 128, bias=1e-5), reads=[ss_t], writes=[ss_t])
                kb.op("act", lambda: S.activation(out=ss4[:], in_=ss4[:], func=AF.Exp, scale=-0.5), reads=[ss_t], writes=[ss_t])
                for qs in range(4):
                    kb.op("dve", lambda qs=qs: V.scalar_tensor_tensor(out=on4[:, qs, :], in0=o4[:, qs, :], scalar=ss4[:, qs:qs + 1], in1=subb[:],
                                                                      op0=ALU.mult, op1=ALU.mult), reads=[o_t, ss_t, ctk], writes=[on_t])
                def tail(g=g, zb=zb, zb_t=zb_t, hh=hh):
                    for qs in range(4):
                        kb.op("pe", lambda qs=qs: P.transpose(TP_[:, qs * 128:(qs + 1) * 128], on4[:, qs, :], ident[:]), reads=[on_t, ctk], writes=[TP_t])
                    kb.op("dve", lambda: V.tensor_tensor(out=zb[:], in0=TP_[:], in1=sgl[g % 2][:], op=ALU.mult),
                          reads=[TP_t, sgl_t[g % 2]], writes=[zb_t])
                    kb.dma("sp", zT[hh * 128:(hh + 1) * 128, g * TT:(g + 1) * TT], zb[:], zb_t, reads=[zb_t])
                pend_tail[0] = tail
            if pend_tail[0] is not None:
                pend_tail[0]()
                pend_tail[0] = None
        kb.finish(zo_t)
    return nc


def attn_maps(hn_list, I, NT):
    maps = []
    Wfull = I["da_w_in"][0].reshape(D, 4, 1024)
    ident = np.eye(128, dtype=np.float32)
    inv = (1.0 / (10000.0 ** (np.arange(0, 64, 2, dtype=np.float32) / 64))).astype(np.float32)
    invf = np.tile(inv, 4).reshape(128, 1).astype(np.float32)
    pos0 = np.arange(TT, dtype=np.float32)[None, :]
    ang = np.arange(NT, dtype=np.float32)[None, :] * invf
    cosd = np.cos(ang).astype(np.float32)
    sind = np.sin(ang).astype(np.float32)
    lqk = np.concatenate([I["da_lq1"][0], I["da_lk1"][0], I["da_lq2"][0], I["da_lk2"][0]])[None, :].astype(np.float32)
    for c in range(8):
        b, hp = c // 4, c % 4
        cs = slice(hp * 256, (hp + 1) * 256)
        maps.append(dict(hnT=np.ascontiguousarray(hn_list[b]), Wd=np.ascontiguousarray(Wfull[:, :, cs].reshape(D, 1024)),
                         sub=np.ascontiguousarray(I["da_subln"][0][None, :]), lqk=np.ascontiguousarray(lqk), invf=invf, pos0=pos0, ident=ident, cosd=cosd, sind=sind))
    return maps


def kernel(**inputs):
    I = {k: np.asarray(v) for k, v in inputs.items()}
    x, p = I["x"], I["p"]
    B, S = x.shape[0], x.shape[1]
    QT_ = S // 4
    nc = build_rwkv(S)
    res = run_bass_kernel_spmd(nc, rwkv_maps(x, I, S), core_ids=list(range(8))).results
    z0T = [np.concatenate([res[b * 4 + g]["zT"] for g in range(4)], axis=0) for b in range(B)]
    rng = [(c // 4, slice((c % 4) * QT_, (c % 4 + 1) * QT_)) for c in range(8)]
    res2 = run_post([z0T[b][:, sl] for b, sl in rng], [x[b, sl].T for b, sl in rng], [p[0, b, sl].T for b, sl in rng],
                    I["rw_w_out"][0], I["pe_w_gate"][0], I["pe_w_proj"][0], I["pe_norm"][0], I["da_norm"][0], last=False)
    hn = [np.concatenate([res2[b * 4 + q]["hnT"] for q in range(4)], axis=1) for b in range(B)]
    nc3 = build_attn(S)
    res3 = run_bass_kernel_spmd(nc3, attn_maps(hn, I, S), core_ids=list(range(8))).results
    z1T = [np.concatenate([res3[b * 4 + g]["zT"] for g in range(4)], axis=0) for b in range(B)]
    res4 = run_post([z1T[b][:, sl] for b, sl in rng], [res2[c]["h1T"] for c in range(8)], [p[1, b, sl].T for b, sl in rng],
                    I["da_w_out"][0], I["pe_w_gate"][1], I["pe_w_proj"][1], I["pe_norm"][1], I["final_norm"], last=True)
    out = np.empty((B, S, D), np.float32)
    for c, (b, sl) in enumerate(rng):
        out[b, sl, :] = res4[c]["oT"].T
    return out
```

```python
import math
from contextlib import ExitStack
import numpy as np
import ml_dtypes
import concourse.bass as bass
import concourse.mybir as mybir
from concourse.bass_utils import run_bass_kernel_spmd

F32 = mybir.dt.float32
BF16 = mybir.dt.bfloat16
AF = mybir.ActivationFunctionType
ALU = mybir.AluOpType
AX = mybir.AxisListType
NPBF = ml_dtypes.bfloat16

D = 1024
SEQ = 16384
TT = 512


class Tok:
    __slots__ = ("name", "w", "r", "dsem", "dcnt", "excl")

    def __init__(self, name, excl=False):
        self.name = name
        self.excl = excl
        self.w = None
        self.r = {}
        self.dsem = None
        self.dcnt = 0


class KB:
    def __init__(self, nc, es):
        self.nc = nc
        self.es = es
        self.eng = dict(pe=nc.tensor, dve=nc.vector, act=nc.scalar, pool=nc.gpsimd, sp=nc.sync)
        self.sem = {e: es.enter_context(nc.semaphore("prog_" + e)) for e in ("pe", "dve", "act", "pool")}
        self.cnt = {e: 0 for e in self.sem}
        self.seen = {e: {} for e in self.eng}
        self.ntok = 0
        self.out_dma = []
        self.coop = None

    def tok(self, name=None):
        self.ntok += 1
        return Tok(name or ("t%d" % self.ntok))

    def toks(self, n, name="t"):
        return [self.tok("%s%d_%d" % (name, self.ntok, i)) for i in range(n)]

    def sb(self, name, shape, dt=F32):
        return self.es.enter_context(self.nc.sbuf_tensor(name, list(shape), dt))

    def ps(self, name, shape, dt=F32):
        return self.es.enter_context(self.nc.psum_tensor(name, list(shape), dt))

    def _deps(self, reads, writes):
        deps = {}
        for t in reads:
            if t.w is not None:
                k = t.w[1]
                if deps.get(k, (None, None, 0))[2] < t.w[2]:
                    deps[k] = t.w
        for t in writes:
            for d in ([t.w] if t.w is not None else []) + list(t.r.values()):
                k = d[1]
                if deps.get(k, (None, None, 0))[2] < d[2]:
                    deps[k] = d
        return deps

    def _wait(self, e, deps, keep_same=False):
        for k, (sem, key, val) in deps.items():
            if key == e and e == "pe" and not keep_same:
                continue
            if self.seen[e].get(key, 0) < val:
                self.eng[e].wait_ge(sem, val)
                self.seen[e][key] = val

    def ptok(self, name=None):
        t = self.tok(name)
        t.excl = True
        return t

    def op(self, e, fn, reads=(), writes=()):
        ex = [t for t in reads if t.excl]
        if ex:
            reads = [t for t in reads if not t.excl]
            writes = list(writes) + ex
        self._wait(e, self._deps(reads, writes))
        ins = fn()
        self.cnt[e] += 1
        ins.then_inc(self.sem[e], 1)
        me = (self.sem[e], e, self.cnt[e])
        for t in reads:
            t.r[e] = me
        for t in writes:
            t.w = me
            t.r = {}
        if self.coop is not None:
            self.coop.emitted()
        return ins

    def dma(self, q, out, in_, owner, reads=(), writes=(), **kw):
        self._wait(q, self._deps(reads, writes), keep_same=True)
        if owner.dsem is None:
            owner.dsem = self.es.enter_context(self.nc.semaphore("dma_" + owner.name))
        ins = self.eng[q].dma_start(out=out, in_=in_, **kw)
        owner.dcnt += 16
        ins.then_inc(owner.dsem, 16)
        me = (owner.dsem, "dma_" + owner.name, owner.dcnt)
        for t in reads:
            t.r[me[1]] = me
        for t in writes:
            t.w = me
            t.r = {}
        return me

    def finish(self, toks):
        for t in toks:
            if t.dsem is not None:
                self.eng["sp"].wait_ge(t.dsem, t.dcnt)


import threading


class Coop:
    def __init__(self, kb):
        self.kb = kb
        self.thread = None
        self.quota = 0
        self.go = threading.Semaphore(0)
        self.back = threading.Semaphore(0)
        self.done = True
        self.err = None
        kb.coop = self

    def start(self, fn):
        self.done = False
        self.quota = 0

        def run():
            self.go.acquire()
            try:
                fn()
            except BaseException as e:
                self.err = e
            self.done = True
            self.back.release()
        self.thread = threading.Thread(target=run)
        self.thread.start()

    def emitted(self):
        if threading.current_thread() is self.thread:
            self.quota -= 1
            if self.quota <= 0:
                self.back.release()
                self.go.acquire()

    def step(self, n):
        if self.done or threading.current_thread() is self.thread:
            return
        self.quota = n
        self.go.release()
        self.back.acquire()
        if self.err is not None:
            raise self.err

    def finish(self):
        while not self.done:
            self.step(1 << 30)
        if self.thread is not None:
            self.thread.join()
        if self.err is not None:
            raise self.err


def new_nc():
    return bass.Bass("TRN2", target_bir_lowering=False)


def dram_in(nc, name, shape, dt=F32):
    return nc.dram_tensor(name, list(shape), dt, kind="ExternalInput").ap()


def dram_out(nc, name, shape, dt=F32):
    return nc.dram_tensor(name, list(shape), dt, kind="ExternalOutput").ap()


def load_weight_bf16(kb, q, w_dram, kchunks, ncols, name, stage, stage_tok, cast_eng="pool"):
    wt = kb.sb(name, [128, kchunks, ncols], BF16)
    tk = kb.tok(name)
    for kc in range(kchunks):
        s = kc % len(stage)
        kb.dma(q, stage[s][:, 0:ncols], w_dram[kc * 128:(kc + 1) * 128, :], stage_tok[s], writes=[stage_tok[s]])
        kb.op(cast_eng, lambda kc=kc, s=s: kb.eng[cast_eng].tensor_copy(out=wt[:, kc, :], in_=stage[s][:, 0:ncols]),
              reads=[stage_tok[s]], writes=[tk])
    return wt, tk


def rstd_from_sumsq(kb, ps_ap, ps_tok, out_ap, out_tok, eps):
    kb.op("act", lambda: kb.nc.scalar.activation(out=out_ap, in_=ps_ap, func=AF.Ln, bias=float(eps), scale=1.0),
          reads=[ps_tok], writes=[out_tok])
    kb.op("act", lambda: kb.nc.scalar.activation(out=out_ap, in_=out_ap, func=AF.Exp, scale=-0.5),
          reads=[out_tok], writes=[out_tok])


def build_post(NT, last):
    nc = new_nc()
    zT = dram_in(nc, "zT", [D, NT], BF16)
    hT = dram_in(nc, "hT", [D, NT], F32)
    pT = dram_in(nc, "pT", [256, NT], F32)
    w_out = dram_in(nc, "w_out", [D, D])
    w_gate = dram_in(nc, "w_gate", [D, D])
    w_proj = dram_in(nc, "w_proj", [256, D])
    vecs = dram_in(nc, "vecs", [128, 16])
    if last:
        oT = dram_out(nc, "oT", [D, NT], F32)
    else:
        h1T = dram_out(nc, "h1T", [D, NT], F32)
        hnT = dram_out(nc, "hnT", [D, NT], BF16)
    ntile = NT // TT
    with ExitStack() as es:
        kb = KB(nc, es)
        ctk = kb.tok("cst")
        ones = kb.sb("ones", [128, 128], BF16)
        kb.op("pool", lambda: nc.gpsimd.memset(ones[:], 1.0 / D), writes=[ctk])
        vec = kb.sb("vec", [128, 16])
        vtk = kb.tok("vec")
        kb.dma("sp", vec[:], vecs[:, :], vtk, writes=[vtk])
        stage = [kb.sb("stg%d" % i, [128, D]) for i in range(2)]
        stok = kb.toks(2, "stg")
        Wo, Wo_t = load_weight_bf16(kb, "sp", w_out, 8, D, "Wo", stage, stok)
        Wg, Wg_t = load_weight_bf16(kb, "sp", w_gate, 8, D, "Wg", stage, stok)
        Wp, Wp_t = load_weight_bf16(kb, "sp", w_proj, 2, D, "Wp", stage, stok)
        zin = [kb.sb("zin%d" % i, [128, 8, TT], BF16) for i in range(2)]
        hin = [kb.sb("hin%d" % i, [128, 8, TT]) for i in range(2)]
        pin = [kb.sb("pin%d" % i, [128, 2, TT]) for i in range(2)]
        pbf = [kb.sb("pbf%d" % i, [128, 2, TT], BF16) for i in range(2)]
        z_t, h_t, p_t, pb_t = kb.toks(2, "z"), kb.toks(2, "h"), kb.toks(2, "p"), kb.toks(2, "pb")
        sq = kb.sb("sq", [128, 8, TT], BF16)
        sq_t = kb.tok("sq")
        hn = kb.sb("hn", [128, 8, TT], BF16)
        hn_t = kb.tok("hn")
        rstd = kb.sb("rstd", [128, TT])
        rstd_t = kb.tok("rstd")
        gsb = [kb.sb("gsb%d" % i, [128, TT]) for i in range(2)]
        g_t = kb.toks(2, "g")
        tmp = [kb.sb("tmp%d" % i, [128, TT]) for i in range(2)]
        tmp_t = kb.toks(2, "tmp")
        obuf = [kb.sb("obuf%d" % i, [128, 8, TT], F32 if last else BF16) for i in range(2)]
        ob_t = kb.toks(2, "ob")
        psb = [kb.ps("psb%d" % i, [128, TT]) for i in range(8)]
        ps_t = [kb.ptok("ps%d" % i) for i in range(8)]
        pc = [0]

        def nextps():
            i = pc[0] % 8
            pc[0] += 1
            return psb[i], ps_t[i]

        zv = zT.rearrange("(c p) t -> p c t", p=128)
        hv = hT.rearrange("(c p) t -> p c t", p=128)
        pv = pT.rearrange("(c p) t -> p c t", p=128)

        def norm(src, src_t, gcol0, dst_fn, dst_toks):
            for dc in range(8):
                kb.op("act", lambda dc=dc: nc.scalar.activation(out=sq[:, dc, :], in_=src[:, dc, :], func=AF.Square),
                      reads=[src_t], writes=[sq_t])
            pb_, pt_ = nextps()
            for dc in range(8):
                kb.op("pe", lambda dc=dc: nc.tensor.matmul(pb_[:], lhsT=ones[:], rhs=sq[:, dc, :], start=(dc == 0), stop=(dc == 7)),
                      reads=[sq_t, ctk], writes=[pt_])
            rstd_from_sumsq(kb, pb_[:], pt_, rstd[:], rstd_t, 1e-6)
            for dc in range(8):
                kb.op("dve", lambda dc=dc: nc.vector.scalar_tensor_tensor(
                    out=dst_fn(dc), in0=src[:, dc, :], scalar=vec[:, gcol0 + dc:gcol0 + dc + 1], in1=rstd[:],
                    op0=ALU.mult, op1=ALU.mult), reads=[src_t, rstd_t, vtk], writes=dst_toks)

        def loads(j):
            s = j % 2
            cs = slice(j * TT, (j + 1) * TT)
            kb.dma("sp", zin[s][:], zv[:, :, cs], z_t[s], writes=[z_t[s]])
            kb.dma("sp", hin[s][:], hv[:, :, cs], h_t[s], writes=[h_t[s]])
            kb.dma("sp", pin[s][:], pv[:, :, cs], p_t[s], writes=[p_t[s]])

        loads(0)
        for j in range(ntile):
            s = j % 2
            cs = slice(j * TT, (j + 1) * TT)
            if j + 1 < ntile:
                loads(j + 1)
            kb.op("pool", lambda s=s: nc.gpsimd.tensor_copy(out=pbf[s][:], in_=pin[s][:]), reads=[p_t[s]], writes=[pb_t[s]])
            h2 = hin[s]
            for dc in range(8):
                pb_, pt_ = nextps()
                for kc in range(8):
                    kb.op("pe", lambda dc=dc, kc=kc, pb_=pb_: nc.tensor.matmul(
                        pb_[:], lhsT=Wo[:, kc, dc * 128:(dc + 1) * 128], rhs=zin[s][:, kc, :], start=(kc == 0), stop=(kc == 7)),
                        reads=[Wo_t, z_t[s]], writes=[pt_])
                kb.op("dve", lambda dc=dc, pb_=pb_: nc.vector.tensor_tensor(out=h2[:, dc, :], in0=h2[:, dc, :], in1=pb_[:], op=ALU.add),
                      reads=[pt_, h_t[s]], writes=[h_t[s]])
            norm(h2, h_t[s], 0, lambda dc: hn[:, dc, :], [hn_t])
            for dc in range(8):
                pg, pgt = nextps()
                for kc in range(8):
                    kb.op("pe", lambda dc=dc, kc=kc, pg=pg: nc.tensor.matmul(
                        pg[:], lhsT=Wg[:, kc, dc * 128:(dc + 1) * 128], rhs=hn[:, kc, :], start=(kc == 0), stop=(kc == 7)),
                        reads=[Wg_t, hn_t], writes=[pgt])
                pp, ppt = nextps()
                for kc in range(2):
                    kb.op("pe", lambda dc=dc, kc=kc, pp=pp: nc.tensor.matmul(
                        pp[:], lhsT=Wp[:, kc, dc * 128:(dc + 1) * 128], rhs=pbf[s][:, kc, :], start=(kc == 0), stop=(kc == 1)),
                        reads=[Wp_t, pb_t[s]], writes=[ppt])
                b = dc % 2
                kb.op("act", lambda pg=pg, b=b: nc.scalar.activation(out=gsb[b][:], in_=pg[:], func=AF.Sigmoid),
                      reads=[pgt], writes=[g_t[b]])
                kb.op("dve", lambda pp=pp, b=b: nc.vector.tensor_tensor(out=tmp[b][:], in0=gsb[b][:], in1=pp[:], op=ALU.mult),
                      reads=[g_t[b], ppt], writes=[tmp_t[b]])
                kb.op("pool", lambda dc=dc, b=b: nc.gpsimd.tensor_tensor(out=h2[:, dc, :], in0=h2[:, dc, :], in1=tmp[b][:], op=ALU.add),
                      reads=[tmp_t[b], h_t[s]], writes=[h_t[s]])
            norm(h2, h_t[s], 8, lambda dc: obuf[s][:, dc, :], [ob_t[s]])
            if last:
                kb.dma("sp", oT.rearrange("(c p) t -> p c t", p=128)[:, :, cs], obuf[s][:], ob_t[s], reads=[ob_t[s]])
            else:
                kb.dma("sp", hnT.rearrange("(c p) t -> p c t", p=128)[:, :, cs], obuf[s][:], ob_t[s], reads=[ob_t[s]])
                kb.dma("sp", h1T.rearrange("(c p) t -> p c t", p=128)[:, :, cs], h2[:], h_t[s], reads=[h_t[s]])
        kb.finish(ob_t + h_t)
    return nc


def run_post(zT_list, hT_list, pT_list, w_out, w_gate, w_proj, pe_norm, nxt_norm, last):
    n = len(zT_list)
    NT = zT_list[0].shape[1]
    nc = build_post(NT, last)
    vecs = np.concatenate([pe_norm.reshape(8, 128).T, nxt_norm.reshape(8, 128).T], axis=1).astype(np.float32)
    maps = [dict(zT=np.ascontiguousarray(zT_list[i]), hT=np.ascontiguousarray(hT_list[i]), pT=np.ascontiguousarray(pT_list[i]),
                 w_out=w_out, w_gate=w_gate, w_proj=w_proj, vecs=np.ascontiguousarray(vecs)) for i in range(n)]
    res = run_bass_kernel_spmd(nc, maps, core_ids=list(range(n))).results
    return res


TA = 256
NCH = TA // 64
GN_EPS = 64e-5
FILL = 3
RW_STOP = 0
RW_DBG = None


def build_rwkv(NT):
    nc = new_nc()
    xT = dram_in(nc, "xT", [D, NT])
    Wc = dram_in(nc, "Wc", [D, 4 * 256])
    w1 = dram_in(nc, "w1", [D, 64])
    a1 = dram_in(nc, "a1", [D, 64])
    w2c = dram_in(nc, "w2c", [64, 256])
    a2c = dram_in(nc, "a2c", [64, 256])
    vecA = dram_in(nc, "vecA", [128, 56])
    vecC = dram_in(nc, "vecC", [128, 2 * 8])
    lnwb = dram_in(nc, "lnwb", [1, 512])
    identd = dram_in(nc, "ident", [128, 128])
    zT = dram_out(nc, "zT", [256, NT], BF16)
    dbg_d = dram_out(nc, "dbg", [128, 2048]) if RW_DBG else None
    dbg_tok = []

    def dbg(name, ap, tok, j=0):
        if RW_DBG == name and j == 0 and not dbg_tok:
            t = kb.tok("dbgt")
            dbg_tok.append(t)
            p, f = ap.shape[0], int(np.prod(ap.shape[1:]))
            kb.dma("sp", dbg_d[0:p, 0:f], ap, t, reads=[tok])
    ntile = NT // TA
    with ExitStack() as es:
        kb = KB(nc, es)
        V, S, G, P = nc.vector, nc.scalar, nc.gpsimd, nc.tensor
        ctk = kb.tok("cst")
        ident = kb.sb("ident_sb", [128, 128])
        kb.dma("sp", ident[:], identd[:, :], ctk, writes=[ctk])
        onesb = kb.sb("onesb", [128, 128], BF16)
        blk = kb.sb("blk", [128, 128], BF16)
        bind = kb.sb("bind", [128, 2])
        kb.op("pool", lambda: G.memset(onesb[:], 1.0 / D), writes=[ctk])
        kb.op("pool", lambda: G.memset(blk[:], 0.0), writes=[ctk])
        kb.op("pool", lambda: G.memset(blk[0:64, 0:64], 1.0), writes=[ctk])
        kb.op("pool", lambda: G.memset(blk[64:128, 64:128], 1.0), writes=[ctk])
        kb.op("pool", lambda: G.memset(bind[:], 0.0), writes=[ctk])
        kb.op("pool", lambda: G.memset(bind[0:64, 0:1], 1.0), writes=[ctk])
        kb.op("pool", lambda: G.memset(bind[64:128, 1:2], 1.0), writes=[ctk])
        mAT = kb.sb("mAT", [64, 4, 4, 64])
        mL = kb.sb("mL", [64, 4, 64])
        kb.op("pool", lambda: G.memset(mAT[:], 1.0), writes=[ctk])
        kb.op("pool", lambda: G.memset(mL[:], 1.0), writes=[ctk])
        for q in range(4):
            kb.op("pool", lambda q=q: G.affine_select(out=mAT[:, :, q, :], in_=mAT[:, :, q, :], pattern=[[0, 4], [1, 64]],
                                                      compare_op=(ALU.is_gt if q % 2 == 0 else ALU.is_ge), fill=0.0, base=0,
                                                      channel_multiplier=-1), writes=[ctk])
        kb.op("pool", lambda: G.affine_select(out=mL[:], in_=mL[:], pattern=[[0, 4], [-1, 64]], compare_op=ALU.is_gt, fill=0.0,
                                              base=0, channel_multiplier=1), writes=[ctk])
        vA = kb.sb("vA", [128, 56])
        vC = kb.sb("vC", [128, 2, 8])
        kb.dma("sp", vA[:], vecA[:, :], ctk, writes=[ctk])
        kb.dma("sp", vC[:].rearrange("p a b -> p (a b)"), vecC[:, :], ctk, writes=[ctk])
        lnw = kb.sb("lnw", [64, 512])
        kb.dma("sp", lnw[:], lnwb.partition_broadcast(64), ctk, writes=[ctk])
        kb.op("dve", lambda: V.tensor_scalar(out=vC[:, :, 5:6], in0=vC[:, :, 0:1], scalar1=-1.0, scalar2=None, op0=ALU.mult),
              reads=[ctk], writes=[ctk])
        kb.op("dve", lambda: V.tensor_scalar(out=vC[:, :, 6:7], in0=vC[:, :, 3:4], scalar1=-1.0, scalar2=1.0, op0=ALU.mult, op1=ALU.add),
              reads=[ctk], writes=[ctk])
        W0, A0, KK_, KA, RK, NW0, OMKA = range(7)

        xin = kb.sb("xin", [128, 8, TA])
        x_t = kb.tok("xin")
        xin_flat = xin[:].rearrange("p a b -> p (a b)")
        stage = [xin_flat[:, i * 1024:(i + 1) * 1024] for i in range(2)]
        stok = [x_t, x_t]
        omu = kb.sb("omu", [128, 48])
        kb.op("dve", lambda: V.tensor_scalar(out=omu[:], in0=vA[:, 8:56], scalar1=-1.0, scalar2=1.0, op0=ALU.mult, op1=ALU.add), reads=[ctk], writes=[ctk])
        W = kb.sb("Wm", [128, 8, 4, 2, 256], BF16)
        W_t = kb.tok("Wm")
        for dc in range(8):
            sgi = dc % 2
            kb.dma("sp", stage[sgi][:, 0:1024], Wc[dc * 128:(dc + 1) * 128, :], x_t, writes=[x_t])
            for i in range(4):
                kb.op("dve", lambda dc=dc, i=i, sgi=sgi: V.tensor_scalar(out=W[:, dc, i, 0, :], in0=stage[sgi][:, i * 256:(i + 1) * 256],
                                                                         scalar1=omu[:, i * 8 + dc:i * 8 + dc + 1], scalar2=None, op0=ALU.mult),
                      reads=[x_t, ctk], writes=[W_t])
                kb.op("dve", lambda dc=dc, i=i, sgi=sgi: V.tensor_scalar(out=W[:, dc, i, 1, :], in0=stage[sgi][:, i * 256:(i + 1) * 256],
                                                                         scalar1=vA[:, 8 + i * 8 + dc:9 + i * 8 + dc], scalar2=None, op0=ALU.mult),
                      reads=[x_t, ctk], writes=[W_t])
        W1 = kb.sb("W1", [128, 8, 2, 64], BF16)
        A1 = kb.sb("A1", [128, 8, 2, 64], BF16)
        W2 = kb.sb("W2", [64, 256], BF16)
        A2 = kb.sb("A2", [64, 256], BF16)
        for (src_d, dstw, i) in ((w1, W1, 4), (a1, A1, 5)):
            kb.dma("sp", stage[0][:, 0:512].rearrange("p (c k) -> p c k", c=8), src_d.rearrange("(c p) k -> p c k", p=128), x_t, writes=[x_t])
            for dc in range(8):
                kb.op("dve", lambda dc=dc, i=i, dstw=dstw: V.tensor_scalar(out=dstw[:, dc, 0, :], in0=stage[0][:, dc * 64:(dc + 1) * 64],
                                                                           scalar1=omu[:, i * 8 + dc:i * 8 + dc + 1], scalar2=None, op0=ALU.mult),
                      reads=[x_t, ctk], writes=[W_t])
                kb.op("dve", lambda dc=dc, i=i, dstw=dstw: V.tensor_scalar(out=dstw[:, dc, 1, :], in0=stage[0][:, dc * 64:(dc + 1) * 64],
                                                                           scalar1=vA[:, 8 + i * 8 + dc:9 + i * 8 + dc], scalar2=None, op0=ALU.mult),
                      reads=[x_t, ctk], writes=[W_t])
        kb.dma("sp", stage[0][0:64, 0:256], w2c[:, :], x_t, writes=[x_t])
        kb.op("dve", lambda: V.tensor_copy(out=W2[:], in_=stage[0][0:64, 0:256]), reads=[x_t], writes=[W_t])
        kb.dma("sp", stage[0][0:64, 0:256], a2c[:, :], x_t, writes=[x_t])
        kb.op("dve", lambda: V.tensor_copy(out=A2[:], in_=stage[0][0:64, 0:256]), reads=[x_t], writes=[W_t])

        hn = kb.sb("hn", [128, 8, TA + 1], BF16)
        hn_t = kb.tok("hn")
        kb.op("pool", lambda: G.memset(hn[:], 0.0), writes=[hn_t])
        sqx = kb.sb("sqx", [128, 8, TA], BF16)
        sqx_t = kb.tok("sqx")
        rstd = kb.sb("rstd", [128, TA])
        rstd_t = kb.tok("rstd")
        xm = [kb.sb("xm%d" % i, [128, 8, TA], BF16) for i in range(2)]
        xm_t = kb.toks(2, "xm")
        th = kb.sb("th", [64, TA], BF16)
        th_t = kb.tok("th")

        def cm(name):
            return kb.sb(name, [128, 2, TA]), kb.tok(name)
        r_, r_t = cm("r_")
        k_, k_t = cm("k_")
        v_, v_t = kb.sb("v_", [128, 2, TA], BF16), kb.tok("v_")
        sg2 = [kb.sb("sg%d" % i, [128, 2, TA]) for i in range(2)]
        sg2_t = kb.toks(2, "sg")
        nlw, nlw_t = cm("nlw")
        a_, a_t = cm("a_")
        kk, kk_t = cm("kk")
        t1, t1_t = cm("t1")
        prod, prod_t = cm("prod")
        CA, CA_t = cm("CA")
        CB, CB_t = cm("CB")
        rn, rn_t = cm("rn")
        sqk = kb.sb("sqk", [128, 2, TA], BF16)
        sqk_t = kb.tok("sqk")
        AR = kb.sb("AR", [128, 2, NCH, 2, 64], BF16)
        W1s = kb.sb("W1s", [64, NCH * 4 * 64])
        W2s = kb.sb("W2s", [64, NCH * 4 * 64])
        W1s_t, W2s_t = kb.tok("W1s"), kb.tok("W2s")
        identh = kb.sb("identh", [128, 128], BF16)
        kb.op("dve", lambda: V.tensor_copy(out=identh[:], in_=ident[:]), reads=[ctk], writes=[ctk])
        AR_t = kb.tok("AR")
        BK = kb.sb("BK", [128, 2, NCH, 2, 64], BF16)
        BK_t = kb.tok("BK")
        def sb64(name, rest, dt, tok):
            full = kb.sb(name, [128] + list(rest), dt)
            kb.op("pool", lambda: G.memset(full[64:128], 0.0), writes=[tok])
            return full, full[0:64]
        ARh2_t = kb.toks(2, "ARh")
        BKh2_t = kb.toks(2, "BKh")
        ARh2F, ARh2 = zip(*[sb64("ARh%d" % i, [4, NCH, 2, 64], BF16, ARh2_t[i]) for i in range(2)])
        BKh2F, BKh2 = zip(*[sb64("BKh%d" % i, [4, NCH, 2, 64], BF16, BKh2_t[i]) for i in range(2)])
        WCs2 = [kb.sb("WCs%d" % i, [64, 4, NCH]) for i in range(2)]
        WCs2_t = kb.toks(2, "WCs")
        TM2_t = [kb.toks(4, "TM") for pp in range(2)]
        _tm = [[sb64("TM%d_%d" % (pp, i), [NCH, 4, 64], BF16, TM2_t[pp][i]) for i in range(4)] for pp in range(2)]
        TM2F = [[x[0] for x in row] for row in _tm]
        TM2 = [[x[1] for x in row] for row in _tm]
        TA_, TB_, TK_, TV_ = range(4)
        RKs2 = [kb.sb("RKs%d" % i, [64, NCH, 4]) for i in range(2)]
        RKs2_t = kb.toks(2, "RKs")
        Ybuf2 = [kb.sb("Ybuf%d" % i, [64, NCH, 4, 64]) for i in range(2)]
        Y2_t = kb.toks(2, "Ybuf")
        ATsk_t = kb.toks(2, "ATs")
        Lsk_t = kb.toks(2, "Ls")
        ATskF, ATsk = zip(*[sb64("ATs%d" % i, [4, 4, 64], BF16, ATsk_t[i]) for i in range(2)])
        LskF, Lsk = zip(*[sb64("Ls%d" % i, [4, 64], BF16, Lsk_t[i]) for i in range(2)])
        Z = [kb.sb("Z%d" % i, [64, 4, 128]) for i in range(2)]
        Z_t = kb.toks(2, "Z")
        Zbk_t = [kb.toks(2, "Zb") for kk_ in range(2)]
        _zb = [[sb64("Zb%d_%d" % (kk_, i), [4, 128], BF16, Zbk_t[kk_][i]) for i in range(2)] for kk_ in range(2)]
        ZbkF = [[x[0] for x in row] for row in _zb]
        Zbk = [[x[1] for x in row] for row in _zb]
        identbF = kb.sb("identb", [128, 64], BF16)
        kb.op("dve", lambda: V.tensor_copy(out=identbF[:], in_=ident[:, 0:64]), reads=[ctk], writes=[ctk])
        LBk_t = [kb.toks(2, "LB") for kk_ in range(2)]
        _lb = [[sb64("LB%d_%d" % (kk_, i), [4, 2, 64], BF16, LBk_t[kk_][i]) for i in range(2)] for kk_ in range(2)]
        LBkF = [[x[0] for x in row] for row in _lb]
        LBk = [[x[1] for x in row] for row in _lb]
        RHsk_t = kb.toks(2, "RHs")
        MTsk_t = kb.toks(2, "MTs")
        RHskF, RHsk = zip(*[sb64("RHs%d" % i, [4, 64], F32, RHsk_t[i]) for i in range(2)])
        MTskF, MTsk = zip(*[sb64("MTs%d" % i, [4, 64], F32, MTsk_t[i]) for i in range(2)])
        Hs_t = kb.toks(2, "Hs")
        HsF, Hs = zip(*[sb64("Hs%d" % i, [4, 64], F32, Hs_t[i]) for i in range(2)])
        kb.op("pool", lambda: G.memset(Hs[0][:], 0.0), writes=[Hs_t[0]])
        st1 = kb.sb("st1", [64, NCH * 4])
        st2 = kb.sb("st2", [64, NCH * 4])
        st3 = kb.sb("st3", [64, NCH * 4])
        st_t = kb.tok("st")
        zout = kb.sb("zout", [128, 2, TA], BF16)
        zo_t = kb.tok("zout")

        PR = kb.ps("PR", [128, 512]); PR_t = kb.ptok("PR")
        TIN = kb.ps("TIN", [128, 512]); TIN_t = kb.ptok("TIN")
        TINb = TIN[:].bitcast(BF16)
        ATpk = [kb.ps("ATp%d" % i, [128, 512]) for i in range(2)]; ATpk_t = [kb.ptok("ATp%d" % i) for i in range(2)]
        APpk = [kb.ps("APp%d" % i, [128, 512]) for i in range(2)]; APpk_t = [kb.ptok("APp%d" % i) for i in range(2)]
        SQpk = [kb.ps("SQp%d" % i, [128, 512]) for i in range(2)]; SQpk_t = [kb.ptok("SQp%d" % i) for i in range(2)]

        xv = xT.rearrange("(c p) t -> p c t", p=128)
        hstate = [0]

        def prep(j):
            p = j % 2
            cs = slice(j * TA, (j + 1) * TA)
            ARh, ARh_t, BKh, BKh_t = ARh2[p], ARh2_t[p], BKh2[p], BKh2_t[p]
            TM, TM_t = TM2[p], TM2_t[p]
            WCs, WCs_t, RKs, RKs_t = WCs2[p], WCs2_t[p], RKs2[p], RKs2_t[p]
            sg, sg_t = sg2[p], sg2_t[p]
            Ybuf, Y_t = Ybuf2[p], Y2_t[p]
            kb.dma("sp", xin[:], xv[:, :, cs], x_t, writes=[x_t])
            if j > 0:
                kb.op("pool", lambda: G.tensor_copy(out=hn[:, :, 0:1], in_=hn[:, :, TA:TA + 1]), reads=[hn_t], writes=[hn_t])
            for dc in range(8):
                kb.op("act", lambda dc=dc: S.activation(out=sqx[:, dc, :], in_=xin[:, dc, :], func=AF.Square), reads=[x_t], writes=[sqx_t])
            for dc in range(8):
                kb.op("pe", lambda dc=dc: P.matmul(PR[:, 0:TA], lhsT=onesb[:], rhs=sqx[:, dc, :], start=(dc == 0), stop=(dc == 7)),
                      reads=[sqx_t, ctk], writes=[PR_t])
            kb.op("act", lambda: S.activation(out=rstd[:], in_=PR[:, 0:TA], func=AF.Ln, bias=1e-6, scale=1.0), reads=[PR_t], writes=[rstd_t])
            kb.op("act", lambda: S.activation(out=rstd[:], in_=rstd[:], func=AF.Exp, scale=-0.5), reads=[rstd_t], writes=[rstd_t])
            for dc in range(8):
                kb.op("dve", lambda dc=dc: V.scalar_tensor_tensor(out=hn[:, dc, 1:TA + 1], in0=xin[:, dc, :], scalar=vA[:, dc:dc + 1],
                                                                 in1=rstd[:], op0=ALU.mult, op1=ALU.mult),
                      reads=[x_t, rstd_t, ctk], writes=[hn_t])
            for i in range(6):
                if i < 4:
                    dst, dst_t = ((r_, r_t), (k_, k_t), (v_, v_t), (sg, sg_t))[i]
                    for hc in range(2):
                        for dc in range(8):
                            kb.op("pe", lambda dc=dc, hc=hc, i=i: P.matmul(
                                PR[:, 0:TA], lhsT=W[:, dc, i, 0, hc * 128:hc * 128 + 128], rhs=hn[:, dc, 1:TA + 1],
                                start=(dc == 0), stop=False), reads=[W_t, hn_t], writes=[PR_t])
                            kb.op("pe", lambda dc=dc, hc=hc, i=i: P.matmul(
                                PR[:, 0:TA], lhsT=W[:, dc, i, 1, hc * 128:hc * 128 + 128], rhs=hn[:, dc, 0:TA],
                                start=False, stop=(dc == 7)), reads=[W_t, hn_t], writes=[PR_t])
                        if i == 3:
                            kb.op("act", lambda hc=hc, dst=dst: S.activation(out=dst[:, hc, :], in_=PR[:, 0:TA], func=AF.Silu),
                                  reads=[PR_t], writes=[dst_t])
                        else:
                            kb.op("dve", lambda hc=hc, dst=dst: V.tensor_copy(out=dst[:, hc, :], in_=PR[:, 0:TA]),
                                  reads=[PR_t], writes=[dst_t])
                else:
                    Wl, W2l = (W1, W2) if i == 4 else (A1, A2)
                    for dc in range(8):
                        kb.op("pe", lambda dc=dc, Wl=Wl: P.matmul(PR[0:64, 0:TA], lhsT=Wl[:, dc, 0, :], rhs=hn[:, dc, 1:TA + 1],
                                                                 start=(dc == 0), stop=False), reads=[W_t, hn_t], writes=[PR_t])
                        kb.op("pe", lambda dc=dc, Wl=Wl: P.matmul(PR[0:64, 0:TA], lhsT=Wl[:, dc, 1, :], rhs=hn[:, dc, 0:TA],
                                                                 start=False, stop=(dc == 7)), reads=[W_t, hn_t], writes=[PR_t])
                    kb.op("act", lambda i=i: S.activation(out=th[:], in_=PR[0:64, 0:TA], func=(AF.Tanh if i == 4 else AF.Copy)),
                          reads=[PR_t], writes=[th_t])
                    for hc in range(2):
                        kb.op("pe", lambda hc=hc, W2l=W2l: P.matmul(PR[:, 0:TA], lhsT=W2l[:, hc * 128:(hc + 1) * 128], rhs=th[:],
                                                                    start=True, stop=True), reads=[W_t, th_t], writes=[PR_t])
                        if i == 4:
                            kb.op("act", lambda hc=hc: S.activation(out=nlw[:, hc, :], in_=PR[:, 0:TA], func=AF.Exp, scale=-1.0,
                                                                    bias=vC[:, hc, NW0:NW0 + 1]), reads=[PR_t, ctk], writes=[nlw_t])
                            kb.op("act", lambda hc=hc: S.activation(out=nlw[:, hc, :], in_=nlw[:, hc, :], func=AF.Ln, scale=1.0, bias=1.0),
                                  reads=[nlw_t], writes=[nlw_t])
                            kb.op("act", lambda hc=hc: S.activation(out=nlw[:, hc, :], in_=nlw[:, hc, :], func=AF.Exp, scale=-1.0, bias=-0.5),
                                  reads=[nlw_t], writes=[nlw_t])
                        else:
                            kb.op("act", lambda hc=hc: S.activation(out=a_[:, hc, :], in_=PR[:, 0:TA], func=AF.Sigmoid, scale=1.0,
                                                                    bias=vC[:, hc, A0:A0 + 1]), reads=[PR_t, ctk], writes=[a_t])
            for hc in range(2):
                kb.op("dve", lambda hc=hc: V.tensor_scalar(out=kk[:, hc, :], in0=k_[:, hc, :], scalar1=vC[:, hc, KK_:KK_ + 1], scalar2=None,
                                                           op0=ALU.mult), reads=[k_t, ctk], writes=[kk_t])
                kb.op("act", lambda hc=hc: S.activation(out=sqk[:, hc, :], in_=kk[:, hc, :], func=AF.Square), reads=[kk_t], writes=[sqk_t])
                kb.op("pe", lambda hc=hc: P.matmul(PR[:, 0:TA], lhsT=blk[:], rhs=sqk[:, hc, :], start=True, stop=True),
                      reads=[sqk_t, ctk], writes=[PR_t])
                kb.op("act", lambda hc=hc: S.activation(out=rn[:, hc, :], in_=PR[:, 0:TA], func=AF.Ln, bias=1e-24, scale=1.0),
                      reads=[PR_t], writes=[rn_t])
                kb.op("act", lambda hc=hc: S.activation(out=rn[:, hc, :], in_=rn[:, hc, :], func=AF.Exp, scale=-0.5), reads=[rn_t], writes=[rn_t])
            kb.op("dve", lambda: V.tensor_tensor(out=kk[:], in0=kk[:], in1=rn[:], op=ALU.mult), reads=[kk_t, rn_t], writes=[kk_t])
            for hc in range(2):
                kb.op("dve", lambda hc=hc: V.tensor_scalar(out=t1[:, hc, :], in0=a_[:, hc, :], scalar1=vC[:, hc, KA:KA + 1],
                                                           scalar2=vC[:, hc, OMKA:OMKA + 1], op0=ALU.mult, op1=ALU.add),
                      reads=[a_t, ctk], writes=[t1_t])
            kb.op("dve", lambda: V.tensor_tensor(out=t1[:], in0=t1[:], in1=k_[:], op=ALU.mult), reads=[t1_t, k_t], writes=[t1_t])
            kb.op("pool", lambda: G.tensor_tensor(out=a_[:], in0=a_[:], in1=kk[:], op=ALU.mult), reads=[a_t, kk_t], writes=[a_t])
            for hc in range(2):
                kb.op("dve", lambda hc=hc: V.scalar_tensor_tensor(out=prod[:, hc, :], in0=r_[:, hc, :], scalar=vC[:, hc, RK:RK + 1],
                                                                 in1=t1[:, hc, :], op0=ALU.mult, op1=ALU.mult),
                      reads=[r_t, t1_t, ctk], writes=[prod_t])
            def v3(t):
                return t[:].rearrange("p a (c t) -> p (a c) t", t=64)
            src, src_t = nlw, nlw_t
            pp = [(CA, CA_t), (CB, CB_t)]
            for li, sft in enumerate((1, 2, 4, 8, 16, 32)):
                dst, dst_t = pp[li % 2]
                kb.op("pool", lambda src=src, dst=dst, sft=sft: G.tensor_tensor(out=v3(dst)[:, :, sft:], in0=v3(src)[:, :, sft:],
                                                                               in1=v3(src)[:, :, 0:64 - sft], op=ALU.add),
                      reads=[src_t], writes=[dst_t])
                kb.op("pool", lambda src=src, dst=dst, sft=sft: G.tensor_copy(out=v3(dst)[:, :, 0:sft], in_=v3(src)[:, :, 0:sft]),
                      reads=[src_t], writes=[dst_t])
                src, src_t = dst, dst_t
            cn, cn_t = src, src_t
            assert cn is CB
            kb.op("pool", lambda: G.tensor_tensor(out=nlw[:], in0=cn[:], in1=nlw[:], op=ALU.subtract), reads=[cn_t, nlw_t], writes=[nlw_t])
            kb.op("act", lambda: S.activation(out=CA[:], in_=cn[:], func=AF.Exp, scale=-1.0), reads=[cn_t], writes=[CA_t])
            kb.op("act", lambda: S.activation(out=nlw[:], in_=nlw[:], func=AF.Exp, scale=-1.0), reads=[nlw_t], writes=[nlw_t])
            kb.op("act", lambda: S.activation(out=CB[:], in_=cn[:], func=AF.Exp, scale=1.0), reads=[cn_t], writes=[CB_t])
            eneg, eneg_t, enegx, enegx_t, epos, epos_t = CA, CA_t, nlw, nlw_t, CB, CB_t

            def c3(t, hc):
                return t[:, hc, :].rearrange("p (c t) -> p c t", t=64)
            for hc in range(2):
                kb.op("dve", lambda hc=hc: V.tensor_tensor(out=AR[:, hc, :, 0, :], in0=c3(kk, hc), in1=c3(enegx, hc), op=ALU.mult),
                      reads=[kk_t, enegx_t], writes=[AR_t])
                kb.op("pool", lambda hc=hc: G.tensor_tensor(out=AR[:, hc, :, 1, :], in0=c3(r_, hc), in1=c3(eneg, hc), op=ALU.mult),
                      reads=[r_t, eneg_t], writes=[AR_t])
                kb.op("dve", lambda hc=hc: V.scalar_tensor_tensor(out=BK[:, hc, :, 0, :], in0=c3(a_, hc), scalar=-1.0, in1=c3(epos, hc),
                                                                 op0=ALU.mult, op1=ALU.mult), reads=[a_t, epos_t], writes=[BK_t])
                kb.op("pool", lambda hc=hc: G.tensor_tensor(out=BK[:, hc, :, 1, :], in0=c3(t1, hc), in1=c3(epos, hc), op=ALU.mult),
                      reads=[t1_t, epos_t], writes=[BK_t])
                for hh in range(2):
                    h = 2 * hc + hh
                    kb.op("dve", lambda hc=hc, hh=hh, h=h: V.tensor_copy(out=WCs[:, h, :], in_=c3(eneg, hc)[hh * 64:(hh + 1) * 64, :, 63]),
                          reads=[eneg_t], writes=[WCs_t])
                    kb.op("dve", lambda hc=hc, hh=hh, h=h: V.tensor_copy(out=ARh[:, h, :, :, :].rearrange("p c a t -> p (c a t)"),
                                                                         in_=AR[hh * 64:(hh + 1) * 64, hc, :, :, :].rearrange("p c a t -> p (c a t)")),
                          reads=[AR_t], writes=[ARh_t])
                    kb.op("dve", lambda hc=hc, hh=hh, h=h: V.tensor_copy(out=BKh[:, h, :, :, :].rearrange("p c a t -> p (c a t)"),
                                                                         in_=BK[hh * 64:(hh + 1) * 64, hc, :, :, :].rearrange("p c a t -> p (c a t)")),
                          reads=[BK_t], writes=[BKh_t])
            srcs = [(lambda hc, c: AR[:, hc, c, 0, :], AR_t), (lambda hc, c: BK[:, hc, c, 0, :], BK_t),
                    (lambda hc, c: BK[:, hc, c, 1, :], BK_t), (lambda hc, c: v_[:, hc, c * 64:(c + 1) * 64], v_t)]
            ne = 0
            for q in range(4):
                fn, ft = srcs[q]
                for c0 in range(0, NCH, 2):
                    for cc in range(2):
                        for hc in range(2):
                            kb.op("pe", lambda fn=fn, c0=c0, cc=cc, hc=hc: P.transpose(
                                TINb[0:64, cc * 256 + hc * 128:cc * 256 + hc * 128 + 128], fn(hc, c0 + cc), identh[:]),
                                reads=[ft, ctk], writes=[TIN_t])
                    e = "act" if ne % 2 == 0 else "dve"
                    ne += 1
                    dstv = TM[q][:, c0:c0 + 2, :, :].rearrange("p c h n -> p (c h n)")
                    if e == "act":
                        kb.op("act", lambda dstv=dstv: S.copy(out=dstv, in_=TINb[0:64, 0:512]), reads=[TIN_t], writes=[TM_t[q]])
                    else:
                        kb.op("dve", lambda dstv=dstv: V.tensor_copy(out=dstv, in_=TINb[0:64, 0:512]), reads=[TIN_t], writes=[TM_t[q]])
            for c in range(NCH):
                for hc in range(2):
                    kb.op("pe", lambda c=c, hc=hc: P.matmul(TIN[0:64, c * 4 + hc * 2:c * 4 + hc * 2 + 2], lhsT=prod[:, hc, c * 64:(c + 1) * 64],
                                                            rhs=bind[:], start=True, stop=True), reads=[prod_t, ctk], writes=[TIN_t])
            kb.op("dve", lambda: V.tensor_copy(out=RKs[:].rearrange("p c h -> p (c h)"), in_=TIN[0:64, 0:NCH * 4]), reads=[TIN_t], writes=[RKs_t])

        hseq = [0]

        def chunk_gen(j, c, k):
            p = j % 2
            ARh, ARh_t, BKh, BKh_t = ARh2[p], ARh2_t[p], BKh2[p], BKh2_t[p]
            TM, TM_t = TM2[p], TM2_t[p]
            WCs, WCs_t = WCs2[p], WCs2_t[p]
            Ybuf, Y_t = Ybuf2[p], Y2_t[p]
            ATp, ATp_t, APp, APp_t, SQp, SQp_t = ATpk[k], ATpk_t[k], APpk[k], APpk_t[k], SQpk[k], SQpk_t[k]
            ATs, ATs_t, Ls, Ls_t = ATsk[k], ATsk_t[k], Lsk[k], Lsk_t[k]
            Zb, Zb_t, LB, LB_t = Zbk[k], Zbk_t[k], LBk[k], LBk_t[k]
            RHs, RHs_t, MTs, MTs_t = RHsk[k], RHsk_t[k], MTsk[k], MTsk_t[k]
            ARhF, BKhF, TMF, ATsF, LsF, ZbF, LBF, RHsF, MTsF = ARh2F[p], BKh2F[p], TM2F[p], ATskF[k], LskF[k], ZbkF[k], LBkF[k], RHskF[k], MTskF[k]
            def opnd(h):
                return h // 2, 64 * (h % 2)
            for half in range(2):
                for hl in range(2):
                    h = 2 * half + hl
                    hc, pb = opnd(h)
                    rhs = ARhF[:, h, c, :, :].rearrange("p a t -> p (a t)")
                    kb.op("pe", lambda hl=hl, h=h, rhs=rhs: P.matmul(ATp[0:64, hl * 256:hl * 256 + 128], lhsT=BKhF[:, h, c, 0, :],
                                                                             rhs=rhs, start=True, stop=True), reads=[ARh_t, BKh_t], writes=[ATp_t])
                    kb.op("pe", lambda hl=hl, h=h, rhs=rhs: P.matmul(ATp[0:64, hl * 256 + 128:hl * 256 + 256], lhsT=BKhF[:, h, c, 1, :],
                                                                             rhs=rhs, start=True, stop=True), reads=[ARh_t, BKh_t], writes=[ATp_t])
                kb.op("dve", lambda half=half: V.tensor_tensor(
                    out=ATs[:, 2 * half:2 * half + 2, :, :].rearrange("p h q t -> p (h q t)"), in0=ATp[0:64, :],
                    in1=mAT[:, 2 * half:2 * half + 2, :, :].rearrange("p h q t -> p (h q t)"), op=ALU.mult),
                    reads=[ATp_t, ctk], writes=[ATs_t])
                yield
            for h in range(4):
                hc, pb = opnd(h)
                kb.op("pe", lambda h=h, hc=hc, pb=pb: P.matmul(SQp[0:64, h * 64:(h + 1) * 64], lhsT=ARhF[:, h, c, 0, :],
                                                               rhs=BKhF[:, h, c, 0, :], start=True, stop=True),
                      reads=[ARh_t, BKh_t], writes=[SQp_t])
            kb.op("dve", lambda: V.tensor_tensor(out=Ls[:].rearrange("p h s -> p (h s)"), in0=SQp[0:64, 0:256],
                                                 in1=mL[:].rearrange("p h s -> p (h s)"), op=ALU.mult), reads=[SQp_t, ctk], writes=[Ls_t])
            yield
            for h in range(4):
                kb.op("pe", lambda h=h: P.matmul(APp[0:64, h * 128 + 64:h * 128 + 128], lhsT=ATsF[:, h, 2, :], rhs=TMF[TV_][:, c, h, :],
                                                 start=(h == 0), stop=False, skip_group_check=True), reads=[ATs_t, TM_t[TV_]], writes=[APp_t])
                kb.op("pe", lambda h=h: P.matmul(APp[0:64, h * 128:h * 128 + 64], lhsT=identbF[:], rhs=TMF[TA_][:, c, h, :],
                                                 start=False, stop=False, skip_group_check=True), reads=[ctk, TM_t[TA_]], writes=[APp_t])
            zc = 0
            kb.op("act", lambda: S.copy(out=Zb[0][:].rearrange("p h x -> p (h x)"), in_=APp[0:64, :]), reads=[APp_t], writes=[Zb_t[0]])
            yield
            Bm = lambda h: ATsF[:, h, 0, :]
            Lm = lambda h: LsF[:, h, :]
            Bm_t, Lm_t = ATs_t, Ls_t
            for lvl in range(6):
                for h in range(4):
                    kb.op("pe", lambda h=h, Bm=Bm, zc=zc: P.matmul(APp[0:64, h * 128:(h + 1) * 128], lhsT=Bm(h), rhs=ZbF[zc][:, h, :],
                                                                   start=False, stop=(lvl == 5), skip_group_check=True), reads=[Bm_t, Zb_t[zc]], writes=[APp_t])
                if lvl < 5:
                    for h in range(4):
                        kb.op("pe", lambda h=h, Bm=Bm, Lm=Lm: P.matmul(SQp[0:64, h * 128:h * 128 + 64], lhsT=Bm(h), rhs=Lm(h), start=True, stop=True),
                              reads=[Bm_t, Lm_t], writes=[SQp_t])
                        kb.op("pe", lambda h=h, Bm=Bm, Lm=Lm: P.matmul(SQp[0:64, h * 128 + 64:h * 128 + 128], lhsT=Lm(h), rhs=Bm(h), start=True, stop=True),
                              reads=[Bm_t, Lm_t], writes=[SQp_t])
                kb.op("dve", lambda zc=zc: V.tensor_copy(out=Zb[1 - zc][:].rearrange("p h x -> p (h x)"), in_=APp[0:64, :]),
                      reads=[APp_t], writes=[Zb_t[1 - zc]])
                yield
                zc = 1 - zc
                if lvl < 5:
                    nb = lvl % 2
                    kb.op("act", lambda nb=nb: S.copy(out=LB[nb][:].rearrange("p h q s -> p (h q s)"), in_=SQp[0:64, :]),
                          reads=[SQp_t], writes=[LB_t[nb]])
                    yield
                    Lm = lambda h, nb=nb: LBF[nb][:, h, 0, :]
                    Bm = lambda h, nb=nb: LBF[nb][:, h, 1, :]
                    Bm_t = Lm_t = LB_t[nb]
            Zf, Zf_t = Zb[zc], Zb_t[zc]
            ZfF = ZbF[zc]
            for h in range(4):
                kb.op("pe", lambda h=h: P.matmul(ATp[0:64, h * 64:(h + 1) * 64], lhsT=ZfF[:, h, 0:64], rhs=ATsF[:, h, 1, :], start=True, stop=True),
                      reads=[Zf_t, ATs_t], writes=[ATp_t])
            kb.op("dve", lambda: V.tensor_tensor(out=RHs[:], in0=ATp[0:64, 0:256].rearrange("p (h t) -> p h t", h=4), in1=ARh[:, :, c, 1, :], op=ALU.add),
                  reads=[ATp_t, ARh_t], writes=[RHs_t])
            yield
            for h in range(4):
                kb.op("pe", lambda h=h: P.matmul(ATp[0:64, 256 + h * 64:256 + (h + 1) * 64], lhsT=ZfF[:, h, 0:64], rhs=TMF[TB_][:, c, h, :], start=True, stop=False),
                      reads=[Zf_t, TM_t[TB_]], writes=[ATp_t])
                kb.op("pe", lambda h=h: P.matmul(ATp[0:64, 256 + h * 64:256 + (h + 1) * 64], lhsT=identbF[:], rhs=identbF[:], start=False, stop=True),
                      reads=[ctk], writes=[ATp_t])
            kb.op("act", lambda: S.copy(out=MTs[:].rearrange("p h n -> p (h n)"), in_=ATp[0:64, 256:512]), reads=[ATp_t], writes=[MTs_t])
            yield
            hcur, hnew = hstate[0], 1 - hstate[0]
            while hseq[0] != j * NCH + c:
                yield
            hcur, hnew = hstate[0], 1 - hstate[0]
            for h in range(4):
                o = SQp[0:64, h * 64:(h + 1) * 64]
                kb.op("pe", lambda h=h, o=o: P.matmul(o, lhsT=RHsF[:, h, :], rhs=HsF[hcur][:, h, :], start=True, stop=False),
                      reads=[RHs_t, Hs_t[hcur]], writes=[SQp_t])
                kb.op("pe", lambda h=h, o=o: P.matmul(o, lhsT=ATsF[:, h, 1, :], rhs=ZfF[:, h, 64:128], start=False, stop=False),
                      reads=[ATs_t, Zf_t], writes=[SQp_t])
                kb.op("pe", lambda h=h, o=o: P.matmul(o, lhsT=ATsF[:, h, 3, :], rhs=TMF[TV_][:, c, h, :], start=False, stop=True),
                      reads=[ATs_t, TM_t[TV_]], writes=[SQp_t])
            kb.op("act", lambda: S.copy(out=Ybuf[:, c, :, :].rearrange("p h v -> p (h v)"), in_=SQp[0:64, 0:256]), reads=[SQp_t], writes=[Y_t])
            yield
            for h in range(4):
                o = SQp[0:64, 256 + h * 64:256 + (h + 1) * 64]
                kb.op("pe", lambda h=h, o=o: P.matmul(o, lhsT=MTsF[:, h, :], rhs=HsF[hcur][:, h, :], start=True, stop=False),
                      reads=[MTs_t, Hs_t[hcur]], writes=[SQp_t])
                kb.op("pe", lambda h=h, o=o: P.matmul(o, lhsT=TMF[TB_][:, c, h, :], rhs=ZfF[:, h, 64:128], start=False, stop=False),
                      reads=[TM_t[TB_], Zf_t], writes=[SQp_t])
                kb.op("pe", lambda h=h, o=o: P.matmul(o, lhsT=TMF[TK_][:, c, h, :], rhs=TMF[TV_][:, c, h, :], start=False, stop=True),
                      reads=[TM_t[TK_], TM_t[TV_]], writes=[SQp_t])
            kb.op("dve", lambda: V.tensor_tensor(out=Hs[hnew][:], in0=SQp[0:64, 256:512].rearrange("p (h v) -> p h v", h=4),
                                                 in1=WCs[:, :, c:c + 1].broadcast_to([64, 4, 64]), op=ALU.mult),
                  reads=[SQp_t, WCs_t], writes=[Hs_t[hnew]])
            yield
            hstate[0] = hnew
            hseq[0] += 1

        def post9(j):
            p = j % 2
            cs = slice(j * TA, (j + 1) * TA)
            ARh, ARh_t, BKh, BKh_t = ARh2[p], ARh2_t[p], BKh2[p], BKh2_t[p]
            TM, TM_t = TM2[p], TM2_t[p]
            WCs, WCs_t, RKs, RKs_t = WCs2[p], WCs2_t[p], RKs2[p], RKs2_t[p]
            sg, sg_t = sg2[p], sg2_t[p]
            Ybuf, Y_t = Ybuf2[p], Y2_t[p]
            Yv = Ybuf[:].rearrange("p c h v -> p (c h) v")
            W1b = W1s[:].rearrange("p (c h v) -> p c h v", c=NCH, h=4)
            W2b = W2s[:].rearrange("p (c h v) -> p c h v", c=NCH, h=4)
            W1_t, W2_t = W1s_t, W2s_t
            W1v = W1b.rearrange("p c h v -> p (c h) v")
            W2v = W2b.rearrange("p c h v -> p (c h) v")
            kb.op("dve", lambda: V.tensor_reduce(out=st1[:], in_=Yv, axis=AX.X, op=ALU.add), reads=[Y_t], writes=[st_t])
            kb.op("act", lambda: S.activation(out=W1b.rearrange("p c h v -> p (c h v)"), in_=Ybuf[:].rearrange("p c h v -> p (c h v)"), func=AF.Square),
                  reads=[Y_t], writes=[W1_t])
            kb.op("dve", lambda: V.tensor_reduce(out=st2[:], in_=W1v, axis=AX.X, op=ALU.add), reads=[W1_t], writes=[st_t])
            kb.op("dve", lambda: V.tensor_scalar(out=st1[:], in0=st1[:], scalar1=1.0 / 64, scalar2=None, op0=ALU.mult), reads=[st_t], writes=[st_t])
            kb.op("dve", lambda: V.tensor_tensor(out=st3[:], in0=st1[:], in1=st1[:], op=ALU.mult), reads=[st_t], writes=[st_t])
            kb.op("dve", lambda: V.tensor_scalar(out=st2[:], in0=st2[:], scalar1=1.0 / 64, scalar2=None, op0=ALU.mult), reads=[st_t], writes=[st_t])
            kb.op("dve", lambda: V.tensor_tensor(out=st2[:], in0=st2[:], in1=st3[:], op=ALU.subtract), reads=[st_t], writes=[st_t])
            kb.op("dve", lambda: V.tensor_scalar(out=st2[:], in0=st2[:], scalar1=0.0, scalar2=None, op0=ALU.max), reads=[st_t], writes=[st_t])
            kb.op("act", lambda: S.activation(out=st2[:], in_=st2[:], func=AF.Ln, bias=GN_EPS, scale=1.0), reads=[st_t], writes=[st_t])
            kb.op("act", lambda: S.activation(out=st2[:], in_=st2[:], func=AF.Exp, scale=-0.5), reads=[st_t], writes=[st_t])
            bc = lambda t: t[:].unsqueeze(2).broadcast_to([64, NCH * 4, 64])
            kb.op("dve", lambda: V.tensor_tensor(out=W1v, in0=Yv, in1=bc(st1), op=ALU.subtract), reads=[Y_t, st_t], writes=[W1_t])
            kb.op("dve", lambda: V.tensor_tensor(out=W1v, in0=W1v, in1=bc(st2), op=ALU.mult), reads=[st_t], writes=[W1_t])
            lw_b = lnw[:, 0:256].unsqueeze(1).broadcast_to([64, NCH, 256])
            lb_b = lnw[:, 256:512].unsqueeze(1).broadcast_to([64, NCH, 256])
            W1c = W1b.rearrange("p c h v -> p c (h v)")
            W2c = W2b.rearrange("p c h v -> p c (h v)")
            kb.op("pool", lambda: G.tensor_tensor(out=W1c, in0=W1c, in1=lw_b, op=ALU.mult), reads=[ctk], writes=[W1_t])
            kb.op("pool", lambda: G.tensor_tensor(out=W1c, in0=W1c, in1=lb_b, op=ALU.add), reads=[ctk], writes=[W1_t])
            kb.op("dve", lambda: V.tensor_tensor(out=W2v, in0=TM[TV_][:].rearrange("p c h v -> p (c h) v"),
                                                 in1=RKs[:].rearrange("p c h -> p (c h)").unsqueeze(2).broadcast_to([64, NCH * 4, 64]), op=ALU.mult),
                  reads=[TM_t[TV_], RKs_t], writes=[W2_t])
            kb.op("dve", lambda: V.tensor_tensor(out=W1v, in0=W1v, in1=W2v, op=ALU.add), reads=[W2_t], writes=[W1_t])
            for hc in range(2):
                for c in range(NCH):
                    kb.op("pe", lambda hc=hc, c=c: P.transpose(PR[:, c * 64:(c + 1) * 64], W1b[:, c, 2 * hc:2 * hc + 2, :].rearrange("p h v -> p (h v)"),
                                                               ident[0:64, 0:64]), reads=[W1_t, ctk], writes=[PR_t])
                kb.op("dve", lambda hc=hc: V.tensor_tensor(out=zout[:, hc, :], in0=PR[:, 0:TA], in1=sg[:, hc, :], op=ALU.mult),
                      reads=[PR_t, sg_t], writes=[zo_t])
            kb.dma("sp", zT.rearrange("(c p) t -> p c t", p=128)[:, :, cs], zout[:], zo_t, reads=[zo_t])

        coop = Coop(kb)
        prep(0)
        if ntile > 1:
            prep(1)
        pending = [(j, c) for j in range(ntile) for c in range(NCH)]
        slots = [None, None]
        sid = [None, None]
        idx = 0
        while idx < len(pending) or slots[0] is not None or slots[1] is not None:
            for k in range(2):
                if slots[k] is None and idx < len(pending):
                    jn, cn = pending[idx]
                    other = sid[1 - k]
                    if cn == 0 and jn > 1:
                        coop.finish()
                    slots[k] = chunk_gen(jn, cn, k)
                    sid[k] = (jn, cn)
                    idx += 1
                if slots[k] is not None:
                    try:
                        next(slots[k])
                    except StopIteration:
                        jj, cc = sid[k]
                        slots[k] = None
                        sid[k] = None
                        if cc == NCH - 1:
                            coop.finish()

                            def filler(jj=jj):
                                post9(jj)
                                if jj + 2 < ntile:
                                    prep(jj + 2)
                            coop.start(filler)
                    coop.step(FILL)
        coop.finish()
        kb.finish([zo_t] + dbg_tok)
    return nc


def rwkv_maps(x, I, NT):
    maps = []
    Wfull = I["rw_w_in"][0].reshape(D, 4, 1024)
    vecA = np.concatenate([I["rw_norm"][0].reshape(8, 128).T] + [I["rw_mu"][0][i].reshape(8, 128).T for i in range(6)], axis=1)
    ident = np.eye(128, dtype=np.float32)
    for c in range(8):
        b, g = c // 4, c % 4
        cs = slice(g * 256, (g + 1) * 256)
        vc = np.zeros((128, 2, 8), np.float32)
        for vi, nm in enumerate(["rw_w0", "rw_a0", "rw_k_k", "rw_k_a"]):
            vc[:, :, vi] = I[nm][0][cs].reshape(2, 128).T
        vc[:, :, 4] = I["rw_r_k"][0].reshape(1024)[cs].reshape(2, 128).T
        maps.append(dict(
            xT=np.ascontiguousarray(x[b, :NT].T),
            Wc=np.ascontiguousarray(Wfull[:, :, cs].reshape(D, 1024)),
            w1=I["rw_w1"][0], a1=I["rw_a1"][0],
            w2c=np.ascontiguousarray(I["rw_w2"][0][:, cs]), a2c=np.ascontiguousarray(I["rw_a2"][0][:, cs]),
            vecA=np.ascontiguousarray(vecA.astype(np.float32)), vecC=np.ascontiguousarray(vc.reshape(128, 16)),
            lnwb=np.ascontiguousarray(np.concatenate([I["rw_ln_w"][0][cs], I["rw_ln_b"][0][cs]])[None, :]),
            ident=ident))
    return maps


LAM_INIT = 0.8 - 0.6 * math.exp(-0.3 * 1)


def build_attn(NT):
    nc = new_nc()
    hnT = dram_in(nc, "hnT", [D, NT], BF16)
    Wd = dram_in(nc, "Wd", [D, 4 * 256])
    sub = dram_in(nc, "sub", [1, 128])
    lqk = dram_in(nc, "lqk", [1, 256])
    invf = dram_in(nc, "invf", [128, 1])
    pos0 = dram_in(nc, "pos0", [1, TT])
    cosd = dram_in(nc, "cosd", [128, NT])
    sind = dram_in(nc, "sind", [128, NT])
    identd = dram_in(nc, "ident", [128, 128])
    zT = dram_out(nc, "zT", [256, NT], BF16)
    ntile = NT // TT
    nblk = NT // 128
    PI = math.pi
    with ExitStack() as es:
        kb = KB(nc, es)
        V, S, G, P = nc.vector, nc.scalar, nc.gpsimd, nc.tensor
        ctk = kb.tok("cst")
        ident = kb.sb("ident_sb", [128, 128])
        kb.dma("sp", ident[:], identd[:, :], ctk, writes=[ctk])
        tri = kb.sb("tri", [128, 128], BF16)
        kb.op("pool", lambda: G.memset(tri[:], 1.0), writes=[ctk])
        kb.op("pool", lambda: G.affine_select(out=tri[:], in_=tri[:], pattern=[[1, 128]], compare_op=ALU.is_ge, fill=0.0, base=0,
                                              channel_multiplier=-1), writes=[ctk])
        subb = kb.sb("subb", [128, 128])
        kb.dma("sp", subb[:], sub.partition_broadcast(128), ctk, writes=[ctk])
        kb.op("dve", lambda: V.tensor_scalar(out=subb[:], in0=subb[:], scalar1=1.0 - LAM_INIT, scalar2=None, op0=ALU.mult), reads=[ctk], writes=[ctk])
        lq = kb.sb("lq", [128, 256])
        kb.dma("sp", lq[:], lqk.partition_broadcast(128), ctk, writes=[ctk])
        lam = kb.sb("lam", [128, 4])
        kb.op("dve", lambda: V.tensor_tensor(out=lq[:, 0:64], in0=lq[:, 0:64], in1=lq[:, 64:128], op=ALU.mult), reads=[ctk], writes=[ctk])
        kb.op("dve", lambda: V.tensor_tensor(out=lq[:, 128:192], in0=lq[:, 128:192], in1=lq[:, 192:256], op=ALU.mult), reads=[ctk], writes=[ctk])
        kb.op("dve", lambda: V.tensor_reduce(out=lam[:, 0:1], in_=lq[:, 0:64], axis=AX.X, op=ALU.add), reads=[ctk], writes=[ctk])
        kb.op("dve", lambda: V.tensor_reduce(out=lam[:, 1:2], in_=lq[:, 128:192], axis=AX.X, op=ALU.add), reads=[ctk], writes=[ctk])
        kb.op("act", lambda: S.activation(out=lam[:, 0:2], in_=lam[:, 0:2], func=AF.Exp), reads=[ctk], writes=[ctk])
        kb.op("dve", lambda: V.tensor_tensor(out=lam[:, 2:3], in0=lam[:, 1:2], in1=lam[:, 0:1], op=ALU.subtract), reads=[ctk], writes=[ctk])
        kb.op("dve", lambda: V.tensor_scalar(out=lam[:, 3:4], in0=lam[:, 2:3], scalar1=-LAM_INIT, scalar2=None, op0=ALU.add), reads=[ctk], writes=[ctk])
        ivf = kb.sb("ivf", [128, 1])
        kb.dma("sp", ivf[:], invf[:, :], ctk, writes=[ctk])
        ngpi = kb.sb("ngpi", [128, 1])
        kb.op("pool", lambda: G.memset(ngpi[:], -PI), writes=[ctk])

        QT2 = [kb.sb("QT%d" % i, [128, NT], BF16) for i in range(2)]
        QT_t = kb.tok("QT")
        for i in range(2):
            kb.op("pool", lambda i=i: G.memset(QT2[i][:], 0.0), writes=[QT_t])
        KT = kb.sb("KT", [128, NT], BF16); KT_t = kb.tok("KT")
        sgd = nc.dram_tensor("sgd", [2, 128, NT], BF16).ap()
        sgd_t = kb.tok("sgd")
        SGt = [kb.sb("SGt%d" % i, [128, TT], BF16) for i in range(2)]
        SGt_t = kb.toks(2, "SGt")
        sgl = [kb.sb("sgl%d" % i, [128, TT], BF16) for i in range(2)]
        sgl_t = kb.toks(2, "sgl")
        Va = kb.sb("Va", [128, nblk, 129], BF16); Va_t = kb.tok("Va")
        kb.op("pool", lambda: G.memset(Va[:], 1.0), writes=[Va_t])
        hin = [kb.sb("ahin%d" % i, [128, 8, TT], BF16) for i in range(2)]
        hin_t = kb.toks(2, "ahin")
        stg = kb.sb("astg", [128, 1024])
        stg_t = kb.tok("astg")
        Wt = {nm: kb.sb("W" + nm, [128, 8, 128], BF16) for nm in ("q", "qr", "k", "kr", "v", "g")}
        W_t = kb.tok("Wt")
        sn = kb.sb("sn", [128, TT]); cs_ = kb.sb("cs_", [128, TT]); sc_t = kb.tok("sincos")
        ta = kb.sb("ta", [128, TT]); tb = kb.sb("tb", [128, TT]); tab_t = kb.tok("tab")
        PT = [kb.sb("PT%d" % i, [128, TT], BF16) for i in range(4)]
        PT_t = kb.toks(4, "PT")
        o1 = kb.sb("o1", [128, 128]); o1_t = kb.tok("o1")
        o4 = kb.sb("o4", [128, 4, 128]); o_t = kb.tok("o4")
        on4 = kb.sb("on4", [128, 4, 128]); on_t = kb.tok("on4")
        rc8 = kb.sb("rc8", [128, 9]); ss4 = kb.sb("ss4", [128, 4]); ss_t = kb.tok("ss4")
        junk = kb.sb("junk", [128, 128])
        rc = kb.sb("rc", [128, 4]); rc_t = kb.tok("rc")
        zout = [kb.sb("azout%d" % i, [128, TT], BF16) for i in range(2)]
        zo_t = kb.toks(2, "azo")
        SB = [kb.ps("SB%d" % i, [128, 512]) for i in range(4)]
        SB_t = [kb.ptok("SB%d" % i) for i in range(4)]
        AC = [kb.ps("AC%d" % i, [128, 512]) for i in range(3)]
        AC_t = [kb.ptok("AC%d" % i) for i in range(3)]
        TP_ = kb.ps("TPp", [128, 512]); TP_t = kb.ptok("TPp")

        def acc(c, qs):
            i = c * 4 + qs
            return AC[i // 3][:, (i % 3) * 129:(i % 3) * 129 + 129], AC_t[i // 3]

        hv = hnT.rearrange("(c p) t -> p c t", p=128)
        for hh in range(2):
            for gi, nm in enumerate(("q", "k", "v", "g")):
                for dc in range(8):
                    kb.dma("sp", stg[:, 0:128], Wd[dc * 128:(dc + 1) * 128, gi * 256 + hh * 128:gi * 256 + hh * 128 + 128], stg_t, writes=[stg_t])
                    kb.op("dve", lambda nm=nm, dc=dc: V.tensor_copy(out=Wt[nm][:, dc, :], in_=stg[:, 0:128]), reads=[stg_t], writes=[W_t])
                    if nm in ("q", "k"):
                        for c in range(2):
                            kb.op("dve", lambda nm=nm, dc=dc, c=c: V.tensor_scalar(out=Wt[nm + "r"][:, dc, c * 64:c * 64 + 32], in0=stg[:, c * 64 + 32:c * 64 + 64],
                                                                                   scalar1=-1.0, scalar2=None, op0=ALU.mult), reads=[stg_t], writes=[W_t])
                            kb.op("dve", lambda nm=nm, dc=dc, c=c: V.tensor_copy(out=Wt[nm + "r"][:, dc, c * 64 + 32:c * 64 + 64], in_=stg[:, c * 64:c * 64 + 32]),
                                  reads=[stg_t], writes=[W_t])
            kb.dma("sp", hin[0][:], hv[:, :, 0:TT], hin_t[0], writes=[hin_t[0]])
            for j in range(ntile):
                s = j % 2
                cs = slice(j * TT, (j + 1) * TT)
                if j + 1 < ntile:
                    kb.dma("sp", hin[1 - s][:], hv[:, :, (j + 1) * TT:(j + 2) * TT], hin_t[1 - s], writes=[hin_t[1 - s]])
                hb, hb_t = hin[s], hin_t[s]
                kb.dma("sp", sn[:], sind[:, cs], sc_t, writes=[sc_t])
                kb.dma("sp", cs_[:], cosd[:, cs], sc_t, writes=[sc_t])
                for nm, dstT, dst_t in (("q", None, QT_t), ("k", KT, KT_t)):
                    for dc in range(8):
                        kb.op("pe", lambda nm=nm, dc=dc: P.matmul(SB[0][:], lhsT=Wt[nm][:, dc, :], rhs=hb[:, dc, :], start=(dc == 0), stop=(dc == 7)),
                              reads=[W_t, hb_t], writes=[SB_t[0]])
                    for dc in range(8):
                        kb.op("pe", lambda nm=nm, dc=dc: P.matmul(SB[1][:], lhsT=Wt[nm + "r"][:, dc, :], rhs=hb[:, dc, :], start=(dc == 0), stop=(dc == 7)),
                              reads=[W_t, hb_t], writes=[SB_t[1]])
                    kb.op("dve", lambda: V.tensor_tensor(out=ta[:], in0=SB[0][:], in1=cs_[:], op=ALU.mult), reads=[SB_t[0], sc_t], writes=[tab_t])
                    kb.op("dve", lambda: V.tensor_tensor(out=tb[:], in0=SB[1][:], in1=sn[:], op=ALU.mult), reads=[SB_t[1], sc_t], writes=[tab_t])
                    if nm == "k":
                        kb.op("dve", lambda dstT=dstT, cs=cs: V.tensor_tensor(out=dstT[:, cs], in0=ta[:], in1=tb[:], op=ALU.add), reads=[tab_t], writes=[dst_t])
                    else:
                        kb.op("dve", lambda cs=cs: V.tensor_tensor(out=QT2[0][0:64, cs], in0=ta[0:64, :], in1=tb[0:64, :], op=ALU.add), reads=[tab_t], writes=[dst_t])
                        kb.op("dve", lambda cs=cs: V.tensor_tensor(out=QT2[1][64:128, cs], in0=ta[64:128, :], in1=tb[64:128, :], op=ALU.add), reads=[tab_t], writes=[dst_t])
                for dc in range(8):
                    kb.op("pe", lambda dc=dc: P.matmul(SB[2][:], lhsT=Wt["g"][:, dc, :], rhs=hb[:, dc, :], start=(dc == 0), stop=(dc == 7)),
                          reads=[W_t, hb_t], writes=[SB_t[2]])
                kb.op("act", lambda s=s: S.activation(out=SGt[s][:], in_=SB[2][:], func=AF.Silu), reads=[SB_t[2]], writes=[SGt_t[s]])
                kb.dma("sp", sgd[hh][:, cs], SGt[s][:], SGt_t[s], reads=[SGt_t[s]], writes=[sgd_t])
                for bi in range(4):
                    for dc in range(8):
                        kb.op("pe", lambda dc=dc, bi=bi: P.matmul(SB[3][:, bi * 128:(bi + 1) * 128], lhsT=hb[:, dc, bi * 128:(bi + 1) * 128], rhs=Wt["v"][:, dc, :],
                                                                  start=(dc == 0), stop=(dc == 7)), reads=[W_t, hb_t], writes=[SB_t[3]])
                kb.op("act", lambda j=j: S.copy(out=Va[:, j * 4:j * 4 + 4, 0:128], in_=SB[3][:].rearrange("p (b v) -> p b v", b=4)), reads=[SB_t[3]], writes=[Va_t])
            gstep = [0]
            pend_tail = [None]
            for g in range(ntile):
                for a3 in range(3):
                    kb.op("dve", lambda a3=a3: V.memset(AC[a3][:], 0.0), writes=[AC_t[a3]])
                kb.dma("sp", sgl[g % 2][:], sgd[hh][:, g * TT:(g + 1) * TT], sgl_t[g % 2], reads=[sgd_t], writes=[sgl_t[g % 2]])
                steps = []
                for kbk in range(4 * g + 4):
                    for c in range(2):
                        steps.append((kbk, c, gstep[0] % 4))
                        gstep[0] += 1

                def qk(st, g=g):
                    kbk, c, bi = st
                    kb.op("pe", lambda: P.matmul(SB[bi][:], lhsT=KT[:, kbk * 128:(kbk + 1) * 128],
                                                 rhs=QT2[c][:, g * TT:(g + 1) * TT], start=True, stop=True),
                          reads=[KT_t, QT_t], writes=[SB_t[bi]])

                def ex(st, g=g):
                    kbk, c, bi = st
                    m = kbk - 4 * g
                    kb.op("act", lambda: S.activation(out=PT[bi][:], in_=SB[bi][:], func=AF.Exp, scale=0.125),
                          reads=[SB_t[bi]], writes=[PT_t[bi]])
                    if m >= 0:
                        kb.op("pool", lambda: G.tensor_tensor(out=PT[bi][:, m * 128:(m + 1) * 128], in0=PT[bi][:, m * 128:(m + 1) * 128],
                                                              in1=tri[:], op=ALU.mult), reads=[ctk], writes=[PT_t[bi]])

                def av(st, g=g):
                    kbk, c, bi = st
                    m = kbk - 4 * g
                    for qs in range(4):
                        if m > qs:
                            continue
                        ap_, at_ = acc(c, qs)
                        kb.op("pe", lambda ap_=ap_, qs=qs: P.matmul(ap_, lhsT=PT[bi][:, qs * 128:(qs + 1) * 128], rhs=Va[:, kbk, :],
                                                                    start=False, stop=False, skip_group_check=True),
                              reads=[PT_t[bi], Va_t], writes=[at_])

                LA = 3
                for i in range(min(LA, len(steps))):
                    qk(steps[i])
                for i, st in enumerate(steps):
                    ex(st)
                    if i + LA < len(steps):
                        qk(steps[i + LA])
                    av(st)
                    if i == 5 and pend_tail[0] is not None:
                        pend_tail[0]()
                        pend_tail[0] = None
                if pend_tail[0] is not None:
                    pend_tail[0]()
                    pend_tail[0] = None
                zb, zb_t = zout[g % 2], zo_t[g % 2]
                for a3 in range(3):
                    na = 3 if a3 < 2 else 2
                    kb.op("dve", lambda a3=a3, na=na: V.reciprocal(out=rc8[:, a3 * 3:a3 * 3 + na],
                                                                   in_=AC[a3][:, 0:na * 129].rearrange("p (a w) -> p a w", w=129)[:, :, 128]),
                          reads=[AC_t[a3]], writes=[rc_t])
                for qs in range(4):
                    a0, a0t = acc(0, qs)
                    a1, a1t = acc(1, qs)
                    kb.op("dve", lambda a1=a1, qs=qs: V.tensor_scalar(out=o1[:], in0=a1[:, 0:128], scalar1=rc8[:, 4 + qs:5 + qs], scalar2=lam[:, 3:4],
                                                                     op0=ALU.mult, op1=ALU.mult), reads=[a1t, rc_t, ctk], writes=[o1_t])
                    kb.op("dve", lambda a0=a0, qs=qs: V.scalar_tensor_tensor(out=o4[:, qs, :], in0=a0[:, 0:128], scalar=rc8[:, qs:qs + 1], in1=o1[:],
                                                                            op0=ALU.mult, op1=ALU.add), reads=[a0t, rc_t, o1_t], writes=[o_t])
                for qs in range(4):
                    kb.op("act", lambda qs=qs: S.activation(out=junk[:], in_=o4[:, qs, :], func=AF.Square, accum_out=ss4[:, qs:qs + 1]),
                          reads=[o_t], writes=[ss_t])
                kb.op("act", lambda: S.activation(out=ss4[:], in_=ss4[:], func=AF.Ln, scale=1.0 / 128, bias=1e-5), reads=[ss_t], writes=[ss_t])
                kb.op("act", lambda: S.activation(out=ss4[:], in_=ss4[:], func=AF.Exp, scale=-0.5), reads=[ss_t], writes=[ss_t])
                for qs in range(4):
                    kb.op("dve", lambda qs=qs: V.scalar_tensor_tensor(out=on4[:, qs, :], in0=o4[:, qs, :], scalar=ss4[:, qs:qs + 1], in1=subb[:],
                                                                      op0=ALU.mult, op1=ALU.mult), reads=[o_t, ss_t, ctk], writes=[on_t])
                def tail(g=g, zb=zb, zb_t=zb_t, hh=hh):
                    for qs in range(4):
                        kb.op("pe", lambda qs=qs: P.transpose(TP_[:, qs * 128:(qs + 1) * 128], on4[:, qs, :], ident[:]), reads=[on_t, ctk], writes=[TP_t])
                    kb.op("dve", lambda: V.tensor_tensor(out=zb[:], in0=TP_[:], in1=sgl[g % 2][:], op=ALU.mult),
                          reads=[TP_t, sgl_t[g % 2]], writes=[zb_t])
                    kb.dma("sp", zT[hh * 128:(hh + 1) * 128, g * TT:(g + 1) * TT], zb[:], zb_t, reads=[zb_t])
                pend_tail[0] = tail
            if pend_tail[0] is not None:
                pend_tail[0]()
                pend_tail[0] = None
        kb.finish(zo_t)
    return nc


def attn_maps(hn_list, I, NT):
    maps = []
    Wfull = I["da_w_in"][0].reshape(D, 4, 1024)
    ident = np.eye(128, dtype=np.float32)
    inv = (1.0 / (10000.0 ** (np.arange(0, 64, 2, dtype=np.float32) / 64))).astype(np.float32)
    invf = np.tile(inv, 4).reshape(128, 1).astype(np.float32)
    pos0 = np.arange(TT, dtype=np.float32)[None, :]
    ang = np.arange(NT, dtype=np.float32)[None, :] * invf
    cosd = np.cos(ang).astype(np.float32)
    sind = np.sin(ang).astype(np.float32)
    lqk = np.concatenate([I["da_lq1"][0], I["da_lk1"][0], I["da_lq2"][0], I["da_lk2"][0]])[None, :].astype(np.float32)
    for c in range(8):
        b, hp = c // 4, c % 4
        cs = slice(hp * 256, (hp + 1) * 256)
        maps.append(dict(hnT=np.ascontiguousarray(hn_list[b]), Wd=np.ascontiguousarray(Wfull[:, :, cs].reshape(D, 1024)),
                         sub=np.ascontiguousarray(I["da_subln"][0][None, :]), lqk=np.ascontiguousarray(lqk), invf=invf, pos0=pos0, ident=ident, cosd=cosd, sind=sind))
    return maps


def kernel(**inputs):
    I = {k: np.asarray(v) for k, v in inputs.items()}
    x, p = I["x"], I["p"]
    B, S = x.shape[0], x.shape[1]
    QT_ = S // 4
    nc = build_rwkv(S)
    res = run_bass_kernel_spmd(nc, rwkv_maps(x, I, S), core_ids=list(range(8))).results
    z0T = [np.concatenate([res[b * 4 + g]["zT"] for g in range(4)], axis=0) for b in range(B)]
    rng = [(c // 4, slice((c % 4) * QT_, (c % 4 + 1) * QT_)) for c in range(8)]
    res2 = run_post([z0T[b][:, sl] for b, sl in rng], [x[b, sl].T for b, sl in rng], [p[0, b, sl].T for b, sl in rng],
                    I["rw_w_out"][0], I["pe_w_gate"][0], I["pe_w_proj"][0], I["pe_norm"][0], I["da_norm"][0], last=False)
    hn = [np.concatenate([res2[b * 4 + q]["hnT"] for q in range(4)], axis=1) for b in range(B)]
    nc3 = build_attn(S)
    res3 = run_bass_kernel_spmd(nc3, attn_maps(hn, I, S), core_ids=list(range(8))).results
    z1T = [np.concatenate([res3[b * 4 + g]["zT"] for g in range(4)], axis=0) for b in range(B)]
    res4 = run_post([z1T[b][:, sl] for b, sl in rng], [res2[c]["h1T"] for c in range(8)], [p[1, b, sl].T for b, sl in rng],
                    I["da_w_out"][0], I["pe_w_gate"][1], I["pe_w_proj"][1], I["pe_norm"][1], I["final_norm"], last=True)
    out = np.empty((B, S, D), np.float32)
    for c, (b, sl) in enumerate(rng):
        out[b, sl, :] = res4[c]["oT"].T
    return out
```
